# Optimizing a Trainium2 kernel written in Bass

```python
import jax, jax.numpy as jnp
from jax import lax
import numpy as np

D_MODEL = 2048
BATCH = 2
SEQ = 8192
DEPTH = 2
DEC_BATCH = 4
DEC_SEQ = 8192
PAST_LEN = 128

D_MIX = 3 * D_MODEL // 2
HEAD_DIM = 64
D_SSD = D_MIX // 2
SSD_HEADS = D_SSD // HEAD_DIM
SSD_GROUPS = 4
SSD_STATE = 128
SSD_CONV = 5
SSD_CHUNK = 128
XBC = D_SSD + 2 * SSD_GROUPS * SSD_STATE
D_SC = D_MIX // 4
SC_CONV = 3
D_ATT = D_MIX // 4
ATT_SLOTS = D_ATT // HEAD_DIM
DILATION_CFG = ((128, 1), (512, 4), (2048, 16))
N_DIL = len(DILATION_CFG)
ATT_HEADS = N_DIL * ATT_SLOTS
ATT_QB = 64
ROPE_DIM = HEAD_DIM // 4
ROPE_THETA = 500000.0
PEER_HEADS = 8
PEER_NKEYS = 128
PEER_EXPERTS = PEER_NKEYS * PEER_NKEYS
PEER_TOPK = 16
PEER_DQ = 256
PEER_CHUNK = 128
EPS = 1e-6
NEG = -1e30

D_IN = D_SSD + XBC + 2 * SSD_HEADS + 3 * D_SC + 3 * ATT_HEADS * HEAD_DIM
SPLITS = (D_SSD, D_SSD + XBC, D_SSD + XBC + 2 * SSD_HEADS, D_SSD + XBC + 2 * SSD_HEADS + 3 * D_SC)

kernel_name = 'hybrid_ssd_shortconv_dilattn_peer_encoder'


def rmsnorm(x, g):
    xf = x.astype(jnp.float32)
    y = xf * lax.rsqrt(jnp.mean(xf * xf, axis=-1, keepdims=True) + EPS) * g.astype(jnp.float32)
    return y.astype(x.dtype)


def dwconv(x, w, pad):
    c = x.shape[-1]
    return lax.conv_general_dilated(x, w[:, None, :].astype(x.dtype), window_strides=(1,),
                                    padding=[(pad, pad)], dimension_numbers=('NWC', 'WIO', 'NWC'),
                                    feature_group_count=c)


def rope_tables(length):
    inv = ROPE_THETA ** (-jnp.arange(0, ROPE_DIM, 2, dtype=jnp.float32) / ROPE_DIM)
    ang = jnp.arange(length, dtype=jnp.float32)[:, None] * inv[None, :]
    return jnp.cos(ang), jnp.sin(ang)


def apply_rope(t, cos, sin):
    tf = t.astype(jnp.float32)
    half = ROPE_DIM // 2
    c = cos[None, :, None, None, :]
    s = sin[None, :, None, None, :]
    x1 = tf[..., :half]
    x2 = tf[..., half:ROPE_DIM]
    out = jnp.concatenate([x1 * c - x2 * s, x2 * c + x1 * s, tf[..., ROPE_DIM:]], axis=-1)
    return out.astype(t.dtype)


def ssd_chunked(x, a, bm, cm):
    b, L, H, P = x.shape
    G, N = bm.shape[2], bm.shape[3]
    R = H // G
    Q = SSD_CHUNK
    c = L // Q
    x = x.reshape(b, c, Q, G, R, P)
    a = a.reshape(b, c, Q, G, R)
    bm = bm.reshape(b, c, Q, G, N)
    cm = cm.reshape(b, c, Q, G, N)
    acum = jnp.cumsum(a, axis=2)
    lower = jnp.tril(jnp.ones((Q, Q), dtype=bool))
    seg = acum[:, :, :, None] - acum[:, :, None, :]
    decay = jnp.exp(jnp.where(lower[None, None, :, :, None, None], seg, -jnp.inf))
    cb = jnp.einsum('bclgn,bcsgn->bclsg', cm, bm)
    y_diag = jnp.einsum('bclsgr,bcsgrp->bclgrp', cb[..., None] * decay, x)
    decay_end = jnp.exp(acum[:, :, -1:] - acum)
    states = jnp.einsum('bclgn,bclgrp->bcgrpn', bm, x * decay_end[..., None])
    chunk_decay = jnp.exp(acum[:, :, -1])

    def step(h, inp):
        dec, st = inp
        return dec[..., None, None] * h + st, h

    h0 = jnp.zeros((b, G, R, P, N), x.dtype)
    _, h_in = lax.scan(step, h0, (jnp.moveaxis(chunk_decay, 1, 0), jnp.moveaxis(states, 1, 0)))
    h_in = jnp.moveaxis(h_in, 0, 1)
    y_off = jnp.einsum('bclgn,bcgrpn->bclgrp', cm, h_in) * jnp.exp(acum)[..., None]
    return (y_diag + y_off).reshape(b, L, H, P)


def ssd_mixer(z, xbc, dt_raw, conv_w, conv_b, dt_bias, a_log, d_skip, norm_w):
    b, L, _ = z.shape
    xbc = jax.nn.silu(dwconv(xbc, conv_w, SSD_CONV // 2) + conv_b.astype(xbc.dtype))
    xs = xbc[..., :D_SSD].reshape(b, L, SSD_HEADS, HEAD_DIM).astype(jnp.float32)
    bm = xbc[..., D_SSD:D_SSD + SSD_GROUPS * SSD_STATE].reshape(b, L, SSD_GROUPS, SSD_STATE).astype(jnp.float32)
    cm = xbc[..., D_SSD + SSD_GROUPS * SSD_STATE:].reshape(b, L, SSD_GROUPS, SSD_STATE).astype(jnp.float32)
    dt = jax.nn.softplus(dt_raw.astype(jnp.float32).reshape(b, L, 2, SSD_HEADS) + dt_bias.astype(jnp.float32))
    A = -jnp.exp(a_log.astype(jnp.float32))
    y_f = ssd_chunked(xs * dt[:, :, 0, :, None], dt[:, :, 0] * A[0], bm, cm)
    fl = lambda t: jnp.flip(t, axis=1)
    y_b = fl(ssd_chunked(fl(xs * dt[:, :, 1, :, None]), fl(dt[:, :, 1] * A[1]), fl(bm), fl(cm)))
    y = y_f + y_b + xs * d_skip.astype(jnp.float32)[:, None]
    y = y.reshape(b, L, D_SSD) * jax.nn.silu(z.astype(jnp.float32))
    yg = y.reshape(b, L, SSD_GROUPS, D_SSD // SSD_GROUPS)
    yg = yg * lax.rsqrt(jnp.mean(yg * yg, axis=-1, keepdims=True) + EPS)
    return (yg.reshape(b, L, D_SSD) * norm_w.astype(jnp.float32)).astype(z.dtype)


def shortconv_mixer(sc, conv_w, norm_w):
    bg, cg, hx = jnp.split(sc, 3, axis=-1)
    y = bg * dwconv(cg * hx, conv_w, SC_CONV // 2)
    return rmsnorm(y, norm_w)


def dilated_band_attention(q, k, v, dil, hw):
    b, S, h, dh = q.shape
    L = S // dil

    def to_sub(t):
        return jnp.swapaxes(t.reshape(b, L, dil, h, dh), 1, 2).reshape(b * dil, L, h, dh)

    def from_sub(t):
        t = t.reshape((b, dil) + t.shape[1:])
        return jnp.swapaxes(t, 1, 2).reshape((b, S) + t.shape[3:])

    qs, ks, vs = to_sub(q), to_sub(k), to_sub(v)
    nb = -(-L // ATT_QB)
    Lp = nb * ATT_QB
    kb_len = ATT_QB + 2 * hw
    qs = jnp.pad(qs, ((0, 0), (0, Lp - L), (0, 0), (0, 0))).reshape(b * dil, nb, ATT_QB, h, dh)
    pad_kv = ((0, 0), (hw, Lp - L + hw), (0, 0), (0, 0))
    kp, vp = jnp.pad(ks, pad_kv), jnp.pad(vs, pad_kv)
    kidx = jnp.arange(nb)[:, None] * ATT_QB + jnp.arange(kb_len)[None, :]
    kblk, vblk = kp[:, kidx], vp[:, kidx]
    s = jnp.einsum('bnqhd,bnkhd->bnhqk', qs, kblk).astype(jnp.float32) * (dh ** -0.5)
    key_pos = kidx - hw
    q_pos = jnp.arange(nb)[:, None] * ATT_QB + jnp.arange(ATT_QB)[None, :]
    rel = key_pos[:, None, :] - q_pos[:, :, None]
    valid = (jnp.abs(rel) <= hw) & (key_pos[:, None, :] >= 0) & (key_pos[:, None, :] < L)
    s = jnp.where(valid[None, :, None], s, NEG)
    lse = jax.nn.logsumexp(s, axis=-1)
    p = jnp.exp(s - lse[..., None])
    o = jnp.einsum('bnhqk,bnkhd->bnqhd', p, vblk.astype(jnp.float32))
    o = o.reshape(b * dil, Lp, h, dh)[:, :L]
    lse = jnp.swapaxes(lse, 2, 3).reshape(b * dil, Lp, h)[:, :L]
    return from_sub(o), from_sub(lse)


def attention_mixer(att, norm_w, cos, sin):
    b, L, _ = att.shape
    att = att.reshape(b, L, 3, N_DIL, ATT_SLOTS, HEAD_DIM)
    q = apply_rope(att[:, :, 0], cos, sin)
    k = apply_rope(att[:, :, 1], cos, sin)
    v = att[:, :, 2]
    outs, lses = [], []
    for g, (win, dil) in enumerate(DILATION_CFG):
        o, lse = dilated_band_attention(q[:, :, g], k[:, :, g], v[:, :, g], dil, win // (2 * dil))
        outs.append(o)
        lses.append(lse)
    w = jax.nn.softmax(jnp.stack(lses, axis=0), axis=0)
    o = jnp.einsum('gbls,gblsd->blsd', w, jnp.stack(outs, axis=0))
    return rmsnorm(o.reshape(b, L, D_ATT).astype(att.dtype), norm_w)


def peer(h, wq, subkeys, u, v):
    b, L, D = h.shape
    t = h.reshape(b * L, D)
    T = t.shape[0]
    q = (t @ wq).reshape(T, PEER_HEADS, 2, PEER_DQ // 2).astype(jnp.float32)
    s = jnp.einsum('thid,hind->thin', q, subkeys.astype(jnp.float32))
    sv, si = lax.top_k(s, PEER_TOPK)
    cand = (sv[:, :, 0, :, None] + sv[:, :, 1, None, :]).reshape(T, PEER_HEADS, PEER_TOPK * PEER_TOPK)
    cidx = (si[:, :, 0, :, None] * PEER_NKEYS + si[:, :, 1, None, :]).reshape(T, PEER_HEADS, PEER_TOPK * PEER_TOPK)
    top, pos = lax.top_k(cand, PEER_TOPK)
    eidx = jnp.take_along_axis(cidx, pos, axis=-1)
    gate = jax.nn.softmax(top, axis=-1)
    nc = T // PEER_CHUNK

    def expert_block(args):
        tc, ec, gc = args
        act = jax.nn.gelu(jnp.einsum('thkd,td->thk', u[ec], tc).astype(jnp.float32), approximate=False) * gc
        return jnp.einsum('thk,thkd->td', act.astype(v.dtype), v[ec])

    out = lax.map(expert_block, (t.reshape(nc, PEER_CHUNK, D),
                                 eidx.reshape(nc, PEER_CHUNK, PEER_HEADS, PEER_TOPK),
                                 gate.reshape(nc, PEER_CHUNK, PEER_HEADS, PEER_TOPK)))
    return out.reshape(b, L, D).astype(h.dtype)


def trunk(x, norm_mix, w_in, ssd_conv_w, ssd_conv_b, ssd_dt_bias, ssd_a_log, ssd_d, ssd_norm,
          sc_conv_w, sc_norm, att_norm, w_out, norm_ffn, peer_wq, peer_subkeys, peer_u, peer_v, norm_final):
    cos, sin = rope_tables(x.shape[1])
    for l in range(DEPTH):
        h = rmsnorm(x, norm_mix[l])
        proj = h @ w_in[l]
        z, xbc, dt_raw, sc, att = jnp.split(proj, SPLITS, axis=-1)
        y = jnp.concatenate([
            ssd_mixer(z, xbc, dt_raw, ssd_conv_w[l], ssd_conv_b[l], ssd_dt_bias[l], ssd_a_log[l], ssd_d[l], ssd_norm[l]),
            shortconv_mixer(sc, sc_conv_w[l], sc_norm[l]),
            attention_mixer(att, att_norm[l], cos, sin)], axis=-1)
        x = x + y @ w_out[l]
        x = x + peer(rmsnorm(x, norm_ffn[l]), peer_wq[l], peer_subkeys[l], peer_u[l], peer_v[l])
    return rmsnorm(x, norm_final)


def setup_inputs(seed: int = 0) -> dict:
    key = jax.random.key(seed)
    ks = jax.random.split(key, 20)
    f32 = jnp.float32
    nrm = lambda k, shape: jax.random.normal(k, shape, f32)
    gain = lambda k, shape: 1.0 + 0.02 * nrm(k, shape)
    dt0 = jnp.exp(jax.random.uniform(ks[6], (DEPTH, 2, SSD_HEADS), f32, np.log(1e-3), np.log(1e-1)))
    return {
        'x_prompt': nrm(ks[0], (BATCH, SEQ, D_MODEL)),
        'x_sample': nrm(ks[1], (DEC_BATCH, DEC_SEQ, D_MODEL)),
        'norm_mix': gain(ks[2], (DEPTH, D_MODEL)),
        'w_in': nrm(ks[3], (DEPTH, D_MODEL, D_IN)) * D_MODEL ** -0.5,
        'ssd_conv_w': nrm(ks[4], (DEPTH, SSD_CONV, XBC)) * SSD_CONV ** -0.5,
        'ssd_conv_b': 0.02 * nrm(ks[5], (DEPTH, XBC)),
        'ssd_dt_bias': dt0 + jnp.log(-jnp.expm1(-dt0)),
        'ssd_a_log': jnp.log(jax.random.uniform(ks[7], (DEPTH, 2, SSD_HEADS), f32, 1.0, 16.0)),
        'ssd_d': gain(ks[8], (DEPTH, SSD_HEADS)),
        'ssd_norm': gain(ks[9], (DEPTH, D_SSD)),
        'sc_conv_w': nrm(ks[10], (DEPTH, SC_CONV, D_SC)) * SC_CONV ** -0.5,
        'sc_norm': gain(ks[11], (DEPTH, D_SC)),
        'att_norm': gain(ks[12], (DEPTH, D_ATT)),
        'w_out': nrm(ks[13], (DEPTH, D_MIX, D_MODEL)) * D_MIX ** -0.5,
        'norm_ffn': gain(ks[14], (DEPTH, D_MODEL)),
        'peer_wq': nrm(ks[15], (DEPTH, D_MODEL, PEER_HEADS * PEER_DQ)) * D_MODEL ** -0.5,
        'peer_subkeys': nrm(ks[16], (DEPTH, PEER_HEADS, 2, PEER_NKEYS, PEER_DQ // 2)) * (PEER_DQ // 2) ** -0.5,
        'peer_u': nrm(ks[17], (DEPTH, PEER_EXPERTS, D_MODEL)) * D_MODEL ** -0.5,
        'peer_v': nrm(ks[18], (DEPTH, PEER_EXPERTS, D_MODEL)) * (PEER_HEADS * PEER_TOPK) ** -0.5,
        'norm_final': gain(ks[19], (D_MODEL,)),
    }


def reference(x_prompt, x_sample, norm_mix, w_in, ssd_conv_w, ssd_conv_b, ssd_dt_bias, ssd_a_log, ssd_d,
              ssd_norm, sc_conv_w, sc_norm, att_norm, w_out, norm_ffn, peer_wq, peer_subkeys, peer_u, peer_v,
              norm_final):
    y_prompt = trunk(x_prompt, norm_mix, w_in, ssd_conv_w, ssd_conv_b, ssd_dt_bias, ssd_a_log, ssd_d, ssd_norm,
                     sc_conv_w, sc_norm, att_norm, w_out, norm_ffn, peer_wq, peer_subkeys, peer_u, peer_v, norm_final)
    y_sample = trunk(x_sample, norm_mix, w_in, ssd_conv_w, ssd_conv_b, ssd_dt_bias, ssd_a_log, ssd_d, ssd_norm,
                     sc_conv_w, sc_norm, att_norm, w_out, norm_ffn, peer_wq, peer_subkeys, peer_u, peer_v, norm_final)
    return (y_prompt, y_sample)
```

```python
import contextlib
import numpy as np
import ml_dtypes
import concourse.bass as bass
import concourse.mybir as mybir
from concourse.bass_utils import run_bass_kernel_spmd

F32 = mybir.dt.float32
BF16 = mybir.dt.bfloat16
I32 = mybir.dt.int32
U32 = mybir.dt.uint32
AF = mybir.ActivationFunctionType
ALU = mybir.AluOpType
AX = mybir.AxisListType

D_MODEL = 2048
DEPTH = 2
D_SSD = 1536
SSD_HEADS = 24
XBC = 2560
D_SC = 768
D_ATT = 768
D_MIX = 3072
D_IN = 13360
EPS = 1e-6
C_Z = 0
C_XBC = 1536
C_DT = 4096
C_SC = 4144
C_ATT = 6448
N_EXP = 16384
ATT_GROUPS = (0, 1, 2)
ATT_STAGE = 9
ALL_PHASES = ("a", "conv", "ssd", "sc", "att", "wout", "peer", "final")


class Buf:
    __slots__ = ("w", "r", "multi")

    def __init__(self, multi=False):
        self.w = {}
        self.r = {}
        self.multi = multi


class T:
    def __init__(self, t, b=None):
        self.t = t
        self.b = b if b is not None else Buf()

    def __getitem__(self, idx):
        return self.t[idx]


class KB:
    def __init__(self, nc, es):
        self.nc = nc
        self.es = es
        self.eng = {"pe": nc.tensor, "act": nc.scalar, "dve": nc.vector, "pool": nc.gpsimd, "sp": nc.sync}
        self.sems = {}
        self.cnt = {}
        for e in ("pe", "act", "dve", "pool"):
            self.sems[e] = es.enter_context(nc.semaphore("c_" + e))
            self.cnt[e] = 0
        self.waited = {e: {} for e in self.eng}
        self.nd = 0
        self.uid = 0
        self.free = []
        self.free_sw = []
        self.scopes = []

    def newsem(self, sw=False):
        fl = self.free_sw if sw else self.free
        if fl:
            key = fl.pop()
        else:
            key = ("dw%d" if sw else "d%d") % self.nd
            self.nd += 1
            self.sems[key] = self.es.enter_context(self.nc.semaphore(key))
            self.cnt[key] = 0
        for sc in self.scopes:
            sc.append(key)
        return key

    def push(self):
        self.scopes.append([])

    def pop(self):
        sc = self.scopes.pop()
        for key in sc:
            fl = self.free_sw if key.startswith("dw") else self.free
            if key not in fl:
                fl.append(key)

    def _deps(self, e, reads, writes):
        need = {}
        for b in reads:
            for s, v in b.w.items():
                if v > need.get(s, 0):
                    need[s] = v
        for b in writes:
            if not b.multi:
                for s, v in b.w.items():
                    if v > need.get(s, 0):
                        need[s] = v
            for s, v in b.r.items():
                if v > need.get(s, 0):
                    need[s] = v
        wd = self.waited[e]
        for s, v in need.items():
            if s == e and e == "pe":
                continue
            if wd.get(s, 0) >= v:
                continue
            if s[0] == "d":
                v = self.cnt[s]
            self.eng[e].wait_ge(self.sems[s], v)
            wd[s] = v

    def _rec(self, key, val, reads, writes):
        for b in reads:
            b.r[key] = val
        for b in writes:
            if b.multi:
                b.w[key] = val
            else:
                b.w = {key: val}
            b.r = {}

    def op(self, e, reads, writes, fn):
        reads = [x.b if isinstance(x, T) else x for x in reads]
        writes = [x.b if isinstance(x, T) else x for x in writes]
        self._deps(e, reads, writes)
        ins = fn(self.eng[e])
        self.cnt[e] += 1
        ins.then_inc(self.sems[e], 1)
        self._rec(e, self.cnt[e], reads, writes)
        return ins

    def dma(self, q, sem, out, in_, reads, writes, indirect=None, sync=False, **kw):
        reads = [x.b if isinstance(x, T) else x for x in reads]
        writes = [x.b if isinstance(x, T) else x for x in writes]
        self._deps(q, reads, writes)
        if indirect is not None:
            ins = self.eng[q].indirect_dma_start(out=out, out_offset=None, in_=in_, in_offset=indirect, **kw)
        else:
            ins = self.eng[q].dma_start(out=out, in_=in_, **kw)
        self.cnt[sem] += 16
        ins.then_inc(self.sems[sem], 16)
        self._rec(sem, self.cnt[sem], reads, writes)
        if sync:
            self.eng[q].wait_ge(self.sems[sem], self.cnt[sem])
            self.waited[q][sem] = self.cnt[sem]
        return ins

    def barrier(self):
        for e in self.eng:
            wd = self.waited[e]
            for s, v in self.cnt.items():
                if s == e or v == 0 or wd.get(s, 0) >= v:
                    continue
                self.eng[e].wait_ge(self.sems[s], v)
                wd[s] = v

    def sb(self, es, shape, dt, name=None):
        self.uid += 1
        t = es.enter_context(self.nc.sbuf_tensor("%s_%d" % (name or "t", self.uid), list(shape), dt))
        return T(t)

    def ps(self, es, shape, dt, name=None):
        self.uid += 1
        t = es.enter_context(self.nc.psum_tensor("%s_%d" % (name or "p", self.uid), list(shape), dt))
        return T(t)


class Ring:
    def __init__(self, k, es, n, shape, dt, name, dma=True, sw=False):
        self.tiles = [k.sb(es, shape, dt, name) for _ in range(n)]
        self.sems = [k.newsem(sw) for _ in range(n)] if dma else [None] * n
        self.i = 0

    def next(self):
        t, s = self.tiles[self.i], self.sems[self.i]
        self.i = (self.i + 1) % len(self.tiles)
        return t, s


def rmsnorm_rstd(k, ssq, rstd, n):
    k.op("dve", [ssq], [rstd], lambda e: e.tensor_scalar(out=rstd[:, 0:1], in0=ssq[:, 0:1], scalar1=1.0 / n,
                                                         scalar2=EPS, op0=ALU.mult, op1=ALU.add))
    k.op("act", [rstd], [rstd], lambda e: e.activation(out=rstd[:, 0:1], in_=rstd[:, 0:1], func=AF.Sqrt))
    k.op("dve", [rstd], [rstd], lambda e: e.reciprocal(out=rstd[:, 0:1], in_=rstd[:, 0:1]))


def phase_a(k, L, lyr, xsrc, xsrc_b, W, S, C):
    nc = k.nc
    TB = 1024 if L >= 1024 else L
    NT = TB // 128
    k.push()
    with contextlib.ExitStack() as es:
        gB = k.sb(es, [128, D_MODEL], F32, "gB")
        gsem = k.newsem()
        k.dma("sp", gsem, gB[:], W["norm_mix"][lyr:lyr + 1, :].broadcast_to([128, D_MODEL]), [], [gB], sync=True)
        xr = Ring(k, es, 2, [128, D_MODEL], F32, "xt")
        junk = k.sb(es, [128, D_MODEL], BF16, "junk")
        hb = [k.sb(es, [128, D_MODEL], BF16, "hb") for _ in range(2)]
        ssq = [k.sb(es, [128, 1], F32, "ssq") for _ in range(2)]
        rstd = [k.sb(es, [128, 1], F32, "rstd") for _ in range(2)]
        hT = k.sb(es, [128, 16, TB], BF16, "hT")
        hTb = [Buf() for _ in range(NT)]
        wf = Ring(k, es, 2, [128, 16, 512], F32, "wf")
        wb = [k.sb(es, [128, 16, 512], BF16, "wb") for _ in range(2)]
        ev = Ring(k, es, 3, [128, 512], F32, "ev")
        evb = Ring(k, es, 3, [128, 512], BF16, "evb")
        cs = Ring(k, es, 2, [128, 512], F32, "cos")
        sn = Ring(k, es, 2, [128, 512], F32, "sin")
        qsb = [k.sb(es, [128, 512], F32, "qsb") for _ in range(2)]
        t1 = [k.sb(es, [128, 512], F32, "t1") for _ in range(2)]
        t2 = [k.sb(es, [128, 512], F32, "t2") for _ in range(2)]
        pst = [k.ps(es, [128, 1024], BF16, "pst") for _ in range(2)]
        psm = [k.ps(es, [128, 512], F32, "psm") for _ in range(4)]
        psr = [k.ps(es, [128, 512], F32, "psr") for _ in range(2)]
        identb = C["identb"]
        rotT = C["rotT"]
        wi = 0
        mi = 0
        ri = 0
        segs = [("z", C_Z, 1536, 512), ("xbc", C_XBC, 2560, 512), ("dt", C_DT, 48, 48),
                ("sc", C_SC, 2304, 384), ("q", C_ATT, 2304, 384), ("k", C_ATT + 2304, 2304, 384),
                ("v", C_ATT + 4608, 2304, 384)]
        for sbk in range(L // TB):
            tok0 = sbk * TB
            for tt in range(NT):
                xt, xs = xr.next()
                k.dma("sp", xs, xt[:], xsrc[tok0 + tt * 128: tok0 + (tt + 1) * 128, :], [xsrc_b], [xt])
                p = tt % 2
                k.op("act", [xt], [junk, ssq[p]], lambda e: e.activation(out=junk[:], in_=xt[:], func=AF.Square,
                                                                          accum_out=ssq[p][:, 0:1]))
                rmsnorm_rstd(k, ssq[p], rstd[p], D_MODEL)
                k.op("dve", [xt, rstd[p], gB], [hb[p]], lambda e: e.scalar_tensor_tensor(
                    out=hb[p][:], in0=xt[:], scalar=rstd[p][:, 0:1], in1=gB[:], op0=ALU.mult, op1=ALU.mult))
                for half in range(2):
                    pt = pst[half]
                    for j in range(8):
                        kc = half * 8 + j
                        k.op("pe", [hb[p], identb], [pt], lambda e: e.transpose(
                            out=pt[:, j * 128:(j + 1) * 128], in_=hb[p][:, kc * 128:(kc + 1) * 128],
                            identity=identb[:]))
                    eng = "act" if half == 0 else "dve"
                    src = pt[:, :].rearrange("p (j t) -> p j t", j=8)
                    dst = hT[:, half * 8:(half + 1) * 8, tt * 128:(tt + 1) * 128]
                    if eng == "act":
                        k.op("act", [pt], [hTb[tt]], lambda e: e.copy(out=dst, in_=src))
                    else:
                        k.op("dve", [pt], [hTb[tt]], lambda e: e.tensor_copy(out=dst, in_=src))
            for kind, c0, ncs, bw in segs:
                for blk in range(ncs // bw):
                    cb = c0 + blk * bw
                    wft, wfs = wf.next()
                    wbt = wb[wi % 2]
                    wi += 1
                    k.dma("sp", wfs, wft[:, :, 0:bw],
                          W["w_in"][lyr, :, cb:cb + bw].rearrange("(kc p) c -> p kc c", p=128), [], [wft])
                    k.op("pool", [wft], [wbt], lambda e: e.tensor_copy(out=wbt[:, :, 0:bw], in_=wft[:, :, 0:bw]))
                    if kind in ("z", "dt", "v"):
                        for tt in range(NT):
                            pm = psm[mi % 4]
                            mi += 1
                            for kc in range(16):
                                k.op("pe", [hTb[tt], wbt], [pm], lambda e: e.matmul(
                                    pm[:, 0:bw], lhsT=hT[:, kc, tt * 128:(tt + 1) * 128], rhs=wbt[:, kc, 0:bw],
                                    start=(kc == 0), stop=(kc == 15)))
                            r0 = tok0 + tt * 128
                            if kind == "v":
                                et, esm = evb.next()
                                k.op("act", [pm], [et], lambda e: e.copy(out=et[:, 0:bw], in_=pm[:, 0:bw]))
                                k.dma("sp", esm, S["V"][r0:r0 + 128, blk * bw:(blk + 1) * bw], et[:, 0:bw],
                                      [et], [S["V_b"]])
                            else:
                                et, esm = ev.next()
                                k.op("act", [pm], [et], lambda e: e.copy(out=et[:, 0:bw], in_=pm[:, 0:bw]))
                                dst = S["Z"] if kind == "z" else S["DT"]
                                dstb = S["Z_b"] if kind == "z" else S["DT_b"]
                                k.dma("sp", esm, dst[r0:r0 + 128, blk * bw:(blk + 1) * bw], et[:, 0:bw],
                                      [et], [dstb])
                    else:
                        for tb in range(TB // 512):
                            t0 = tok0 + tb * 512
                            hdeps = [hTb[tb * 4 + i] for i in range(4)]
                            if kind in ("q", "k"):
                                ct, csm = cs.next()
                                st, ssm = sn.next()
                                k.dma("sp", csm, ct[:], C["cosT"][:, t0:t0 + 512], [], [ct])
                                k.dma("sp", ssm, st[:], C["sinT"][:, t0:t0 + 512], [], [st])
                            for ch in range(bw // 128):
                                pm = psm[mi % 4]
                                mi += 1
                                for kc in range(16):
                                    k.op("pe", hdeps + [wbt], [pm], lambda e: e.matmul(
                                        pm[:, :], lhsT=wbt[:, kc, ch * 128:(ch + 1) * 128],
                                        rhs=hT[:, kc, tb * 512:(tb + 1) * 512], start=(kc == 0), stop=(kc == 15)))
                                row = blk * bw + ch * 128
                                if kind in ("xbc", "sc"):
                                    et, esm = ev.next()
                                    k.op("act", [pm], [et], lambda e: e.copy(out=et[:], in_=pm[:]))
                                    dst = S["XBCT"] if kind == "xbc" else S["SCT"]
                                    dstb = S["XBCT_b"] if kind == "xbc" else S["SCT_b"]
                                    k.dma("sp", esm, dst[row:row + 128, t0:t0 + 512], et[:], [et], [dstb])
                                else:
                                    r = ri % 2
                                    ri += 1
                                    pr = psr[r]
                                    k.op("act", [pm], [qsb[r]], lambda e: e.copy(out=qsb[r][:], in_=pm[:]))
                                    k.op("pe", [qsb[r], rotT], [pr], lambda e: e.matmul(
                                        pr[:, :], lhsT=rotT[:], rhs=qsb[r][:], start=True, stop=True))
                                    k.op("pool", [qsb[r], ct], [t1[r]], lambda e: e.tensor_tensor(
                                        out=t1[r][:], in0=qsb[r][:], in1=ct[:], op=ALU.mult))
                                    k.op("dve", [pr, st], [t2[r]], lambda e: e.tensor_tensor(
                                        out=t2[r][:], in0=pr[:], in1=st[:], op=ALU.mult))
                                    et, esm = evb.next()
                                    k.op("dve", [t1[r], t2[r]], [et], lambda e: e.tensor_tensor(
                                        out=et[:], in0=t1[r][:], in1=t2[r][:], op=ALU.add))
                                    dst = S["QT"] if kind == "q" else S["KT"]
                                    dstb = S["QT_b"] if kind == "q" else S["KT_b"]
                                    k.dma("sp", esm, dst[row:row + 128, t0:t0 + 512], et[:], [et], [dstb])
        k.barrier()
    k.pop()


def phase_final(k, L, xsrc, xsrc_b, W, y, y_b):
    k.push()
    with contextlib.ExitStack() as es:
        gB = k.sb(es, [128, D_MODEL], F32, "gB")
        gsem = k.newsem()
        k.dma("sp", gsem, gB[:], W["norm_final"][0:1, :].broadcast_to([128, D_MODEL]), [], [gB], sync=True)
        xr = Ring(k, es, 2, [128, D_MODEL], F32, "xt")
        yr = Ring(k, es, 2, [128, D_MODEL], F32, "yt")
        junk = k.sb(es, [128, D_MODEL], BF16, "junk")
        ssq = [k.sb(es, [128, 1], F32, "ssq") for _ in range(2)]
        rstd = [k.sb(es, [128, 1], F32, "rstd") for _ in range(2)]
        for tt in range(L // 128):
            xt, xs = xr.next()
            yt, ys = yr.next()
            p = tt % 2
            k.dma("sp", xs, xt[:], xsrc[tt * 128:(tt + 1) * 128, :], [xsrc_b], [xt])
            k.op("act", [xt], [junk, ssq[p]], lambda e: e.activation(out=junk[:], in_=xt[:], func=AF.Square,
                                                                      accum_out=ssq[p][:, 0:1]))
            rmsnorm_rstd(k, ssq[p], rstd[p], D_MODEL)
            k.op("dve", [xt, rstd[p], gB], [yt], lambda e: e.scalar_tensor_tensor(
                out=yt[:], in0=xt[:], scalar=rstd[p][:, 0:1], in1=gB[:], op0=ALU.mult, op1=ALU.mult))
            k.dma("sp", ys, y[tt * 128:(tt + 1) * 128, :], yt[:], [yt], [y_b])
        k.barrier()
    k.pop()


def phase_conv(k, L, lyr, W, S, C):
    LB = min(L, 4096)
    identf, identb = C["identf"], C["identb"]
    k.push()
    with contextlib.ExitStack() as es:
        cw = k.sb(es, [128, 100], F32, "cw")
        cbias = k.sb(es, [128, 20], F32, "cbias")
        sem = k.newsem()
        k.dma("sp", sem, cw[:], W["ssd_conv_wl"][lyr], [], [cw], sync=True)
        k.dma("sp", sem, cbias[:], W["ssd_conv_bl"][lyr], [], [cbias], sync=True)
        xin = Ring(k, es, 2, [128, LB + 4], F32, "cin")
        acc = k.sb(es, [128, LB], F32, "cacc")
        so = Ring(k, es, 2, [128, LB], F32, "so")
        sob = Ring(k, es, 2, [128, LB], BF16, "sob")
        tr = Ring(k, es, 3, [128, 4, 128], F32, "tr")
        trb = Ring(k, es, 3, [128, 4, 128], BF16, "trb")
        ptr = [k.ps(es, [128, 512], F32, "ptr") for _ in range(2)]
        ptb = [k.ps(es, [128, 512], BF16, "ptb") for _ in range(2)]
        pi = 0
        for c in range(20):
            for lb in range(L // LB):
                t0 = lb * LB
                xt, xs = xin.next()
                lo = 2 if t0 == 0 else 0
                hi = 2 if t0 + LB == L else 0
                if lo:
                    k.op("pool", [], [xt], lambda e: e.memset(xt[:, 0:2], 0.0))
                if hi:
                    k.op("pool", [], [xt], lambda e: e.memset(xt[:, LB + 2:LB + 4], 0.0))
                k.dma("sp", xs, xt[:, lo:LB + 4 - hi], S["XBCT"][c * 128:(c + 1) * 128, t0 - 2 + lo:t0 + LB + 2 - hi],
                      [S["XBCT_b"]], [xt])
                k.op("dve", [xt, cw, cbias], [acc], lambda e: e.tensor_scalar(
                    out=acc[:], in0=xt[:, 0:LB], scalar1=cw[:, c * 5:c * 5 + 1], scalar2=cbias[:, c:c + 1],
                    op0=ALU.mult, op1=ALU.add))
                for t in range(1, 5):
                    k.op("dve", [xt, cw, acc], [acc], lambda e: e.scalar_tensor_tensor(
                        out=acc[:], in0=xt[:, t:t + LB], scalar=cw[:, c * 5 + t:c * 5 + t + 1], in1=acc[:],
                        op0=ALU.mult, op1=ALU.add))
                if c < 12:
                    st, ssm = so.next()
                    k.op("act", [acc], [st], lambda e: e.activation(out=st[:], in_=acc[:], func=AF.Silu))
                    for tg in range(LB // 512):
                        pt = ptr[pi % 2]
                        pi += 1
                        for i in range(4):
                            k.op("pe", [st, identf], [pt], lambda e: e.transpose(
                                out=pt[:, i * 128:(i + 1) * 128], in_=st[:, (tg * 4 + i) * 128:(tg * 4 + i + 1) * 128],
                                identity=identf[:]))
                        tt, ts = tr.next()
                        k.op("act", [pt], [tt], lambda e: e.copy(out=tt[:], in_=pt[:, :].rearrange("p (a c) -> p a c", a=4)))
                        r0 = t0 + tg * 512
                        k.dma("sp", ts, S["X"][r0:r0 + 512, c * 128:(c + 1) * 128].rearrange("(a p) c -> p a c", p=128),
                              tt[:], [tt], [S["X_b"]])
                else:
                    st, ssm = sob.next()
                    k.op("act", [acc], [st], lambda e: e.activation(out=st[:], in_=acc[:], func=AF.Silu))
                    name = "BT" if c < 16 else "CT"
                    rr = (c - 12) % 4
                    k.dma("sp", ssm, S[name][rr * 128:(rr + 1) * 128, t0:t0 + LB], st[:], [st], [S[name + "_b"]])
                    if c < 16:
                        for tg in range(LB // 512):
                            pt = ptb[pi % 2]
                            pi += 1
                            for i in range(4):
                                k.op("pe", [st, identb], [pt], lambda e: e.transpose(
                                    out=pt[:, i * 128:(i + 1) * 128],
                                    in_=st[:, (tg * 4 + i) * 128:(tg * 4 + i + 1) * 128], identity=identb[:]))
                            tt, ts = trb.next()
                            k.op("dve", [pt], [tt], lambda e: e.tensor_copy(
                                out=tt[:], in_=pt[:, :].rearrange("p (a c) -> p a c", a=4)))
                            r0 = t0 + tg * 512
                            k.dma("sp", ts,
                                  S["B"][r0:r0 + 512, rr * 128:(rr + 1) * 128].rearrange("(a p) c -> p a c", p=128),
                                  tt[:], [tt], [S["B_b"]])
        k.barrier()
    k.pop()


SSD_RUNS = {0: [(0, 0, 4)], 1: [(0, 4, 6), (1, 6, 8)], 2: [(1, 8, 12)], 3: [(2, 12, 16)],
            4: [(2, 16, 18), (3, 18, 20)], 5: [(3, 20, 24)]}


def phase_ssd(k, L, lyr, W, S, C):
    NCH = L // 128
    identf, onesf, onec = C["identf"], C["onesf"], C["onec"]
    k.push()
    with contextlib.ExitStack() as es:
        sem = k.newsem()
        Abc = k.sb(es, [128, 48], F32, "Abc")
        dtb = k.sb(es, [128, 48], F32, "dtb")
        Dsk = k.sb(es, [128, 24], F32, "Dsk")
        normw = k.sb(es, [128, D_SSD], F32, "normw")
        k.dma("sp", sem, Abc[:], W["ssd_a_log"][lyr:lyr + 1, :].broadcast_to([128, 48]), [], [Abc], sync=True)
        k.dma("sp", sem, dtb[:], W["ssd_dt_bias"][lyr:lyr + 1, :].broadcast_to([128, 48]), [], [dtb], sync=True)
        k.dma("sp", sem, Dsk[:], W["ssd_d"][lyr:lyr + 1, :].broadcast_to([128, 24]), [], [Dsk], sync=True)
        k.dma("sp", sem, normw[:], W["ssd_norm"][lyr:lyr + 1, :].broadcast_to([128, D_SSD]), [], [normw], sync=True)
        k.op("act", [Abc], [Abc], lambda e: e.activation(out=Abc[:], in_=Abc[:], func=AF.Exp))
        k.op("dve", [Abc], [Abc], lambda e: e.tensor_scalar(out=Abc[:], in0=Abc[:], scalar1=-1.0, scalar2=None,
                                                            op0=ALU.mult))
        xr = Ring(k, es, 2, [128, 24, 64], F32, "xk")
        zr = Ring(k, es, 2, [128, D_SSD], F32, "zk")
        ybr = Ring(k, es, 2, [128, D_SSD], F32, "ybk")
        dtr = Ring(k, es, 2, [128, 48], F32, "dtk")
        bkr = Ring(k, es, 2, [128, 512], BF16, "bk")
        btr = Ring(k, es, 2, [128, 4, 128], BF16, "btk")
        ctr = Ring(k, es, 2, [128, 4, 128], BF16, "ctk")
        dts = k.sb(es, [128, 48], F32, "dts")
        dt = k.sb(es, [128, 48], F32, "dt")
        a = k.sb(es, [128, 24], F32, "a")
        sm = k.sb(es, [128, 64], F32, "sm")
        eac = k.sb(es, [128, 24], F32, "eac")
        dend = k.sb(es, [128, 24], F32, "dend")
        cdec = k.sb(es, [128, 24], F32, "cdec")
        w2 = k.sb(es, [128, 24], F32, "w2")
        xdt = k.sb(es, [128, D_SSD], BF16, "xdt")
        xde = k.sb(es, [128, D_SSD], BF16, "xde")
        cbs = k.sb(es, [128, 4, 128], F32, "cbs")
        rhs1 = k.sb(es, [128, 24, 128], F32, "rhs1")
        dec = [k.sb(es, [128, 4, 128], F32, "dec") for _ in range(3)]
        MT = [k.sb(es, [128, 12, 128], BF16, "MT") for _ in range(2)]
        H = [k.sb(es, [128, 768], F32, "H") for _ in range(2)]
        Hb = [k.sb(es, [128, 768], BF16, "Hb") for _ in range(2)]
        yo = k.sb(es, [128, 24, 64], F32, "yo")
        yd = k.sb(es, [128, 24, 64], F32, "yd")
        ydb = [Buf(), Buf()]
        ystore = Ring(k, es, 2, [128, D_SSD], F32, "ystore")
        y1 = k.sb(es, [128, D_SSD], F32, "y1")
        sz = k.sb(es, [128, D_SSD], F32, "sz")
        junk = k.sb(es, [128, 384], F32, "junk")
        gss = k.sb(es, [128, 4], F32, "gss")
        ytr = Ring(k, es, 2, [128, 12, 128], BF16, "ytr")
        big = [k.ps(es, [128, 1024], F32, "big") for _ in range(2)]
        pseg = [k.ps(es, [128, 512], F32, "pseg") for _ in range(2)]
        pcb = k.ps(es, [128, 512], F32, "pcb")
        ptr = k.ps(es, [128, 512], F32, "ptr")
        st = {"big": 0, "seg": 0, "dec": 0}

        def nbig():
            st["big"] += 1
            return big[st["big"] % 2]

        for d in (1, 0):
            Td = C["U"] if d == 0 else C["Lo"]
            nTd = C["nU"] if d == 0 else C["nLo"]
            nm = C["nmf"] if d == 0 else C["nmb"]
            for hf in range(2):
                k.op("pool", [], [H[hf]], lambda e: e.memset(H[hf][:], 0.0))
                k.op("pool", [], [Hb[hf]], lambda e: e.memset(Hb[hf][:], 0.0))
            order = range(NCH) if d == 0 else range(NCH - 1, -1, -1)
            for c in order:
                r0 = c * 128
                xk, s1 = xr.next()
                k.dma("sp", s1, xk[:], S["X"][r0:r0 + 128, :].rearrange("p (h d) -> p h d", h=24), [S["X_b"]], [xk])
                dtk, s2 = dtr.next()
                k.dma("sp", s2, dtk[:], S["DT"][r0:r0 + 128, :], [S["DT_b"]], [dtk])
                bk, s3 = bkr.next()
                k.dma("sp", s3, bk[:], S["B"][r0:r0 + 128, :], [S["B_b"]], [bk])
                btk, s4 = btr.next()
                k.dma("sp", s4, btk[:], S["BT"][:, r0:r0 + 128].rearrange("(g n) t -> n g t", n=128), [S["BT_b"]], [btk])
                ctk, s5 = ctr.next()
                k.dma("sp", s5, ctk[:], S["CT"][:, r0:r0 + 128].rearrange("(g n) t -> n g t", n=128), [S["CT_b"]], [ctk])
                if d == 0:
                    zk, s6 = zr.next()
                    k.dma("sp", s6, zk[:], S["Z"][r0:r0 + 128, :], [S["Z_b"]], [zk])
                    ybk, s7 = ybr.next()
                    k.dma("sp", s7, ybk[:], S["YB"][r0:r0 + 128, :], [S["YB_b"]], [ybk])
                k.op("dve", [dtk, dtb], [dts], lambda e: e.tensor_tensor(out=dts[:], in0=dtk[:], in1=dtb[:], op=ALU.add))
                k.op("act", [dts], [dts], lambda e: e.activation(out=dts[:], in_=dts[:], func=AF.Exp))
                k.op("act", [dts, onec], [dt], lambda e: e.activation(out=dt[:], in_=dts[:], func=AF.Ln,
                                                                      bias=onec[:, 0:1], scale=1.0))
                dtd = dt[:, d * 24:(d + 1) * 24]
                k.op("dve", [dt, Abc], [a], lambda e: e.tensor_tensor(out=a[:], in0=dtd, in1=Abc[:, d * 24:(d + 1) * 24],
                                                                      op=ALU.mult))
                pb = nbig()
                k.op("pe", [Td, a], [pb], lambda e: e.matmul(pb[:, 0:24], lhsT=Td[:], rhs=a[:], start=True, stop=True))
                k.op("pe", [onesf, a], [pb], lambda e: e.matmul(pb[:, 24:48], lhsT=onesf[:], rhs=a[:], start=True, stop=True))
                k.op("act", [pb], [sm], lambda e: e.copy(out=sm[:, 0:48], in_=pb[:, 0:48]))
                k.op("act", [sm], [eac], lambda e: e.activation(out=eac[:], in_=sm[:, 0:24], func=AF.Exp))
                k.op("dve", [sm], [dend], lambda e: e.tensor_tensor(out=dend[:], in0=sm[:, 24:48], in1=sm[:, 0:24],
                                                                   op=ALU.subtract))
                k.op("act", [dend], [dend], lambda e: e.activation(out=dend[:], in_=dend[:], func=AF.Exp))
                k.op("act", [sm], [cdec], lambda e: e.activation(out=cdec[:], in_=sm[:, 24:48], func=AF.Exp))
                k.op("dve", [dt, dend], [w2], lambda e: e.tensor_tensor(out=w2[:], in0=dtd, in1=dend[:], op=ALU.mult))
                k.op("dve", [xk, dt], [xdt], lambda e: e.tensor_tensor(
                    out=xdt[:, :].rearrange("p (h d) -> p h d", d=64), in0=xk[:], in1=dtd.unsqueeze(2).to_broadcast([128, 24, 64]), op=ALU.mult))
                k.op("pool", [xk, w2], [xde], lambda e: e.tensor_tensor(
                    out=xde[:, :].rearrange("p (h d) -> p h d", d=64), in0=xk[:], in1=w2[:, :].unsqueeze(2).to_broadcast([128, 24, 64]), op=ALU.mult))
                for g in range(4):
                    k.op("pe", [btk, ctk], [pcb], lambda e: e.matmul(
                        pcb[:, g * 128:(g + 1) * 128], lhsT=btk[:, g, :], rhs=ctk[:, g, :], start=True, stop=True))
                k.op("act", [pcb], [cbs], lambda e: e.copy(out=cbs[:], in_=pcb[:, :].rearrange("p (g l) -> p g l", g=4)))
                k.op("pool", [a, Td], [rhs1], lambda e: e.tensor_tensor(
                    out=rhs1[:], in0=a[:, :].unsqueeze(2).to_broadcast([128, 24, 128]),
                    in1=Td[:, :].unsqueeze(1).to_broadcast([128, 24, 128]), op=ALU.mult))
                for hf in range(2):
                    h0 = hf * 12
                    pb = nbig()
                    for u in range(3 * hf, 3 * hf + 3):
                        for (g, ha, hb_) in SSD_RUNS[u]:
                            k.op("pe", [ctk, Hb[hf]], [pb], lambda e: e.matmul(
                                pb[:, (ha - h0) * 64:(hb_ - h0) * 64], lhsT=ctk[:, g, :],
                                rhs=Hb[hf][:, (ha - h0) * 64:(hb_ - h0) * 64], start=True, stop=True))
                    k.op("dve", [pb, eac], [yo], lambda e: e.tensor_tensor(
                        out=yo[:, h0:h0 + 12, :], in0=pb[:, 0:768].rearrange("p (h d) -> p h d", h=12),
                        in1=eac[:, h0:h0 + 12].unsqueeze(2).to_broadcast([128, 12, 64]), op=ALU.mult))
                    mt = MT[hf]
                    for u in range(3 * hf, 3 * hf + 3):
                        pg = pseg[st["seg"] % 2]
                        st["seg"] += 1
                        pg3 = pg[:, :].rearrange("p (h l) -> p h l", h=4)
                        k.op("pe", [onesf, rhs1], [pg], lambda e: e.matmul(
                            pg3, lhsT=onesf[:], rhs=rhs1[:, 4 * u:4 * u + 4, :], start=True, stop=False))
                        k.op("pe", [nTd, a], [pg], lambda e: e.matmul(
                            pg3, lhsT=nTd[:], rhs=a[:, 4 * u:4 * u + 4].unsqueeze(2).to_broadcast([128, 4, 128]),
                            start=False, stop=False))
                        k.op("pe", [identf, nm], [pg], lambda e: e.matmul(
                            pg3, lhsT=identf[:], rhs=nm[:, :].unsqueeze(1).to_broadcast([128, 4, 128]),
                            start=False, stop=True))
                        dc = dec[st["dec"] % 3]
                        st["dec"] += 1
                        k.op("act", [pg], [dc], lambda e: e.activation(out=dc[:], in_=pg3, func=AF.Exp))
                        for (g, ha, hb_) in SSD_RUNS[u]:
                            n = hb_ - ha
                            k.op("dve", [dc, cbs], [mt], lambda e: e.tensor_tensor(
                                out=mt[:, ha - h0:hb_ - h0, :], in0=dc[:, ha - 4 * u:hb_ - 4 * u, :],
                                in1=cbs[:, g:g + 1, :].to_broadcast([128, n, 128]), op=ALU.mult))
                    pb2 = nbig()
                    for h in range(h0, h0 + 12):
                        k.op("pe", [mt, xdt], [pb2], lambda e: e.matmul(
                            pb2[:, (h - h0) * 64:(h - h0 + 1) * 64], lhsT=mt[:, h - h0, :], rhs=xdt[:, h * 64:(h + 1) * 64],
                            start=True, stop=True))
                    k.op("dve", [pb2, yo], [ydb[hf]], lambda e: e.tensor_tensor(
                        out=yd[:, h0:h0 + 12, :], in0=pb2[:, 0:768].rearrange("p (h d) -> p h d", h=12),
                        in1=yo[:, h0:h0 + 12, :], op=ALU.add))
                for hf in range(2):
                    h0 = hf * 12
                    pb3 = nbig()
                    for u in range(3 * hf, 3 * hf + 3):
                        for (g, ha, hb_) in SSD_RUNS[u]:
                            k.op("pe", [bk, xde], [pb3], lambda e: e.matmul(
                                pb3[:, (ha - h0) * 64:(hb_ - h0) * 64], lhsT=bk[:, g * 128:(g + 1) * 128],
                                rhs=xde[:, ha * 64:hb_ * 64], start=True, stop=True))
                    k.op("pool", [H[hf], cdec], [H[hf]], lambda e: e.tensor_tensor(
                        out=H[hf][:, :].rearrange("p (h d) -> p h d", d=64), in0=H[hf][:, :].rearrange("p (h d) -> p h d", d=64),
                        in1=cdec[:, h0:h0 + 12].unsqueeze(2).to_broadcast([128, 12, 64]), op=ALU.mult))
                    k.op("dve", [H[hf], pb3], [H[hf]], lambda e: e.tensor_tensor(
                        out=H[hf][:], in0=H[hf][:], in1=pb3[:, 0:768], op=ALU.add))
                    k.op("act", [H[hf]], [Hb[hf]], lambda e: e.copy(out=Hb[hf][:], in_=H[hf][:]))
                ydf = yd[:, :, :].rearrange("p h d -> p (h d)")
                if d == 1:
                    yt, ysm = ystore.next()
                    k.op("pool", ydb, [yt], lambda e: e.tensor_copy(out=yt[:], in_=ydf))
                    k.dma("sp", ysm, S["YB"][r0:r0 + 128, :], yt[:], [yt], [S["YB_b"]])
                else:
                    k.op("dve", ydb + [ybk], [y1], lambda e: e.tensor_tensor(out=y1[:], in0=ydf, in1=ybk[:], op=ALU.add))
                    k.op("pool", [xk, Dsk], [sz], lambda e: e.tensor_tensor(
                        out=sz[:, :].rearrange("p (h d) -> p h d", h=24), in0=xk[:],
                        in1=Dsk[:, :].unsqueeze(2).to_broadcast([128, 24, 64]), op=ALU.mult))
                    k.op("dve", [y1, sz], [y1], lambda e: e.tensor_tensor(out=y1[:], in0=y1[:], in1=sz[:], op=ALU.add))
                    k.op("act", [zk], [sz], lambda e: e.activation(out=sz[:], in_=zk[:], func=AF.Silu))
                    k.op("dve", [y1, sz], [y1], lambda e: e.tensor_tensor(out=y1[:], in0=y1[:], in1=sz[:], op=ALU.mult))
                    for g in range(4):
                        k.op("act", [y1], [junk, gss], lambda e: e.activation(
                            out=junk[:], in_=y1[:, g * 384:(g + 1) * 384], func=AF.Square, accum_out=gss[:, g:g + 1]))
                    k.op("dve", [gss], [gss], lambda e: e.tensor_scalar(out=gss[:], in0=gss[:], scalar1=1.0 / 384,
                                                                        scalar2=EPS, op0=ALU.mult, op1=ALU.add))
                    k.op("act", [gss], [gss], lambda e: e.activation(out=gss[:], in_=gss[:], func=AF.Sqrt))
                    k.op("dve", [gss], [gss], lambda e: e.reciprocal(out=gss[:], in_=gss[:]))
                    k.op("dve", [y1, gss], [y1], lambda e: e.tensor_tensor(
                        out=y1[:, :].rearrange("p (g c) -> p g c", g=4), in0=y1[:, :].rearrange("p (g c) -> p g c", g=4),
                        in1=gss[:, :].unsqueeze(2).to_broadcast([128, 4, 384]), op=ALU.mult))
                    k.op("pool", [y1, normw], [y1], lambda e: e.tensor_tensor(out=y1[:], in0=y1[:], in1=normw[:], op=ALU.mult))
                    yt, ysm = ytr.next()
                    for q in range(3):
                        for i in range(4):
                            cc = q * 4 + i
                            k.op("pe", [y1, identf], [ptr], lambda e: e.transpose(
                                out=ptr[:, i * 128:(i + 1) * 128], in_=y1[:, cc * 128:(cc + 1) * 128], identity=identf[:]))
                        k.op("act", [ptr], [yt], lambda e: e.copy(
                            out=yt[:, q * 4:(q + 1) * 4, :], in_=ptr[:, :].rearrange("p (a t) -> p a t", a=4)))
                    k.dma("sp", ysm, S["YT"][0:D_SSD, r0:r0 + 128].rearrange("(c p) t -> p c t", p=128), yt[:],
                          [yt], [S["YT_b"]])
        k.barrier()
    k.pop()


def phase_sc(k, L, lyr, W, S, C):
    onesf = C["onesf"]
    TBK = 512
    k.push()
    with contextlib.ExitStack() as es:
        sem = k.newsem()
        cw = k.sb(es, [128, 18], F32, "scw")
        nw = k.sb(es, [128, 6], F32, "scn")
        k.dma("sp", sem, cw[:], W["sc_conv_wl"][lyr], [], [cw], sync=True)
        k.dma("sp", sem, nw[:], W["sc_norm_l"][lyr], [], [nw], sync=True)
        bgr = Ring(k, es, 2, [128, TBK], F32, "bg")
        cgr = Ring(k, es, 2, [128, TBK + 2], F32, "cg")
        hxr = Ring(k, es, 2, [128, TBK + 2], F32, "hx")
        prod = k.sb(es, [128, TBK + 2], F32, "prod")
        yc = [k.sb(es, [128, TBK], F32, "yc") for _ in range(6)]
        ysq = [k.sb(es, [128, TBK], F32, "ysq") for _ in range(2)]
        rst = k.sb(es, [128, TBK], F32, "rst")
        outr = Ring(k, es, 3, [128, TBK], BF16, "sco")
        pss = k.ps(es, [128, 512], F32, "pss")
        for tb in range(L // TBK):
            t0 = tb * TBK
            lo = 1 if t0 == 0 else 0
            hi = 1 if t0 + TBK == L else 0
            for c in range(6):
                bg, s0 = bgr.next()
                cg, s1 = cgr.next()
                hx, s2 = hxr.next()
                k.dma("sp", s0, bg[:], S["SCT"][c * 128:(c + 1) * 128, t0:t0 + TBK], [S["SCT_b"]], [bg])
                for (tl, sm_, base) in ((cg, s1, 768), (hx, s2, 1536)):
                    if lo:
                        k.op("pool", [], [tl], lambda e: e.memset(tl[:, 0:1], 0.0))
                    if hi:
                        k.op("pool", [], [tl], lambda e: e.memset(tl[:, TBK + 1:TBK + 2], 0.0))
                    k.dma("sp", sm_, tl[:, lo:TBK + 2 - hi],
                          S["SCT"][base + c * 128:base + (c + 1) * 128, t0 - 1 + lo:t0 + TBK + 1 - hi],
                          [S["SCT_b"]], [tl])
                k.op("pool", [cg, hx], [prod], lambda e: e.tensor_tensor(out=prod[:], in0=cg[:], in1=hx[:], op=ALU.mult))
                y = yc[c]
                k.op("dve", [prod, cw], [y], lambda e: e.tensor_scalar(
                    out=y[:], in0=prod[:, 0:TBK], scalar1=cw[:, c * 3:c * 3 + 1], scalar2=None, op0=ALU.mult))
                for t in (1, 2):
                    k.op("dve", [prod, cw, y], [y], lambda e: e.scalar_tensor_tensor(
                        out=y[:], in0=prod[:, t:t + TBK], scalar=cw[:, c * 3 + t:c * 3 + t + 1], in1=y[:],
                        op0=ALU.mult, op1=ALU.add))
                k.op("dve", [y, bg], [y], lambda e: e.tensor_tensor(out=y[:], in0=y[:], in1=bg[:], op=ALU.mult))
                q = ysq[c % 2]
                k.op("act", [y], [q], lambda e: e.activation(out=q[:], in_=y[:], func=AF.Square))
                k.op("pe", [onesf, q], [pss], lambda e: e.matmul(pss[:, :], lhsT=onesf[:], rhs=q[:],
                                                                 start=(c == 0), stop=(c == 5)))
            k.op("dve", [pss], [rst], lambda e: e.tensor_scalar(out=rst[:], in0=pss[:], scalar1=1.0 / D_SC, scalar2=EPS,
                                                                op0=ALU.mult, op1=ALU.add))
            k.op("act", [rst], [rst], lambda e: e.activation(out=rst[:], in_=rst[:], func=AF.Sqrt))
            k.op("dve", [rst], [rst], lambda e: e.reciprocal(out=rst[:], in_=rst[:]))
            for c in range(6):
                o, so_ = outr.next()
                k.op("dve", [yc[c], nw, rst], [o], lambda e: e.scalar_tensor_tensor(
                    out=o[:], in0=yc[c][:], scalar=nw[:, c:c + 1], in1=rst[:], op0=ALU.mult, op1=ALU.mult))
                k.dma("sp", so_, S["YT"][D_SSD + c * 128:D_SSD + (c + 1) * 128, t0:t0 + TBK], o[:], [o], [S["YT_b"]])
        k.barrier()
    k.pop()


def phase_att(k, L, lyr, W, S, C):
    identb = C["identb"]
    with contextlib.ExitStack() as es:
        for g, dil in enumerate((1, 4, 16)):
            if g not in ATT_GROUPS:
                continue
            Lsub = L // dil
            nqb = Lsub // 128
            W_ = 128 * dil
            k.push()
            with contextlib.ExitStack() as es2:
                kcr = Ring(k, es2, 3, [128, 2, W_], BF16, "kc")
                qwr = Ring(k, es2, 2, [128, 2, 2, W_], BF16, "qw")
                qwr2 = [k.newsem() for _ in range(2)]
                vcr = [Ring(k, es2, 3, [128, 4, 65], BF16, "vc") for _ in range(dil)]
                ptr_ = Ring(k, es2, 3, [128, 512], BF16, "pT", dma=False)
                otr = Ring(k, es2, 3, [128, 4, 65], F32, "ot")
                pst_ = [k.ps(es2, [128, 512], F32, "pst") for _ in range(3)]
                pso = [k.ps(es2, [128, 512], F32, "pso") for _ in range(2)]
                for r in range(dil):
                    for t, _s in zip(vcr[r].tiles, vcr[r].sems):
                        k.op("pool", [], [t], lambda e: e.memset(t[:], 1.0))
                for t in kcr.tiles + qwr.tiles:
                    k.op("pool", [], [t], lambda e: e.memset(t[:], 0.0))
                cnt = {"st": 0, "o": 0}
                for hg in range(3):
                    row0 = g * 768 + hg * 256
                    kch = {}
                    vch = {}

                    def load_chunk(j):
                        p0 = 128 * j - 64
                        lo = 64 if j == 0 else 0
                        hi = 64 if j == nqb else 128
                        kt, ks = kcr.next()
                        tok_lo = (p0 + lo) * dil
                        tok_hi = (p0 + hi) * dil
                        k.dma("sp", ks, kt[:, :, lo * dil:hi * dil],
                              S["KT"][row0:row0 + 256, tok_lo:tok_hi].rearrange("(hp p) t -> p hp t", p=128),
                              [S["KT_b"]], [kt])
                        kch[j] = kt
                        vs = []
                        for r in range(dil):
                            vt, vsm = vcr[r].next()
                            rows = S["V"][tok_lo:tok_hi, row0:row0 + 256].rearrange("(i r) (h d) -> r i h d", r=dil, d=64)[r]
                            k.dma("sp", vsm, vt[lo:hi, :, 0:64], rows, [S["V_b"]], [vt])
                            vs.append(vt)
                        vch[j] = vs

                    load_chunk(0)
                    for w in range(nqb):
                        load_chunk(w + 1)
                        qs2 = qwr2[qwr.i]
                        qt, qs = qwr.next()
                        qsrc = S["QT"][row0:row0 + 256, w * W_:(w + 1) * W_].rearrange("(hp p) t -> p hp t", p=128)
                        k.dma("sp", qs, qt[0:64, :, 0, :], qsrc[0:64], [S["QT_b"]], [qt])
                        k.dma("sp", qs2, qt[64:128, :, 1, :], qsrc[64:128], [S["QT_b"]], [qt])
                        for r in range(dil):
                            po = pso[cnt["o"] % 2]
                            cnt["o"] += 1
                            for c in (0, 1):
                                if ATT_STAGE < 2:
                                    break
                                j = w + c
                                lo = 64 if j == 0 else 0
                                hi = 64 if j == nqb else 128
                                kt = kch[j]
                                vt = vch[j][r]
                                ps_ = pst_[cnt["st"] % 3]
                                cnt["st"] += 1
                                for hp in range(2):
                                    kv = kt[:, hp, :].rearrange("p (i r) -> p i r", r=dil)[:, :, r]
                                    qv = qt[:, hp, :, :].rearrange("p a (i r) -> p a i r", r=dil)[:, :, :, r]
                                    k.op("pe", [kt, qt], [ps_], lambda e: e.matmul(
                                        ps_[:, hp * 256:(hp + 1) * 256].rearrange("p (a i) -> p a i", a=2), lhsT=kv, rhs=qv,
                                        start=(hp == 0), stop=False))
                                if c == 0:
                                    mk = C["maskA0"] if j == 0 else C["maskA"]
                                else:
                                    mk = C["maskB1"] if j == nqb else C["maskB"]
                                k.op("pe", [identb, mk], [ps_], lambda e: e.matmul(
                                    ps_[:, :], lhsT=identb[:], rhs=mk[:], start=False, stop=True))
                                if ATT_STAGE < 3:
                                    continue
                                pt, _ = ptr_.next()
                                k.op("act", [ps_], [pt], lambda e: e.activation(out=pt[:, :], in_=ps_[:, :],
                                                                               func=AF.Exp, scale=0.125))
                                if ATT_STAGE < 4:
                                    continue
                                for hh in range(4):
                                    k.op("pe", [pt, vt], [po], lambda e: e.matmul(
                                        po[:, hh * 65:(hh + 1) * 65], lhsT=pt[:, hh * 128:(hh + 1) * 128],
                                        rhs=vt[:, hh, :], start=(c == 0 and hh == 0), stop=(c == 1 and hh == 3)))
                            if ATT_STAGE < 5:
                                continue
                            ot, osm = otr.next()
                            k.op("act", [po], [ot], lambda e: e.copy(out=ot[:], in_=po[:, 0:260].rearrange("p (h d) -> p h d", h=4)))
                            dst = S["O%d" % g][w * W_:(w + 1) * W_, hg * 260:(hg + 1) * 260].rearrange(
                                "(i r) (h d) -> r i h d", r=dil, d=65)[r]
                            k.dma("sp", osm, dst, ot[:], [ot], [S["O%d_b" % g]])
                k.barrier()
            k.pop()
        if ATT_STAGE < 6:
            return
        k.push()
        with contextlib.ExitStack() as es2:
            identf = C["identf"]
            sem = k.newsem()
            nw = k.sb(es2, [128, D_ATT], F32, "attn")
            k.dma("sp", sem, nw[:], W["att_norm"][lyr:lyr + 1, :].broadcast_to([128, D_ATT]), [], [nw], sync=True)
            o0 = Ring(k, es2, 2, [128, 12, 65], F32, "o0")
            o1 = Ring(k, es2, 2, [128, 12, 65], F32, "o1")
            o2 = Ring(k, es2, 2, [128, 12, 65], F32, "o2")
            rl = k.sb(es2, [128, 12], F32, "rl")
            ov = k.sb(es2, [128, 12, 64], F32, "ov")
            junk = k.sb(es2, [128, D_ATT], F32, "junk")
            ssq = k.sb(es2, [128, 1], F32, "ssq")
            rstd = k.sb(es2, [128, 1], F32, "rstd")
            ytr = Ring(k, es2, 2, [128, 6, 128], BF16, "aytr")
            ptr = [k.ps(es2, [128, 512], F32, "ptr") for _ in range(2)]
            for tt in range(L // 128):
                r0 = tt * 128
                a0, s0 = o0.next()
                a1, s1 = o1.next()
                a2, s2 = o2.next()
                k.dma("sp", s0, a0[:], S["O0"][r0:r0 + 128, :].rearrange("p (h d) -> p h d", h=12), [S["O0_b"]], [a0])
                k.dma("sp", s1, a1[:], S["O1"][r0:r0 + 128, :].rearrange("p (h d) -> p h d", h=12), [S["O1_b"]], [a1])
                k.dma("sp", s2, a2[:], S["O2"][r0:r0 + 128, :].rearrange("p (h d) -> p h d", h=12), [S["O2_b"]], [a2])
                k.op("pool", [a0, a1], [a0], lambda e: e.tensor_tensor(out=a0[:], in0=a0[:], in1=a1[:], op=ALU.add))
                k.op("dve", [a0, a2], [a0], lambda e: e.tensor_tensor(out=a0[:], in0=a0[:], in1=a2[:], op=ALU.add))
                k.op("dve", [a0], [rl], lambda e: e.reciprocal(out=rl[:], in_=a0[:, :, 64]))
                k.op("dve", [a0, rl], [ov], lambda e: e.tensor_tensor(
                    out=ov[:], in0=a0[:, :, 0:64], in1=rl[:, :].unsqueeze(2).to_broadcast([128, 12, 64]), op=ALU.mult))
                ovf = ov[:, :, :].rearrange("p h d -> p (h d)")
                k.op("act", [ov], [junk, ssq], lambda e: e.activation(out=junk[:], in_=ovf, func=AF.Square,
                                                                      accum_out=ssq[:, 0:1]))
                rmsnorm_rstd(k, ssq, rstd, D_ATT)
                k.op("dve", [ov, rstd, nw], [junk], lambda e: e.scalar_tensor_tensor(
                    out=junk[:], in0=ovf, scalar=rstd[:, 0:1], in1=nw[:], op0=ALU.mult, op1=ALU.mult))
                yt, ysm = ytr.next()
                for q in range(2):
                    pt = ptr[q]
                    n = 4 if q == 0 else 2
                    for i in range(n):
                        cc = q * 4 + i
                        k.op("pe", [junk, identf], [pt], lambda e: e.transpose(
                            out=pt[:, i * 128:(i + 1) * 128], in_=junk[:, cc * 128:(cc + 1) * 128], identity=identf[:]))
                    k.op("act", [pt], [yt], lambda e: e.copy(
                        out=yt[:, q * 4:q * 4 + n, :], in_=pt[:, 0:n * 128].rearrange("p (a t) -> p a t", a=n)))
                k.dma("sp", ysm, S["YT"][2304:3072, r0:r0 + 128].rearrange("(c p) t -> p c t", p=128), yt[:],
                      [yt], [S["YT_b"]])
            k.barrier()
        k.pop()


def phase_wout(k, L, lyr, xsrc, xsrc_b, W, S, C):
    k.push()
    with contextlib.ExitStack() as es:
        wo = k.sb(es, [128, 24, D_MODEL], BF16, "wo")
        stg = Ring(k, es, 2, [128, 2, D_MODEL], F32, "wstg")
        for c4 in range(12):
            st, ss = stg.next()
            k.dma("sp", ss, st[:], W["w_out"][lyr, c4 * 256:(c4 + 1) * 256, :].rearrange("(c p) n -> p c n", p=128), [], [st])
            eng = "pool" if c4 % 2 == 0 else "act"
            if eng == "pool":
                k.op("pool", [st], [wo], lambda e: e.tensor_copy(out=wo[:, c4 * 2:(c4 + 1) * 2, :], in_=st[:]))
            else:
                k.op("act", [st], [wo], lambda e: e.copy(out=wo[:, c4 * 2:(c4 + 1) * 2, :], in_=st[:]))
        ytr = Ring(k, es, 2, [128, 24, 128], BF16, "yT")
        xr = Ring(k, es, 2, [128, D_MODEL], F32, "xo")
        outr = Ring(k, es, 2, [128, D_MODEL], F32, "xn")
        psm = [k.ps(es, [128, 512], F32, "psm") for _ in range(4)]
        for tt in range(L // 128):
            r0 = tt * 128
            yt, ys = ytr.next()
            k.dma("sp", ys, yt[:], S["YT"][:, r0:r0 + 128].rearrange("(c p) t -> p c t", p=128), [S["YT_b"]], [yt])
            xt, xs = xr.next()
            k.dma("sp", xs, xt[:], xsrc[r0:r0 + 128, :], [xsrc_b], [xt])
            ot, osm = outr.next()
            for nb in range(4):
                pm = psm[nb]
                for c in range(24):
                    k.op("pe", [yt, wo], [pm], lambda e: e.matmul(pm[:, :], lhsT=yt[:, c, :], rhs=wo[:, c, nb * 512:(nb + 1) * 512],
                                                                   start=(c == 0), stop=(c == 23)))
                k.op("dve", [pm, xt], [ot], lambda e: e.tensor_tensor(out=ot[:, nb * 512:(nb + 1) * 512], in0=pm[:],
                                                                      in1=xt[:, nb * 512:(nb + 1) * 512], op=ALU.add))
            k.dma("sp", osm, S["XRES"][r0:r0 + 128, :], ot[:], [ot], [S["XRES_b"]])
        k.barrier()
    k.pop()


def phase_peer(k, L, lyr, W, S, C):
    identf, identb = C["identf"], C["identb"]
    NEG = -1.0e30
    k.push()
    with contextlib.ExitStack() as es:
        sem = k.newsem()
        gB = k.sb(es, [128, D_MODEL], F32, "gB")
        k.dma("sp", sem, gB[:], W["norm_ffn"][lyr:lyr + 1, :].broadcast_to([128, D_MODEL]), [], [gB], sync=True)
        wq = k.sb(es, [128, 16, D_MODEL], BF16, "wq")
        skT = k.sb(es, [128, 16, 128], BF16, "skT")
        pq = [k.ps(es, [128, 512], F32, "pq") for _ in range(4)]
        pst = [k.ps(es, [128, 1024], BF16, "pst") for _ in range(2)]
        with contextlib.ExitStack() as es1:
            stg = Ring(k, es1, 2, [128, 2, D_MODEL], F32, "wstg")
            for c2 in range(8):
                st, ss = stg.next()
                k.dma("sp", ss, st[:], W["peer_wq"][lyr, c2 * 256:(c2 + 1) * 256, :].rearrange("(c p) n -> p c n", p=128), [], [st])
                if c2 % 2 == 0:
                    k.op("pool", [st], [wq], lambda e: e.tensor_copy(out=wq[:, c2 * 2:(c2 + 1) * 2, :], in_=st[:]))
                else:
                    k.op("act", [st], [wq], lambda e: e.copy(out=wq[:, c2 * 2:(c2 + 1) * 2, :], in_=st[:]))
            skf = k.sb(es1, [128, 16, 128], F32, "skf")
            k.dma("sp", sem, skf[:], W["peer_subkeys"][lyr].rearrange("m n d -> n m d"), [], [skf], sync=True)
            for q4 in range(4):
                pm = pq[q4]
                for i in range(4):
                    m = q4 * 4 + i
                    k.op("pe", [skf, identf], [pm], lambda e: e.transpose(out=pm[:, i * 128:(i + 1) * 128], in_=skf[:, m, :],
                                                                          identity=identf[:]))
                k.op("act", [pm], [skT], lambda e: e.copy(out=skT[:, q4 * 4:(q4 + 1) * 4, :],
                                                          in_=pm[:, :].rearrange("p (a n) -> p a n", a=4)))
            k.barrier()
        xr = Ring(k, es, 1, [128, D_MODEL], F32, "xm")
        junkb = k.sb(es, [128, D_MODEL], BF16, "junkb")
        ssq = k.sb(es, [128, 1], F32, "ssq")
        rstd = k.sb(es, [128, 1], F32, "rstd")
        hn = k.sb(es, [128, D_MODEL], F32, "hn")
        hnb = k.sb(es, [128, D_MODEL], BF16, "hnb")
        hnT = k.sb(es, [128, 16, 128], BF16, "hnT")
        qTb = k.sb(es, [128, 16, 128], BF16, "qTb")
        sc = k.sb(es, [128, 16, 128], F32, "sc")
        scw = k.sb(es, [128, 128], F32, "scw")
        sv = k.sb(es, [128, 16, 16], F32, "sv")
        si = k.sb(es, [128, 16, 16], U32, "si")
        sif = k.sb(es, [128, 16, 16], F32, "sif")
        cand = k.sb(es, [128, 8, 256], F32, "cand")
        cidx = k.sb(es, [128, 8, 256], F32, "cidx")
        cw_ = k.sb(es, [128, 256], F32, "cw_")
        top = k.sb(es, [128, 8, 16], F32, "top")
        gate = k.sb(es, [128, 8, 16], F32, "gate")
        zs = k.sb(es, [128, 8], F32, "zs")
        j256 = k.sb(es, [128, 256], F32, "j256")
        eidf = k.sb(es, [128, 128], F32, "eidf")
        eid = k.sb(es, [128, 128], I32, "eid")
        pre = k.sb(es, [128, 128], F32, "pre")
        actg = k.sb(es, [128, 128], F32, "actg")
        acc = k.sb(es, [128, D_MODEL], F32, "acc")
        gr = Ring(k, es, 4, [128, D_MODEL], F32, "gath", sw=True)
        outr = Ring(k, es, 1, [128, D_MODEL], F32, "xn")
        utab = W["peer_u"]
        vtab = W["peer_v"]
        for tt in range(L // 128):
            r0 = tt * 128
            xm, xs = xr.next()
            k.dma("sp", xs, xm[:], S["XRES"][r0:r0 + 128, :], [S["XRES_b"]], [xm])
            k.op("act", [xm], [junkb, ssq], lambda e: e.activation(out=junkb[:], in_=xm[:], func=AF.Square,
                                                                   accum_out=ssq[:, 0:1]))
            rmsnorm_rstd(k, ssq, rstd, D_MODEL)
            k.op("dve", [xm, rstd, gB], [hn], lambda e: e.scalar_tensor_tensor(
                out=hn[:], in0=xm[:], scalar=rstd[:, 0:1], in1=gB[:], op0=ALU.mult, op1=ALU.mult))
            k.op("pool", [hn], [hnb], lambda e: e.tensor_copy(out=hnb[:], in_=hn[:]))
            for half in range(2):
                pt = pst[half]
                for j in range(8):
                    kc = half * 8 + j
                    k.op("pe", [hnb, identb], [pt], lambda e: e.transpose(
                        out=pt[:, j * 128:(j + 1) * 128], in_=hnb[:, kc * 128:(kc + 1) * 128], identity=identb[:]))
                k.op("act", [pt], [hnT], lambda e: e.copy(out=hnT[:, half * 8:(half + 1) * 8, :],
                                                          in_=pt[:, :].rearrange("p (j t) -> p j t", j=8)))
            for q4 in range(4):
                pm = pq[q4]
                for i in range(4):
                    m = q4 * 4 + i
                    for kc in range(16):
                        k.op("pe", [wq, hnT], [pm], lambda e: e.matmul(
                            pm[:, i * 128:(i + 1) * 128], lhsT=wq[:, kc, m * 128:(m + 1) * 128], rhs=hnT[:, kc, :],
                            start=(kc == 0), stop=(kc == 15)))
                k.op("act", [pm], [qTb], lambda e: e.copy(out=qTb[:, q4 * 4:(q4 + 1) * 4, :],
                                                          in_=pm[:, :].rearrange("p (a t) -> p a t", a=4)))
            for q4 in range(4):
                pm = pq[q4]
                for i in range(4):
                    m = q4 * 4 + i
                    k.op("pe", [qTb, skT], [pm], lambda e: e.matmul(
                        pm[:, i * 128:(i + 1) * 128], lhsT=qTb[:, m, :], rhs=skT[:, m, :], start=True, stop=True))
                k.op("act", [pm], [sc], lambda e: e.copy(out=sc[:, q4 * 4:(q4 + 1) * 4, :],
                                                         in_=pm[:, :].rearrange("p (a n) -> p a n", a=4)))
            for m in range(16):
                k.op("dve", [sc], [sv], lambda e: e.max(out=sv[:, m, 0:8], in_=sc[:, m, :]))
                k.op("dve", [sc, sv], [scw], lambda e: e.match_replace(out=scw[:], in_to_replace=sv[:, m, 0:8],
                                                                       in_values=sc[:, m, :], imm_value=NEG))
                k.op("dve", [scw], [sv], lambda e: e.max(out=sv[:, m, 8:16], in_=scw[:]))
                k.op("dve", [sc, sv], [si], lambda e: e.max_index(out=si[:, m, 0:8], in_max=sv[:, m, 0:8], in_values=sc[:, m, :]))
                k.op("dve", [sc, sv], [si], lambda e: e.max_index(out=si[:, m, 8:16], in_max=sv[:, m, 8:16], in_values=sc[:, m, :]))
            k.op("dve", [si], [sif], lambda e: e.tensor_copy(out=sif[:], in_=si[:]))
            svv = sv[:, :, :].rearrange("p (h two) a -> p h two a", two=2)
            sfv = sif[:, :, :].rearrange("p (h two) a -> p h two a", two=2)
            c4 = cand[:, :, :].rearrange("p h (a b) -> p h a b", a=16)
            x4 = cidx[:, :, :].rearrange("p h (a b) -> p h a b", a=16)
            k.op("dve", [sv], [cand], lambda e: e.tensor_tensor(
                out=c4, in0=svv[:, :, 0, :].unsqueeze(3).to_broadcast([128, 8, 16, 16]),
                in1=svv[:, :, 1, :].unsqueeze(2).to_broadcast([128, 8, 16, 16]), op=ALU.add))
            k.op("dve", [sif], [sif], lambda e: e.tensor_scalar(out=sfv[:, :, 0, :], in0=sfv[:, :, 0, :], scalar1=128.0,
                                                                scalar2=float(lyr * N_EXP), op0=ALU.mult, op1=ALU.add))
            k.op("dve", [sif], [cidx], lambda e: e.tensor_tensor(
                out=x4, in0=sfv[:, :, 0, :].unsqueeze(3).to_broadcast([128, 8, 16, 16]),
                in1=sfv[:, :, 1, :].unsqueeze(2).to_broadcast([128, 8, 16, 16]), op=ALU.add))
            for h in range(8):
                k.op("dve", [cand], [top], lambda e: e.max(out=top[:, h, 0:8], in_=cand[:, h, :]))
                k.op("dve", [cand, top], [cw_], lambda e: e.match_replace(out=cw_[:], in_to_replace=top[:, h, 0:8],
                                                                          in_values=cand[:, h, :], imm_value=NEG))
                k.op("dve", [cw_], [top], lambda e: e.max(out=top[:, h, 8:16], in_=cw_[:]))
            k.op("dve", [top], [gate], lambda e: e.tensor_tensor(
                out=gate[:], in0=top[:], in1=top[:, :, 0:1].to_broadcast([128, 8, 16]), op=ALU.subtract))
            k.op("act", [gate], [gate], lambda e: e.activation(out=gate[:], in_=gate[:], func=AF.Exp))
            k.op("dve", [gate], [zs], lambda e: e.tensor_reduce(out=zs[:], in_=gate[:], axis=AX.X, op=ALU.add))
            k.op("dve", [zs], [zs], lambda e: e.reciprocal(out=zs[:], in_=zs[:]))
            k.op("dve", [gate, zs], [gate], lambda e: e.tensor_tensor(
                out=gate[:], in0=gate[:], in1=zs[:, :].unsqueeze(2).to_broadcast([128, 8, 16]), op=ALU.mult))
            for h in range(8):
                for kk in range(16):
                    k.op("dve", [cand, top, cidx], [j256, eidf], lambda e: e.scalar_tensor_tensor(
                        out=j256[:], in0=cand[:, h, :], scalar=top[:, h, kk:kk + 1], in1=cidx[:, h, :],
                        op0=ALU.is_equal, op1=ALU.mult, accum_out=eidf[:, h * 16 + kk:h * 16 + kk + 1]))
            k.op("dve", [eidf], [eidf], lambda e: e.tensor_scalar(out=eidf[:], in0=eidf[:], scalar1=float(lyr * N_EXP),
                                                                  scalar2=float(lyr * N_EXP + N_EXP - 1), op0=ALU.max, op1=ALU.min))
            k.op("dve", [eidf], [eid], lambda e: e.tensor_copy(out=eid[:], in_=eidf[:]))
            for hk in range(128):
                gt, gs = gr.next()
                k.dma("pool", gs, gt[:], utab, [eid], [gt], indirect=bass.IndirectOffsetOnAxis(ap=eid[:, hk:hk + 1], axis=0))
                k.op("dve", [gt, hn], [gt, pre], lambda e: e.scalar_tensor_tensor(
                    out=gt[:], in0=gt[:], scalar=1.0, in1=hn[:], op0=ALU.mult, op1=ALU.mult,
                    accum_out=pre[:, hk:hk + 1]))
            k.op("act", [pre], [actg], lambda e: e.activation(out=actg[:], in_=pre[:], func=AF.Gelu))
            k.op("dve", [actg, gate], [actg], lambda e: e.tensor_tensor(
                out=actg[:], in0=actg[:], in1=gate[:, :, :].rearrange("p h a -> p (h a)"), op=ALU.mult))
            for hk in range(128):
                gt, gs = gr.next()
                k.dma("pool", gs, gt[:], vtab, [eid], [gt], indirect=bass.IndirectOffsetOnAxis(ap=eid[:, hk:hk + 1], axis=0))
                if hk == 0:
                    k.op("dve", [gt, actg], [acc], lambda e: e.tensor_scalar(out=acc[:], in0=gt[:], scalar1=actg[:, 0:1],
                                                                            scalar2=None, op0=ALU.mult))
                else:
                    k.op("dve", [gt, actg, acc], [acc], lambda e: e.scalar_tensor_tensor(
                        out=acc[:], in0=gt[:], scalar=actg[:, hk:hk + 1], in1=acc[:], op0=ALU.mult, op1=ALU.add))
            ot, osm = outr.next()
            k.op("pool", [xm, acc], [ot], lambda e: e.tensor_tensor(out=ot[:], in0=xm[:], in1=acc[:], op=ALU.add))
            k.dma("sp", osm, S["XRES"][r0:r0 + 128, :], ot[:], [ot], [S["XRES_b"]])
        k.barrier()
    k.pop()


def rope_consts(L):
    inv = (500000.0 ** (-np.arange(0, 16, 2, dtype=np.float32) / 16)).astype(np.float32)
    ang = np.arange(L, dtype=np.float32)[:, None] * inv[None, :]
    cosT = np.ones((128, L), np.float32)
    sinT = np.zeros((128, L), np.float32)
    for p in range(128):
        d = p % 64
        if d < 16:
            cosT[p] = np.cos(ang[:, d % 8])
            sinT[p] = np.sin(ang[:, d % 8])
    rotT = np.zeros((128, 128), np.float32)
    for m in range(128):
        d = m % 64
        if d < 8:
            rotT[m + 8, m] = -1.0
        elif d < 16:
            rotT[m - 8, m] = 1.0
    return cosT, sinT, rotT


WEIGHT_SHAPES = {
    "norm_mix": (DEPTH, D_MODEL), "w_in": (DEPTH, D_MODEL, D_IN), "ssd_dt_bias": (DEPTH, 48), "ssd_a_log": (DEPTH, 48),
    "ssd_d": (DEPTH, SSD_HEADS), "ssd_norm": (DEPTH, D_SSD),
    "ssd_conv_wl": (DEPTH, 128, 100), "ssd_conv_bl": (DEPTH, 128, 20), "sc_conv_wl": (DEPTH, 128, 18),
    "sc_norm_l": (DEPTH, 128, 6), "att_norm": (DEPTH, D_ATT), "w_out": (DEPTH, D_MIX, D_MODEL),
    "norm_ffn": (DEPTH, D_MODEL), "peer_wq": (DEPTH, D_MODEL, D_MODEL), "peer_subkeys": (DEPTH, 16, 128, 128),
    "peer_u": (DEPTH * N_EXP, D_MODEL), "peer_v": (DEPTH * N_EXP, D_MODEL), "norm_final": (1, D_MODEL),
}


class LazyW(dict):
    def __init__(self, nc):
        super().__init__()
        self.nc = nc

    def __missing__(self, n):
        v = self.nc.dram_tensor(n, list(WEIGHT_SHAPES[n]), F32, kind="ExternalInput").ap()
        self[n] = v
        return v


def build(L, phases=ALL_PHASES, debug=(), nlayers=DEPTH, feed=()):
    nc = bass.Bass("TRN2", target_bir_lowering=False)
    x = nc.dram_tensor("x", [L, D_MODEL], F32, kind="ExternalInput").ap()
    y = nc.dram_tensor("y", [L, D_MODEL], F32, kind="ExternalOutput").ap()
    W = LazyW(nc)
    S = {}

    def scratch(name, shape, dt):
        kind = "ExternalOutput" if name in debug else ("ExternalInput" if name in feed else "Internal")
        S[name] = nc.dram_tensor("s_" + name, list(shape), dt, kind=kind).ap()
        S[name + "_b"] = Buf(multi=True)

    scratch("Z", [L, D_SSD], F32)
    scratch("XBCT", [XBC, L], F32)
    scratch("DT", [L, 48], F32)
    scratch("SCT", [3 * D_SC, L], F32)
    scratch("QT", [2304, L], BF16)
    scratch("KT", [2304, L], BF16)
    scratch("V", [L, 2304], BF16)
    scratch("XRES", [L, D_MODEL], F32)
    scratch("X", [L, D_SSD], F32)
    scratch("B", [L, 512], BF16)
    scratch("BT", [512, L], BF16)
    scratch("CT", [512, L], BF16)
    scratch("YB", [L, D_SSD], F32)
    scratch("YT", [D_MIX, L], BF16)
    scratch("O0", [L, 780], F32)
    scratch("O1", [L, 780], F32)
    scratch("O2", [L, 780], F32)
    x_b = Buf(multi=True)
    y_b = Buf(multi=True)
    with contextlib.ExitStack() as es:
        k = KB(nc, es)
        C = {}
        csem = k.newsem()

        def cload(name, shape=(128, 128), dt=F32):
            d = nc.dram_tensor(name, list(shape), F32, kind="ExternalInput").ap()
            if dt == BF16:
                tb = k.sb(es, list(shape), BF16, name + "b")
                t = k.sb(es_stage, list(shape), F32, name)
                k.dma("sp", csem, t[:], d[:, :], [], [t], sync=True)
                k.op("dve", [t], [tb], lambda e: e.tensor_copy(out=tb[:], in_=t[:]))
                return tb
            t = k.sb(es, list(shape), F32, name)
            k.dma("sp", csem, t[:], d[:, :], [], [t], sync=True)
            return t

        C["identf"] = cload("ident")
        C["rotT"] = cload("rotT")
        C["onesf"] = cload("onesf")
        for n in ("U", "Lo", "nU", "nLo", "nmf", "nmb"):
            C[n] = cload(n)
        onec = k.sb(es, [128, 1], F32, "onec")
        k.op("dve", [], [onec], lambda e: e.memset(onec[:], 1.0))
        C["onec"] = onec
        C["identb"] = k.sb(es, [128, 128], BF16, "identb")
        k.op("dve", [C["identf"]], [C["identb"]], lambda e: e.tensor_copy(out=C["identb"][:], in_=C["identf"][:]))
        bnames = ("maskA", "maskB", "maskA0", "maskB1")
        tbs = {n: k.sb(es, [128, 512], BF16, n + "b") for n in bnames}
        with contextlib.ExitStack() as es_stage:
            for n in bnames:
                d = nc.dram_tensor(n, [128, 512], F32, kind="ExternalInput").ap()
                t = k.sb(es_stage, [128, 512], F32, n)
                k.dma("sp", csem, t[:], d[:, :], [], [t], sync=True)
                k.op("dve", [t], [tbs[n]], lambda e: e.tensor_copy(out=tbs[n][:], in_=t[:]))
                C[n] = tbs[n]
            k.barrier()
        C["cosT"] = nc.dram_tensor("cosT", [128, L], F32, kind="ExternalInput").ap()
        C["sinT"] = nc.dram_tensor("sinT", [128, L], F32, kind="ExternalInput").ap()
        k.barrier()
        xsrc, xsrc_b = x, x_b
        for lyr in range(nlayers):
            if "a" in phases:
                phase_a(k, L, lyr, xsrc, xsrc_b, W, S, C)
            if "conv" in phases:
                phase_conv(k, L, lyr, W, S, C)
            if "ssd" in phases:
                phase_ssd(k, L, lyr, W, S, C)
            if "sc" in phases:
                phase_sc(k, L, lyr, W, S, C)
            if "att" in phases:
                phase_att(k, L, lyr, W, S, C)
            if "wout" in phases:
                phase_wout(k, L, lyr, xsrc, xsrc_b, W, S, C)
            if "peer" in phases:
                phase_peer(k, L, lyr, W, S, C)
            if "wout" in phases:
                xsrc, xsrc_b = S["XRES"], S["XRES_b"]
        if "final" in phases:
            phase_final(k, L, xsrc, xsrc_b, W, y, y_b)
        k.barrier()
    return nc


def host_consts(L):
    cosT, sinT, rotT = rope_consts(L)
    i = np.arange(128)
    U = (i[:, None] <= i[None, :]).astype(np.float32)
    Lo = (i[:, None] >= i[None, :]).astype(np.float32)
    NEGM = -30000.0
    nmf = np.where(i[None, :] >= i[:, None], 0.0, NEGM).astype(np.float32)
    nmb = np.where(i[None, :] <= i[:, None], 0.0, NEGM).astype(np.float32)
    mA = np.where(i[:, None] >= i[None, :], 0.0, NEGM).astype(np.float32)
    mB = np.where(i[:, None] <= i[None, :], 0.0, NEGM).astype(np.float32)
    eye = np.eye(128, dtype=np.float32)
    mA0 = mA.copy()
    mA0[0:64, :] = NEGM
    mB1 = mB.copy()
    mB1[64:128, :] = NEGM
    return {"maskA0": np.ascontiguousarray(np.tile(mA0, (1, 4))), "maskB1": np.ascontiguousarray(np.tile(mB1, (1, 4))),"cosT": cosT, "sinT": sinT, "rotT": rotT, "ident": eye,
            "onesf": np.ones((128, 128), np.float32), "U": U, "Lo": Lo, "nU": -U, "nLo": -Lo, "nmf": nmf, "nmb": nmb,
            "maskA": np.ascontiguousarray(np.tile(mA, (1, 4))), "maskB": np.ascontiguousarray(np.tile(mB, (1, 4)))}


def prep_weights(inp):
    w = {}
    f = lambda a: np.asarray(a, dtype=np.float32)
    for n, s in WEIGHT_SHAPES.items():
        if n in inp:
            w[n] = np.ascontiguousarray(f(inp[n]).reshape(s))
    if "ssd_conv_w" in inp:
        cw = f(inp["ssd_conv_w"]).reshape(DEPTH, 5, 20, 128)
        w["ssd_conv_wl"] = np.ascontiguousarray(cw.transpose(0, 3, 2, 1).reshape(DEPTH, 128, 100))
        cb = f(inp["ssd_conv_b"]).reshape(DEPTH, 20, 128)
        w["ssd_conv_bl"] = np.ascontiguousarray(cb.transpose(0, 2, 1))
    if "sc_conv_w" in inp:
        sw = f(inp["sc_conv_w"]).reshape(DEPTH, 3, 6, 128)
        w["sc_conv_wl"] = np.ascontiguousarray(sw.transpose(0, 3, 2, 1).reshape(DEPTH, 128, 18))
        sn = f(inp["sc_norm"]).reshape(DEPTH, 6, 128)
        w["sc_norm_l"] = np.ascontiguousarray(sn.transpose(0, 2, 1))
    return w


def kernel(**inputs):
    L = 8192
    xs = [np.asarray(inputs["x_prompt"][i]) for i in range(2)] + [np.asarray(inputs["x_sample"][i]) for i in range(4)]
    xs = xs + [xs[0], xs[1]]
    w = prep_weights(inputs)
    consts = host_consts(L)
    nc = build(L)
    in_maps = []
    for c in range(8):
        m = {"x": np.ascontiguousarray(xs[c], dtype=np.float32)}
        m.update(w)
        m.update(consts)
        in_maps.append(m)
    res = run_bass_kernel_spmd(nc, in_maps, core_ids=list(range(8)))
    ys = [res.results[c]["y"] for c in range(6)]
    return (np.stack(ys[0:2], axis=0).astype(np.float32), np.stack(ys[2:6], axis=0).astype(np.float32))
```

```python
import contextlib
import numpy as np
import ml_dtypes
import concourse.bass as bass
import concourse.mybir as mybir
from concourse.bass_utils import run_bass_kernel_spmd

F32 = mybir.dt.float32
BF16 = mybir.dt.bfloat16
I32 = mybir.dt.int32
U32 = mybir.dt.uint32
AF = mybir.ActivationFunctionType
ALU = mybir.AluOpType
AX = mybir.AxisListType

D_MODEL = 2048
DEPTH = 2
D_SSD = 1536
SSD_HEADS = 24
XBC = 2560
D_SC = 768
D_ATT = 768
D_MIX = 3072
D_IN = 13360
EPS = 1e-6
C_Z = 0
C_XBC = 1536
C_DT = 4096
C_SC = 4144
C_ATT = 6448
N_EXP = 16384
ATT_GROUPS = (0, 1, 2)
ATT_STAGE = 9
PEER_DBG = ""
ALL_PHASES = ("a", "conv", "ssd", "sc", "att", "wout", "peer", "final")


class Buf:
    __slots__ = ("w", "r", "multi")

    def __init__(self, multi=False):
        self.w = {}
        self.r = {}
        self.multi = multi


class T:
    def __init__(self, t, b=None):
        self.t = t
        self.b = b if b is not None else Buf()

    def __getitem__(self, idx):
        return self.t[idx]


class KB:
    def __init__(self, nc, es):
        self.nc = nc
        self.es = es
        self.eng = {"pe": nc.tensor, "act": nc.scalar, "dve": nc.vector, "pool": nc.gpsimd, "sp": nc.sync}
        self.sems = {}
        self.cnt = {}
        for e in ("pe", "act", "dve", "pool"):
            self.sems[e] = es.enter_context(nc.semaphore("c_" + e))
            self.cnt[e] = 0
        self.waited = {e: {} for e in self.eng}
        self.nd = 0
        self.uid = 0
        self.free = []
        self.free_sw = []
        self.scopes = []

    def newsem(self, sw=False):
        fl = self.free_sw if sw else self.free
        if fl:
            key = fl.pop()
        else:
            key = ("dw%d" if sw else "d%d") % self.nd
            self.nd += 1
            self.sems[key] = self.es.enter_context(self.nc.semaphore(key))
            self.cnt[key] = 0
        for sc in self.scopes:
            sc.append(key)
        return key

    def push(self):
        self.scopes.append([])

    def pop(self):
        sc = self.scopes.pop()
        for key in sc:
            fl = self.free_sw if key.startswith("dw") else self.free
            if key not in fl:
                fl.append(key)

    def _deps(self, e, reads, writes):
        need = {}
        for b in reads:
            for s, v in b.w.items():
                if v > need.get(s, 0):
                    need[s] = v
        for b in writes:
            if not b.multi:
                for s, v in b.w.items():
                    if v > need.get(s, 0):
                        need[s] = v
            for s, v in b.r.items():
                if v > need.get(s, 0):
                    need[s] = v
        wd = self.waited[e]
        for s, v in need.items():
            if s == e and e == "pe":
                continue
            if wd.get(s, 0) >= v:
                continue
            if s[0] == "d":
                v = self.cnt[s]
            self.eng[e].wait_ge(self.sems[s], v)
            wd[s] = v

    def _rec(self, key, val, reads, writes):
        for b in reads:
            b.r[key] = val
        for b in writes:
            if b.multi:
                b.w[key] = val
            else:
                b.w = {key: val}
            b.r = {}

    def op(self, e, reads, writes, fn):
        reads = [x.b if isinstance(x, T) else x for x in reads]
        writes = [x.b if isinstance(x, T) else x for x in writes]
        self._deps(e, reads, writes)
        ins = fn(self.eng[e])
        self.cnt[e] += 1
        ins.then_inc(self.sems[e], 1)
        self._rec(e, self.cnt[e], reads, writes)
        return ins

    def dma(self, q, sem, out, in_, reads, writes, indirect=None, sync=False, **kw):
        reads = [x.b if isinstance(x, T) else x for x in reads]
        writes = [x.b if isinstance(x, T) else x for x in writes]
        self._deps(q, reads, writes)
        if indirect is not None:
            ins = self.eng[q].indirect_dma_start(out=out, out_offset=None, in_=in_, in_offset=indirect, **kw)
        else:
            ins = self.eng[q].dma_start(out=out, in_=in_, **kw)
        self.cnt[sem] += 16
        ins.then_inc(self.sems[sem], 16)
        self._rec(sem, self.cnt[sem], reads, writes)
        if sync:
            self.eng[q].wait_ge(self.sems[sem], self.cnt[sem])
            self.waited[q][sem] = self.cnt[sem]
        return ins

    def barrier(self):
        for e in self.eng:
            wd = self.waited[e]
            for s, v in self.cnt.items():
                if s == e or v == 0 or wd.get(s, 0) >= v:
                    continue
                self.eng[e].wait_ge(self.sems[s], v)
                wd[s] = v

    def sb(self, es, shape, dt, name=None):
        self.uid += 1
        t = es.enter_context(self.nc.sbuf_tensor("%s_%d" % (name or "t", self.uid), list(shape), dt))
        return T(t)

    def ps(self, es, shape, dt, name=None):
        self.uid += 1
        t = es.enter_context(self.nc.psum_tensor("%s_%d" % (name or "p", self.uid), list(shape), dt))
        return T(t)


class Ring:
    def __init__(self, k, es, n, shape, dt, name, dma=True, sw=False):
        self.tiles = [k.sb(es, shape, dt, name) for _ in range(n)]
        self.sems = [k.newsem(sw) for _ in range(n)] if dma else [None] * n
        self.i = 0

    def next(self):
        t, s = self.tiles[self.i], self.sems[self.i]
        self.i = (self.i + 1) % len(self.tiles)
        return t, s


def rmsnorm_rstd(k, ssq, rstd, n):
    k.op("dve", [ssq], [rstd], lambda e: e.tensor_scalar(out=rstd[:, 0:1], in0=ssq[:, 0:1], scalar1=1.0 / n,
                                                         scalar2=EPS, op0=ALU.mult, op1=ALU.add))
    k.op("act", [rstd], [rstd], lambda e: e.activation(out=rstd[:, 0:1], in_=rstd[:, 0:1], func=AF.Sqrt))
    k.op("dve", [rstd], [rstd], lambda e: e.reciprocal(out=rstd[:, 0:1], in_=rstd[:, 0:1]))


def phase_a(k, L, lyr, xsrc, xsrc_b, W, S, C):
    nc = k.nc
    TB = 1024 if L >= 1024 else L
    NT = TB // 128
    k.push()
    with contextlib.ExitStack() as es:
        gB = k.sb(es, [128, D_MODEL], F32, "gB")
        gsem = k.newsem()
        k.dma("sp", gsem, gB[:], W["norm_mix"][lyr:lyr + 1, :].broadcast_to([128, D_MODEL]), [], [gB], sync=True)
        xr = Ring(k, es, 2, [128, D_MODEL], F32, "xt")
        junk = k.sb(es, [128, D_MODEL], BF16, "junk")
        hb = [k.sb(es, [128, D_MODEL], BF16, "hb") for _ in range(2)]
        ssq = [k.sb(es, [128, 1], F32, "ssq") for _ in range(2)]
        rstd = [k.sb(es, [128, 1], F32, "rstd") for _ in range(2)]
        hT = k.sb(es, [128, 16, TB], BF16, "hT")
        hTb = [Buf() for _ in range(NT)]
        wf = Ring(k, es, 2, [128, 16, 512], F32, "wf")
        wb = [k.sb(es, [128, 16, 512], BF16, "wb") for _ in range(2)]
        ev = Ring(k, es, 3, [128, 512], F32, "ev")
        evb = Ring(k, es, 3, [128, 512], BF16, "evb")
        cs = Ring(k, es, 2, [128, 512], F32, "cos")
        sn = Ring(k, es, 2, [128, 512], F32, "sin")
        qsb = [k.sb(es, [128, 512], F32, "qsb") for _ in range(2)]
        t1 = [k.sb(es, [128, 512], F32, "t1") for _ in range(2)]
        t2 = [k.sb(es, [128, 512], F32, "t2") for _ in range(2)]
        pst = [k.ps(es, [128, 1024], BF16, "pst") for _ in range(2)]
        psm = [k.ps(es, [128, 512], F32, "psm") for _ in range(4)]
        psr = [k.ps(es, [128, 512], F32, "psr") for _ in range(2)]
        identb = C["identb"]
        rotT = C["rotT"]
        wi = 0
        mi = 0
        ri = 0
        segs = [("z", C_Z, 1536, 512), ("xbc", C_XBC, 2560, 512), ("dt", C_DT, 48, 48),
                ("sc", C_SC, 2304, 384), ("q", C_ATT, 2304, 384), ("k", C_ATT + 2304, 2304, 384),
                ("v", C_ATT + 4608, 2304, 384)]
        for sbk in range(L // TB):
            tok0 = sbk * TB
            for tt in range(NT):
                xt, xs = xr.next()
                k.dma("sp", xs, xt[:], xsrc[tok0 + tt * 128: tok0 + (tt + 1) * 128, :], [xsrc_b], [xt])
                p = tt % 2
                k.op("act", [xt], [junk, ssq[p]], lambda e: e.activation(out=junk[:], in_=xt[:], func=AF.Square,
                                                                          accum_out=ssq[p][:, 0:1]))
                rmsnorm_rstd(k, ssq[p], rstd[p], D_MODEL)
                k.op("dve", [xt, rstd[p], gB], [hb[p]], lambda e: e.scalar_tensor_tensor(
                    out=hb[p][:], in0=xt[:], scalar=rstd[p][:, 0:1], in1=gB[:], op0=ALU.mult, op1=ALU.mult))
                for half in range(2):
                    pt = pst[half]
                    for j in range(8):
                        kc = half * 8 + j
                        k.op("pe", [hb[p], identb], [pt], lambda e: e.transpose(
                            out=pt[:, j * 128:(j + 1) * 128], in_=hb[p][:, kc * 128:(kc + 1) * 128],
                            identity=identb[:]))
                    eng = "act" if half == 0 else "dve"
                    src = pt[:, :].rearrange("p (j t) -> p j t", j=8)
                    dst = hT[:, half * 8:(half + 1) * 8, tt * 128:(tt + 1) * 128]
                    if eng == "act":
                        k.op("act", [pt], [hTb[tt]], lambda e: e.copy(out=dst, in_=src))
                    else:
                        k.op("dve", [pt], [hTb[tt]], lambda e: e.tensor_copy(out=dst, in_=src))
            for kind, c0, ncs, bw in segs:
                for blk in range(ncs // bw):
                    cb = c0 + blk * bw
                    wft, wfs = wf.next()
                    wbt = wb[wi % 2]
                    wi += 1
                    k.dma("sp", wfs, wft[:, :, 0:bw],
                          W["w_in"][lyr, :, cb:cb + bw].rearrange("(kc p) c -> p kc c", p=128), [], [wft])
                    k.op("pool", [wft], [wbt], lambda e: e.tensor_copy(out=wbt[:, :, 0:bw], in_=wft[:, :, 0:bw]))
                    if kind in ("z", "dt", "v"):
                        for tt in range(NT):
                            pm = psm[mi % 4]
                            mi += 1
                            for kc in range(16):
                                k.op("pe", [hTb[tt], wbt], [pm], lambda e: e.matmul(
                                    pm[:, 0:bw], lhsT=hT[:, kc, tt * 128:(tt + 1) * 128], rhs=wbt[:, kc, 0:bw],
                                    start=(kc == 0), stop=(kc == 15)))
                            r0 = tok0 + tt * 128
                            if kind == "v":
                                et, esm = evb.next()
                                k.op("act", [pm], [et], lambda e: e.copy(out=et[:, 0:bw], in_=pm[:, 0:bw]))
                                k.dma("sp", esm, S["V"][r0:r0 + 128, blk * bw:(blk + 1) * bw], et[:, 0:bw],
                                      [et], [S["V_b"]])
                            else:
                                et, esm = ev.next()
                                k.op("act", [pm], [et], lambda e: e.copy(out=et[:, 0:bw], in_=pm[:, 0:bw]))
                                dst = S["Z"] if kind == "z" else S["DT"]
                                dstb = S["Z_b"] if kind == "z" else S["DT_b"]
                                k.dma("sp", esm, dst[r0:r0 + 128, blk * bw:(blk + 1) * bw], et[:, 0:bw],
                                      [et], [dstb])
                    else:
                        for tb in range(TB // 512):
                            t0 = tok0 + tb * 512
                            hdeps = [hTb[tb * 4 + i] for i in range(4)]
                            if kind in ("q", "k"):
                                ct, csm = cs.next()
                                st, ssm = sn.next()
                                k.dma("sp", csm, ct[:], C["cosT"][:, t0:t0 + 512], [], [ct])
                                k.dma("sp", ssm, st[:], C["sinT"][:, t0:t0 + 512], [], [st])
                            for ch in range(bw // 128):
                                pm = psm[mi % 4]
                                mi += 1
                                for kc in range(16):
                                    k.op("pe", hdeps + [wbt], [pm], lambda e: e.matmul(
                                        pm[:, :], lhsT=wbt[:, kc, ch * 128:(ch + 1) * 128],
                                        rhs=hT[:, kc, tb * 512:(tb + 1) * 512], start=(kc == 0), stop=(kc == 15)))
                                row = blk * bw + ch * 128
                                if kind in ("xbc", "sc"):
                                    et, esm = ev.next()
                                    k.op("act", [pm], [et], lambda e: e.copy(out=et[:], in_=pm[:]))
                                    dst = S["XBCT"] if kind == "xbc" else S["SCT"]
                                    dstb = S["XBCT_b"] if kind == "xbc" else S["SCT_b"]
                                    k.dma("sp", esm, dst[row:row + 128, t0:t0 + 512], et[:], [et], [dstb])
                                else:
                                    r = ri % 2
                                    ri += 1
                                    pr = psr[r]
                                    k.op("act", [pm], [qsb[r]], lambda e: e.copy(out=qsb[r][:], in_=pm[:]))
                                    k.op("pe", [qsb[r], rotT], [pr], lambda e: e.matmul(
                                        pr[:, :], lhsT=rotT[:], rhs=qsb[r][:], start=True, stop=True))
                                    k.op("pool", [qsb[r], ct], [t1[r]], lambda e: e.tensor_tensor(
                                        out=t1[r][:], in0=qsb[r][:], in1=ct[:], op=ALU.mult))
                                    k.op("dve", [pr, st], [t2[r]], lambda e: e.tensor_tensor(
                                        out=t2[r][:], in0=pr[:], in1=st[:], op=ALU.mult))
                                    et, esm = evb.next()
                                    k.op("dve", [t1[r], t2[r]], [et], lambda e: e.tensor_tensor(
                                        out=et[:], in0=t1[r][:], in1=t2[r][:], op=ALU.add))
                                    dst = S["QT"] if kind == "q" else S["KT"]
                                    dstb = S["QT_b"] if kind == "q" else S["KT_b"]
                                    k.dma("sp", esm, dst[row:row + 128, t0:t0 + 512], et[:], [et], [dstb])
        k.barrier()
    k.pop()


def phase_final(k, L, xsrc, xsrc_b, W, y, y_b):
    k.push()
    with contextlib.ExitStack() as es:
        gB = k.sb(es, [128, D_MODEL], F32, "gB")
        gsem = k.newsem()
        k.dma("sp", gsem, gB[:], W["norm_final"][0:1, :].broadcast_to([128, D_MODEL]), [], [gB], sync=True)
        xr = Ring(k, es, 2, [128, D_MODEL], F32, "xt")
        yr = Ring(k, es, 2, [128, D_MODEL], F32, "yt")
        junk = k.sb(es, [128, D_MODEL], BF16, "junk")
        ssq = [k.sb(es, [128, 1], F32, "ssq") for _ in range(2)]
        rstd = [k.sb(es, [128, 1], F32, "rstd") for _ in range(2)]
        for tt in range(L // 128):
            xt, xs = xr.next()
            yt, ys = yr.next()
            p = tt % 2
            k.dma("sp", xs, xt[:], xsrc[tt * 128:(tt + 1) * 128, :], [xsrc_b], [xt])
            k.op("act", [xt], [junk, ssq[p]], lambda e: e.activation(out=junk[:], in_=xt[:], func=AF.Square,
                                                                      accum_out=ssq[p][:, 0:1]))
            rmsnorm_rstd(k, ssq[p], rstd[p], D_MODEL)
            k.op("dve", [xt, rstd[p], gB], [yt], lambda e: e.scalar_tensor_tensor(
                out=yt[:], in0=xt[:], scalar=rstd[p][:, 0:1], in1=gB[:], op0=ALU.mult, op1=ALU.mult))
            k.dma("sp", ys, y[tt * 128:(tt + 1) * 128, :], yt[:], [yt], [y_b])
        k.barrier()
    k.pop()


def phase_conv(k, L, lyr, W, S, C):
    LB = min(L, 4096)
    identf, identb = C["identf"], C["identb"]
    k.push()
    with contextlib.ExitStack() as es:
        cw = k.sb(es, [128, 100], F32, "cw")
        cbias = k.sb(es, [128, 20], F32, "cbias")
        sem = k.newsem()
        k.dma("sp", sem, cw[:], W["ssd_conv_wl"][lyr], [], [cw], sync=True)
        k.dma("sp", sem, cbias[:], W["ssd_conv_bl"][lyr], [], [cbias], sync=True)
        xin = Ring(k, es, 2, [128, LB + 4], F32, "cin")
        acc = k.sb(es, [128, LB], F32, "cacc")
        so = Ring(k, es, 2, [128, LB], F32, "so")
        sob = Ring(k, es, 2, [128, LB], BF16, "sob")
        tr = Ring(k, es, 3, [128, 4, 128], F32, "tr")
        trb = Ring(k, es, 3, [128, 4, 128], BF16, "trb")
        ptr = [k.ps(es, [128, 512], F32, "ptr") for _ in range(2)]
        ptb = [k.ps(es, [128, 512], BF16, "ptb") for _ in range(2)]
        pi = 0
        for c in range(20):
            for lb in range(L // LB):
                t0 = lb * LB
                xt, xs = xin.next()
                lo = 2 if t0 == 0 else 0
                hi = 2 if t0 + LB == L else 0
                if lo:
                    k.op("pool", [], [xt], lambda e: e.memset(xt[:, 0:2], 0.0))
                if hi:
                    k.op("pool", [], [xt], lambda e: e.memset(xt[:, LB + 2:LB + 4], 0.0))
                k.dma("sp", xs, xt[:, lo:LB + 4 - hi], S["XBCT"][c * 128:(c + 1) * 128, t0 - 2 + lo:t0 + LB + 2 - hi],
                      [S["XBCT_b"]], [xt])
                k.op("dve", [xt, cw, cbias], [acc], lambda e: e.tensor_scalar(
                    out=acc[:], in0=xt[:, 0:LB], scalar1=cw[:, c * 5:c * 5 + 1], scalar2=cbias[:, c:c + 1],
                    op0=ALU.mult, op1=ALU.add))
                for t in range(1, 5):
                    k.op("dve", [xt, cw, acc], [acc], lambda e: e.scalar_tensor_tensor(
                        out=acc[:], in0=xt[:, t:t + LB], scalar=cw[:, c * 5 + t:c * 5 + t + 1], in1=acc[:],
                        op0=ALU.mult, op1=ALU.add))
                if c < 12:
                    st, ssm = so.next()
                    k.op("act", [acc], [st], lambda e: e.activation(out=st[:], in_=acc[:], func=AF.Silu))
                    for tg in range(LB // 512):
                        pt = ptr[pi % 2]
                        pi += 1
                        for i in range(4):
                            k.op("pe", [st, identf], [pt], lambda e: e.transpose(
                                out=pt[:, i * 128:(i + 1) * 128], in_=st[:, (tg * 4 + i) * 128:(tg * 4 + i + 1) * 128],
                                identity=identf[:]))
                        tt, ts = tr.next()
                        k.op("act", [pt], [tt], lambda e: e.copy(out=tt[:], in_=pt[:, :].rearrange("p (a c) -> p a c", a=4)))
                        r0 = t0 + tg * 512
                        k.dma("sp", ts, S["X"][r0:r0 + 512, c * 128:(c + 1) * 128].rearrange("(a p) c -> p a c", p=128),
                              tt[:], [tt], [S["X_b"]])
                else:
                    st, ssm = sob.next()
                    k.op("act", [acc], [st], lambda e: e.activation(out=st[:], in_=acc[:], func=AF.Silu))
                    name = "BT" if c < 16 else "CT"
                    rr = (c - 12) % 4
                    k.dma("sp", ssm, S[name][rr * 128:(rr + 1) * 128, t0:t0 + LB], st[:], [st], [S[name + "_b"]])
                    if c < 16:
                        for tg in range(LB // 512):
                            pt = ptb[pi % 2]
                            pi += 1
                            for i in range(4):
                                k.op("pe", [st, identb], [pt], lambda e: e.transpose(
                                    out=pt[:, i * 128:(i + 1) * 128],
                                    in_=st[:, (tg * 4 + i) * 128:(tg * 4 + i + 1) * 128], identity=identb[:]))
                            tt, ts = trb.next()
                            k.op("dve", [pt], [tt], lambda e: e.tensor_copy(
                                out=tt[:], in_=pt[:, :].rearrange("p (a c) -> p a c", a=4)))
                            r0 = t0 + tg * 512
                            k.dma("sp", ts,
                                  S["B"][r0:r0 + 512, rr * 128:(rr + 1) * 128].rearrange("(a p) c -> p a c", p=128),
                                  tt[:], [tt], [S["B_b"]])
        k.barrier()
    k.pop()


SSD_RUNS = {0: [(0, 0, 4)], 1: [(0, 4, 6), (1, 6, 8)], 2: [(1, 8, 12)], 3: [(2, 12, 16)],
            4: [(2, 16, 18), (3, 18, 20)], 5: [(3, 20, 24)]}


def phase_ssd(k, L, lyr, W, S, C):
    NCH = L // 128
    identf, onesf, onec = C["identf"], C["onesf"], C["onec"]
    k.push()
    with contextlib.ExitStack() as es:
        sem = k.newsem()
        Abc = k.sb(es, [128, 48], F32, "Abc")
        dtb = k.sb(es, [128, 48], F32, "dtb")
        Dsk = k.sb(es, [128, 24], F32, "Dsk")
        normw = k.sb(es, [128, D_SSD], F32, "normw")
        k.dma("sp", sem, Abc[:], W["ssd_a_log"][lyr:lyr + 1, :].broadcast_to([128, 48]), [], [Abc], sync=True)
        k.dma("sp", sem, dtb[:], W["ssd_dt_bias"][lyr:lyr + 1, :].broadcast_to([128, 48]), [], [dtb], sync=True)
        k.dma("sp", sem, Dsk[:], W["ssd_d"][lyr:lyr + 1, :].broadcast_to([128, 24]), [], [Dsk], sync=True)
        k.dma("sp", sem, normw[:], W["ssd_norm"][lyr:lyr + 1, :].broadcast_to([128, D_SSD]), [], [normw], sync=True)
        k.op("act", [Abc], [Abc], lambda e: e.activation(out=Abc[:], in_=Abc[:], func=AF.Exp))
        k.op("dve", [Abc], [Abc], lambda e: e.tensor_scalar(out=Abc[:], in0=Abc[:], scalar1=-1.0, scalar2=None,
                                                            op0=ALU.mult))
        xr = Ring(k, es, 2, [128, 24, 64], F32, "xk")
        zr = Ring(k, es, 2, [128, D_SSD], F32, "zk")
        ybr = Ring(k, es, 2, [128, D_SSD], F32, "ybk")
        dtr = Ring(k, es, 2, [128, 48], F32, "dtk")
        bkr = Ring(k, es, 2, [128, 512], BF16, "bk")
        btr = Ring(k, es, 2, [128, 4, 128], BF16, "btk")
        ctr = Ring(k, es, 2, [128, 4, 128], BF16, "ctk")
        dts = k.sb(es, [128, 48], F32, "dts")
        dt = k.sb(es, [128, 48], F32, "dt")
        a = k.sb(es, [128, 24], F32, "a")
        sm = k.sb(es, [128, 64], F32, "sm")
        eac = k.sb(es, [128, 24], F32, "eac")
        dend = k.sb(es, [128, 24], F32, "dend")
        cdec = k.sb(es, [128, 24], F32, "cdec")
        w2 = k.sb(es, [128, 24], F32, "w2")
        xdt = k.sb(es, [128, D_SSD], BF16, "xdt")
        xde = k.sb(es, [128, D_SSD], BF16, "xde")
        cbs = k.sb(es, [128, 4, 128], F32, "cbs")
        rhs1 = k.sb(es, [128, 24, 128], F32, "rhs1")
        dec = [k.sb(es, [128, 4, 128], F32, "dec") for _ in range(3)]
        MT = [k.sb(es, [128, 12, 128], BF16, "MT") for _ in range(2)]
        H = [k.sb(es, [128, 768], F32, "H") for _ in range(2)]
        Hb = [k.sb(es, [128, 768], BF16, "Hb") for _ in range(2)]
        yo = k.sb(es, [128, 24, 64], F32, "yo")
        yd = k.sb(es, [128, 24, 64], F32, "yd")
        ydb = [Buf(), Buf()]
        ystore = Ring(k, es, 2, [128, D_SSD], F32, "ystore")
        y1 = k.sb(es, [128, D_SSD], F32, "y1")
        sz = k.sb(es, [128, D_SSD], F32, "sz")
        junk = k.sb(es, [128, 384], F32, "junk")
        gss = k.sb(es, [128, 4], F32, "gss")
        ytr = Ring(k, es, 2, [128, 12, 128], BF16, "ytr")
        big = [k.ps(es, [128, 1024], F32, "big") for _ in range(2)]
        pseg = [k.ps(es, [128, 512], F32, "pseg") for _ in range(2)]
        pcb = k.ps(es, [128, 512], F32, "pcb")
        ptr = k.ps(es, [128, 512], F32, "ptr")
        st = {"big": 0, "seg": 0, "dec": 0}

        def nbig():
            st["big"] += 1
            return big[st["big"] % 2]

        for d in (1, 0):
            Td = C["U"] if d == 0 else C["Lo"]
            nTd = C["nU"] if d == 0 else C["nLo"]
            nm = C["nmf"] if d == 0 else C["nmb"]
            for hf in range(2):
                k.op("pool", [], [H[hf]], lambda e: e.memset(H[hf][:], 0.0))
                k.op("pool", [], [Hb[hf]], lambda e: e.memset(Hb[hf][:], 0.0))
            order = range(NCH) if d == 0 else range(NCH - 1, -1, -1)
            for c in order:
                r0 = c * 128
                xk, s1 = xr.next()
                k.dma("sp", s1, xk[:], S["X"][r0:r0 + 128, :].rearrange("p (h d) -> p h d", h=24), [S["X_b"]], [xk])
                dtk, s2 = dtr.next()
                k.dma("sp", s2, dtk[:], S["DT"][r0:r0 + 128, :], [S["DT_b"]], [dtk])
                bk, s3 = bkr.next()
                k.dma("sp", s3, bk[:], S["B"][r0:r0 + 128, :], [S["B_b"]], [bk])
                btk, s4 = btr.next()
                k.dma("sp", s4, btk[:], S["BT"][:, r0:r0 + 128].rearrange("(g n) t -> n g t", n=128), [S["BT_b"]], [btk])
                ctk, s5 = ctr.next()
                k.dma("sp", s5, ctk[:], S["CT"][:, r0:r0 + 128].rearrange("(g n) t -> n g t", n=128), [S["CT_b"]], [ctk])
                if d == 0:
                    zk, s6 = zr.next()
                    k.dma("sp", s6, zk[:], S["Z"][r0:r0 + 128, :], [S["Z_b"]], [zk])
                    ybk, s7 = ybr.next()
                    k.dma("sp", s7, ybk[:], S["YB"][r0:r0 + 128, :], [S["YB_b"]], [ybk])
                k.op("dve", [dtk, dtb], [dts], lambda e: e.tensor_tensor(out=dts[:], in0=dtk[:], in1=dtb[:], op=ALU.add))
                k.op("act", [dts], [dts], lambda e: e.activation(out=dts[:], in_=dts[:], func=AF.Exp))
                k.op("act", [dts, onec], [dt], lambda e: e.activation(out=dt[:], in_=dts[:], func=AF.Ln,
                                                                      bias=onec[:, 0:1], scale=1.0))
                dtd = dt[:, d * 24:(d + 1) * 24]
                k.op("dve", [dt, Abc], [a], lambda e: e.tensor_tensor(out=a[:], in0=dtd, in1=Abc[:, d * 24:(d + 1) * 24],
                                                                      op=ALU.mult))
                pb = nbig()
                k.op("pe", [Td, a], [pb], lambda e: e.matmul(pb[:, 0:24], lhsT=Td[:], rhs=a[:], start=True, stop=True))
                k.op("pe", [onesf, a], [pb], lambda e: e.matmul(pb[:, 24:48], lhsT=onesf[:], rhs=a[:], start=True, stop=True))
                k.op("act", [pb], [sm], lambda e: e.copy(out=sm[:, 0:48], in_=pb[:, 0:48]))
                k.op("act", [sm], [eac], lambda e: e.activation(out=eac[:], in_=sm[:, 0:24], func=AF.Exp))
                k.op("dve", [sm], [dend], lambda e: e.tensor_tensor(out=dend[:], in0=sm[:, 24:48], in1=sm[:, 0:24],
                                                                   op=ALU.subtract))
                k.op("act", [dend], [dend], lambda e: e.activation(out=dend[:], in_=dend[:], func=AF.Exp))
                k.op("act", [sm], [cdec], lambda e: e.activation(out=cdec[:], in_=sm[:, 24:48], func=AF.Exp))
                k.op("dve", [dt, dend], [w2], lambda e: e.tensor_tensor(out=w2[:], in0=dtd, in1=dend[:], op=ALU.mult))
                k.op("dve", [xk, dt], [xdt], lambda e: e.tensor_tensor(
                    out=xdt[:, :].rearrange("p (h d) -> p h d", d=64), in0=xk[:], in1=dtd.unsqueeze(2).to_broadcast([128, 24, 64]), op=ALU.mult))
                k.op("pool", [xk, w2], [xde], lambda e: e.tensor_tensor(
                    out=xde[:, :].rearrange("p (h d) -> p h d", d=64), in0=xk[:], in1=w2[:, :].unsqueeze(2).to_broadcast([128, 24, 64]), op=ALU.mult))
                for g in range(4):
                    k.op("pe", [btk, ctk], [pcb], lambda e: e.matmul(
                        pcb[:, g * 128:(g + 1) * 128], lhsT=btk[:, g, :], rhs=ctk[:, g, :], start=True, stop=True))
                k.op("act", [pcb], [cbs], lambda e: e.copy(out=cbs[:], in_=pcb[:, :].rearrange("p (g l) -> p g l", g=4)))
                k.op("pool", [a, Td], [rhs1], lambda e: e.tensor_tensor(
                    out=rhs1[:], in0=a[:, :].unsqueeze(2).to_broadcast([128, 24, 128]),
                    in1=Td[:, :].unsqueeze(1).to_broadcast([128, 24, 128]), op=ALU.mult))
                for hf in range(2):
                    h0 = hf * 12
                    pb = nbig()
                    for u in range(3 * hf, 3 * hf + 3):
                        for (g, ha, hb_) in SSD_RUNS[u]:
                            k.op("pe", [ctk, Hb[hf]], [pb], lambda e: e.matmul(
                                pb[:, (ha - h0) * 64:(hb_ - h0) * 64], lhsT=ctk[:, g, :],
                                rhs=Hb[hf][:, (ha - h0) * 64:(hb_ - h0) * 64], start=True, stop=True))
                    k.op("dve", [pb, eac], [yo], lambda e: e.tensor_tensor(
                        out=yo[:, h0:h0 + 12, :], in0=pb[:, 0:768].rearrange("p (h d) -> p h d", h=12),
                        in1=eac[:, h0:h0 + 12].unsqueeze(2).to_broadcast([128, 12, 64]), op=ALU.mult))
                    mt = MT[hf]
                    for u in range(3 * hf, 3 * hf + 3):
                        pg = pseg[st["seg"] % 2]
                        st["seg"] += 1
                        pg3 = pg[:, :].rearrange("p (h l) -> p h l", h=4)
                        k.op("pe", [onesf, rhs1], [pg], lambda e: e.matmul(
                            pg3, lhsT=onesf[:], rhs=rhs1[:, 4 * u:4 * u + 4, :], start=True, stop=False))
                        k.op("pe", [nTd, a], [pg], lambda e: e.matmul(
                            pg3, lhsT=nTd[:], rhs=a[:, 4 * u:4 * u + 4].unsqueeze(2).to_broadcast([128, 4, 128]),
                            start=False, stop=False))
                        k.op("pe", [identf, nm], [pg], lambda e: e.matmul(
                            pg3, lhsT=identf[:], rhs=nm[:, :].unsqueeze(1).to_broadcast([128, 4, 128]),
                            start=False, stop=True))
                        dc = dec[st["dec"] % 3]
                        st["dec"] += 1
                        k.op("act", [pg], [dc], lambda e: e.activation(out=dc[:], in_=pg3, func=AF.Exp))
                        for (g, ha, hb_) in SSD_RUNS[u]:
                            n = hb_ - ha
                            k.op("dve", [dc, cbs], [mt], lambda e: e.tensor_tensor(
                                out=mt[:, ha - h0:hb_ - h0, :], in0=dc[:, ha - 4 * u:hb_ - 4 * u, :],
                                in1=cbs[:, g:g + 1, :].to_broadcast([128, n, 128]), op=ALU.mult))
                    pb2 = nbig()
                    for h in range(h0, h0 + 12):
                        k.op("pe", [mt, xdt], [pb2], lambda e: e.matmul(
                            pb2[:, (h - h0) * 64:(h - h0 + 1) * 64], lhsT=mt[:, h - h0, :], rhs=xdt[:, h * 64:(h + 1) * 64],
                            start=True, stop=True))
                    k.op("dve", [pb2, yo], [ydb[hf]], lambda e: e.tensor_tensor(
                        out=yd[:, h0:h0 + 12, :], in0=pb2[:, 0:768].rearrange("p (h d) -> p h d", h=12),
                        in1=yo[:, h0:h0 + 12, :], op=ALU.add))
                for hf in range(2):
                    h0 = hf * 12
                    pb3 = nbig()
                    for u in range(3 * hf, 3 * hf + 3):
                        for (g, ha, hb_) in SSD_RUNS[u]:
                            k.op("pe", [bk, xde], [pb3], lambda e: e.matmul(
                                pb3[:, (ha - h0) * 64:(hb_ - h0) * 64], lhsT=bk[:, g * 128:(g + 1) * 128],
                                rhs=xde[:, ha * 64:hb_ * 64], start=True, stop=True))
                    k.op("pool", [H[hf], cdec], [H[hf]], lambda e: e.tensor_tensor(
                        out=H[hf][:, :].rearrange("p (h d) -> p h d", d=64), in0=H[hf][:, :].rearrange("p (h d) -> p h d", d=64),
                        in1=cdec[:, h0:h0 + 12].unsqueeze(2).to_broadcast([128, 12, 64]), op=ALU.mult))
                    k.op("dve", [H[hf], pb3], [H[hf]], lambda e: e.tensor_tensor(
                        out=H[hf][:], in0=H[hf][:], in1=pb3[:, 0:768], op=ALU.add))
                    k.op("act", [H[hf]], [Hb[hf]], lambda e: e.copy(out=Hb[hf][:], in_=H[hf][:]))
                ydf = yd[:, :, :].rearrange("p h d -> p (h d)")
                if d == 1:
                    yt, ysm = ystore.next()
                    k.op("pool", ydb, [yt], lambda e: e.tensor_copy(out=yt[:], in_=ydf))
                    k.dma("sp", ysm, S["YB"][r0:r0 + 128, :], yt[:], [yt], [S["YB_b"]])
                else:
                    k.op("dve", ydb + [ybk], [y1], lambda e: e.tensor_tensor(out=y1[:], in0=ydf, in1=ybk[:], op=ALU.add))
                    k.op("pool", [xk, Dsk], [sz], lambda e: e.tensor_tensor(
                        out=sz[:, :].rearrange("p (h d) -> p h d", h=24), in0=xk[:],
                        in1=Dsk[:, :].unsqueeze(2).to_broadcast([128, 24, 64]), op=ALU.mult))
                    k.op("dve", [y1, sz], [y1], lambda e: e.tensor_tensor(out=y1[:], in0=y1[:], in1=sz[:], op=ALU.add))
                    k.op("act", [zk], [sz], lambda e: e.activation(out=sz[:], in_=zk[:], func=AF.Silu))
                    k.op("dve", [y1, sz], [y1], lambda e: e.tensor_tensor(out=y1[:], in0=y1[:], in1=sz[:], op=ALU.mult))
                    for g in range(4):
                        k.op("act", [y1], [junk, gss], lambda e: e.activation(
                            out=junk[:], in_=y1[:, g * 384:(g + 1) * 384], func=AF.Square, accum_out=gss[:, g:g + 1]))
                    k.op("dve", [gss], [gss], lambda e: e.tensor_scalar(out=gss[:], in0=gss[:], scalar1=1.0 / 384,
                                                                        scalar2=EPS, op0=ALU.mult, op1=ALU.add))
                    k.op("act", [gss], [gss], lambda e: e.activation(out=gss[:], in_=gss[:], func=AF.Sqrt))
                    k.op("dve", [gss], [gss], lambda e: e.reciprocal(out=gss[:], in_=gss[:]))
                    k.op("dve", [y1, gss], [y1], lambda e: e.tensor_tensor(
                        out=y1[:, :].rearrange("p (g c) -> p g c", g=4), in0=y1[:, :].rearrange("p (g c) -> p g c", g=4),
                        in1=gss[:, :].unsqueeze(2).to_broadcast([128, 4, 384]), op=ALU.mult))
                    k.op("pool", [y1, normw], [y1], lambda e: e.tensor_tensor(out=y1[:], in0=y1[:], in1=normw[:], op=ALU.mult))
                    yt, ysm = ytr.next()
                    for q in range(3):
                        for i in range(4):
                            cc = q * 4 + i
                            k.op("pe", [y1, identf], [ptr], lambda e: e.transpose(
                                out=ptr[:, i * 128:(i + 1) * 128], in_=y1[:, cc * 128:(cc + 1) * 128], identity=identf[:]))
                        k.op("act", [ptr], [yt], lambda e: e.copy(
                            out=yt[:, q * 4:(q + 1) * 4, :], in_=ptr[:, :].rearrange("p (a t) -> p a t", a=4)))
                    k.dma("sp", ysm, S["YT"][0:D_SSD, r0:r0 + 128].rearrange("(c p) t -> p c t", p=128), yt[:],
                          [yt], [S["YT_b"]])
        k.barrier()
    k.pop()


def phase_sc(k, L, lyr, W, S, C):
    onesf = C["onesf"]
    TBK = 512
    k.push()
    with contextlib.ExitStack() as es:
        sem = k.newsem()
        cw = k.sb(es, [128, 18], F32, "scw")
        nw = k.sb(es, [128, 6], F32, "scn")
        k.dma("sp", sem, cw[:], W["sc_conv_wl"][lyr], [], [cw], sync=True)
        k.dma("sp", sem, nw[:], W["sc_norm_l"][lyr], [], [nw], sync=True)
        bgr = Ring(k, es, 2, [128, TBK], F32, "bg")
        cgr = Ring(k, es, 2, [128, TBK + 2], F32, "cg")
        hxr = Ring(k, es, 2, [128, TBK + 2], F32, "hx")
        prod = k.sb(es, [128, TBK + 2], F32, "prod")
        yc = [k.sb(es, [128, TBK], F32, "yc") for _ in range(6)]
        ysq = [k.sb(es, [128, TBK], F32, "ysq") for _ in range(2)]
        rst = k.sb(es, [128, TBK], F32, "rst")
        outr = Ring(k, es, 3, [128, TBK], BF16, "sco")
        pss = k.ps(es, [128, 512], F32, "pss")
        for tb in range(L // TBK):
            t0 = tb * TBK
            lo = 1 if t0 == 0 else 0
            hi = 1 if t0 + TBK == L else 0
            for c in range(6):
                bg, s0 = bgr.next()
                cg, s1 = cgr.next()
                hx, s2 = hxr.next()
                k.dma("sp", s0, bg[:], S["SCT"][c * 128:(c + 1) * 128, t0:t0 + TBK], [S["SCT_b"]], [bg])
                for (tl, sm_, base) in ((cg, s1, 768), (hx, s2, 1536)):
                    if lo:
                        k.op("pool", [], [tl], lambda e: e.memset(tl[:, 0:1], 0.0))
                    if hi:
                        k.op("pool", [], [tl], lambda e: e.memset(tl[:, TBK + 1:TBK + 2], 0.0))
                    k.dma("sp", sm_, tl[:, lo:TBK + 2 - hi],
                          S["SCT"][base + c * 128:base + (c + 1) * 128, t0 - 1 + lo:t0 + TBK + 1 - hi],
                          [S["SCT_b"]], [tl])
                k.op("pool", [cg, hx], [prod], lambda e: e.tensor_tensor(out=prod[:], in0=cg[:], in1=hx[:], op=ALU.mult))
                y = yc[c]
                k.op("dve", [prod, cw], [y], lambda e: e.tensor_scalar(
                    out=y[:], in0=prod[:, 0:TBK], scalar1=cw[:, c * 3:c * 3 + 1], scalar2=None, op0=ALU.mult))
                for t in (1, 2):
                    k.op("dve", [prod, cw, y], [y], lambda e: e.scalar_tensor_tensor(
                        out=y[:], in0=prod[:, t:t + TBK], scalar=cw[:, c * 3 + t:c * 3 + t + 1], in1=y[:],
                        op0=ALU.mult, op1=ALU.add))
                k.op("dve", [y, bg], [y], lambda e: e.tensor_tensor(out=y[:], in0=y[:], in1=bg[:], op=ALU.mult))
                q = ysq[c % 2]
                k.op("act", [y], [q], lambda e: e.activation(out=q[:], in_=y[:], func=AF.Square))
                k.op("pe", [onesf, q], [pss], lambda e: e.matmul(pss[:, :], lhsT=onesf[:], rhs=q[:],
                                                                 start=(c == 0), stop=(c == 5)))
            k.op("dve", [pss], [rst], lambda e: e.tensor_scalar(out=rst[:], in0=pss[:], scalar1=1.0 / D_SC, scalar2=EPS,
                                                                op0=ALU.mult, op1=ALU.add))
            k.op("act", [rst], [rst], lambda e: e.activation(out=rst[:], in_=rst[:], func=AF.Sqrt))
            k.op("dve", [rst], [rst], lambda e: e.reciprocal(out=rst[:], in_=rst[:]))
            for c in range(6):
                o, so_ = outr.next()
                k.op("dve", [yc[c], nw, rst], [o], lambda e: e.scalar_tensor_tensor(
                    out=o[:], in0=yc[c][:], scalar=nw[:, c:c + 1], in1=rst[:], op0=ALU.mult, op1=ALU.mult))
                k.dma("sp", so_, S["YT"][D_SSD + c * 128:D_SSD + (c + 1) * 128, t0:t0 + TBK], o[:], [o], [S["YT_b"]])
        k.barrier()
    k.pop()


def phase_att(k, L, lyr, W, S, C):
    identb = C["identb"]
    with contextlib.ExitStack() as es:
        for g, dil in enumerate((1, 4, 16)):
            if g not in ATT_GROUPS:
                continue
            Lsub = L // dil
            nqb = Lsub // 128
            W_ = 128 * dil
            k.push()
            with contextlib.ExitStack() as es2:
                kcr = Ring(k, es2, 3, [128, 2, W_], BF16, "kc")
                qwr = Ring(k, es2, 2, [128, 2, 2, W_], BF16, "qw")
                qwr2 = [k.newsem() for _ in range(2)]
                vcr = [Ring(k, es2, 3, [128, 4, 65], BF16, "vc") for _ in range(dil)]
                ptr_ = Ring(k, es2, 3, [128, 512], BF16, "pT", dma=False)
                otr = Ring(k, es2, 3, [128, 4, 65], F32, "ot")
                pst_ = [k.ps(es2, [128, 512], F32, "pst") for _ in range(3)]
                pso = [k.ps(es2, [128, 512], F32, "pso") for _ in range(2)]
                for r in range(dil):
                    for t, _s in zip(vcr[r].tiles, vcr[r].sems):
                        k.op("pool", [], [t], lambda e: e.memset(t[:], 1.0))
                for t in kcr.tiles + qwr.tiles:
                    k.op("pool", [], [t], lambda e: e.memset(t[:], 0.0))
                cnt = {"st": 0, "o": 0}
                for hg in range(3):
                    row0 = g * 768 + hg * 256
                    kch = {}
                    vch = {}

                    def load_chunk(j):
                        p0 = 128 * j - 64
                        lo = 64 if j == 0 else 0
                        hi = 64 if j == nqb else 128
                        kt, ks = kcr.next()
                        tok_lo = (p0 + lo) * dil
                        tok_hi = (p0 + hi) * dil
                        k.dma("sp", ks, kt[:, :, lo * dil:hi * dil],
                              S["KT"][row0:row0 + 256, tok_lo:tok_hi].rearrange("(hp p) t -> p hp t", p=128),
                              [S["KT_b"]], [kt])
                        kch[j] = kt
                        vs = []
                        for r in range(dil):
                            vt, vsm = vcr[r].next()
                            rows = S["V"][tok_lo:tok_hi, row0:row0 + 256].rearrange("(i r) (h d) -> r i h d", r=dil, d=64)[r]
                            k.dma("sp", vsm, vt[lo:hi, :, 0:64], rows, [S["V_b"]], [vt])
                            vs.append(vt)
                        vch[j] = vs

                    load_chunk(0)
                    for w in range(nqb):
                        load_chunk(w + 1)
                        qs2 = qwr2[qwr.i]
                        qt, qs = qwr.next()
                        qsrc = S["QT"][row0:row0 + 256, w * W_:(w + 1) * W_].rearrange("(hp p) t -> p hp t", p=128)
                        k.dma("sp", qs, qt[0:64, :, 0, :], qsrc[0:64], [S["QT_b"]], [qt])
                        k.dma("sp", qs2, qt[64:128, :, 1, :], qsrc[64:128], [S["QT_b"]], [qt])
                        for r in range(dil):
                            po = pso[cnt["o"] % 2]
                            cnt["o"] += 1
                            for c in (0, 1):
                                if ATT_STAGE < 2:
                                    break
                                j = w + c
                                lo = 64 if j == 0 else 0
                                hi = 64 if j == nqb else 128
                                kt = kch[j]
                                vt = vch[j][r]
                                ps_ = pst_[cnt["st"] % 3]
                                cnt["st"] += 1
                                for hp in range(2):
                                    kv = kt[:, hp, :].rearrange("p (i r) -> p i r", r=dil)[:, :, r]
                                    qv = qt[:, hp, :, :].rearrange("p a (i r) -> p a i r", r=dil)[:, :, :, r]
                                    k.op("pe", [kt, qt], [ps_], lambda e: e.matmul(
                                        ps_[:, hp * 256:(hp + 1) * 256].rearrange("p (a i) -> p a i", a=2), lhsT=kv, rhs=qv,
                                        start=(hp == 0), stop=False))
                                if c == 0:
                                    mk = C["maskA0"] if j == 0 else C["maskA"]
                                else:
                                    mk = C["maskB1"] if j == nqb else C["maskB"]
                                k.op("pe", [identb, mk], [ps_], lambda e: e.matmul(
                                    ps_[:, :], lhsT=identb[:], rhs=mk[:], start=False, stop=True))
                                if ATT_STAGE < 3:
                                    continue
                                pt, _ = ptr_.next()
                                k.op("act", [ps_], [pt], lambda e: e.activation(out=pt[:, :], in_=ps_[:, :],
                                                                               func=AF.Exp, scale=0.125))
                                if ATT_STAGE < 4:
                                    continue
                                for hh in range(4):
                                    k.op("pe", [pt, vt], [po], lambda e: e.matmul(
                                        po[:, hh * 65:(hh + 1) * 65], lhsT=pt[:, hh * 128:(hh + 1) * 128],
                                        rhs=vt[:, hh, :], start=(c == 0 and hh == 0), stop=(c == 1 and hh == 3)))
                            if ATT_STAGE < 5:
                                continue
                            ot, osm = otr.next()
                            k.op("act", [po], [ot], lambda e: e.copy(out=ot[:], in_=po[:, 0:260].rearrange("p (h d) -> p h d", h=4)))
                            dst = S["O%d" % g][w * W_:(w + 1) * W_, hg * 260:(hg + 1) * 260].rearrange(
                                "(i r) (h d) -> r i h d", r=dil, d=65)[r]
                            k.dma("sp", osm, dst, ot[:], [ot], [S["O%d_b" % g]])
                k.barrier()
            k.pop()
        if ATT_STAGE < 6:
            return
        k.push()
        with contextlib.ExitStack() as es2:
            identf = C["identf"]
            sem = k.newsem()
            nw = k.sb(es2, [128, D_ATT], F32, "attn")
            k.dma("sp", sem, nw[:], W["att_norm"][lyr:lyr + 1, :].broadcast_to([128, D_ATT]), [], [nw], sync=True)
            o0 = Ring(k, es2, 2, [128, 12, 65], F32, "o0")
            o1 = Ring(k, es2, 2, [128, 12, 65], F32, "o1")
            o2 = Ring(k, es2, 2, [128, 12, 65], F32, "o2")
            rl = k.sb(es2, [128, 12], F32, "rl")
            ov = k.sb(es2, [128, 12, 64], F32, "ov")
            junk = k.sb(es2, [128, D_ATT], F32, "junk")
            ssq = k.sb(es2, [128, 1], F32, "ssq")
            rstd = k.sb(es2, [128, 1], F32, "rstd")
            ytr = Ring(k, es2, 2, [128, 6, 128], BF16, "aytr")
            ptr = [k.ps(es2, [128, 512], F32, "ptr") for _ in range(2)]
            for tt in range(L // 128):
                r0 = tt * 128
                a0, s0 = o0.next()
                a1, s1 = o1.next()
                a2, s2 = o2.next()
                k.dma("sp", s0, a0[:], S["O0"][r0:r0 + 128, :].rearrange("p (h d) -> p h d", h=12), [S["O0_b"]], [a0])
                k.dma("sp", s1, a1[:], S["O1"][r0:r0 + 128, :].rearrange("p (h d) -> p h d", h=12), [S["O1_b"]], [a1])
                k.dma("sp", s2, a2[:], S["O2"][r0:r0 + 128, :].rearrange("p (h d) -> p h d", h=12), [S["O2_b"]], [a2])
                k.op("pool", [a0, a1], [a0], lambda e: e.tensor_tensor(out=a0[:], in0=a0[:], in1=a1[:], op=ALU.add))
                k.op("dve", [a0, a2], [a0], lambda e: e.tensor_tensor(out=a0[:], in0=a0[:], in1=a2[:], op=ALU.add))
                k.op("dve", [a0], [rl], lambda e: e.reciprocal(out=rl[:], in_=a0[:, :, 64]))
                k.op("dve", [a0, rl], [ov], lambda e: e.tensor_tensor(
                    out=ov[:], in0=a0[:, :, 0:64], in1=rl[:, :].unsqueeze(2).to_broadcast([128, 12, 64]), op=ALU.mult))
                ovf = ov[:, :, :].rearrange("p h d -> p (h d)")
                k.op("act", [ov], [junk, ssq], lambda e: e.activation(out=junk[:], in_=ovf, func=AF.Square,
                                                                      accum_out=ssq[:, 0:1]))
                rmsnorm_rstd(k, ssq, rstd, D_ATT)
                k.op("dve", [ov, rstd, nw], [junk], lambda e: e.scalar_tensor_tensor(
                    out=junk[:], in0=ovf, scalar=rstd[:, 0:1], in1=nw[:], op0=ALU.mult, op1=ALU.mult))
                yt, ysm = ytr.next()
                for q in range(2):
                    pt = ptr[q]
                    n = 4 if q == 0 else 2
                    for i in range(n):
                        cc = q * 4 + i
                        k.op("pe", [junk, identf], [pt], lambda e: e.transpose(
                            out=pt[:, i * 128:(i + 1) * 128], in_=junk[:, cc * 128:(cc + 1) * 128], identity=identf[:]))
                    k.op("act", [pt], [yt], lambda e: e.copy(
                        out=yt[:, q * 4:q * 4 + n, :], in_=pt[:, 0:n * 128].rearrange("p (a t) -> p a t", a=n)))
                k.dma("sp", ysm, S["YT"][2304:3072, r0:r0 + 128].rearrange("(c p) t -> p c t", p=128), yt[:],
                      [yt], [S["YT_b"]])
            k.barrier()
        k.pop()


def phase_wout(k, L, lyr, xsrc, xsrc_b, W, S, C):
    k.push()
    with contextlib.ExitStack() as es:
        wo = k.sb(es, [128, 24, D_MODEL], BF16, "wo")
        stg = Ring(k, es, 2, [128, 2, D_MODEL], F32, "wstg")
        for c4 in range(12):
            st, ss = stg.next()
            k.dma("sp", ss, st[:], W["w_out"][lyr, c4 * 256:(c4 + 1) * 256, :].rearrange("(c p) n -> p c n", p=128), [], [st])
            eng = "pool" if c4 % 2 == 0 else "act"
            if eng == "pool":
                k.op("pool", [st], [wo], lambda e: e.tensor_copy(out=wo[:, c4 * 2:(c4 + 1) * 2, :], in_=st[:]))
            else:
                k.op("act", [st], [wo], lambda e: e.copy(out=wo[:, c4 * 2:(c4 + 1) * 2, :], in_=st[:]))
        ytr = Ring(k, es, 2, [128, 24, 128], BF16, "yT")
        xr = Ring(k, es, 2, [128, D_MODEL], F32, "xo")
        outr = Ring(k, es, 2, [128, D_MODEL], F32, "xn")
        psm = [k.ps(es, [128, 512], F32, "psm") for _ in range(4)]
        for tt in range(L // 128):
            r0 = tt * 128
            yt, ys = ytr.next()
            k.dma("sp", ys, yt[:], S["YT"][:, r0:r0 + 128].rearrange("(c p) t -> p c t", p=128), [S["YT_b"]], [yt])
            xt, xs = xr.next()
            k.dma("sp", xs, xt[:], xsrc[r0:r0 + 128, :], [xsrc_b], [xt])
            ot, osm = outr.next()
            for nb in range(4):
                pm = psm[nb]
                for c in range(24):
                    k.op("pe", [yt, wo], [pm], lambda e: e.matmul(pm[:, :], lhsT=yt[:, c, :], rhs=wo[:, c, nb * 512:(nb + 1) * 512],
                                                                   start=(c == 0), stop=(c == 23)))
                k.op("dve", [pm, xt], [ot], lambda e: e.tensor_tensor(out=ot[:, nb * 512:(nb + 1) * 512], in0=pm[:],
                                                                      in1=xt[:, nb * 512:(nb + 1) * 512], op=ALU.add))
            k.dma("sp", osm, S["XRES"][r0:r0 + 128, :], ot[:], [ot], [S["XRES_b"]])
        k.barrier()
    k.pop()


def peer_tables_bf16(k, lyr, W, S):
    k.push()
    with contextlib.ExitStack() as es:
        stg = Ring(k, es, 3, [128, 4096], F32, "tstg")
        outb = Ring(k, es, 3, [128, 4096], BF16, "tout")
        n = 0
        for name, dst in (("peer_u", "UB"), ("peer_v", "VB")):
            src = W[name][lyr * N_EXP:(lyr + 1) * N_EXP, :].rearrange("(c p two) d -> c p (two d)", p=128, two=2)
            dv = S[dst].rearrange("(c p two) d -> c p (two d)", p=128, two=2)
            for c in range(N_EXP // 256):
                st, ss = stg.next()
                ob, os_ = outb.next()
                k.dma("sp", ss, st[:], src[c], [], [st])
                e = ("act", "pool", "dve")[n % 3]
                n += 1
                if e == "act":
                    k.op("act", [st], [ob], lambda en: en.copy(out=ob[:], in_=st[:]))
                else:
                    k.op(e, [st], [ob], lambda en: en.tensor_copy(out=ob[:], in_=st[:]))
                k.dma("sp", os_, dv[c], ob[:], [ob], [S[dst + "_b"]])
        k.barrier()
    k.pop()


def phase_peer(k, L, lyr, W, S, C):
    identf, identb = C["identf"], C["identb"]
    NEG = -1.0e30
    peer_tables_bf16(k, lyr, W, S)
    k.push()
    with contextlib.ExitStack() as es:
        sem = k.newsem()
        gB = k.sb(es, [128, D_MODEL], F32, "gB")
        k.dma("sp", sem, gB[:], W["norm_ffn"][lyr:lyr + 1, :].broadcast_to([128, D_MODEL]), [], [gB], sync=True)
        wq = k.sb(es, [128, 16, D_MODEL], BF16, "wq")
        skT = k.sb(es, [128, 16, 128], BF16, "skT")
        pq = [k.ps(es, [128, 512], F32, "pq") for _ in range(4)]
        pqs = [k.ps(es, [128, 512], F32, "pqs") for _ in range(2)]
        pst = [k.ps(es, [128, 1024], BF16, "pst") for _ in range(2)]
        with contextlib.ExitStack() as es1:
            stg = Ring(k, es1, 2, [128, 2, D_MODEL], F32, "wstg")
            for c2 in range(8):
                st, ss = stg.next()
                k.dma("sp", ss, st[:], W["peer_wq"][lyr, c2 * 256:(c2 + 1) * 256, :].rearrange("(c p) n -> p c n", p=128), [], [st])
                if c2 % 2 == 0:
                    k.op("pool", [st], [wq], lambda e: e.tensor_copy(out=wq[:, c2 * 2:(c2 + 1) * 2, :], in_=st[:]))
                else:
                    k.op("act", [st], [wq], lambda e: e.copy(out=wq[:, c2 * 2:(c2 + 1) * 2, :], in_=st[:]))
            skf = k.sb(es1, [128, 16, 128], F32, "skf")
            k.dma("sp", sem, skf[:], W["peer_subkeys"][lyr].rearrange("m n d -> n m d"), [], [skf], sync=True)
            for q4 in range(4):
                pm = pq[q4]
                for i in range(4):
                    m = q4 * 4 + i
                    k.op("pe", [skf, identf], [pm], lambda e: e.transpose(out=pm[:, i * 128:(i + 1) * 128], in_=skf[:, m, :],
                                                                          identity=identf[:]))
                k.op("act", [pm], [skT], lambda e: e.copy(out=skT[:, q4 * 4:(q4 + 1) * 4, :],
                                                          in_=pm[:, :].rearrange("p (a n) -> p a n", a=4)))
            k.barrier()
        xr = Ring(k, es, 2, [128, D_MODEL], F32, "xm")
        ssq = k.sb(es, [128, 1], F32, "ssq")
        rstd = k.sb(es, [128, 1], F32, "rstd")
        hn = k.sb(es, [128, D_MODEL], F32, "hn")
        hnb = k.sb(es, [128, D_MODEL], BF16, "hnb")
        hnT = k.sb(es, [128, 16, 128], BF16, "hnT")
        qTb = k.sb(es, [128, 16, 128], BF16, "qTb")
        sc = k.sb(es, [128, 16, 128], F32, "sc")
        scw = k.sb(es, [128, 128], F32, "scw")
        sv = k.sb(es, [128, 16, 16], F32, "sv")
        si = k.sb(es, [128, 16, 16], U32, "si")
        sif = k.sb(es, [128, 16, 16], F32, "sif")
        cand = k.sb(es, [128, 8, 256], F32, "cand")
        cidx = k.sb(es, [128, 8, 256], F32, "cidx")
        cw_ = k.sb(es, [128, 256], F32, "cw_")
        top = k.sb(es, [128, 8, 16], F32, "top")
        gate = k.sb(es, [128, 8, 16], F32, "gate")
        zs = k.sb(es, [128, 8], F32, "zs")
        j256 = k.sb(es, [128, 256], F32, "j256")
        eidf = k.sb(es, [128, 128], F32, "eidf")
        eidf_w = Buf(multi=True)
        eids = [k.sb(es, [128, 128], I32, "eid") for _ in range(2)]
        pre = k.sb(es, [128, 128], F32, "pre")
        pre_w = Buf(multi=True)
        actg = k.sb(es, [128, 128], F32, "actg")
        dg = [k.sb(es, [128, 16, 128], BF16, "dg") for _ in range(2)]
        gr = Ring(k, es, 8, [128, D_MODEL], BF16, "gath", sw=True)
        outr = Ring(k, es, 1, [128, D_MODEL], F32, "xn")
        utab = S["UB"]
        vtab = S["VB"]
        xms = {}

        def stage_a(tt):
            r0 = tt * 128
            eid = eids[tt % 2]
            xm, xs = xr.next()
            xms[tt] = xm
            k.dma("sp", xs, xm[:], S["XRES"][r0:r0 + 128, :], [S["XRES_b"]], [xm])
            k.op("act", [xm], [hnb, ssq], lambda e: e.activation(out=hnb[:], in_=xm[:], func=AF.Square,
                                                                   accum_out=ssq[:, 0:1]))
            rmsnorm_rstd(k, ssq, rstd, D_MODEL)
            k.op("dve", [xm, rstd, gB], [hn], lambda e: e.scalar_tensor_tensor(
                out=hn[:], in0=xm[:], scalar=rstd[:, 0:1], in1=gB[:], op0=ALU.mult, op1=ALU.mult))
            k.op("act", [hn], [hnb], lambda e: e.copy(out=hnb[:], in_=hn[:]))
            for half in range(2):
                pt = pst[half]
                for j in range(8):
                    kc = half * 8 + j
                    k.op("pe", [hnb, identb], [pt], lambda e: e.transpose(
                        out=pt[:, j * 128:(j + 1) * 128], in_=hnb[:, kc * 128:(kc + 1) * 128], identity=identb[:]))
                k.op("act", [pt], [hnT], lambda e: e.copy(out=hnT[:, half * 8:(half + 1) * 8, :],
                                                          in_=pt[:, :].rearrange("p (j t) -> p j t", j=8)))
            for q4 in range(4):
                pm = pqs[q4 % 2]
                for i in range(4):
                    m = q4 * 4 + i
                    for kc in range(16):
                        k.op("pe", [wq, hnT], [pm], lambda e: e.matmul(
                            pm[:, i * 128:(i + 1) * 128], lhsT=wq[:, kc, m * 128:(m + 1) * 128], rhs=hnT[:, kc, :],
                            start=(kc == 0), stop=(kc == 15)))
                k.op("act", [pm], [qTb], lambda e: e.copy(out=qTb[:, q4 * 4:(q4 + 1) * 4, :],
                                                          in_=pm[:, :].rearrange("p (a t) -> p a t", a=4)))
            for q4 in range(4):
                pm = pqs[q4 % 2]
                for i in range(4):
                    m = q4 * 4 + i
                    k.op("pe", [qTb, skT], [pm], lambda e: e.matmul(
                        pm[:, i * 128:(i + 1) * 128], lhsT=qTb[:, m, :], rhs=skT[:, m, :], start=True, stop=True))
                k.op("act", [pm], [sc], lambda e: e.copy(out=sc[:, q4 * 4:(q4 + 1) * 4, :],
                                                         in_=pm[:, :].rearrange("p (a n) -> p a n", a=4)))
            for m in range(16):
                k.op("dve", [sc], [sv], lambda e: e.max(out=sv[:, m, 0:8], in_=sc[:, m, :]))
                k.op("dve", [sc, sv], [scw], lambda e: e.match_replace(out=scw[:], in_to_replace=sv[:, m, 0:8],
                                                                       in_values=sc[:, m, :], imm_value=NEG))
                k.op("dve", [scw], [sv], lambda e: e.max(out=sv[:, m, 8:16], in_=scw[:]))
                k.op("dve", [sc, sv], [si], lambda e: e.max_index(out=si[:, m, 0:8], in_max=sv[:, m, 0:8], in_values=sc[:, m, :]))
                k.op("dve", [sc, sv], [si], lambda e: e.max_index(out=si[:, m, 8:16], in_max=sv[:, m, 8:16], in_values=sc[:, m, :]))
            k.op("dve", [si], [sif], lambda e: e.tensor_copy(out=sif[:], in_=si[:]))
            svv = sv[:, :, :].rearrange("p (h two) a -> p h two a", two=2)
            sfv = sif[:, :, :].rearrange("p (h two) a -> p h two a", two=2)
            c4 = cand[:, :, :].rearrange("p h (a b) -> p h a b", a=16)
            x4 = cidx[:, :, :].rearrange("p h (a b) -> p h a b", a=16)
            k.op("dve", [sv], [cand], lambda e: e.tensor_tensor(
                out=c4, in0=svv[:, :, 0, :].unsqueeze(3).to_broadcast([128, 8, 16, 16]),
                in1=svv[:, :, 1, :].unsqueeze(2).to_broadcast([128, 8, 16, 16]), op=ALU.add))
            k.op("dve", [sif], [sif], lambda e: e.tensor_scalar(out=sfv[:, :, 0, :], in0=sfv[:, :, 0, :], scalar1=128.0,
                                                                scalar2=None, op0=ALU.mult))
            k.op("dve", [sif], [cidx], lambda e: e.tensor_tensor(
                out=x4, in0=sfv[:, :, 0, :].unsqueeze(3).to_broadcast([128, 8, 16, 16]),
                in1=sfv[:, :, 1, :].unsqueeze(2).to_broadcast([128, 8, 16, 16]), op=ALU.add))
            for h in range(8):
                k.op("dve", [cand], [top], lambda e: e.max(out=top[:, h, 0:8], in_=cand[:, h, :]))
                k.op("dve", [cand, top], [cw_], lambda e: e.match_replace(out=cw_[:], in_to_replace=top[:, h, 0:8],
                                                                          in_values=cand[:, h, :], imm_value=NEG))
                k.op("dve", [cw_], [top], lambda e: e.max(out=top[:, h, 8:16], in_=cw_[:]))
            k.op("dve", [top], [gate], lambda e: e.tensor_tensor(
                out=gate[:], in0=top[:], in1=top[:, :, 0:1].to_broadcast([128, 8, 16]), op=ALU.subtract))
            k.op("act", [gate], [gate], lambda e: e.activation(out=gate[:], in_=gate[:], func=AF.Exp))
            k.op("dve", [gate], [zs], lambda e: e.tensor_reduce(out=zs[:], in_=gate[:], axis=AX.X, op=ALU.add))
            k.op("dve", [zs], [zs], lambda e: e.reciprocal(out=zs[:], in_=zs[:]))
            k.op("dve", [gate, zs], [gate], lambda e: e.tensor_tensor(
                out=gate[:], in0=gate[:], in1=zs[:, :].unsqueeze(2).to_broadcast([128, 8, 16]), op=ALU.mult))
            k.op("dve", [], [eidf], lambda e: e.memset(eidf[:], 0.0))
            for h in range(8):
                for kk in range(16):
                    k.op("dve", [cand, top, cidx, eidf], [j256, eidf_w], lambda e: e.scalar_tensor_tensor(
                        out=j256[:], in0=cand[:, h, :], scalar=top[:, h, kk:kk + 1], in1=cidx[:, h, :],
                        op0=ALU.is_equal, op1=ALU.mult, accum_out=eidf[:, h * 16 + kk:h * 16 + kk + 1]))
            k.op("dve", [eidf, eidf_w], [eidf], lambda e: e.tensor_scalar(out=eidf[:], in0=eidf[:], scalar1=0.0,
                                                                          scalar2=float(N_EXP - 1), op0=ALU.max, op1=ALU.min))
            k.op("dve", [eidf], [eid], lambda e: e.tensor_copy(out=eid[:], in_=eidf[:]))

        def stage_b(tt):
            eid = eids[tt % 2]
            k.op("dve", [], [pre], lambda e: e.memset(pre[:], 0.0))
            for hk in range(128):
                gt, gs = gr.next()
                k.dma("pool", gs, gt[:], utab, [eid, S["UB_b"]], [gt], indirect=bass.IndirectOffsetOnAxis(ap=eid[:, hk:hk + 1], axis=0))
                k.op("dve", [gt, hnb, pre], [gt, pre_w], lambda e: e.scalar_tensor_tensor(
                    out=gt[:], in0=gt[:], scalar=1.0, in1=hnb[:], op0=ALU.mult, op1=ALU.mult,
                    accum_out=pre[:, hk:hk + 1]))
            k.op("act", [pre, pre_w], [actg], lambda e: e.activation(out=actg[:], in_=pre[:], func=AF.Gelu))
            k.op("dve", [actg, gate], [actg], lambda e: e.tensor_tensor(
                out=actg[:], in0=actg[:], in1=gate[:, :, :].rearrange("p h a -> p (h a)"), op=ALU.mult))

        def stage_c(tt):
            r0 = tt * 128
            eid = eids[tt % 2]
            xm = xms.pop(tt)
            for q in range(8):
                d_ = dg[q % 2]
                for hq in range(16):
                    hk = q * 16 + hq
                    k.op("act", [actg, identb], [d_], lambda e: e.activation(
                        out=d_[:, hq, :], in_=identb[:, :], func=AF.Copy, scale=actg[:, hk:hk + 1]))
                for hq in range(16):
                    hk = q * 16 + hq
                    gt, gs = gr.next()
                    k.dma("pool", gs, gt[:], vtab, [eid, S["VB_b"]], [gt], indirect=bass.IndirectOffsetOnAxis(ap=eid[:, hk:hk + 1], axis=0))
                    for nb in range(4):
                        k.op("pe", [d_, gt], [pq[nb]], lambda e: e.matmul(
                            pq[nb][:, :], lhsT=d_[:, hq, :], rhs=gt[:, nb * 512:(nb + 1) * 512],
                            start=(hk == 0), stop=(hk == 127)))
            ot, osm = outr.next()
            for nb in range(4):
                k.op("dve", [pq[nb], xm], [ot], lambda e: e.tensor_tensor(
                    out=ot[:, nb * 512:(nb + 1) * 512], in0=pq[nb][:], in1=xm[:, nb * 512:(nb + 1) * 512], op=ALU.add))
            k.dma("sp", osm, S["XRES"][r0:r0 + 128, :], ot[:], [ot], [S["XRES_b"]])

        NTT = L // 128
        stage_a(0)
        for tt in range(NTT):
            stage_b(tt)
            if tt + 1 < NTT:
                stage_a(tt + 1)
            stage_c(tt)
        k.barrier()
    k.pop()


def rope_consts(L):
    inv = (500000.0 ** (-np.arange(0, 16, 2, dtype=np.float32) / 16)).astype(np.float32)
    ang = np.arange(L, dtype=np.float32)[:, None] * inv[None, :]
    cosT = np.ones((128, L), np.float32)
    sinT = np.zeros((128, L), np.float32)
    for p in range(128):
        d = p % 64
        if d < 16:
            cosT[p] = np.cos(ang[:, d % 8])
            sinT[p] = np.sin(ang[:, d % 8])
    rotT = np.zeros((128, 128), np.float32)
    for m in range(128):
        d = m % 64
        if d < 8:
            rotT[m + 8, m] = -1.0
        elif d < 16:
            rotT[m - 8, m] = 1.0
    return cosT, sinT, rotT


WEIGHT_SHAPES = {
    "norm_mix": (DEPTH, D_MODEL), "w_in": (DEPTH, D_MODEL, D_IN), "ssd_dt_bias": (DEPTH, 48), "ssd_a_log": (DEPTH, 48),
    "ssd_d": (DEPTH, SSD_HEADS), "ssd_norm": (DEPTH, D_SSD),
    "ssd_conv_wl": (DEPTH, 128, 100), "ssd_conv_bl": (DEPTH, 128, 20), "sc_conv_wl": (DEPTH, 128, 18),
    "sc_norm_l": (DEPTH, 128, 6), "att_norm": (DEPTH, D_ATT), "w_out": (DEPTH, D_MIX, D_MODEL),
    "norm_ffn": (DEPTH, D_MODEL), "peer_wq": (DEPTH, D_MODEL, D_MODEL), "peer_subkeys": (DEPTH, 16, 128, 128),
    "peer_u": (DEPTH * N_EXP, D_MODEL), "peer_v": (DEPTH * N_EXP, D_MODEL), "norm_final": (1, D_MODEL),
}


class LazyW(dict):
    def __init__(self, nc):
        super().__init__()
        self.nc = nc

    def __missing__(self, n):
        v = self.nc.dram_tensor(n, list(WEIGHT_SHAPES[n]), F32, kind="ExternalInput").ap()
        self[n] = v
        return v


def build(L, phases=ALL_PHASES, debug=(), nlayers=DEPTH, feed=()):
    nc = bass.Bass("TRN2", target_bir_lowering=False)
    x = nc.dram_tensor("x", [L, D_MODEL], F32, kind="ExternalInput").ap()
    y = nc.dram_tensor("y", [L, D_MODEL], F32, kind="ExternalOutput").ap()
    W = LazyW(nc)
    S = {}

    def scratch(name, shape, dt):
        kind = "ExternalOutput" if name in debug else ("ExternalInput" if name in feed else "Internal")
        S[name] = nc.dram_tensor("s_" + name, list(shape), dt, kind=kind).ap()
        S[name + "_b"] = Buf(multi=True)

    scratch("Z", [L, D_SSD], F32)
    scratch("XBCT", [XBC, L], F32)
    scratch("DT", [L, 48], F32)
    scratch("SCT", [3 * D_SC, L], F32)
    scratch("QT", [2304, L], BF16)
    scratch("KT", [2304, L], BF16)
    scratch("V", [L, 2304], BF16)
    scratch("XRES", [L, D_MODEL], F32)
    scratch("X", [L, D_SSD], F32)
    scratch("B", [L, 512], BF16)
    scratch("BT", [512, L], BF16)
    scratch("CT", [512, L], BF16)
    scratch("YB", [L, D_SSD], F32)
    scratch("YT", [D_MIX, L], BF16)
    scratch("O0", [L, 780], F32)
    scratch("O1", [L, 780], F32)
    scratch("O2", [L, 780], F32)
    scratch("UB", [N_EXP, D_MODEL], BF16)
    scratch("VB", [N_EXP, D_MODEL], BF16)
    x_b = Buf(multi=True)
    y_b = Buf(multi=True)
    with contextlib.ExitStack() as es:
        k = KB(nc, es)
        C = {}
        csem = k.newsem()

        def cload(name, shape=(128, 128), dt=F32):
            d = nc.dram_tensor(name, list(shape), F32, kind="ExternalInput").ap()
            if dt == BF16:
                tb = k.sb(es, list(shape), BF16, name + "b")
                t = k.sb(es_stage, list(shape), F32, name)
                k.dma("sp", csem, t[:], d[:, :], [], [t], sync=True)
                k.op("dve", [t], [tb], lambda e: e.tensor_copy(out=tb[:], in_=t[:]))
                return tb
            t = k.sb(es, list(shape), F32, name)
            k.dma("sp", csem, t[:], d[:, :], [], [t], sync=True)
            return t

        C["identf"] = cload("ident")
        C["rotT"] = cload("rotT")
        C["onesf"] = cload("onesf")
        for n in ("U", "Lo", "nU", "nLo", "nmf", "nmb"):
            C[n] = cload(n)
        onec = k.sb(es, [128, 1], F32, "onec")
        k.op("dve", [], [onec], lambda e: e.memset(onec[:], 1.0))
        C["onec"] = onec
        C["identb"] = k.sb(es, [128, 128], BF16, "identb")
        k.op("dve", [C["identf"]], [C["identb"]], lambda e: e.tensor_copy(out=C["identb"][:], in_=C["identf"][:]))
        bnames = ("maskA", "maskB", "maskA0", "maskB1")
        tbs = {n: k.sb(es, [128, 512], BF16, n + "b") for n in bnames}
        with contextlib.ExitStack() as es_stage:
            for n in bnames:
                d = nc.dram_tensor(n, [128, 512], F32, kind="ExternalInput").ap()
                t = k.sb(es_stage, [128, 512], F32, n)
                k.dma("sp", csem, t[:], d[:, :], [], [t], sync=True)
                k.op("dve", [t], [tbs[n]], lambda e: e.tensor_copy(out=tbs[n][:], in_=t[:]))
                C[n] = tbs[n]
            k.barrier()
        C["cosT"] = nc.dram_tensor("cosT", [128, L], F32, kind="ExternalInput").ap()
        C["sinT"] = nc.dram_tensor("sinT", [128, L], F32, kind="ExternalInput").ap()
        k.barrier()
        xsrc, xsrc_b = x, x_b
        for lyr in range(nlayers):
            if "a" in phases:
                phase_a(k, L, lyr, xsrc, xsrc_b, W, S, C)
            if "conv" in phases:
                phase_conv(k, L, lyr, W, S, C)
            if "ssd" in phases:
                phase_ssd(k, L, lyr, W, S, C)
            if "sc" in phases:
                phase_sc(k, L, lyr, W, S, C)
            if "att" in phases:
                phase_att(k, L, lyr, W, S, C)
            if "wout" in phases:
                phase_wout(k, L, lyr, xsrc, xsrc_b, W, S, C)
            if "peer" in phases:
                phase_peer(k, L, lyr, W, S, C)
            if "wout" in phases:
                xsrc, xsrc_b = S["XRES"], S["XRES_b"]
        if "final" in phases:
            phase_final(k, L, xsrc, xsrc_b, W, y, y_b)
        k.barrier()
    return nc


def host_consts(L):
    cosT, sinT, rotT = rope_consts(L)
    i = np.arange(128)
    U = (i[:, None] <= i[None, :]).astype(np.float32)
    Lo = (i[:, None] >= i[None, :]).astype(np.float32)
    NEGM = -30000.0
    nmf = np.where(i[None, :] >= i[:, None], 0.0, NEGM).astype(np.float32)
    nmb = np.where(i[None, :] <= i[:, None], 0.0, NEGM).astype(np.float32)
    mA = np.where(i[:, None] >= i[None, :], 0.0, NEGM).astype(np.float32)
    mB = np.where(i[:, None] <= i[None, :], 0.0, NEGM).astype(np.float32)
    eye = np.eye(128, dtype=np.float32)
    mA0 = mA.copy()
    mA0[0:64, :] = NEGM
    mB1 = mB.copy()
    mB1[64:128, :] = NEGM
    return {"maskA0": np.ascontiguousarray(np.tile(mA0, (1, 4))), "maskB1": np.ascontiguousarray(np.tile(mB1, (1, 4))),"cosT": cosT, "sinT": sinT, "rotT": rotT, "ident": eye,
            "onesf": np.ones((128, 128), np.float32), "U": U, "Lo": Lo, "nU": -U, "nLo": -Lo, "nmf": nmf, "nmb": nmb,
            "maskA": np.ascontiguousarray(np.tile(mA, (1, 4))), "maskB": np.ascontiguousarray(np.tile(mB, (1, 4)))}


def prep_weights(inp):
    w = {}
    f = lambda a: np.asarray(a, dtype=np.float32)
    for n, s in WEIGHT_SHAPES.items():
        if n in inp:
            w[n] = np.ascontiguousarray(f(inp[n]).reshape(s))
    if "ssd_conv_w" in inp:
        cw = f(inp["ssd_conv_w"]).reshape(DEPTH, 5, 20, 128)
        w["ssd_conv_wl"] = np.ascontiguousarray(cw.transpose(0, 3, 2, 1).reshape(DEPTH, 128, 100))
        cb = f(inp["ssd_conv_b"]).reshape(DEPTH, 20, 128)
        w["ssd_conv_bl"] = np.ascontiguousarray(cb.transpose(0, 2, 1))
    if "sc_conv_w" in inp:
        sw = f(inp["sc_conv_w"]).reshape(DEPTH, 3, 6, 128)
        w["sc_conv_wl"] = np.ascontiguousarray(sw.transpose(0, 3, 2, 1).reshape(DEPTH, 128, 18))
        sn = f(inp["sc_norm"]).reshape(DEPTH, 6, 128)
        w["sc_norm_l"] = np.ascontiguousarray(sn.transpose(0, 2, 1))
    return w


def kernel(**inputs):
    L = 8192
    xs = [np.asarray(inputs["x_prompt"][i]) for i in range(2)] + [np.asarray(inputs["x_sample"][i]) for i in range(4)]
    xs = xs + [xs[0], xs[1]]
    w = prep_weights(inputs)
    consts = host_consts(L)
    nc = build(L)
    in_maps = []
    for c in range(8):
        m = {"x": np.ascontiguousarray(xs[c], dtype=np.float32)}
        m.update(w)
        m.update(consts)
        in_maps.append(m)
    res = run_bass_kernel_spmd(nc, in_maps, core_ids=list(range(8)))
    ys = [res.results[c]["y"] for c in range(6)]
    return (np.stack(ys[0:2], axis=0).astype(np.float32), np.stack(ys[2:6], axis=0).astype(np.float32))
```

```python
import contextlib
import numpy as np
import ml_dtypes
import concourse.bass as bass
import concourse.mybir as mybir
from concourse.bass_utils import run_bass_kernel_spmd

F32 = mybir.dt.float32
BF16 = mybir.dt.bfloat16
I32 = mybir.dt.int32
U32 = mybir.dt.uint32
AF = mybir.ActivationFunctionType
ALU = mybir.AluOpType
AX = mybir.AxisListType

D_MODEL = 2048
DEPTH = 2
D_SSD = 1536
SSD_HEADS = 24
XBC = 2560
D_SC = 768
D_ATT = 768
D_MIX = 3072
D_IN = 13360
EPS = 1e-6
C_Z = 0
C_XBC = 1536
C_DT = 4096
C_SC = 4144
C_ATT = 6448
N_EXP = 16384
ATT_GROUPS = (0, 1, 2)
ATT_STAGE = 9
PEER_DBG = ""
STORES_ON_ACT = True
ALL_PHASES = ("a", "conv", "ssd", "sc", "att", "wout", "peer", "final")


class Buf:
    __slots__ = ("w", "r", "multi")

    def __init__(self, multi=False):
        self.w = {}
        self.r = {}
        self.multi = multi


class T:
    def __init__(self, t, b=None):
        self.t = t
        self.b = b if b is not None else Buf()

    def __getitem__(self, idx):
        return self.t[idx]


class KB:
    def __init__(self, nc, es):
        self.nc = nc
        self.es = es
        self.eng = {"pe": nc.tensor, "act": nc.scalar, "dve": nc.vector, "pool": nc.gpsimd, "sp": nc.sync}
        self.sems = {}
        self.cnt = {}
        for e in ("pe", "act", "dve", "pool"):
            self.sems[e] = es.enter_context(nc.semaphore("c_" + e))
            self.cnt[e] = 0
        self.waited = {e: {} for e in self.eng}
        self.nd = 0
        self.uid = 0
        self.free = []
        self.free_sw = []
        self.scopes = []

    def newsem(self, sw=False):
        fl = self.free_sw if sw else self.free
        if fl:
            key = fl.pop()
        else:
            key = ("dw%d" if sw else "d%d") % self.nd
            self.nd += 1
            self.sems[key] = self.es.enter_context(self.nc.semaphore(key))
            self.cnt[key] = 0
        for sc in self.scopes:
            sc.append(key)
        return key

    def push(self):
        self.scopes.append([])

    def pop(self):
        sc = self.scopes.pop()
        for key in sc:
            fl = self.free_sw if key.startswith("dw") else self.free
            if key not in fl:
                fl.append(key)

    def _deps(self, e, reads, writes):
        need = {}
        for b in reads:
            for s, v in b.w.items():
                if v > need.get(s, 0):
                    need[s] = v
        for b in writes:
            if not b.multi:
                for s, v in b.w.items():
                    if v > need.get(s, 0):
                        need[s] = v
            for s, v in b.r.items():
                if v > need.get(s, 0):
                    need[s] = v
        wd = self.waited[e]
        for s, v in need.items():
            if s == e and e == "pe":
                continue
            if wd.get(s, 0) >= v:
                continue
            if s[0] == "d":
                v = self.cnt[s]
            self.eng[e].wait_ge(self.sems[s], v)
            wd[s] = v

    def _rec(self, key, val, reads, writes):
        for b in reads:
            b.r[key] = val
        for b in writes:
            if b.multi:
                b.w[key] = val
            else:
                b.w = {key: val}
            b.r = {}

    def op(self, e, reads, writes, fn):
        reads = [x.b if isinstance(x, T) else x for x in reads]
        writes = [x.b if isinstance(x, T) else x for x in writes]
        self._deps(e, reads, writes)
        ins = fn(self.eng[e])
        self.cnt[e] += 1
        ins.then_inc(self.sems[e], 1)
        self._rec(e, self.cnt[e], reads, writes)
        return ins

    def dma(self, q, sem, out, in_, reads, writes, indirect=None, sync=False, **kw):
        reads = [x.b if isinstance(x, T) else x for x in reads]
        writes = [x.b if isinstance(x, T) else x for x in writes]
        if q == "sp" and STORES_ON_ACT and any(b.multi for b in writes):
            q = "act"
        self._deps(q, reads, writes)
        if indirect is not None:
            ins = self.eng[q].indirect_dma_start(out=out, out_offset=None, in_=in_, in_offset=indirect, **kw)
        else:
            ins = self.eng[q].dma_start(out=out, in_=in_, **kw)
        self.cnt[sem] += 16
        ins.then_inc(self.sems[sem], 16)
        self._rec(sem, self.cnt[sem], reads, writes)
        if sync:
            self.eng[q].wait_ge(self.sems[sem], self.cnt[sem])
            self.waited[q][sem] = self.cnt[sem]
        return ins

    def barrier(self):
        for e in self.eng:
            wd = self.waited[e]
            for s, v in self.cnt.items():
                if s == e or v == 0 or wd.get(s, 0) >= v:
                    continue
                self.eng[e].wait_ge(self.sems[s], v)
                wd[s] = v

    def sb(self, es, shape, dt, name=None):
        self.uid += 1
        t = es.enter_context(self.nc.sbuf_tensor("%s_%d" % (name or "t", self.uid), list(shape), dt))
        return T(t)

    def ps(self, es, shape, dt, name=None):
        self.uid += 1
        t = es.enter_context(self.nc.psum_tensor("%s_%d" % (name or "p", self.uid), list(shape), dt))
        return T(t)


class Ring:
    def __init__(self, k, es, n, shape, dt, name, dma=True, sw=False):
        self.tiles = [k.sb(es, shape, dt, name) for _ in range(n)]
        self.sems = [k.newsem(sw) for _ in range(n)] if dma else [None] * n
        self.i = 0

    def next(self):
        t, s = self.tiles[self.i], self.sems[self.i]
        self.i = (self.i + 1) % len(self.tiles)
        return t, s


def rmsnorm_rstd(k, ssq, rstd, n):
    k.op("dve", [ssq], [rstd], lambda e: e.tensor_scalar(out=rstd[:, 0:1], in0=ssq[:, 0:1], scalar1=1.0 / n,
                                                         scalar2=EPS, op0=ALU.mult, op1=ALU.add))
    k.op("act", [rstd], [rstd], lambda e: e.activation(out=rstd[:, 0:1], in_=rstd[:, 0:1], func=AF.Sqrt))
    k.op("dve", [rstd], [rstd], lambda e: e.reciprocal(out=rstd[:, 0:1], in_=rstd[:, 0:1]))


def phase_a(k, L, lyr, xsrc, xsrc_b, W, S, C):
    nc = k.nc
    TB = 1024 if L >= 1024 else L
    NT = TB // 128
    k.push()
    with contextlib.ExitStack() as es:
        gB = k.sb(es, [128, D_MODEL], F32, "gB")
        gsem = k.newsem()
        k.dma("sp", gsem, gB[:], W["norm_mix"][lyr:lyr + 1, :].broadcast_to([128, D_MODEL]), [], [gB], sync=True)
        xr = Ring(k, es, 2, [128, D_MODEL], F32, "xt")
        junk = k.sb(es, [128, D_MODEL], BF16, "junk")
        hb = [k.sb(es, [128, D_MODEL], BF16, "hb") for _ in range(2)]
        ssq = [k.sb(es, [128, 1], F32, "ssq") for _ in range(2)]
        rstd = [k.sb(es, [128, 1], F32, "rstd") for _ in range(2)]
        hT = k.sb(es, [128, 16, TB], BF16, "hT")
        hTb = [Buf() for _ in range(NT)]
        wf = Ring(k, es, 2, [128, 16, 512], F32, "wf")
        wb = [k.sb(es, [128, 16, 512], BF16, "wb") for _ in range(2)]
        ev = Ring(k, es, 3, [128, 512], F32, "ev")
        evb = Ring(k, es, 3, [128, 512], BF16, "evb")
        cs = Ring(k, es, 2, [128, 512], F32, "cos")
        sn = Ring(k, es, 2, [128, 512], F32, "sin")
        qsb = [k.sb(es, [128, 512], F32, "qsb") for _ in range(2)]
        t1 = [k.sb(es, [128, 512], F32, "t1") for _ in range(2)]
        t2 = [k.sb(es, [128, 512], F32, "t2") for _ in range(2)]
        pst = [k.ps(es, [128, 1024], BF16, "pst") for _ in range(2)]
        psm = [k.ps(es, [128, 512], F32, "psm") for _ in range(4)]
        psr = [k.ps(es, [128, 512], F32, "psr") for _ in range(2)]
        identb = C["identb"]
        rotT = C["rotT"]
        wi = 0
        mi = 0
        ri = 0
        segs = [("z", C_Z, 1536, 512), ("xbc", C_XBC, 2560, 512), ("dt", C_DT, 48, 48),
                ("sc", C_SC, 2304, 384), ("q", C_ATT, 2304, 384), ("k", C_ATT + 2304, 2304, 384),
                ("v", C_ATT + 4608, 2304, 384)]
        for sbk in range(L // TB):
            tok0 = sbk * TB
            for tt in range(NT):
                xt, xs = xr.next()
                k.dma("sp", xs, xt[:], xsrc[tok0 + tt * 128: tok0 + (tt + 1) * 128, :], [xsrc_b], [xt])
                p = tt % 2
                k.op("act", [xt], [junk, ssq[p]], lambda e: e.activation(out=junk[:], in_=xt[:], func=AF.Square,
                                                                          accum_out=ssq[p][:, 0:1]))
                rmsnorm_rstd(k, ssq[p], rstd[p], D_MODEL)
                k.op("dve", [xt, rstd[p], gB], [hb[p]], lambda e: e.scalar_tensor_tensor(
                    out=hb[p][:], in0=xt[:], scalar=rstd[p][:, 0:1], in1=gB[:], op0=ALU.mult, op1=ALU.mult))
                for half in range(2):
                    pt = pst[half]
                    for j in range(8):
                        kc = half * 8 + j
                        k.op("pe", [hb[p], identb], [pt], lambda e: e.transpose(
                            out=pt[:, j * 128:(j + 1) * 128], in_=hb[p][:, kc * 128:(kc + 1) * 128],
                            identity=identb[:]))
                    eng = "act" if half == 0 else "dve"
                    src = pt[:, :].rearrange("p (j t) -> p j t", j=8)
                    dst = hT[:, half * 8:(half + 1) * 8, tt * 128:(tt + 1) * 128]
                    if eng == "act":
                        k.op("act", [pt], [hTb[tt]], lambda e: e.copy(out=dst, in_=src))
                    else:
                        k.op("dve", [pt], [hTb[tt]], lambda e: e.tensor_copy(out=dst, in_=src))
            for kind, c0, ncs, bw in segs:
                for blk in range(ncs // bw):
                    cb = c0 + blk * bw
                    wft, wfs = wf.next()
                    wbt = wb[wi % 2]
                    wi += 1
                    k.dma("sp", wfs, wft[:, :, 0:bw],
                          W["w_in"][lyr, :, cb:cb + bw].rearrange("(kc p) c -> p kc c", p=128), [], [wft])
                    k.op("pool", [wft], [wbt], lambda e: e.tensor_copy(out=wbt[:, :, 0:bw], in_=wft[:, :, 0:bw]))
                    if kind in ("z", "dt", "v"):
                        for tt in range(NT):
                            pm = psm[mi % 4]
                            mi += 1
                            for kc in range(16):
                                k.op("pe", [hTb[tt], wbt], [pm], lambda e: e.matmul(
                                    pm[:, 0:bw], lhsT=hT[:, kc, tt * 128:(tt + 1) * 128], rhs=wbt[:, kc, 0:bw],
                                    start=(kc == 0), stop=(kc == 15)))
                            r0 = tok0 + tt * 128
                            if kind == "v":
                                et, esm = evb.next()
                                k.op("act", [pm], [et], lambda e: e.copy(out=et[:, 0:bw], in_=pm[:, 0:bw]))
                                k.dma("sp", esm, S["V"][r0:r0 + 128, blk * bw:(blk + 1) * bw], et[:, 0:bw],
                                      [et], [S["V_b"]])
                            else:
                                et, esm = ev.next()
                                k.op("act", [pm], [et], lambda e: e.copy(out=et[:, 0:bw], in_=pm[:, 0:bw]))
                                dst = S["Z"] if kind == "z" else S["DT"]
                                dstb = S["Z_b"] if kind == "z" else S["DT_b"]
                                k.dma("sp", esm, dst[r0:r0 + 128, blk * bw:(blk + 1) * bw], et[:, 0:bw],
                                      [et], [dstb])
                    else:
                        for tb in range(TB // 512):
                            t0 = tok0 + tb * 512
                            hdeps = [hTb[tb * 4 + i] for i in range(4)]
                            if kind in ("q", "k"):
                                ct, csm = cs.next()
                                st, ssm = sn.next()
                                k.dma("sp", csm, ct[:], C["cosT"][:, t0:t0 + 512], [], [ct])
                                k.dma("sp", ssm, st[:], C["sinT"][:, t0:t0 + 512], [], [st])
                            for ch in range(bw // 128):
                                pm = psm[mi % 4]
                                mi += 1
                                for kc in range(16):
                                    k.op("pe", hdeps + [wbt], [pm], lambda e: e.matmul(
                                        pm[:, :], lhsT=wbt[:, kc, ch * 128:(ch + 1) * 128],
                                        rhs=hT[:, kc, tb * 512:(tb + 1) * 512], start=(kc == 0), stop=(kc == 15)))
                                row = blk * bw + ch * 128
                                if kind in ("xbc", "sc"):
                                    et, esm = ev.next()
                                    k.op("act", [pm], [et], lambda e: e.copy(out=et[:], in_=pm[:]))
                                    dst = S["XBCT"] if kind == "xbc" else S["SCT"]
                                    dstb = S["XBCT_b"] if kind == "xbc" else S["SCT_b"]
                                    k.dma("sp", esm, dst[row:row + 128, t0:t0 + 512], et[:], [et], [dstb])
                                else:
                                    r = ri % 2
                                    ri += 1
                                    pr = psr[r]
                                    k.op("act", [pm], [qsb[r]], lambda e: e.copy(out=qsb[r][:], in_=pm[:]))
                                    k.op("pe", [qsb[r], rotT], [pr], lambda e: e.matmul(
                                        pr[:, :], lhsT=rotT[:], rhs=qsb[r][:], start=True, stop=True))
                                    k.op("pool", [qsb[r], ct], [t1[r]], lambda e: e.tensor_tensor(
                                        out=t1[r][:], in0=qsb[r][:], in1=ct[:], op=ALU.mult))
                                    k.op("dve", [pr, st], [t2[r]], lambda e: e.tensor_tensor(
                                        out=t2[r][:], in0=pr[:], in1=st[:], op=ALU.mult))
                                    et, esm = evb.next()
                                    k.op("dve", [t1[r], t2[r]], [et], lambda e: e.tensor_tensor(
                                        out=et[:], in0=t1[r][:], in1=t2[r][:], op=ALU.add))
                                    dst = S["QT"] if kind == "q" else S["KT"]
                                    dstb = S["QT_b"] if kind == "q" else S["KT_b"]
                                    k.dma("sp", esm, dst[row:row + 128, t0:t0 + 512], et[:], [et], [dstb])
        k.barrier()
    k.pop()


def phase_final(k, L, xsrc, xsrc_b, W, y, y_b):
    k.push()
    with contextlib.ExitStack() as es:
        gB = k.sb(es, [128, D_MODEL], F32, "gB")
        gsem = k.newsem()
        k.dma("sp", gsem, gB[:], W["norm_final"][0:1, :].broadcast_to([128, D_MODEL]), [], [gB], sync=True)
        xr = Ring(k, es, 2, [128, D_MODEL], F32, "xt")
        yr = Ring(k, es, 2, [128, D_MODEL], F32, "yt")
        junk = k.sb(es, [128, D_MODEL], BF16, "junk")
        ssq = [k.sb(es, [128, 1], F32, "ssq") for _ in range(2)]
        rstd = [k.sb(es, [128, 1], F32, "rstd") for _ in range(2)]
        for tt in range(L // 128):
            xt, xs = xr.next()
            yt, ys = yr.next()
            p = tt % 2
            k.dma("sp", xs, xt[:], xsrc[tt * 128:(tt + 1) * 128, :], [xsrc_b], [xt])
            k.op("act", [xt], [junk, ssq[p]], lambda e: e.activation(out=junk[:], in_=xt[:], func=AF.Square,
                                                                      accum_out=ssq[p][:, 0:1]))
            rmsnorm_rstd(k, ssq[p], rstd[p], D_MODEL)
            k.op("dve", [xt, rstd[p], gB], [yt], lambda e: e.scalar_tensor_tensor(
                out=yt[:], in0=xt[:], scalar=rstd[p][:, 0:1], in1=gB[:], op0=ALU.mult, op1=ALU.mult))
            k.dma("sp", ys, y[tt * 128:(tt + 1) * 128, :], yt[:], [yt], [y_b])
        k.barrier()
    k.pop()


def phase_conv(k, L, lyr, W, S, C):
    LB = min(L, 4096)
    identf, identb = C["identf"], C["identb"]
    k.push()
    with contextlib.ExitStack() as es:
        cw = k.sb(es, [128, 100], F32, "cw")
        cbias = k.sb(es, [128, 20], F32, "cbias")
        sem = k.newsem()
        k.dma("sp", sem, cw[:], W["ssd_conv_wl"][lyr], [], [cw], sync=True)
        k.dma("sp", sem, cbias[:], W["ssd_conv_bl"][lyr], [], [cbias], sync=True)
        xin = Ring(k, es, 2, [128, LB + 4], F32, "cin")
        acc = k.sb(es, [128, LB], F32, "cacc")
        so = Ring(k, es, 2, [128, LB], F32, "so")
        sob = Ring(k, es, 2, [128, LB], BF16, "sob")
        tr = Ring(k, es, 3, [128, 4, 128], F32, "tr")
        trb = Ring(k, es, 3, [128, 4, 128], BF16, "trb")
        ptr = [k.ps(es, [128, 512], F32, "ptr") for _ in range(2)]
        ptb = [k.ps(es, [128, 512], BF16, "ptb") for _ in range(2)]
        pi = 0
        for c in range(20):
            for lb in range(L // LB):
                t0 = lb * LB
                xt, xs = xin.next()
                lo = 2 if t0 == 0 else 0
                hi = 2 if t0 + LB == L else 0
                if lo:
                    k.op("pool", [], [xt], lambda e: e.memset(xt[:, 0:2], 0.0))
                if hi:
                    k.op("pool", [], [xt], lambda e: e.memset(xt[:, LB + 2:LB + 4], 0.0))
                k.dma("sp", xs, xt[:, lo:LB + 4 - hi], S["XBCT"][c * 128:(c + 1) * 128, t0 - 2 + lo:t0 + LB + 2 - hi],
                      [S["XBCT_b"]], [xt])
                k.op("dve", [xt, cw, cbias], [acc], lambda e: e.tensor_scalar(
                    out=acc[:], in0=xt[:, 0:LB], scalar1=cw[:, c * 5:c * 5 + 1], scalar2=cbias[:, c:c + 1],
                    op0=ALU.mult, op1=ALU.add))
                for t in range(1, 5):
                    k.op("dve", [xt, cw, acc], [acc], lambda e: e.scalar_tensor_tensor(
                        out=acc[:], in0=xt[:, t:t + LB], scalar=cw[:, c * 5 + t:c * 5 + t + 1], in1=acc[:],
                        op0=ALU.mult, op1=ALU.add))
                if c < 12:
                    st, ssm = so.next()
                    k.op("act", [acc], [st], lambda e: e.activation(out=st[:], in_=acc[:], func=AF.Silu))
                    for tg in range(LB // 512):
                        pt = ptr[pi % 2]
                        pi += 1
                        for i in range(4):
                            k.op("pe", [st, identf], [pt], lambda e: e.transpose(
                                out=pt[:, i * 128:(i + 1) * 128], in_=st[:, (tg * 4 + i) * 128:(tg * 4 + i + 1) * 128],
                                identity=identf[:]))
                        tt, ts = tr.next()
                        k.op("act", [pt], [tt], lambda e: e.copy(out=tt[:], in_=pt[:, :].rearrange("p (a c) -> p a c", a=4)))
                        r0 = t0 + tg * 512
                        k.dma("sp", ts, S["X"][r0:r0 + 512, c * 128:(c + 1) * 128].rearrange("(a p) c -> p a c", p=128),
                              tt[:], [tt], [S["X_b"]])
                else:
                    st, ssm = sob.next()
                    k.op("act", [acc], [st], lambda e: e.activation(out=st[:], in_=acc[:], func=AF.Silu))
                    name = "BT" if c < 16 else "CT"
                    rr = (c - 12) % 4
                    k.dma("sp", ssm, S[name][rr * 128:(rr + 1) * 128, t0:t0 + LB], st[:], [st], [S[name + "_b"]])
                    if c < 16:
                        for tg in range(LB // 512):
                            pt = ptb[pi % 2]
                            pi += 1
                            for i in range(4):
                                k.op("pe", [st, identb], [pt], lambda e: e.transpose(
                                    out=pt[:, i * 128:(i + 1) * 128],
                                    in_=st[:, (tg * 4 + i) * 128:(tg * 4 + i + 1) * 128], identity=identb[:]))
                            tt, ts = trb.next()
                            k.op("dve", [pt], [tt], lambda e: e.tensor_copy(
                                out=tt[:], in_=pt[:, :].rearrange("p (a c) -> p a c", a=4)))
                            r0 = t0 + tg * 512
                            k.dma("sp", ts,
                                  S["B"][r0:r0 + 512, rr * 128:(rr + 1) * 128].rearrange("(a p) c -> p a c", p=128),
                                  tt[:], [tt], [S["B_b"]])
        k.barrier()
    k.pop()


SSD_RUNS = {0: [(0, 0, 4)], 1: [(0, 4, 6), (1, 6, 8)], 2: [(1, 8, 12)], 3: [(2, 12, 16)],
            4: [(2, 16, 18), (3, 18, 20)], 5: [(3, 20, 24)]}


def phase_ssd(k, L, lyr, W, S, C):
    NCH = L // 128
    identf, onesf, onec = C["identf"], C["onesf"], C["onec"]
    k.push()
    with contextlib.ExitStack() as es:
        sem = k.newsem()
        Abc = k.sb(es, [128, 48], F32, "Abc")
        dtb = k.sb(es, [128, 48], F32, "dtb")
        Dsk = k.sb(es, [128, 24], F32, "Dsk")
        normw = k.sb(es, [128, D_SSD], F32, "normw")
        k.dma("sp", sem, Abc[:], W["ssd_a_log"][lyr:lyr + 1, :].broadcast_to([128, 48]), [], [Abc], sync=True)
        k.dma("sp", sem, dtb[:], W["ssd_dt_bias"][lyr:lyr + 1, :].broadcast_to([128, 48]), [], [dtb], sync=True)
        k.dma("sp", sem, Dsk[:], W["ssd_d"][lyr:lyr + 1, :].broadcast_to([128, 24]), [], [Dsk], sync=True)
        k.dma("sp", sem, normw[:], W["ssd_norm"][lyr:lyr + 1, :].broadcast_to([128, D_SSD]), [], [normw], sync=True)
        k.op("act", [Abc], [Abc], lambda e: e.activation(out=Abc[:], in_=Abc[:], func=AF.Exp))
        k.op("dve", [Abc], [Abc], lambda e: e.tensor_scalar(out=Abc[:], in0=Abc[:], scalar1=-1.0, scalar2=None,
                                                            op0=ALU.mult))
        xr = Ring(k, es, 2, [128, 24, 64], F32, "xk")
        zr = Ring(k, es, 2, [128, D_SSD], F32, "zk")
        ybr = Ring(k, es, 2, [128, D_SSD], F32, "ybk")
        dtr = Ring(k, es, 2, [128, 48], F32, "dtk")
        bkr = Ring(k, es, 2, [128, 512], BF16, "bk")
        btr = Ring(k, es, 2, [128, 4, 128], BF16, "btk")
        ctr = Ring(k, es, 2, [128, 4, 128], BF16, "ctk")
        dts = k.sb(es, [128, 48], F32, "dts")
        dt = k.sb(es, [128, 48], F32, "dt")
        a = k.sb(es, [128, 24], F32, "a")
        sm = k.sb(es, [128, 64], F32, "sm")
        eac = k.sb(es, [128, 24], F32, "eac")
        dend = k.sb(es, [128, 24], F32, "dend")
        cdec = k.sb(es, [128, 24], F32, "cdec")
        w2 = k.sb(es, [128, 24], F32, "w2")
        xdt = k.sb(es, [128, D_SSD], BF16, "xdt")
        xde = k.sb(es, [128, D_SSD], BF16, "xde")
        cbs = k.sb(es, [128, 4, 128], F32, "cbs")
        rhs1 = k.sb(es, [128, 24, 128], F32, "rhs1")
        dec = [k.sb(es, [128, 4, 128], F32, "dec") for _ in range(3)]
        MT = [k.sb(es, [128, 12, 128], BF16, "MT") for _ in range(2)]
        H = [k.sb(es, [128, 768], F32, "H") for _ in range(2)]
        Hb = [k.sb(es, [128, 768], BF16, "Hb") for _ in range(2)]
        yo = k.sb(es, [128, 24, 64], F32, "yo")
        yd = k.sb(es, [128, 24, 64], F32, "yd")
        ydb = [Buf(), Buf()]
        ystore = Ring(k, es, 2, [128, D_SSD], F32, "ystore")
        y1 = k.sb(es, [128, D_SSD], F32, "y1")
        sz = k.sb(es, [128, D_SSD], F32, "sz")
        junk = k.sb(es, [128, 384], F32, "junk")
        gss = k.sb(es, [128, 4], F32, "gss")
        ytr = Ring(k, es, 2, [128, 12, 128], BF16, "ytr")
        big = [k.ps(es, [128, 1024], F32, "big") for _ in range(2)]
        pseg = [k.ps(es, [128, 512], F32, "pseg") for _ in range(2)]
        pcb = k.ps(es, [128, 512], F32, "pcb")
        ptr = k.ps(es, [128, 512], F32, "ptr")
        st = {"big": 0, "seg": 0, "dec": 0}

        def nbig():
            st["big"] += 1
            return big[st["big"] % 2]

        for d in (1, 0):
            Td = C["U"] if d == 0 else C["Lo"]
            nTd = C["nU"] if d == 0 else C["nLo"]
            nm = C["nmf"] if d == 0 else C["nmb"]
            for hf in range(2):
                k.op("pool", [], [H[hf]], lambda e: e.memset(H[hf][:], 0.0))
                k.op("pool", [], [Hb[hf]], lambda e: e.memset(Hb[hf][:], 0.0))
            order = range(NCH) if d == 0 else range(NCH - 1, -1, -1)
            for c in order:
                r0 = c * 128
                xk, s1 = xr.next()
                k.dma("sp", s1, xk[:], S["X"][r0:r0 + 128, :].rearrange("p (h d) -> p h d", h=24), [S["X_b"]], [xk])
                dtk, s2 = dtr.next()
                k.dma("sp", s2, dtk[:], S["DT"][r0:r0 + 128, :], [S["DT_b"]], [dtk])
                bk, s3 = bkr.next()
                k.dma("sp", s3, bk[:], S["B"][r0:r0 + 128, :], [S["B_b"]], [bk])
                btk, s4 = btr.next()
                k.dma("sp", s4, btk[:], S["BT"][:, r0:r0 + 128].rearrange("(g n) t -> n g t", n=128), [S["BT_b"]], [btk])
                ctk, s5 = ctr.next()
                k.dma("sp", s5, ctk[:], S["CT"][:, r0:r0 + 128].rearrange("(g n) t -> n g t", n=128), [S["CT_b"]], [ctk])
                if d == 0:
                    zk, s6 = zr.next()
                    k.dma("sp", s6, zk[:], S["Z"][r0:r0 + 128, :], [S["Z_b"]], [zk])
                    ybk, s7 = ybr.next()
                    k.dma("sp", s7, ybk[:], S["YB"][r0:r0 + 128, :], [S["YB_b"]], [ybk])
                k.op("dve", [dtk, dtb], [dts], lambda e: e.tensor_tensor(out=dts[:], in0=dtk[:], in1=dtb[:], op=ALU.add))
                k.op("act", [dts], [dts], lambda e: e.activation(out=dts[:], in_=dts[:], func=AF.Exp))
                k.op("act", [dts, onec], [dt], lambda e: e.activation(out=dt[:], in_=dts[:], func=AF.Ln,
                                                                      bias=onec[:, 0:1], scale=1.0))
                dtd = dt[:, d * 24:(d + 1) * 24]
                k.op("dve", [dt, Abc], [a], lambda e: e.tensor_tensor(out=a[:], in0=dtd, in1=Abc[:, d * 24:(d + 1) * 24],
                                                                      op=ALU.mult))
                pb = nbig()
                k.op("pe", [Td, a], [pb], lambda e: e.matmul(pb[:, 0:24], lhsT=Td[:], rhs=a[:], start=True, stop=True))
                k.op("pe", [onesf, a], [pb], lambda e: e.matmul(pb[:, 24:48], lhsT=onesf[:], rhs=a[:], start=True, stop=True))
                k.op("act", [pb], [sm], lambda e: e.copy(out=sm[:, 0:48], in_=pb[:, 0:48]))
                k.op("act", [sm], [eac], lambda e: e.activation(out=eac[:], in_=sm[:, 0:24], func=AF.Exp))
                k.op("dve", [sm], [dend], lambda e: e.tensor_tensor(out=dend[:], in0=sm[:, 24:48], in1=sm[:, 0:24],
                                                                   op=ALU.subtract))
                k.op("act", [dend], [dend], lambda e: e.activation(out=dend[:], in_=dend[:], func=AF.Exp))
                k.op("act", [sm], [cdec], lambda e: e.activation(out=cdec[:], in_=sm[:, 24:48], func=AF.Exp))
                k.op("dve", [dt, dend], [w2], lambda e: e.tensor_tensor(out=w2[:], in0=dtd, in1=dend[:], op=ALU.mult))
                k.op("dve", [xk, dt], [xdt], lambda e: e.tensor_tensor(
                    out=xdt[:, :].rearrange("p (h d) -> p h d", d=64), in0=xk[:], in1=dtd.unsqueeze(2).to_broadcast([128, 24, 64]), op=ALU.mult))
                k.op("pool", [xk, w2], [xde], lambda e: e.tensor_tensor(
                    out=xde[:, :].rearrange("p (h d) -> p h d", d=64), in0=xk[:], in1=w2[:, :].unsqueeze(2).to_broadcast([128, 24, 64]), op=ALU.mult))
                for g in range(4):
                    k.op("pe", [btk, ctk], [pcb], lambda e: e.matmul(
                        pcb[:, g * 128:(g + 1) * 128], lhsT=btk[:, g, :], rhs=ctk[:, g, :], start=True, stop=True))
                k.op("act", [pcb], [cbs], lambda e: e.copy(out=cbs[:], in_=pcb[:, :].rearrange("p (g l) -> p g l", g=4)))
                k.op("pool", [a, Td], [rhs1], lambda e: e.tensor_tensor(
                    out=rhs1[:], in0=a[:, :].unsqueeze(2).to_broadcast([128, 24, 128]),
                    in1=Td[:, :].unsqueeze(1).to_broadcast([128, 24, 128]), op=ALU.mult))
                for hf in range(2):
                    h0 = hf * 12
                    pb = nbig()
                    for u in range(3 * hf, 3 * hf + 3):
                        for (g, ha, hb_) in SSD_RUNS[u]:
                            k.op("pe", [ctk, Hb[hf]], [pb], lambda e: e.matmul(
                                pb[:, (ha - h0) * 64:(hb_ - h0) * 64], lhsT=ctk[:, g, :],
                                rhs=Hb[hf][:, (ha - h0) * 64:(hb_ - h0) * 64], start=True, stop=True))
                    k.op("dve", [pb, eac], [yo], lambda e: e.tensor_tensor(
                        out=yo[:, h0:h0 + 12, :], in0=pb[:, 0:768].rearrange("p (h d) -> p h d", h=12),
                        in1=eac[:, h0:h0 + 12].unsqueeze(2).to_broadcast([128, 12, 64]), op=ALU.mult))
                    mt = MT[hf]
                    for u in range(3 * hf, 3 * hf + 3):
                        pg = pseg[st["seg"] % 2]
                        st["seg"] += 1
                        pg3 = pg[:, :].rearrange("p (h l) -> p h l", h=4)
                        k.op("pe", [onesf, rhs1], [pg], lambda e: e.matmul(
                            pg3, lhsT=onesf[:], rhs=rhs1[:, 4 * u:4 * u + 4, :], start=True, stop=False))
                        k.op("pe", [nTd, a], [pg], lambda e: e.matmul(
                            pg3, lhsT=nTd[:], rhs=a[:, 4 * u:4 * u + 4].unsqueeze(2).to_broadcast([128, 4, 128]),
                            start=False, stop=False))
                        k.op("pe", [identf, nm], [pg], lambda e: e.matmul(
                            pg3, lhsT=identf[:], rhs=nm[:, :].unsqueeze(1).to_broadcast([128, 4, 128]),
                            start=False, stop=True))
                        dc = dec[st["dec"] % 3]
                        st["dec"] += 1
                        k.op("act", [pg], [dc], lambda e: e.activation(out=dc[:], in_=pg3, func=AF.Exp))
                        for (g, ha, hb_) in SSD_RUNS[u]:
                            n = hb_ - ha
                            k.op("dve", [dc, cbs], [mt], lambda e: e.tensor_tensor(
                                out=mt[:, ha - h0:hb_ - h0, :], in0=dc[:, ha - 4 * u:hb_ - 4 * u, :],
                                in1=cbs[:, g:g + 1, :].to_broadcast([128, n, 128]), op=ALU.mult))
                    pb2 = nbig()
                    for h in range(h0, h0 + 12):
                        k.op("pe", [mt, xdt], [pb2], lambda e: e.matmul(
                            pb2[:, (h - h0) * 64:(h - h0 + 1) * 64], lhsT=mt[:, h - h0, :], rhs=xdt[:, h * 64:(h + 1) * 64],
                            start=True, stop=True))
                    k.op("dve", [pb2, yo], [ydb[hf]], lambda e: e.tensor_tensor(
                        out=yd[:, h0:h0 + 12, :], in0=pb2[:, 0:768].rearrange("p (h d) -> p h d", h=12),
                        in1=yo[:, h0:h0 + 12, :], op=ALU.add))
                for hf in range(2):
                    h0 = hf * 12
                    pb3 = nbig()
                    for u in range(3 * hf, 3 * hf + 3):
                        for (g, ha, hb_) in SSD_RUNS[u]:
                            k.op("pe", [bk, xde], [pb3], lambda e: e.matmul(
                                pb3[:, (ha - h0) * 64:(hb_ - h0) * 64], lhsT=bk[:, g * 128:(g + 1) * 128],
                                rhs=xde[:, ha * 64:hb_ * 64], start=True, stop=True))
                    k.op("pool", [H[hf], cdec], [H[hf]], lambda e: e.tensor_tensor(
                        out=H[hf][:, :].rearrange("p (h d) -> p h d", d=64), in0=H[hf][:, :].rearrange("p (h d) -> p h d", d=64),
                        in1=cdec[:, h0:h0 + 12].unsqueeze(2).to_broadcast([128, 12, 64]), op=ALU.mult))
                    k.op("dve", [H[hf], pb3], [H[hf]], lambda e: e.tensor_tensor(
                        out=H[hf][:], in0=H[hf][:], in1=pb3[:, 0:768], op=ALU.add))
                    k.op("act", [H[hf]], [Hb[hf]], lambda e: e.copy(out=Hb[hf][:], in_=H[hf][:]))
                ydf = yd[:, :, :].rearrange("p h d -> p (h d)")
                if d == 1:
                    yt, ysm = ystore.next()
                    k.op("pool", ydb, [yt], lambda e: e.tensor_copy(out=yt[:], in_=ydf))
                    k.dma("sp", ysm, S["YB"][r0:r0 + 128, :], yt[:], [yt], [S["YB_b"]])
                else:
                    k.op("dve", ydb + [ybk], [y1], lambda e: e.tensor_tensor(out=y1[:], in0=ydf, in1=ybk[:], op=ALU.add))
                    k.op("pool", [xk, Dsk], [sz], lambda e: e.tensor_tensor(
                        out=sz[:, :].rearrange("p (h d) -> p h d", h=24), in0=xk[:],
                        in1=Dsk[:, :].unsqueeze(2).to_broadcast([128, 24, 64]), op=ALU.mult))
                    k.op("dve", [y1, sz], [y1], lambda e: e.tensor_tensor(out=y1[:], in0=y1[:], in1=sz[:], op=ALU.add))
                    k.op("act", [zk], [sz], lambda e: e.activation(out=sz[:], in_=zk[:], func=AF.Silu))
                    k.op("dve", [y1, sz], [y1], lambda e: e.tensor_tensor(out=y1[:], in0=y1[:], in1=sz[:], op=ALU.mult))
                    for g in range(4):
                        k.op("act", [y1], [junk, gss], lambda e: e.activation(
                            out=junk[:], in_=y1[:, g * 384:(g + 1) * 384], func=AF.Square, accum_out=gss[:, g:g + 1]))
                    k.op("dve", [gss], [gss], lambda e: e.tensor_scalar(out=gss[:], in0=gss[:], scalar1=1.0 / 384,
                                                                        scalar2=EPS, op0=ALU.mult, op1=ALU.add))
                    k.op("act", [gss], [gss], lambda e: e.activation(out=gss[:], in_=gss[:], func=AF.Sqrt))
                    k.op("dve", [gss], [gss], lambda e: e.reciprocal(out=gss[:], in_=gss[:]))
                    k.op("dve", [y1, gss], [y1], lambda e: e.tensor_tensor(
                        out=y1[:, :].rearrange("p (g c) -> p g c", g=4), in0=y1[:, :].rearrange("p (g c) -> p g c", g=4),
                        in1=gss[:, :].unsqueeze(2).to_broadcast([128, 4, 384]), op=ALU.mult))
                    k.op("pool", [y1, normw], [y1], lambda e: e.tensor_tensor(out=y1[:], in0=y1[:], in1=normw[:], op=ALU.mult))
                    yt, ysm = ytr.next()
                    for q in range(3):
                        for i in range(4):
                            cc = q * 4 + i
                            k.op("pe", [y1, identf], [ptr], lambda e: e.transpose(
                                out=ptr[:, i * 128:(i + 1) * 128], in_=y1[:, cc * 128:(cc + 1) * 128], identity=identf[:]))
                        k.op("act", [ptr], [yt], lambda e: e.copy(
                            out=yt[:, q * 4:(q + 1) * 4, :], in_=ptr[:, :].rearrange("p (a t) -> p a t", a=4)))
                    k.dma("sp", ysm, S["YT"][0:D_SSD, r0:r0 + 128].rearrange("(c p) t -> p c t", p=128), yt[:],
                          [yt], [S["YT_b"]])
        k.barrier()
    k.pop()


def phase_sc(k, L, lyr, W, S, C):
    onesf = C["onesf"]
    TBK = 512
    k.push()
    with contextlib.ExitStack() as es:
        sem = k.newsem()
        cw = k.sb(es, [128, 18], F32, "scw")
        nw = k.sb(es, [128, 6], F32, "scn")
        k.dma("sp", sem, cw[:], W["sc_conv_wl"][lyr], [], [cw], sync=True)
        k.dma("sp", sem, nw[:], W["sc_norm_l"][lyr], [], [nw], sync=True)
        bgr = Ring(k, es, 2, [128, TBK], F32, "bg")
        cgr = Ring(k, es, 2, [128, TBK + 2], F32, "cg")
        hxr = Ring(k, es, 2, [128, TBK + 2], F32, "hx")
        prod = k.sb(es, [128, TBK + 2], F32, "prod")
        yc = [k.sb(es, [128, TBK], F32, "yc") for _ in range(6)]
        ysq = [k.sb(es, [128, TBK], F32, "ysq") for _ in range(2)]
        rst = k.sb(es, [128, TBK], F32, "rst")
        outr = Ring(k, es, 3, [128, TBK], BF16, "sco")
        pss = k.ps(es, [128, 512], F32, "pss")
        for tb in range(L // TBK):
            t0 = tb * TBK
            lo = 1 if t0 == 0 else 0
            hi = 1 if t0 + TBK == L else 0
            for c in range(6):
                bg, s0 = bgr.next()
                cg, s1 = cgr.next()
                hx, s2 = hxr.next()
                k.dma("sp", s0, bg[:], S["SCT"][c * 128:(c + 1) * 128, t0:t0 + TBK], [S["SCT_b"]], [bg])
                for (tl, sm_, base) in ((cg, s1, 768), (hx, s2, 1536)):
                    if lo:
                        k.op("pool", [], [tl], lambda e: e.memset(tl[:, 0:1], 0.0))
                    if hi:
                        k.op("pool", [], [tl], lambda e: e.memset(tl[:, TBK + 1:TBK + 2], 0.0))
                    k.dma("sp", sm_, tl[:, lo:TBK + 2 - hi],
                          S["SCT"][base + c * 128:base + (c + 1) * 128, t0 - 1 + lo:t0 + TBK + 1 - hi],
                          [S["SCT_b"]], [tl])
                k.op("pool", [cg, hx], [prod], lambda e: e.tensor_tensor(out=prod[:], in0=cg[:], in1=hx[:], op=ALU.mult))
                y = yc[c]
                k.op("dve", [prod, cw], [y], lambda e: e.tensor_scalar(
                    out=y[:], in0=prod[:, 0:TBK], scalar1=cw[:, c * 3:c * 3 + 1], scalar2=None, op0=ALU.mult))
                for t in (1, 2):
                    k.op("dve", [prod, cw, y], [y], lambda e: e.scalar_tensor_tensor(
                        out=y[:], in0=prod[:, t:t + TBK], scalar=cw[:, c * 3 + t:c * 3 + t + 1], in1=y[:],
                        op0=ALU.mult, op1=ALU.add))
                k.op("dve", [y, bg], [y], lambda e: e.tensor_tensor(out=y[:], in0=y[:], in1=bg[:], op=ALU.mult))
                q = ysq[c % 2]
                k.op("act", [y], [q], lambda e: e.activation(out=q[:], in_=y[:], func=AF.Square))
                k.op("pe", [onesf, q], [pss], lambda e: e.matmul(pss[:, :], lhsT=onesf[:], rhs=q[:],
                                                                 start=(c == 0), stop=(c == 5)))
            k.op("dve", [pss], [rst], lambda e: e.tensor_scalar(out=rst[:], in0=pss[:], scalar1=1.0 / D_SC, scalar2=EPS,
                                                                op0=ALU.mult, op1=ALU.add))
            k.op("act", [rst], [rst], lambda e: e.activation(out=rst[:], in_=rst[:], func=AF.Sqrt))
            k.op("dve", [rst], [rst], lambda e: e.reciprocal(out=rst[:], in_=rst[:]))
            for c in range(6):
                o, so_ = outr.next()
                k.op("dve", [yc[c], nw, rst], [o], lambda e: e.scalar_tensor_tensor(
                    out=o[:], in0=yc[c][:], scalar=nw[:, c:c + 1], in1=rst[:], op0=ALU.mult, op1=ALU.mult))
                k.dma("sp", so_, S["YT"][D_SSD + c * 128:D_SSD + (c + 1) * 128, t0:t0 + TBK], o[:], [o], [S["YT_b"]])
        k.barrier()
    k.pop()


def phase_att(k, L, lyr, W, S, C):
    identb = C["identb"]
    with contextlib.ExitStack() as es:
        for g, dil in enumerate((1, 4, 16)):
            if g not in ATT_GROUPS:
                continue
            Lsub = L // dil
            nqb = Lsub // 128
            W_ = 128 * dil
            k.push()
            with contextlib.ExitStack() as es2:
                kcr = Ring(k, es2, 3, [128, 2, W_], BF16, "kc")
                qwr = Ring(k, es2, 2, [128, 2, 2, W_], BF16, "qw")
                qwr2 = [k.newsem() for _ in range(2)]
                vcr = [Ring(k, es2, 3, [128, 4, 65], BF16, "vc") for _ in range(dil)]
                ptr_ = Ring(k, es2, 3, [128, 512], BF16, "pT", dma=False)
                otr = Ring(k, es2, 3, [128, 4, 65], F32, "ot")
                pst_ = [k.ps(es2, [128, 512], F32, "pst") for _ in range(3)]
                pso = [k.ps(es2, [128, 512], F32, "pso") for _ in range(2)]
                for r in range(dil):
                    for t, _s in zip(vcr[r].tiles, vcr[r].sems):
                        k.op("pool", [], [t], lambda e: e.memset(t[:], 1.0))
                for t in kcr.tiles + qwr.tiles:
                    k.op("pool", [], [t], lambda e: e.memset(t[:], 0.0))
                cnt = {"st": 0, "o": 0}
                for hg in range(3):
                    row0 = g * 768 + hg * 256
                    kch = {}
                    vch = {}

                    def load_chunk(j):
                        p0 = 128 * j - 64
                        lo = 64 if j == 0 else 0
                        hi = 64 if j == nqb else 128
                        kt, ks = kcr.next()
                        tok_lo = (p0 + lo) * dil
                        tok_hi = (p0 + hi) * dil
                        k.dma("sp", ks, kt[:, :, lo * dil:hi * dil],
                              S["KT"][row0:row0 + 256, tok_lo:tok_hi].rearrange("(hp p) t -> p hp t", p=128),
                              [S["KT_b"]], [kt])
                        kch[j] = kt
                        vs = []
                        for r in range(dil):
                            vt, vsm = vcr[r].next()
                            rows = S["V"][tok_lo:tok_hi, row0:row0 + 256].rearrange("(i r) (h d) -> r i h d", r=dil, d=64)[r]
                            k.dma("sp", vsm, vt[lo:hi, :, 0:64], rows, [S["V_b"]], [vt])
                            vs.append(vt)
                        vch[j] = vs

                    load_chunk(0)
                    for w in range(nqb):
                        load_chunk(w + 1)
                        qs2 = qwr2[qwr.i]
                        qt, qs = qwr.next()
                        qsrc = S["QT"][row0:row0 + 256, w * W_:(w + 1) * W_].rearrange("(hp p) t -> p hp t", p=128)
                        k.dma("sp", qs, qt[0:64, :, 0, :], qsrc[0:64], [S["QT_b"]], [qt])
                        k.dma("sp", qs2, qt[64:128, :, 1, :], qsrc[64:128], [S["QT_b"]], [qt])
                        for r in range(dil):
                            po = pso[cnt["o"] % 2]
                            cnt["o"] += 1
                            for c in (0, 1):
                                if ATT_STAGE < 2:
                                    break
                                j = w + c
                                lo = 64 if j == 0 else 0
                                hi = 64 if j == nqb else 128
                                kt = kch[j]
                                vt = vch[j][r]
                                ps_ = pst_[cnt["st"] % 3]
                                cnt["st"] += 1
                                for hp in range(2):
                                    kv = kt[:, hp, :].rearrange("p (i r) -> p i r", r=dil)[:, :, r]
                                    qv = qt[:, hp, :, :].rearrange("p a (i r) -> p a i r", r=dil)[:, :, :, r]
                                    k.op("pe", [kt, qt], [ps_], lambda e: e.matmul(
                                        ps_[:, hp * 256:(hp + 1) * 256].rearrange("p (a i) -> p a i", a=2), lhsT=kv, rhs=qv,
                                        start=(hp == 0), stop=False))
                                if c == 0:
                                    mk = C["maskA0"] if j == 0 else C["maskA"]
                                else:
                                    mk = C["maskB1"] if j == nqb else C["maskB"]
                                k.op("pe", [identb, mk], [ps_], lambda e: e.matmul(
                                    ps_[:, :], lhsT=identb[:], rhs=mk[:], start=False, stop=True))
                                if ATT_STAGE < 3:
                                    continue
                                pt, _ = ptr_.next()
                                k.op("act", [ps_], [pt], lambda e: e.activation(out=pt[:, :], in_=ps_[:, :],
                                                                               func=AF.Exp, scale=0.125))
                                if ATT_STAGE < 4:
                                    continue
                                for hh in range(4):
                                    k.op("pe", [pt, vt], [po], lambda e: e.matmul(
                                        po[:, hh * 65:(hh + 1) * 65], lhsT=pt[:, hh * 128:(hh + 1) * 128],
                                        rhs=vt[:, hh, :], start=(c == 0 and hh == 0), stop=(c == 1 and hh == 3)))
                            if ATT_STAGE < 5:
                                continue
                            ot, osm = otr.next()
                            k.op("act", [po], [ot], lambda e: e.copy(out=ot[:], in_=po[:, 0:260].rearrange("p (h d) -> p h d", h=4)))
                            dst = S["O%d" % g][w * W_:(w + 1) * W_, hg * 260:(hg + 1) * 260].rearrange(
                                "(i r) (h d) -> r i h d", r=dil, d=65)[r]
                            k.dma("sp", osm, dst, ot[:], [ot], [S["O%d_b" % g]])
                k.barrier()
            k.pop()
        if ATT_STAGE < 6:
            return
        k.push()
        with contextlib.ExitStack() as es2:
            identf = C["identf"]
            sem = k.newsem()
            nw = k.sb(es2, [128, D_ATT], F32, "attn")
            k.dma("sp", sem, nw[:], W["att_norm"][lyr:lyr + 1, :].broadcast_to([128, D_ATT]), [], [nw], sync=True)
            o0 = Ring(k, es2, 2, [128, 12, 65], F32, "o0")
            o1 = Ring(k, es2, 2, [128, 12, 65], F32, "o1")
            o2 = Ring(k, es2, 2, [128, 12, 65], F32, "o2")
            rl = k.sb(es2, [128, 12], F32, "rl")
            ov = k.sb(es2, [128, 12, 64], F32, "ov")
            junk = k.sb(es2, [128, D_ATT], F32, "junk")
            ssq = k.sb(es2, [128, 1], F32, "ssq")
            rstd = k.sb(es2, [128, 1], F32, "rstd")
            ytr = Ring(k, es2, 2, [128, 6, 128], BF16, "aytr")
            ptr = [k.ps(es2, [128, 512], F32, "ptr") for _ in range(2)]
            for tt in range(L // 128):
                r0 = tt * 128
                a0, s0 = o0.next()
                a1, s1 = o1.next()
                a2, s2 = o2.next()
                k.dma("sp", s0, a0[:], S["O0"][r0:r0 + 128, :].rearrange("p (h d) -> p h d", h=12), [S["O0_b"]], [a0])
                k.dma("sp", s1, a1[:], S["O1"][r0:r0 + 128, :].rearrange("p (h d) -> p h d", h=12), [S["O1_b"]], [a1])
                k.dma("sp", s2, a2[:], S["O2"][r0:r0 + 128, :].rearrange("p (h d) -> p h d", h=12), [S["O2_b"]], [a2])
                k.op("pool", [a0, a1], [a0], lambda e: e.tensor_tensor(out=a0[:], in0=a0[:], in1=a1[:], op=ALU.add))
                k.op("dve", [a0, a2], [a0], lambda e: e.tensor_tensor(out=a0[:], in0=a0[:], in1=a2[:], op=ALU.add))
                k.op("dve", [a0], [rl], lambda e: e.reciprocal(out=rl[:], in_=a0[:, :, 64]))
                k.op("dve", [a0, rl], [ov], lambda e: e.tensor_tensor(
                    out=ov[:], in0=a0[:, :, 0:64], in1=rl[:, :].unsqueeze(2).to_broadcast([128, 12, 64]), op=ALU.mult))
                ovf = ov[:, :, :].rearrange("p h d -> p (h d)")
                k.op("act", [ov], [junk, ssq], lambda e: e.activation(out=junk[:], in_=ovf, func=AF.Square,
                                                                      accum_out=ssq[:, 0:1]))
                rmsnorm_rstd(k, ssq, rstd, D_ATT)
                k.op("dve", [ov, rstd, nw], [junk], lambda e: e.scalar_tensor_tensor(
                    out=junk[:], in0=ovf, scalar=rstd[:, 0:1], in1=nw[:], op0=ALU.mult, op1=ALU.mult))
                yt, ysm = ytr.next()
                for q in range(2):
                    pt = ptr[q]
                    n = 4 if q == 0 else 2
                    for i in range(n):
                        cc = q * 4 + i
                        k.op("pe", [junk, identf], [pt], lambda e: e.transpose(
                            out=pt[:, i * 128:(i + 1) * 128], in_=junk[:, cc * 128:(cc + 1) * 128], identity=identf[:]))
                    k.op("act", [pt], [yt], lambda e: e.copy(
                        out=yt[:, q * 4:q * 4 + n, :], in_=pt[:, 0:n * 128].rearrange("p (a t) -> p a t", a=n)))
                k.dma("sp", ysm, S["YT"][2304:3072, r0:r0 + 128].rearrange("(c p) t -> p c t", p=128), yt[:],
                      [yt], [S["YT_b"]])
            k.barrier()
        k.pop()


def phase_wout(k, L, lyr, xsrc, xsrc_b, W, S, C):
    k.push()
    with contextlib.ExitStack() as es:
        wo = k.sb(es, [128, 24, D_MODEL], BF16, "wo")
        stg = Ring(k, es, 2, [128, 2, D_MODEL], F32, "wstg")
        for c4 in range(12):
            st, ss = stg.next()
            k.dma("sp", ss, st[:], W["w_out"][lyr, c4 * 256:(c4 + 1) * 256, :].rearrange("(c p) n -> p c n", p=128), [], [st])
            eng = "pool" if c4 % 2 == 0 else "act"
            if eng == "pool":
                k.op("pool", [st], [wo], lambda e: e.tensor_copy(out=wo[:, c4 * 2:(c4 + 1) * 2, :], in_=st[:]))
            else:
                k.op("act", [st], [wo], lambda e: e.copy(out=wo[:, c4 * 2:(c4 + 1) * 2, :], in_=st[:]))
        ytr = Ring(k, es, 2, [128, 24, 128], BF16, "yT")
        xr = Ring(k, es, 2, [128, D_MODEL], F32, "xo")
        outr = Ring(k, es, 2, [128, D_MODEL], F32, "xn")
        psm = [k.ps(es, [128, 512], F32, "psm") for _ in range(4)]
        for tt in range(L // 128):
            r0 = tt * 128
            yt, ys = ytr.next()
            k.dma("sp", ys, yt[:], S["YT"][:, r0:r0 + 128].rearrange("(c p) t -> p c t", p=128), [S["YT_b"]], [yt])
            xt, xs = xr.next()
            k.dma("sp", xs, xt[:], xsrc[r0:r0 + 128, :], [xsrc_b], [xt])
            ot, osm = outr.next()
            for nb in range(4):
                pm = psm[nb]
                for c in range(24):
                    k.op("pe", [yt, wo], [pm], lambda e: e.matmul(pm[:, :], lhsT=yt[:, c, :], rhs=wo[:, c, nb * 512:(nb + 1) * 512],
                                                                   start=(c == 0), stop=(c == 23)))
                k.op("dve", [pm, xt], [ot], lambda e: e.tensor_tensor(out=ot[:, nb * 512:(nb + 1) * 512], in0=pm[:],
                                                                      in1=xt[:, nb * 512:(nb + 1) * 512], op=ALU.add))
            k.dma("sp", osm, S["XRES"][r0:r0 + 128, :], ot[:], [ot], [S["XRES_b"]])
        k.barrier()
    k.pop()


def peer_tables_bf16(k, lyr, W, S):
    k.push()
    with contextlib.ExitStack() as es:
        stg = Ring(k, es, 3, [128, 4096], F32, "tstg")
        outb = Ring(k, es, 3, [128, 4096], BF16, "tout")
        n = 0
        for name, dst in (("peer_u", "UB"), ("peer_v", "VB")):
            src = W[name][lyr * N_EXP:(lyr + 1) * N_EXP, :].rearrange("(c p two) d -> c p (two d)", p=128, two=2)
            dv = S[dst].rearrange("(c p two) d -> c p (two d)", p=128, two=2)
            for c in range(N_EXP // 256):
                st, ss = stg.next()
                ob, os_ = outb.next()
                k.dma("sp", ss, st[:], src[c], [], [st])
                e = ("act", "pool", "dve")[n % 3]
                n += 1
                if e == "act":
                    k.op("act", [st], [ob], lambda en: en.copy(out=ob[:], in_=st[:]))
                else:
                    k.op(e, [st], [ob], lambda en: en.tensor_copy(out=ob[:], in_=st[:]))
                k.dma("sp", os_, dv[c], ob[:], [ob], [S[dst + "_b"]])
        k.barrier()
    k.pop()


def phase_peer_a(k, L, lyr, W, S, C):
    identf, identb = C["identf"], C["identb"]
    NEG = -1.0e30
    k.push()
    with contextlib.ExitStack() as es:
        sem = k.newsem()
        gB = k.sb(es, [128, D_MODEL], F32, "gB")
        k.dma("sp", sem, gB[:], W["norm_ffn"][lyr:lyr + 1, :].broadcast_to([128, D_MODEL]), [], [gB], sync=True)
        wq = k.sb(es, [128, 16, D_MODEL], BF16, "wq")
        skT = k.sb(es, [128, 16, 128], BF16, "skT")
        pq = [k.ps(es, [128, 512], F32, "pq") for _ in range(4)]
        pqs = [k.ps(es, [128, 512], F32, "pqs") for _ in range(2)]
        pst = [k.ps(es, [128, 1024], BF16, "pst") for _ in range(2)]
        with contextlib.ExitStack() as es1:
            stg = Ring(k, es1, 2, [128, 2, D_MODEL], F32, "wstg")
            for c2 in range(8):
                st, ss = stg.next()
                k.dma("sp", ss, st[:], W["peer_wq"][lyr, c2 * 256:(c2 + 1) * 256, :].rearrange("(c p) n -> p c n", p=128), [], [st])
                if c2 % 2 == 0:
                    k.op("pool", [st], [wq], lambda e: e.tensor_copy(out=wq[:, c2 * 2:(c2 + 1) * 2, :], in_=st[:]))
                else:
                    k.op("act", [st], [wq], lambda e: e.copy(out=wq[:, c2 * 2:(c2 + 1) * 2, :], in_=st[:]))
            skf = k.sb(es1, [128, 16, 128], F32, "skf")
            k.dma("sp", sem, skf[:], W["peer_subkeys"][lyr].rearrange("m n d -> n m d"), [], [skf], sync=True)
            for q4 in range(4):
                pm = pq[q4]
                for i in range(4):
                    m = q4 * 4 + i
                    k.op("pe", [skf, identf], [pm], lambda e: e.transpose(out=pm[:, i * 128:(i + 1) * 128], in_=skf[:, m, :],
                                                                          identity=identf[:]))
                k.op("act", [pm], [skT], lambda e: e.copy(out=skT[:, q4 * 4:(q4 + 1) * 4, :],
                                                          in_=pm[:, :].rearrange("p (a n) -> p a n", a=4)))
            k.barrier()
        xr = Ring(k, es, 2, [128, D_MODEL], F32, "xm")
        ssq = k.sb(es, [128, 1], F32, "ssq")
        rstd = k.sb(es, [128, 1], F32, "rstd")
        hn = k.sb(es, [128, D_MODEL], F32, "hn")
        hnbr = Ring(k, es, 2, [128, D_MODEL], BF16, "hnb")
        eidr = Ring(k, es, 2, [128, 128], I32, "eido")
        gater = Ring(k, es, 2, [128, 8, 16], F32, "gateo")
        hnT = k.sb(es, [128, 16, 128], BF16, "hnT")
        qTb = k.sb(es, [128, 16, 128], BF16, "qTb")
        sc = k.sb(es, [128, 16, 128], F32, "sc")
        scw = k.sb(es, [128, 128], F32, "scw")
        sv = k.sb(es, [128, 16, 16], F32, "sv")
        si = k.sb(es, [128, 16, 16], U32, "si")
        sif = k.sb(es, [128, 16, 16], F32, "sif")
        cand = k.sb(es, [128, 8, 256], F32, "cand")
        cidx = k.sb(es, [128, 8, 256], F32, "cidx")
        cw_ = k.sb(es, [128, 256], F32, "cw_")
        top = k.sb(es, [128, 8, 16], F32, "top")
        zs = k.sb(es, [128, 8], F32, "zs")
        j256 = k.sb(es, [128, 256], F32, "j256")
        eidf = k.sb(es, [128, 128], F32, "eidf")
        eidf_w = Buf(multi=True)
        pre = k.sb(es, [128, 128], F32, "pre")
        pre_w = Buf(multi=True)
        actg = k.sb(es, [128, 128], F32, "actg")

        def stage_a(tt):
            r0 = tt * 128
            eid, eid_s = eidr.next()
            gate, gate_s = gater.next()
            hnb, hnb_s = hnbr.next()
            xm, xs = xr.next()
            k.dma("sp", xs, xm[:], S["XRES"][r0:r0 + 128, :], [S["XRES_b"]], [xm])
            k.op("act", [xm], [hnb, ssq], lambda e: e.activation(out=hnb[:], in_=xm[:], func=AF.Square,
                                                                   accum_out=ssq[:, 0:1]))
            rmsnorm_rstd(k, ssq, rstd, D_MODEL)
            k.op("dve", [xm, rstd, gB], [hn], lambda e: e.scalar_tensor_tensor(
                out=hn[:], in0=xm[:], scalar=rstd[:, 0:1], in1=gB[:], op0=ALU.mult, op1=ALU.mult))
            k.op("act", [hn], [hnb], lambda e: e.copy(out=hnb[:], in_=hn[:]))
            for half in range(2):
                pt = pst[half]
                for j in range(8):
                    kc = half * 8 + j
                    k.op("pe", [hnb, identb], [pt], lambda e: e.transpose(
                        out=pt[:, j * 128:(j + 1) * 128], in_=hnb[:, kc * 128:(kc + 1) * 128], identity=identb[:]))
                k.op("act", [pt], [hnT], lambda e: e.copy(out=hnT[:, half * 8:(half + 1) * 8, :],
                                                          in_=pt[:, :].rearrange("p (j t) -> p j t", j=8)))
            for q4 in range(4):
                pm = pqs[q4 % 2]
                for i in range(4):
                    m = q4 * 4 + i
                    for kc in range(16):
                        k.op("pe", [wq, hnT], [pm], lambda e: e.matmul(
                            pm[:, i * 128:(i + 1) * 128], lhsT=wq[:, kc, m * 128:(m + 1) * 128], rhs=hnT[:, kc, :],
                            start=(kc == 0), stop=(kc == 15)))
                k.op("act", [pm], [qTb], lambda e: e.copy(out=qTb[:, q4 * 4:(q4 + 1) * 4, :],
                                                          in_=pm[:, :].rearrange("p (a t) -> p a t", a=4)))
            for q4 in range(4):
                pm = pqs[q4 % 2]
                for i in range(4):
                    m = q4 * 4 + i
                    k.op("pe", [qTb, skT], [pm], lambda e: e.matmul(
                        pm[:, i * 128:(i + 1) * 128], lhsT=qTb[:, m, :], rhs=skT[:, m, :], start=True, stop=True))
                k.op("act", [pm], [sc], lambda e: e.copy(out=sc[:, q4 * 4:(q4 + 1) * 4, :],
                                                         in_=pm[:, :].rearrange("p (a n) -> p a n", a=4)))
            for m in range(16):
                k.op("dve", [sc], [sv], lambda e: e.max(out=sv[:, m, 0:8], in_=sc[:, m, :]))
                k.op("dve", [sc, sv], [scw], lambda e: e.match_replace(out=scw[:], in_to_replace=sv[:, m, 0:8],
                                                                       in_values=sc[:, m, :], imm_value=NEG))
                k.op("dve", [scw], [sv], lambda e: e.max(out=sv[:, m, 8:16], in_=scw[:]))
                k.op("dve", [sc, sv], [si], lambda e: e.max_index(out=si[:, m, 0:8], in_max=sv[:, m, 0:8], in_values=sc[:, m, :]))
                k.op("dve", [sc, sv], [si], lambda e: e.max_index(out=si[:, m, 8:16], in_max=sv[:, m, 8:16], in_values=sc[:, m, :]))
            k.op("dve", [si], [sif], lambda e: e.tensor_copy(out=sif[:], in_=si[:]))
            svv = sv[:, :, :].rearrange("p (h two) a -> p h two a", two=2)
            sfv = sif[:, :, :].rearrange("p (h two) a -> p h two a", two=2)
            c4 = cand[:, :, :].rearrange("p h (a b) -> p h a b", a=16)
            x4 = cidx[:, :, :].rearrange("p h (a b) -> p h a b", a=16)
            k.op("dve", [sv], [cand], lambda e: e.tensor_tensor(
                out=c4, in0=svv[:, :, 0, :].unsqueeze(3).to_broadcast([128, 8, 16, 16]),
                in1=svv[:, :, 1, :].unsqueeze(2).to_broadcast([128, 8, 16, 16]), op=ALU.add))
            k.op("dve", [sif], [sif], lambda e: e.tensor_scalar(out=sfv[:, :, 0, :], in0=sfv[:, :, 0, :], scalar1=128.0,
                                                                scalar2=None, op0=ALU.mult))
            k.op("dve", [sif], [cidx], lambda e: e.tensor_tensor(
                out=x4, in0=sfv[:, :, 0, :].unsqueeze(3).to_broadcast([128, 8, 16, 16]),
                in1=sfv[:, :, 1, :].unsqueeze(2).to_broadcast([128, 8, 16, 16]), op=ALU.add))
            for h in range(8):
                k.op("dve", [cand], [top], lambda e: e.max(out=top[:, h, 0:8], in_=cand[:, h, :]))
                k.op("dve", [cand, top], [cw_], lambda e: e.match_replace(out=cw_[:], in_to_replace=top[:, h, 0:8],
                                                                          in_values=cand[:, h, :], imm_value=NEG))
                k.op("dve", [cw_], [top], lambda e: e.max(out=top[:, h, 8:16], in_=cw_[:]))
            k.op("dve", [top], [gate], lambda e: e.tensor_tensor(
                out=gate[:], in0=top[:], in1=top[:, :, 0:1].to_broadcast([128, 8, 16]), op=ALU.subtract))
            k.op("act", [gate], [gate], lambda e: e.activation(out=gate[:], in_=gate[:], func=AF.Exp))
            k.op("dve", [gate], [zs], lambda e: e.tensor_reduce(out=zs[:], in_=gate[:], axis=AX.X, op=ALU.add))
            k.op("dve", [zs], [zs], lambda e: e.reciprocal(out=zs[:], in_=zs[:]))
            k.op("dve", [gate, zs], [gate], lambda e: e.tensor_tensor(
                out=gate[:], in0=gate[:], in1=zs[:, :].unsqueeze(2).to_broadcast([128, 8, 16]), op=ALU.mult))
            k.op("dve", [], [eidf], lambda e: e.memset(eidf[:], 0.0))
            for h in range(8):
                for kk in range(16):
                    k.op("dve", [cand, top, cidx, eidf], [j256, eidf_w], lambda e: e.scalar_tensor_tensor(
                        out=j256[:], in0=cand[:, h, :], scalar=top[:, h, kk:kk + 1], in1=cidx[:, h, :],
                        op0=ALU.is_equal, op1=ALU.mult, accum_out=eidf[:, h * 16 + kk:h * 16 + kk + 1]))
            k.op("dve", [eidf, eidf_w], [eidf], lambda e: e.tensor_scalar(out=eidf[:], in0=eidf[:], scalar1=0.0,
                                                                          scalar2=float(N_EXP - 1), op0=ALU.max, op1=ALU.min))
            k.op("dve", [eidf], [eid], lambda e: e.tensor_copy(out=eid[:], in_=eidf[:]))

            k.dma("sp", hnb_s, S["HNB"][r0:r0 + 128, :], hnb[:], [hnb], [S["HNB_b"]])
            k.dma("sp", eid_s, S["EID"][r0:r0 + 128, :], eid[:], [eid], [S["EID_b"]])
            k.dma("sp", gate_s, S["GATE"][r0:r0 + 128, :], gate[:, :, :].rearrange("p h a -> p (h a)"), [gate], [S["GATE_b"]])

        for tt in range(L // 128):
            stage_a(tt)
        k.barrier()
    k.pop()


def phase_peer_b(k, L, lyr, W, S, C):
    identb = C["identb"]
    NT = L // 128
    k.push()
    with contextlib.ExitStack() as es:
        hnbr = Ring(k, es, 3, [128, D_MODEL], BF16, "hnbi")
        eidr = Ring(k, es, 3, [128, 128], I32, "eidi")
        gater = Ring(k, es, 3, [128, 128], F32, "gatei")
        xmr = Ring(k, es, 3, [128, D_MODEL], F32, "xmi")
        ugr = Ring(k, es, 16, [128, D_MODEL], BF16, "ug", sw=True)
        vgr = Ring(k, es, 10, [128, D_MODEL], BF16, "vg", sw=True)
        dg = k.sb(es, [128, 128, 128], BF16, "dg")
        dgb = [Buf() for _ in range(8)]
        pres = [k.sb(es, [128, 128], F32, "pre") for _ in range(2)]
        pre_ws = [Buf(multi=True) for _ in range(2)]
        actg = k.sb(es, [128, 128], F32, "actg")
        outr = Ring(k, es, 2, [128, D_MODEL], F32, "xn")
        pq = [k.ps(es, [128, 512], F32, "pq") for _ in range(4)]
        utab = S["UB"]
        vtab = S["VB"]
        st = {}

        def load(tt):
            r0 = tt * 128
            hnb, s1 = hnbr.next()
            eid, s2 = eidr.next()
            gate, s3 = gater.next()
            xm, s4 = xmr.next()
            k.dma("sp", s1, hnb[:], S["HNB"][r0:r0 + 128, :], [S["HNB_b"]], [hnb])
            k.dma("sp", s2, eid[:], S["EID"][r0:r0 + 128, :], [S["EID_b"]], [eid])
            k.dma("sp", s3, gate[:], S["GATE"][r0:r0 + 128, :], [S["GATE_b"]], [gate])
            k.dma("sp", s4, xm[:], S["XRES"][r0:r0 + 128, :], [S["XRES_b"]], [xm])
            st[tt] = (hnb, eid, gate, xm)

        def u_start(tt):
            pre = pres[tt % 2]
            k.op("dve", [], [pre], lambda e: e.memset(pre[:], 0.0))

        def u_step(tt, hk):
            hnb, eid, gate, xm = st[tt]
            pre, pre_w = pres[tt % 2], pre_ws[tt % 2]
            gt, gs = ugr.next()
            k.dma("pool", gs, gt[:], utab, [eid, S["UB_b"]], [gt], indirect=bass.IndirectOffsetOnAxis(ap=eid[:, hk:hk + 1], axis=0))
            k.op("dve", [gt, hnb, pre], [gt, pre_w], lambda e: e.scalar_tensor_tensor(
                out=gt[:], in0=gt[:], scalar=1.0, in1=hnb[:], op0=ALU.mult, op1=ALU.mult,
                accum_out=pre[:, hk:hk + 1]))

        def u_finish(tt):
            hnb, eid, gate, xm = st[tt]
            pre, pre_w = pres[tt % 2], pre_ws[tt % 2]
            k.op("act", [pre, pre_w], [actg], lambda e: e.activation(out=actg[:], in_=pre[:], func=AF.Gelu))
            k.op("dve", [actg, gate], [actg], lambda e: e.tensor_tensor(out=actg[:], in0=actg[:], in1=gate[:], op=ALU.mult))
            for hk in range(128):
                k.op("act", [actg, identb], [dgb[hk // 16]], lambda e: e.activation(
                    out=dg[:, hk, :], in_=identb[:, :], func=AF.Copy, scale=actg[:, hk:hk + 1]))

        def v_step(tt, hk):
            hnb, eid, gate, xm = st[tt]
            gt, gs = vgr.next()
            k.dma("pool", gs, gt[:], vtab, [eid, S["VB_b"]], [gt], indirect=bass.IndirectOffsetOnAxis(ap=eid[:, hk:hk + 1], axis=0))
            for nb in range(4):
                k.op("pe", [dgb[hk // 16], gt], [pq[nb]], lambda e: e.matmul(
                    pq[nb][:, :], lhsT=dg[:, hk, :], rhs=gt[:, nb * 512:(nb + 1) * 512],
                    start=(hk == 0), stop=(hk == 127)))

        def v_finish(tt):
            hnb, eid, gate, xm = st.pop(tt)
            r0 = tt * 128
            ot, osm = outr.next()
            for nb in range(4):
                k.op("dve", [pq[nb], xm], [ot], lambda e: e.tensor_tensor(
                    out=ot[:, nb * 512:(nb + 1) * 512], in0=pq[nb][:], in1=xm[:, nb * 512:(nb + 1) * 512], op=ALU.add))
            k.dma("sp", osm, S["XRES"][r0:r0 + 128, :], ot[:], [ot], [S["XRES_b"]])

        load(0)
        if NT > 1:
            load(1)
        u_start(0)
        for hk in range(128):
            u_step(0, hk)
        u_finish(0)
        for tt in range(NT):
            if tt + 2 < NT:
                load(tt + 2)
            if tt + 1 < NT:
                u_start(tt + 1)
            for hk in range(128):
                if tt + 1 < NT:
                    u_step(tt + 1, hk)
                v_step(tt, hk)
            v_finish(tt)
            if tt + 1 < NT:
                u_finish(tt + 1)
        k.barrier()
    k.pop()


def phase_peer(k, L, lyr, W, S, C):
    peer_tables_bf16(k, lyr, W, S)
    phase_peer_a(k, L, lyr, W, S, C)
    phase_peer_b(k, L, lyr, W, S, C)


def rope_consts(L):
    inv = (500000.0 ** (-np.arange(0, 16, 2, dtype=np.float32) / 16)).astype(np.float32)
    ang = np.arange(L, dtype=np.float32)[:, None] * inv[None, :]
    cosT = np.ones((128, L), np.float32)
    sinT = np.zeros((128, L), np.float32)
    for p in range(128):
        d = p % 64
        if d < 16:
            cosT[p] = np.cos(ang[:, d % 8])
            sinT[p] = np.sin(ang[:, d % 8])
    rotT = np.zeros((128, 128), np.float32)
    for m in range(128):
        d = m % 64
        if d < 8:
            rotT[m + 8, m] = -1.0
        elif d < 16:
            rotT[m - 8, m] = 1.0
    return cosT, sinT, rotT


WEIGHT_SHAPES = {
    "norm_mix": (DEPTH, D_MODEL), "w_in": (DEPTH, D_MODEL, D_IN), "ssd_dt_bias": (DEPTH, 48), "ssd_a_log": (DEPTH, 48),
    "ssd_d": (DEPTH, SSD_HEADS), "ssd_norm": (DEPTH, D_SSD),
    "ssd_conv_wl": (DEPTH, 128, 100), "ssd_conv_bl": (DEPTH, 128, 20), "sc_conv_wl": (DEPTH, 128, 18),
    "sc_norm_l": (DEPTH, 128, 6), "att_norm": (DEPTH, D_ATT), "w_out": (DEPTH, D_MIX, D_MODEL),
    "norm_ffn": (DEPTH, D_MODEL), "peer_wq": (DEPTH, D_MODEL, D_MODEL), "peer_subkeys": (DEPTH, 16, 128, 128),
    "peer_u": (DEPTH * N_EXP, D_MODEL), "peer_v": (DEPTH * N_EXP, D_MODEL), "norm_final": (1, D_MODEL),
}


class LazyW(dict):
    def __init__(self, nc):
        super().__init__()
        self.nc = nc

    def __missing__(self, n):
        v = self.nc.dram_tensor(n, list(WEIGHT_SHAPES[n]), F32, kind="ExternalInput").ap()
        self[n] = v
        return v


def build(L, phases=ALL_PHASES, debug=(), nlayers=DEPTH, feed=()):
    nc = bass.Bass("TRN2", target_bir_lowering=False)
    x = nc.dram_tensor("x", [L, D_MODEL], F32, kind="ExternalInput").ap()
    y = nc.dram_tensor("y", [L, D_MODEL], F32, kind="ExternalOutput").ap()
    W = LazyW(nc)
    S = {}

    def scratch(name, shape, dt):
        kind = "ExternalOutput" if name in debug else ("ExternalInput" if name in feed else "Internal")
        S[name] = nc.dram_tensor("s_" + name, list(shape), dt, kind=kind).ap()
        S[name + "_b"] = Buf(multi=True)

    scratch("Z", [L, D_SSD], F32)
    scratch("XBCT", [XBC, L], F32)
    scratch("DT", [L, 48], F32)
    scratch("SCT", [3 * D_SC, L], F32)
    scratch("QT", [2304, L], BF16)
    scratch("KT", [2304, L], BF16)
    scratch("V", [L, 2304], BF16)
    scratch("XRES", [L, D_MODEL], F32)
    scratch("X", [L, D_SSD], F32)
    scratch("B", [L, 512], BF16)
    scratch("BT", [512, L], BF16)
    scratch("CT", [512, L], BF16)
    scratch("YB", [L, D_SSD], F32)
    scratch("YT", [D_MIX, L], BF16)
    scratch("O0", [L, 780], F32)
    scratch("O1", [L, 780], F32)
    scratch("O2", [L, 780], F32)
    scratch("UB", [N_EXP, D_MODEL], BF16)
    scratch("VB", [N_EXP, D_MODEL], BF16)
    scratch("HNB", [L, D_MODEL], BF16)
    scratch("EID", [L, 128], I32)
    scratch("GATE", [L, 128], F32)
    x_b = Buf(multi=True)
    y_b = Buf(multi=True)
    with contextlib.ExitStack() as es:
        k = KB(nc, es)
        C = {}
        csem = k.newsem()

        def cload(name, shape=(128, 128), dt=F32):
            d = nc.dram_tensor(name, list(shape), F32, kind="ExternalInput").ap()
            if dt == BF16:
                tb = k.sb(es, list(shape), BF16, name + "b")
                t = k.sb(es_stage, list(shape), F32, name)
                k.dma("sp", csem, t[:], d[:, :], [], [t], sync=True)
                k.op("dve", [t], [tb], lambda e: e.tensor_copy(out=tb[:], in_=t[:]))
                return tb
            t = k.sb(es, list(shape), F32, name)
            k.dma("sp", csem, t[:], d[:, :], [], [t], sync=True)
            return t

        C["identf"] = cload("ident")
        C["rotT"] = cload("rotT")
        C["onesf"] = cload("onesf")
        for n in ("U", "Lo", "nU", "nLo", "nmf", "nmb"):
            C[n] = cload(n)
        onec = k.sb(es, [128, 1], F32, "onec")
        k.op("dve", [], [onec], lambda e: e.memset(onec[:], 1.0))
        C["onec"] = onec
        C["identb"] = k.sb(es, [128, 128], BF16, "identb")
        k.op("dve", [C["identf"]], [C["identb"]], lambda e: e.tensor_copy(out=C["identb"][:], in_=C["identf"][:]))
        bnames = ("maskA", "maskB", "maskA0", "maskB1")
        tbs = {n: k.sb(es, [128, 512], BF16, n + "b") for n in bnames}
        with contextlib.ExitStack() as es_stage:
            for n in bnames:
                d = nc.dram_tensor(n, [128, 512], F32, kind="ExternalInput").ap()
                t = k.sb(es_stage, [128, 512], F32, n)
                k.dma("sp", csem, t[:], d[:, :], [], [t], sync=True)
                k.op("dve", [t], [tbs[n]], lambda e: e.tensor_copy(out=tbs[n][:], in_=t[:]))
                C[n] = tbs[n]
            k.barrier()
        C["cosT"] = nc.dram_tensor("cosT", [128, L], F32, kind="ExternalInput").ap()
        C["sinT"] = nc.dram_tensor("sinT", [128, L], F32, kind="ExternalInput").ap()
        k.barrier()
        xsrc, xsrc_b = x, x_b
        for lyr in range(nlayers):
            if "a" in phases:
                phase_a(k, L, lyr, xsrc, xsrc_b, W, S, C)
            if "conv" in phases:
                phase_conv(k, L, lyr, W, S, C)
            if "ssd" in phases:
                phase_ssd(k, L, lyr, W, S, C)
            if "sc" in phases:
                phase_sc(k, L, lyr, W, S, C)
            if "att" in phases:
                phase_att(k, L, lyr, W, S, C)
            if "wout" in phases:
                phase_wout(k, L, lyr, xsrc, xsrc_b, W, S, C)
            if "peer" in phases:
                phase_peer(k, L, lyr, W, S, C)
            if "wout" in phases:
                xsrc, xsrc_b = S["XRES"], S["XRES_b"]
        if "final" in phases:
            phase_final(k, L, xsrc, xsrc_b, W, y, y_b)
        k.barrier()
    return nc


def host_consts(L):
    cosT, sinT, rotT = rope_consts(L)
    i = np.arange(128)
    U = (i[:, None] <= i[None, :]).astype(np.float32)
    Lo = (i[:, None] >= i[None, :]).astype(np.float32)
    NEGM = -30000.0
    nmf = np.where(i[None, :] >= i[:, None], 0.0, NEGM).astype(np.float32)
    nmb = np.where(i[None, :] <= i[:, None], 0.0, NEGM).astype(np.float32)
    mA = np.where(i[:, None] >= i[None, :], 0.0, NEGM).astype(np.float32)
    mB = np.where(i[:, None] <= i[None, :], 0.0, NEGM).astype(np.float32)
    eye = np.eye(128, dtype=np.float32)
    mA0 = mA.copy()
    mA0[0:64, :] = NEGM
    mB1 = mB.copy()
    mB1[64:128, :] = NEGM
    return {"maskA0": np.ascontiguousarray(np.tile(mA0, (1, 4))), "maskB1": np.ascontiguousarray(np.tile(mB1, (1, 4))),"cosT": cosT, "sinT": sinT, "rotT": rotT, "ident": eye,
            "onesf": np.ones((128, 128), np.float32), "U": U, "Lo": Lo, "nU": -U, "nLo": -Lo, "nmf": nmf, "nmb": nmb,
            "maskA": np.ascontiguousarray(np.tile(mA, (1, 4))), "maskB": np.ascontiguousarray(np.tile(mB, (1, 4)))}


def prep_weights(inp):
    w = {}
    f = lambda a: np.asarray(a, dtype=np.float32)
    for n, s in WEIGHT_SHAPES.items():
        if n in inp:
            w[n] = np.ascontiguousarray(f(inp[n]).reshape(s))
    if "ssd_conv_w" in inp:
        cw = f(inp["ssd_conv_w"]).reshape(DEPTH, 5, 20, 128)
        w["ssd_conv_wl"] = np.ascontiguousarray(cw.transpose(0, 3, 2, 1).reshape(DEPTH, 128, 100))
        cb = f(inp["ssd_conv_b"]).reshape(DEPTH, 20, 128)
        w["ssd_conv_bl"] = np.ascontiguousarray(cb.transpose(0, 2, 1))
    if "sc_conv_w" in inp:
        sw = f(inp["sc_conv_w"]).reshape(DEPTH, 3, 6, 128)
        w["sc_conv_wl"] = np.ascontiguousarray(sw.transpose(0, 3, 2, 1).reshape(DEPTH, 128, 18))
        sn = f(inp["sc_norm"]).reshape(DEPTH, 6, 128)
        w["sc_norm_l"] = np.ascontiguousarray(sn.transpose(0, 2, 1))
    return w


def kernel(**inputs):
    L = 8192
    xs = [np.asarray(inputs["x_prompt"][i]) for i in range(2)] + [np.asarray(inputs["x_sample"][i]) for i in range(4)]
    xs = xs + [xs[0], xs[1]]
    w = prep_weights(inputs)
    consts = host_consts(L)
    nc = build(L)
    in_maps = []
    for c in range(8):
        m = {"x": np.ascontiguousarray(xs[c], dtype=np.float32)}
        m.update(w)
        m.update(consts)
        in_maps.append(m)
    res = run_bass_kernel_spmd(nc, in_maps, core_ids=list(range(8)))
    ys = [res.results[c]["y"] for c in range(6)]
    return (np.stack(ys[0:2], axis=0).astype(np.float32), np.stack(ys[2:6], axis=0).astype(np.float32))
```

```python
import contextlib
import numpy as np
import ml_dtypes
import concourse.bass as bass
import concourse.mybir as mybir
from concourse.bass_utils import run_bass_kernel_spmd

F32 = mybir.dt.float32
BF16 = mybir.dt.bfloat16
I32 = mybir.dt.int32
U32 = mybir.dt.uint32
AF = mybir.ActivationFunctionType
ALU = mybir.AluOpType
AX = mybir.AxisListType

D_MODEL = 2048
DEPTH = 2
D_SSD = 1536
SSD_HEADS = 24
XBC = 2560
D_SC = 768
D_ATT = 768
D_MIX = 3072
D_IN = 13360
EPS = 1e-6
C_Z = 0
C_XBC = 1536
C_DT = 4096
C_SC = 4144
C_ATT = 6448
N_EXP = 16384
ATT_GROUPS = (0, 1, 2)
ATT_STAGE = 9
PEER_DBG = ""
STORES_ON_ACT = True
ALL_PHASES = ("a", "conv", "ssd", "sc", "att", "wout", "peer", "final")


class Buf:
    __slots__ = ("w", "r", "multi")

    def __init__(self, multi=False):
        self.w = {}
        self.r = {}
        self.multi = multi


class T:
    def __init__(self, t, b=None):
        self.t = t
        self.b = b if b is not None else Buf()

    def __getitem__(self, idx):
        return self.t[idx]


class KB:
    def __init__(self, nc, es):
        self.nc = nc
        self.es = es
        self.eng = {"pe": nc.tensor, "act": nc.scalar, "dve": nc.vector, "pool": nc.gpsimd, "sp": nc.sync}
        self.sems = {}
        self.cnt = {}
        for e in ("pe", "act", "dve", "pool"):
            self.sems[e] = es.enter_context(nc.semaphore("c_" + e))
            self.cnt[e] = 0
        self.waited = {e: {} for e in self.eng}
        self.nd = 0
        self.uid = 0
        self.free = []
        self.free_sw = []
        self.scopes = []

    def newsem(self, sw=False):
        fl = self.free_sw if sw else self.free
        if fl:
            key = fl.pop()
        else:
            key = ("dw%d" if sw else "d%d") % self.nd
            self.nd += 1
            self.sems[key] = self.es.enter_context(self.nc.semaphore(key))
            self.cnt[key] = 0
        for sc in self.scopes:
            sc.append(key)
        return key

    def push(self):
        self.scopes.append([])

    def pop(self):
        sc = self.scopes.pop()
        for key in sc:
            fl = self.free_sw if key.startswith("dw") else self.free
            if key not in fl:
                fl.append(key)

    def _deps(self, e, reads, writes):
        need = {}
        for b in reads:
            for s, v in b.w.items():
                if v > need.get(s, 0):
                    need[s] = v
        for b in writes:
            if not b.multi:
                for s, v in b.w.items():
                    if v > need.get(s, 0):
                        need[s] = v
            for s, v in b.r.items():
                if v > need.get(s, 0):
                    need[s] = v
        wd = self.waited[e]
        for s, v in need.items():
            if s == e and e == "pe":
                continue
            if wd.get(s, 0) >= v:
                continue
            if s[0] == "d":
                v = self.cnt[s]
            self.eng[e].wait_ge(self.sems[s], v)
            wd[s] = v

    def _rec(self, key, val, reads, writes):
        for b in reads:
            b.r[key] = val
        for b in writes:
            if b.multi:
                b.w[key] = val
            else:
                b.w = {key: val}
            b.r = {}

    def op(self, e, reads, writes, fn):
        reads = [x.b if isinstance(x, T) else x for x in reads]
        writes = [x.b if isinstance(x, T) else x for x in writes]
        self._deps(e, reads, writes)
        ins = fn(self.eng[e])
        self.cnt[e] += 1
        ins.then_inc(self.sems[e], 1)
        self._rec(e, self.cnt[e], reads, writes)
        return ins

    def dma(self, q, sem, out, in_, reads, writes, indirect=None, sync=False, **kw):
        reads = [x.b if isinstance(x, T) else x for x in reads]
        writes = [x.b if isinstance(x, T) else x for x in writes]
        if q == "sp" and STORES_ON_ACT and any(b.multi for b in writes):
            q = "act"
        self._deps(q, reads, writes)
        if indirect is not None:
            ins = self.eng[q].indirect_dma_start(out=out, out_offset=None, in_=in_, in_offset=indirect, **kw)
        else:
            ins = self.eng[q].dma_start(out=out, in_=in_, **kw)
        self.cnt[sem] += 16
        ins.then_inc(self.sems[sem], 16)
        self._rec(sem, self.cnt[sem], reads, writes)
        if sync:
            self.eng[q].wait_ge(self.sems[sem], self.cnt[sem])
            self.waited[q][sem] = self.cnt[sem]
        return ins

    def barrier(self):
        for e in self.eng:
            wd = self.waited[e]
            for s, v in self.cnt.items():
                if s == e or v == 0 or wd.get(s, 0) >= v:
                    continue
                self.eng[e].wait_ge(self.sems[s], v)
                wd[s] = v

    def sb(self, es, shape, dt, name=None):
        self.uid += 1
        t = es.enter_context(self.nc.sbuf_tensor("%s_%d" % (name or "t", self.uid), list(shape), dt))
        return T(t)

    def ps(self, es, shape, dt, name=None):
        self.uid += 1
        t = es.enter_context(self.nc.psum_tensor("%s_%d" % (name or "p", self.uid), list(shape), dt))
        return T(t)


class Ring:
    def __init__(self, k, es, n, shape, dt, name, dma=True, sw=False):
        self.tiles = [k.sb(es, shape, dt, name) for _ in range(n)]
        self.sems = [k.newsem(sw) for _ in range(n)] if dma else [None] * n
        self.i = 0

    def next(self):
        t, s = self.tiles[self.i], self.sems[self.i]
        self.i = (self.i + 1) % len(self.tiles)
        return t, s


def rmsnorm_rstd(k, ssq, rstd, n):
    k.op("dve", [ssq], [rstd], lambda e: e.tensor_scalar(out=rstd[:, 0:1], in0=ssq[:, 0:1], scalar1=1.0 / n,
                                                         scalar2=EPS, op0=ALU.mult, op1=ALU.add))
    k.op("act", [rstd], [rstd], lambda e: e.activation(out=rstd[:, 0:1], in_=rstd[:, 0:1], func=AF.Sqrt))
    k.op("dve", [rstd], [rstd], lambda e: e.reciprocal(out=rstd[:, 0:1], in_=rstd[:, 0:1]))


def phase_a(k, L, lyr, xsrc, xsrc_b, W, S, C):
    nc = k.nc
    TB = 1024 if L >= 1024 else L
    NT = TB // 128
    k.push()
    with contextlib.ExitStack() as es:
        gB = k.sb(es, [128, D_MODEL], F32, "gB")
        gsem = k.newsem()
        k.dma("sp", gsem, gB[:], W["norm_mix"][lyr:lyr + 1, :].broadcast_to([128, D_MODEL]), [], [gB], sync=True)
        xr = Ring(k, es, 2, [128, D_MODEL], F32, "xt")
        junk = k.sb(es, [128, D_MODEL], BF16, "junk")
        hb = [k.sb(es, [128, D_MODEL], BF16, "hb") for _ in range(2)]
        ssq = [k.sb(es, [128, 1], F32, "ssq") for _ in range(2)]
        rstd = [k.sb(es, [128, 1], F32, "rstd") for _ in range(2)]
        hT = k.sb(es, [128, 16, TB], BF16, "hT")
        hTb = [Buf() for _ in range(NT)]
        wf = Ring(k, es, 2, [128, 16, 512], F32, "wf")
        wb = [k.sb(es, [128, 16, 512], BF16, "wb") for _ in range(2)]
        ev = Ring(k, es, 3, [128, 512], F32, "ev")
        evb = Ring(k, es, 3, [128, 512], BF16, "evb")
        cs = Ring(k, es, 2, [128, 512], F32, "cos")
        sn = Ring(k, es, 2, [128, 512], F32, "sin")
        qsb = [k.sb(es, [128, 512], F32, "qsb") for _ in range(2)]
        t1 = [k.sb(es, [128, 512], F32, "t1") for _ in range(2)]
        t2 = [k.sb(es, [128, 512], F32, "t2") for _ in range(2)]
        pst = [k.ps(es, [128, 1024], BF16, "pst") for _ in range(2)]
        psm = [k.ps(es, [128, 512], F32, "psm") for _ in range(4)]
        psr = [k.ps(es, [128, 512], F32, "psr") for _ in range(2)]
        identb = C["identb"]
        rotT = C["rotT"]
        wi = 0
        mi = 0
        ri = 0
        segs = [("z", C_Z, 1536, 512), ("xbc", C_XBC, 2560, 512), ("dt", C_DT, 48, 48),
                ("sc", C_SC, 2304, 384), ("q", C_ATT, 2304, 384), ("k", C_ATT + 2304, 2304, 384),
                ("v", C_ATT + 4608, 2304, 384)]
        for sbk in range(L // TB):
            tok0 = sbk * TB
            for tt in range(NT):
                xt, xs = xr.next()
                k.dma("sp", xs, xt[:], xsrc[tok0 + tt * 128: tok0 + (tt + 1) * 128, :], [xsrc_b], [xt])
                p = tt % 2
                k.op("act", [xt], [junk, ssq[p]], lambda e: e.activation(out=junk[:], in_=xt[:], func=AF.Square,
                                                                          accum_out=ssq[p][:, 0:1]))
                rmsnorm_rstd(k, ssq[p], rstd[p], D_MODEL)
                k.op("dve", [xt, rstd[p], gB], [hb[p]], lambda e: e.scalar_tensor_tensor(
                    out=hb[p][:], in0=xt[:], scalar=rstd[p][:, 0:1], in1=gB[:], op0=ALU.mult, op1=ALU.mult))
                for half in range(2):
                    pt = pst[half]
                    for j in range(8):
                        kc = half * 8 + j
                        k.op("pe", [hb[p], identb], [pt], lambda e: e.transpose(
                            out=pt[:, j * 128:(j + 1) * 128], in_=hb[p][:, kc * 128:(kc + 1) * 128],
                            identity=identb[:]))
                    eng = "act" if half == 0 else "dve"
                    src = pt[:, :].rearrange("p (j t) -> p j t", j=8)
                    dst = hT[:, half * 8:(half + 1) * 8, tt * 128:(tt + 1) * 128]
                    if eng == "act":
                        k.op("act", [pt], [hTb[tt]], lambda e: e.copy(out=dst, in_=src))
                    else:
                        k.op("dve", [pt], [hTb[tt]], lambda e: e.tensor_copy(out=dst, in_=src))
            for kind, c0, ncs, bw in segs:
                for blk in range(ncs // bw):
                    cb = c0 + blk * bw
                    wft, wfs = wf.next()
                    wbt = wb[wi % 2]
                    wi += 1
                    k.dma("sp", wfs, wft[:, :, 0:bw],
                          W["w_in"][lyr, :, cb:cb + bw].rearrange("(kc p) c -> p kc c", p=128), [], [wft])
                    k.op("pool", [wft], [wbt], lambda e: e.tensor_copy(out=wbt[:, :, 0:bw], in_=wft[:, :, 0:bw]))
                    if kind in ("z", "dt", "v"):
                        for tt in range(NT):
                            pm = psm[mi % 4]
                            mi += 1
                            for kc in range(16):
                                k.op("pe", [hTb[tt], wbt], [pm], lambda e: e.matmul(
                                    pm[:, 0:bw], lhsT=hT[:, kc, tt * 128:(tt + 1) * 128], rhs=wbt[:, kc, 0:bw],
                                    start=(kc == 0), stop=(kc == 15)))
                            r0 = tok0 + tt * 128
                            if kind == "v":
                                et, esm = evb.next()
                                k.op("act", [pm], [et], lambda e: e.copy(out=et[:, 0:bw], in_=pm[:, 0:bw]))
                                k.dma("sp", esm, S["V"][r0:r0 + 128, blk * bw:(blk + 1) * bw], et[:, 0:bw],
                                      [et], [S["V_b"]])
                            else:
                                et, esm = ev.next()
                                k.op("act", [pm], [et], lambda e: e.copy(out=et[:, 0:bw], in_=pm[:, 0:bw]))
                                dst = S["Z"] if kind == "z" else S["DT"]
                                dstb = S["Z_b"] if kind == "z" else S["DT_b"]
                                k.dma("sp", esm, dst[r0:r0 + 128, blk * bw:(blk + 1) * bw], et[:, 0:bw],
                                      [et], [dstb])
                    else:
                        for tb in range(TB // 512):
                            t0 = tok0 + tb * 512
                            hdeps = [hTb[tb * 4 + i] for i in range(4)]
                            if kind in ("q", "k"):
                                ct, csm = cs.next()
                                st, ssm = sn.next()
                                k.dma("sp", csm, ct[:], C["cosT"][:, t0:t0 + 512], [], [ct])
                                k.dma("sp", ssm, st[:], C["sinT"][:, t0:t0 + 512], [], [st])
                            for ch in range(bw // 128):
                                pm = psm[mi % 4]
                                mi += 1
                                for kc in range(16):
                                    k.op("pe", hdeps + [wbt], [pm], lambda e: e.matmul(
                                        pm[:, :], lhsT=wbt[:, kc, ch * 128:(ch + 1) * 128],
                                        rhs=hT[:, kc, tb * 512:(tb + 1) * 512], start=(kc == 0), stop=(kc == 15)))
                                row = blk * bw + ch * 128
                                if kind in ("xbc", "sc"):
                                    et, esm = ev.next()
                                    k.op("act", [pm], [et], lambda e: e.copy(out=et[:], in_=pm[:]))
                                    dst = S["XBCT"] if kind == "xbc" else S["SCT"]
                                    dstb = S["XBCT_b"] if kind == "xbc" else S["SCT_b"]
                                    k.dma("sp", esm, dst[row:row + 128, t0:t0 + 512], et[:], [et], [dstb])
                                else:
                                    r = ri % 2
                                    ri += 1
                                    pr = psr[r]
                                    k.op("act", [pm], [qsb[r]], lambda e: e.copy(out=qsb[r][:], in_=pm[:]))
                                    k.op("pe", [qsb[r], rotT], [pr], lambda e: e.matmul(
                                        pr[:, :], lhsT=rotT[:], rhs=qsb[r][:], start=True, stop=True))
                                    k.op("pool", [qsb[r], ct], [t1[r]], lambda e: e.tensor_tensor(
                                        out=t1[r][:], in0=qsb[r][:], in1=ct[:], op=ALU.mult))
                                    k.op("dve", [pr, st], [t2[r]], lambda e: e.tensor_tensor(
                                        out=t2[r][:], in0=pr[:], in1=st[:], op=ALU.mult))
                                    et, esm = evb.next()
                                    k.op("dve", [t1[r], t2[r]], [et], lambda e: e.tensor_tensor(
                                        out=et[:], in0=t1[r][:], in1=t2[r][:], op=ALU.add))
                                    dst = S["QT"] if kind == "q" else S["KT"]
                                    dstb = S["QT_b"] if kind == "q" else S["KT_b"]
                                    k.dma("sp", esm, dst[row:row + 128, t0:t0 + 512], et[:], [et], [dstb])
        k.barrier()
    k.pop()


def phase_final(k, L, xsrc, xsrc_b, W, y, y_b):
    k.push()
    with contextlib.ExitStack() as es:
        gB = k.sb(es, [128, D_MODEL], F32, "gB")
        gsem = k.newsem()
        k.dma("sp", gsem, gB[:], W["norm_final"][0:1, :].broadcast_to([128, D_MODEL]), [], [gB], sync=True)
        xr = Ring(k, es, 2, [128, D_MODEL], F32, "xt")
        yr = Ring(k, es, 2, [128, D_MODEL], F32, "yt")
        junk = k.sb(es, [128, D_MODEL], BF16, "junk")
        ssq = [k.sb(es, [128, 1], F32, "ssq") for _ in range(2)]
        rstd = [k.sb(es, [128, 1], F32, "rstd") for _ in range(2)]
        for tt in range(L // 128):
            xt, xs = xr.next()
            yt, ys = yr.next()
            p = tt % 2
            k.dma("sp", xs, xt[:], xsrc[tt * 128:(tt + 1) * 128, :], [xsrc_b], [xt])
            k.op("act", [xt], [junk, ssq[p]], lambda e: e.activation(out=junk[:], in_=xt[:], func=AF.Square,
                                                                      accum_out=ssq[p][:, 0:1]))
            rmsnorm_rstd(k, ssq[p], rstd[p], D_MODEL)
            k.op("dve", [xt, rstd[p], gB], [yt], lambda e: e.scalar_tensor_tensor(
                out=yt[:], in0=xt[:], scalar=rstd[p][:, 0:1], in1=gB[:], op0=ALU.mult, op1=ALU.mult))
            k.dma("sp", ys, y[tt * 128:(tt + 1) * 128, :], yt[:], [yt], [y_b])
        k.barrier()
    k.pop()


def phase_conv(k, L, lyr, W, S, C):
    LB = min(L, 4096)
    identf, identb = C["identf"], C["identb"]
    k.push()
    with contextlib.ExitStack() as es:
        cw = k.sb(es, [128, 100], F32, "cw")
        cbias = k.sb(es, [128, 20], F32, "cbias")
        sem = k.newsem()
        k.dma("sp", sem, cw[:], W["ssd_conv_wl"][lyr], [], [cw], sync=True)
        k.dma("sp", sem, cbias[:], W["ssd_conv_bl"][lyr], [], [cbias], sync=True)
        xin = Ring(k, es, 2, [128, LB + 4], F32, "cin")
        acc = k.sb(es, [128, LB], F32, "cacc")
        so = Ring(k, es, 2, [128, LB], F32, "so")
        sob = Ring(k, es, 2, [128, LB], BF16, "sob")
        tr = Ring(k, es, 3, [128, 4, 128], F32, "tr")
        trb = Ring(k, es, 3, [128, 4, 128], BF16, "trb")
        ptr = [k.ps(es, [128, 512], F32, "ptr") for _ in range(2)]
        ptb = [k.ps(es, [128, 512], BF16, "ptb") for _ in range(2)]
        pi = 0
        for c in range(20):
            for lb in range(L // LB):
                t0 = lb * LB
                xt, xs = xin.next()
                lo = 2 if t0 == 0 else 0
                hi = 2 if t0 + LB == L else 0
                if lo:
                    k.op("pool", [], [xt], lambda e: e.memset(xt[:, 0:2], 0.0))
                if hi:
                    k.op("pool", [], [xt], lambda e: e.memset(xt[:, LB + 2:LB + 4], 0.0))
                k.dma("sp", xs, xt[:, lo:LB + 4 - hi], S["XBCT"][c * 128:(c + 1) * 128, t0 - 2 + lo:t0 + LB + 2 - hi],
                      [S["XBCT_b"]], [xt])
                k.op("dve", [xt, cw, cbias], [acc], lambda e: e.tensor_scalar(
                    out=acc[:], in0=xt[:, 0:LB], scalar1=cw[:, c * 5:c * 5 + 1], scalar2=cbias[:, c:c + 1],
                    op0=ALU.mult, op1=ALU.add))
                for t in range(1, 5):
                    k.op("dve", [xt, cw, acc], [acc], lambda e: e.scalar_tensor_tensor(
                        out=acc[:], in0=xt[:, t:t + LB], scalar=cw[:, c * 5 + t:c * 5 + t + 1], in1=acc[:],
                        op0=ALU.mult, op1=ALU.add))
                if c < 12:
                    st, ssm = so.next()
                    k.op("act", [acc], [st], lambda e: e.activation(out=st[:], in_=acc[:], func=AF.Silu))
                    for tg in range(LB // 512):
                        pt = ptr[pi % 2]
                        pi += 1
                        for i in range(4):
                            k.op("pe", [st, identf], [pt], lambda e: e.transpose(
                                out=pt[:, i * 128:(i + 1) * 128], in_=st[:, (tg * 4 + i) * 128:(tg * 4 + i + 1) * 128],
                                identity=identf[:]))
                        tt, ts = tr.next()
                        k.op("act", [pt], [tt], lambda e: e.copy(out=tt[:], in_=pt[:, :].rearrange("p (a c) -> p a c", a=4)))
                        r0 = t0 + tg * 512
                        k.dma("sp", ts, S["X"][r0:r0 + 512, c * 128:(c + 1) * 128].rearrange("(a p) c -> p a c", p=128),
                              tt[:], [tt], [S["X_b"]])
                else:
                    st, ssm = sob.next()
                    k.op("act", [acc], [st], lambda e: e.activation(out=st[:], in_=acc[:], func=AF.Silu))
                    name = "BT" if c < 16 else "CT"
                    rr = (c - 12) % 4
                    k.dma("sp", ssm, S[name][rr * 128:(rr + 1) * 128, t0:t0 + LB], st[:], [st], [S[name + "_b"]])
                    if c < 16:
                        for tg in range(LB // 512):
                            pt = ptb[pi % 2]
                            pi += 1
                            for i in range(4):
                                k.op("pe", [st, identb], [pt], lambda e: e.transpose(
                                    out=pt[:, i * 128:(i + 1) * 128],
                                    in_=st[:, (tg * 4 + i) * 128:(tg * 4 + i + 1) * 128], identity=identb[:]))
                            tt, ts = trb.next()
                            k.op("dve", [pt], [tt], lambda e: e.tensor_copy(
                                out=tt[:], in_=pt[:, :].rearrange("p (a c) -> p a c", a=4)))
                            r0 = t0 + tg * 512
                            k.dma("sp", ts,
                                  S["B"][r0:r0 + 512, rr * 128:(rr + 1) * 128].rearrange("(a p) c -> p a c", p=128),
                                  tt[:], [tt], [S["B_b"]])
        k.barrier()
    k.pop()


SSD_RUNS = {0: [(0, 0, 4)], 1: [(0, 4, 6), (1, 6, 8)], 2: [(1, 8, 12)], 3: [(2, 12, 16)],
            4: [(2, 16, 18), (3, 18, 20)], 5: [(3, 20, 24)]}


def phase_ssd(k, L, lyr, W, S, C):
    NCH = L // 128
    identf, onesf, onec = C["identf"], C["onesf"], C["onec"]
    k.push()
    with contextlib.ExitStack() as es:
        sem = k.newsem()
        Abc = k.sb(es, [128, 48], F32, "Abc")
        dtb = k.sb(es, [128, 48], F32, "dtb")
        Dsk = k.sb(es, [128, 24], F32, "Dsk")
        normw = k.sb(es, [128, D_SSD], F32, "normw")
        k.dma("sp", sem, Abc[:], W["ssd_a_log"][lyr:lyr + 1, :].broadcast_to([128, 48]), [], [Abc], sync=True)
        k.dma("sp", sem, dtb[:], W["ssd_dt_bias"][lyr:lyr + 1, :].broadcast_to([128, 48]), [], [dtb], sync=True)
        k.dma("sp", sem, Dsk[:], W["ssd_d"][lyr:lyr + 1, :].broadcast_to([128, 24]), [], [Dsk], sync=True)
        k.dma("sp", sem, normw[:], W["ssd_norm"][lyr:lyr + 1, :].broadcast_to([128, D_SSD]), [], [normw], sync=True)
        k.op("act", [Abc], [Abc], lambda e: e.activation(out=Abc[:], in_=Abc[:], func=AF.Exp))
        k.op("dve", [Abc], [Abc], lambda e: e.tensor_scalar(out=Abc[:], in0=Abc[:], scalar1=-1.0, scalar2=None,
                                                            op0=ALU.mult))
        xr = Ring(k, es, 2, [128, 24, 64], F32, "xk")
        zr = Ring(k, es, 2, [128, D_SSD], F32, "zk")
        ybr = Ring(k, es, 2, [128, D_SSD], F32, "ybk")
        dtr = Ring(k, es, 2, [128, 48], F32, "dtk")
        bkr = Ring(k, es, 2, [128, 512], BF16, "bk")
        btr = Ring(k, es, 2, [128, 4, 128], BF16, "btk")
        ctr = Ring(k, es, 2, [128, 4, 128], BF16, "ctk")
        dts = k.sb(es, [128, 48], F32, "dts")
        dt = k.sb(es, [128, 48], F32, "dt")
        a = k.sb(es, [128, 24], F32, "a")
        sm = k.sb(es, [128, 64], F32, "sm")
        eac = k.sb(es, [128, 24], F32, "eac")
        dend = k.sb(es, [128, 24], F32, "dend")
        cdec = k.sb(es, [128, 24], F32, "cdec")
        w2 = k.sb(es, [128, 24], F32, "w2")
        xdt = k.sb(es, [128, D_SSD], BF16, "xdt")
        xde = k.sb(es, [128, D_SSD], BF16, "xde")
        cbs = k.sb(es, [128, 4, 128], F32, "cbs")
        rhs1 = k.sb(es, [128, 24, 128], F32, "rhs1")
        dec = [k.sb(es, [128, 4, 128], F32, "dec") for _ in range(3)]
        MT = [k.sb(es, [128, 12, 128], BF16, "MT") for _ in range(2)]
        H = [k.sb(es, [128, 768], F32, "H") for _ in range(2)]
        Hb = [k.sb(es, [128, 768], BF16, "Hb") for _ in range(2)]
        yo = k.sb(es, [128, 24, 64], F32, "yo")
        yd = k.sb(es, [128, 24, 64], F32, "yd")
        ydb = [Buf(), Buf()]
        ystore = Ring(k, es, 2, [128, D_SSD], F32, "ystore")
        y1 = k.sb(es, [128, D_SSD], F32, "y1")
        sz = k.sb(es, [128, D_SSD], F32, "sz")
        junk = k.sb(es, [128, 384], F32, "junk")
        gss = k.sb(es, [128, 4], F32, "gss")
        ytr = Ring(k, es, 2, [128, 12, 128], BF16, "ytr")
        big = [k.ps(es, [128, 1024], F32, "big") for _ in range(2)]
        pseg = [k.ps(es, [128, 512], F32, "pseg") for _ in range(2)]
        pcb = k.ps(es, [128, 512], F32, "pcb")
        ptr = k.ps(es, [128, 512], F32, "ptr")
        st = {"big": 0, "seg": 0, "dec": 0}

        def nbig():
            st["big"] += 1
            return big[st["big"] % 2]

        for d in (1, 0):
            Td = C["U"] if d == 0 else C["Lo"]
            nTd = C["nU"] if d == 0 else C["nLo"]
            nm = C["nmf"] if d == 0 else C["nmb"]
            for hf in range(2):
                k.op("pool", [], [H[hf]], lambda e: e.memset(H[hf][:], 0.0))
                k.op("pool", [], [Hb[hf]], lambda e: e.memset(Hb[hf][:], 0.0))
            order = range(NCH) if d == 0 else range(NCH - 1, -1, -1)
            for c in order:
                r0 = c * 128
                xk, s1 = xr.next()
                k.dma("sp", s1, xk[:], S["X"][r0:r0 + 128, :].rearrange("p (h d) -> p h d", h=24), [S["X_b"]], [xk])
                dtk, s2 = dtr.next()
                k.dma("sp", s2, dtk[:], S["DT"][r0:r0 + 128, :], [S["DT_b"]], [dtk])
                bk, s3 = bkr.next()
                k.dma("sp", s3, bk[:], S["B"][r0:r0 + 128, :], [S["B_b"]], [bk])
                btk, s4 = btr.next()
                k.dma("sp", s4, btk[:], S["BT"][:, r0:r0 + 128].rearrange("(g n) t -> n g t", n=128), [S["BT_b"]], [btk])
                ctk, s5 = ctr.next()
                k.dma("sp", s5, ctk[:], S["CT"][:, r0:r0 + 128].rearrange("(g n) t -> n g t", n=128), [S["CT_b"]], [ctk])
                if d == 0:
                    zk, s6 = zr.next()
                    k.dma("sp", s6, zk[:], S["Z"][r0:r0 + 128, :], [S["Z_b"]], [zk])
                    ybk, s7 = ybr.next()
                    k.dma("sp", s7, ybk[:], S["YB"][r0:r0 + 128, :], [S["YB_b"]], [ybk])
                k.op("dve", [dtk, dtb], [dts], lambda e: e.tensor_tensor(out=dts[:], in0=dtk[:], in1=dtb[:], op=ALU.add))
                k.op("act", [dts], [dts], lambda e: e.activation(out=dts[:], in_=dts[:], func=AF.Exp))
                k.op("act", [dts, onec], [dt], lambda e: e.activation(out=dt[:], in_=dts[:], func=AF.Ln,
                                                                      bias=onec[:, 0:1], scale=1.0))
                dtd = dt[:, d * 24:(d + 1) * 24]
                k.op("dve", [dt, Abc], [a], lambda e: e.tensor_tensor(out=a[:], in0=dtd, in1=Abc[:, d * 24:(d + 1) * 24],
                                                                      op=ALU.mult))
                pb = nbig()
                k.op("pe", [Td, a], [pb], lambda e: e.matmul(pb[:, 0:24], lhsT=Td[:], rhs=a[:], start=True, stop=True))
                k.op("pe", [onesf, a], [pb], lambda e: e.matmul(pb[:, 24:48], lhsT=onesf[:], rhs=a[:], start=True, stop=True))
                k.op("act", [pb], [sm], lambda e: e.copy(out=sm[:, 0:48], in_=pb[:, 0:48]))
                k.op("act", [sm], [eac], lambda e: e.activation(out=eac[:], in_=sm[:, 0:24], func=AF.Exp))
                k.op("dve", [sm], [dend], lambda e: e.tensor_tensor(out=dend[:], in0=sm[:, 24:48], in1=sm[:, 0:24],
                                                                   op=ALU.subtract))
                k.op("act", [dend], [dend], lambda e: e.activation(out=dend[:], in_=dend[:], func=AF.Exp))
                k.op("act", [sm], [cdec], lambda e: e.activation(out=cdec[:], in_=sm[:, 24:48], func=AF.Exp))
                k.op("dve", [dt, dend], [w2], lambda e: e.tensor_tensor(out=w2[:], in0=dtd, in1=dend[:], op=ALU.mult))
                k.op("dve", [xk, dt], [xdt], lambda e: e.tensor_tensor(
                    out=xdt[:, :].rearrange("p (h d) -> p h d", d=64), in0=xk[:], in1=dtd.unsqueeze(2).to_broadcast([128, 24, 64]), op=ALU.mult))
                k.op("pool", [xk, w2], [xde], lambda e: e.tensor_tensor(
                    out=xde[:, :].rearrange("p (h d) -> p h d", d=64), in0=xk[:], in1=w2[:, :].unsqueeze(2).to_broadcast([128, 24, 64]), op=ALU.mult))
                for g in range(4):
                    k.op("pe", [btk, ctk], [pcb], lambda e: e.matmul(
                        pcb[:, g * 128:(g + 1) * 128], lhsT=btk[:, g, :], rhs=ctk[:, g, :], start=True, stop=True))
                k.op("act", [pcb], [cbs], lambda e: e.copy(out=cbs[:], in_=pcb[:, :].rearrange("p (g l) -> p g l", g=4)))
                k.op("pool", [a, Td], [rhs1], lambda e: e.tensor_tensor(
                    out=rhs1[:], in0=a[:, :].unsqueeze(2).to_broadcast([128, 24, 128]),
                    in1=Td[:, :].unsqueeze(1).to_broadcast([128, 24, 128]), op=ALU.mult))
                for hf in range(2):
                    h0 = hf * 12
                    pb = nbig()
                    for u in range(3 * hf, 3 * hf + 3):
                        for (g, ha, hb_) in SSD_RUNS[u]:
                            k.op("pe", [ctk, Hb[hf]], [pb], lambda e: e.matmul(
                                pb[:, (ha - h0) * 64:(hb_ - h0) * 64], lhsT=ctk[:, g, :],
                                rhs=Hb[hf][:, (ha - h0) * 64:(hb_ - h0) * 64], start=True, stop=True))
                    k.op("dve", [pb, eac], [yo], lambda e: e.tensor_tensor(
                        out=yo[:, h0:h0 + 12, :], in0=pb[:, 0:768].rearrange("p (h d) -> p h d", h=12),
                        in1=eac[:, h0:h0 + 12].unsqueeze(2).to_broadcast([128, 12, 64]), op=ALU.mult))
                    mt = MT[hf]
                    for u in range(3 * hf, 3 * hf + 3):
                        pg = pseg[st["seg"] % 2]
                        st["seg"] += 1
                        pg3 = pg[:, :].rearrange("p (h l) -> p h l", h=4)
                        k.op("pe", [onesf, rhs1], [pg], lambda e: e.matmul(
                            pg3, lhsT=onesf[:], rhs=rhs1[:, 4 * u:4 * u + 4, :], start=True, stop=False))
                        k.op("pe", [nTd, a], [pg], lambda e: e.matmul(
                            pg3, lhsT=nTd[:], rhs=a[:, 4 * u:4 * u + 4].unsqueeze(2).to_broadcast([128, 4, 128]),
                            start=False, stop=False))
                        k.op("pe", [identf, nm], [pg], lambda e: e.matmul(
                            pg3, lhsT=identf[:], rhs=nm[:, :].unsqueeze(1).to_broadcast([128, 4, 128]),
                            start=False, stop=True))
                        dc = dec[st["dec"] % 3]
                        st["dec"] += 1
                        k.op("act", [pg], [dc], lambda e: e.activation(out=dc[:], in_=pg3, func=AF.Exp))
                        for (g, ha, hb_) in SSD_RUNS[u]:
                            n = hb_ - ha
                            k.op("dve", [dc, cbs], [mt], lambda e: e.tensor_tensor(
                                out=mt[:, ha - h0:hb_ - h0, :], in0=dc[:, ha - 4 * u:hb_ - 4 * u, :],
                                in1=cbs[:, g:g + 1, :].to_broadcast([128, n, 128]), op=ALU.mult))
                    pb2 = nbig()
                    for h in range(h0, h0 + 12):
                        k.op("pe", [mt, xdt], [pb2], lambda e: e.matmul(
                            pb2[:, (h - h0) * 64:(h - h0 + 1) * 64], lhsT=mt[:, h - h0, :], rhs=xdt[:, h * 64:(h + 1) * 64],
                            start=True, stop=True))
                    k.op("dve", [pb2, yo], [ydb[hf]], lambda e: e.tensor_tensor(
                        out=yd[:, h0:h0 + 12, :], in0=pb2[:, 0:768].rearrange("p (h d) -> p h d", h=12),
                        in1=yo[:, h0:h0 + 12, :], op=ALU.add))
                for hf in range(2):
                    h0 = hf * 12
                    pb3 = nbig()
                    for u in range(3 * hf, 3 * hf + 3):
                        for (g, ha, hb_) in SSD_RUNS[u]:
                            k.op("pe", [bk, xde], [pb3], lambda e: e.matmul(
                                pb3[:, (ha - h0) * 64:(hb_ - h0) * 64], lhsT=bk[:, g * 128:(g + 1) * 128],
                                rhs=xde[:, ha * 64:hb_ * 64], start=True, stop=True))
                    k.op("pool", [H[hf], cdec], [H[hf]], lambda e: e.tensor_tensor(
                        out=H[hf][:, :].rearrange("p (h d) -> p h d", d=64), in0=H[hf][:, :].rearrange("p (h d) -> p h d", d=64),
                        in1=cdec[:, h0:h0 + 12].unsqueeze(2).to_broadcast([128, 12, 64]), op=ALU.mult))
                    k.op("dve", [H[hf], pb3], [H[hf]], lambda e: e.tensor_tensor(
                        out=H[hf][:], in0=H[hf][:], in1=pb3[:, 0:768], op=ALU.add))
                    k.op("act", [H[hf]], [Hb[hf]], lambda e: e.copy(out=Hb[hf][:], in_=H[hf][:]))
                ydf = yd[:, :, :].rearrange("p h d -> p (h d)")
                if d == 1:
                    yt, ysm = ystore.next()
                    k.op("pool", ydb, [yt], lambda e: e.tensor_copy(out=yt[:], in_=ydf))
                    k.dma("sp", ysm, S["YB"][r0:r0 + 128, :], yt[:], [yt], [S["YB_b"]])
                else:
                    k.op("dve", ydb + [ybk], [y1], lambda e: e.tensor_tensor(out=y1[:], in0=ydf, in1=ybk[:], op=ALU.add))
                    k.op("pool", [xk, Dsk], [sz], lambda e: e.tensor_tensor(
                        out=sz[:, :].rearrange("p (h d) -> p h d", h=24), in0=xk[:],
                        in1=Dsk[:, :].unsqueeze(2).to_broadcast([128, 24, 64]), op=ALU.mult))
                    k.op("dve", [y1, sz], [y1], lambda e: e.tensor_tensor(out=y1[:], in0=y1[:], in1=sz[:], op=ALU.add))
                    k.op("act", [zk], [sz], lambda e: e.activation(out=sz[:], in_=zk[:], func=AF.Silu))
                    k.op("dve", [y1, sz], [y1], lambda e: e.tensor_tensor(out=y1[:], in0=y1[:], in1=sz[:], op=ALU.mult))
                    for g in range(4):
                        k.op("act", [y1], [junk, gss], lambda e: e.activation(
                            out=junk[:], in_=y1[:, g * 384:(g + 1) * 384], func=AF.Square, accum_out=gss[:, g:g + 1]))
                    k.op("dve", [gss], [gss], lambda e: e.tensor_scalar(out=gss[:], in0=gss[:], scalar1=1.0 / 384,
                                                                        scalar2=EPS, op0=ALU.mult, op1=ALU.add))
                    k.op("act", [gss], [gss], lambda e: e.activation(out=gss[:], in_=gss[:], func=AF.Sqrt))
                    k.op("dve", [gss], [gss], lambda e: e.reciprocal(out=gss[:], in_=gss[:]))
                    k.op("dve", [y1, gss], [y1], lambda e: e.tensor_tensor(
                        out=y1[:, :].rearrange("p (g c) -> p g c", g=4), in0=y1[:, :].rearrange("p (g c) -> p g c", g=4),
                        in1=gss[:, :].unsqueeze(2).to_broadcast([128, 4, 384]), op=ALU.mult))
                    k.op("pool", [y1, normw], [y1], lambda e: e.tensor_tensor(out=y1[:], in0=y1[:], in1=normw[:], op=ALU.mult))
                    yt, ysm = ytr.next()
                    for q in range(3):
                        for i in range(4):
                            cc = q * 4 + i
                            k.op("pe", [y1, identf], [ptr], lambda e: e.transpose(
                                out=ptr[:, i * 128:(i + 1) * 128], in_=y1[:, cc * 128:(cc + 1) * 128], identity=identf[:]))
                        k.op("act", [ptr], [yt], lambda e: e.copy(
                            out=yt[:, q * 4:(q + 1) * 4, :], in_=ptr[:, :].rearrange("p (a t) -> p a t", a=4)))
                    k.dma("sp", ysm, S["YT"][0:D_SSD, r0:r0 + 128].rearrange("(c p) t -> p c t", p=128), yt[:],
                          [yt], [S["YT_b"]])
        k.barrier()
    k.pop()


def phase_sc(k, L, lyr, W, S, C):
    onesf = C["onesf"]
    TBK = 512
    k.push()
    with contextlib.ExitStack() as es:
        sem = k.newsem()
        cw = k.sb(es, [128, 18], F32, "scw")
        nw = k.sb(es, [128, 6], F32, "scn")
        k.dma("sp", sem, cw[:], W["sc_conv_wl"][lyr], [], [cw], sync=True)
        k.dma("sp", sem, nw[:], W["sc_norm_l"][lyr], [], [nw], sync=True)
        bgr = Ring(k, es, 2, [128, TBK], F32, "bg")
        cgr = Ring(k, es, 2, [128, TBK + 2], F32, "cg")
        hxr = Ring(k, es, 2, [128, TBK + 2], F32, "hx")
        prod = k.sb(es, [128, TBK + 2], F32, "prod")
        yc = [k.sb(es, [128, TBK], F32, "yc") for _ in range(6)]
        ysq = [k.sb(es, [128, TBK], F32, "ysq") for _ in range(2)]
        rst = k.sb(es, [128, TBK], F32, "rst")
        outr = Ring(k, es, 3, [128, TBK], BF16, "sco")
        pss = k.ps(es, [128, 512], F32, "pss")
        for tb in range(L // TBK):
            t0 = tb * TBK
            lo = 1 if t0 == 0 else 0
            hi = 1 if t0 + TBK == L else 0
            for c in range(6):
                bg, s0 = bgr.next()
                cg, s1 = cgr.next()
                hx, s2 = hxr.next()
                k.dma("sp", s0, bg[:], S["SCT"][c * 128:(c + 1) * 128, t0:t0 + TBK], [S["SCT_b"]], [bg])
                for (tl, sm_, base) in ((cg, s1, 768), (hx, s2, 1536)):
                    if lo:
                        k.op("pool", [], [tl], lambda e: e.memset(tl[:, 0:1], 0.0))
                    if hi:
                        k.op("pool", [], [tl], lambda e: e.memset(tl[:, TBK + 1:TBK + 2], 0.0))
                    k.dma("sp", sm_, tl[:, lo:TBK + 2 - hi],
                          S["SCT"][base + c * 128:base + (c + 1) * 128, t0 - 1 + lo:t0 + TBK + 1 - hi],
                          [S["SCT_b"]], [tl])
                k.op("pool", [cg, hx], [prod], lambda e: e.tensor_tensor(out=prod[:], in0=cg[:], in1=hx[:], op=ALU.mult))
                y = yc[c]
                k.op("dve", [prod, cw], [y], lambda e: e.tensor_scalar(
                    out=y[:], in0=prod[:, 0:TBK], scalar1=cw[:, c * 3:c * 3 + 1], scalar2=None, op0=ALU.mult))
                for t in (1, 2):
                    k.op("dve", [prod, cw, y], [y], lambda e: e.scalar_tensor_tensor(
                        out=y[:], in0=prod[:, t:t + TBK], scalar=cw[:, c * 3 + t:c * 3 + t + 1], in1=y[:],
                        op0=ALU.mult, op1=ALU.add))
                k.op("dve", [y, bg], [y], lambda e: e.tensor_tensor(out=y[:], in0=y[:], in1=bg[:], op=ALU.mult))
                q = ysq[c % 2]
                k.op("act", [y], [q], lambda e: e.activation(out=q[:], in_=y[:], func=AF.Square))
                k.op("pe", [onesf, q], [pss], lambda e: e.matmul(pss[:, :], lhsT=onesf[:], rhs=q[:],
                                                                 start=(c == 0), stop=(c == 5)))
            k.op("dve", [pss], [rst], lambda e: e.tensor_scalar(out=rst[:], in0=pss[:], scalar1=1.0 / D_SC, scalar2=EPS,
                                                                op0=ALU.mult, op1=ALU.add))
            k.op("act", [rst], [rst], lambda e: e.activation(out=rst[:], in_=rst[:], func=AF.Sqrt))
            k.op("dve", [rst], [rst], lambda e: e.reciprocal(out=rst[:], in_=rst[:]))
            for c in range(6):
                o, so_ = outr.next()
                k.op("dve", [yc[c], nw, rst], [o], lambda e: e.scalar_tensor_tensor(
                    out=o[:], in0=yc[c][:], scalar=nw[:, c:c + 1], in1=rst[:], op0=ALU.mult, op1=ALU.mult))
                k.dma("sp", so_, S["YT"][D_SSD + c * 128:D_SSD + (c + 1) * 128, t0:t0 + TBK], o[:], [o], [S["YT_b"]])
        k.barrier()
    k.pop()


def phase_att(k, L, lyr, W, S, C):
    identb = C["identb"]
    with contextlib.ExitStack() as es:
        for g, dil in enumerate((1, 4, 16)):
            if g not in ATT_GROUPS:
                continue
            Lsub = L // dil
            nqb = Lsub // 128
            W_ = 128 * dil
            k.push()
            with contextlib.ExitStack() as es2:
                kcr = Ring(k, es2, 3, [128, 2, W_], BF16, "kc")
                qwr = Ring(k, es2, 2, [128, 2, 2, W_], BF16, "qw")
                qwr2 = [k.newsem() for _ in range(2)]
                vcr = [Ring(k, es2, 3, [128, 4, 65], BF16, "vc") for _ in range(dil)]
                ptr_ = Ring(k, es2, 3, [128, 512], BF16, "pT", dma=False)
                otr = Ring(k, es2, 3, [128, 4, 65], F32, "ot")
                pst_ = [k.ps(es2, [128, 512], F32, "pst") for _ in range(3)]
                pso = [k.ps(es2, [128, 512], F32, "pso") for _ in range(2)]
                for r in range(dil):
                    for t, _s in zip(vcr[r].tiles, vcr[r].sems):
                        k.op("pool", [], [t], lambda e: e.memset(t[:], 1.0))
                for t in kcr.tiles + qwr.tiles:
                    k.op("pool", [], [t], lambda e: e.memset(t[:], 0.0))
                cnt = {"st": 0, "o": 0}
                for hg in range(3):
                    row0 = g * 768 + hg * 256
                    kch = {}
                    vch = {}

                    def load_chunk(j):
                        p0 = 128 * j - 64
                        lo = 64 if j == 0 else 0
                        hi = 64 if j == nqb else 128
                        kt, ks = kcr.next()
                        tok_lo = (p0 + lo) * dil
                        tok_hi = (p0 + hi) * dil
                        k.dma("sp", ks, kt[:, :, lo * dil:hi * dil],
                              S["KT"][row0:row0 + 256, tok_lo:tok_hi].rearrange("(hp p) t -> p hp t", p=128),
                              [S["KT_b"]], [kt])
                        kch[j] = kt
                        vs = []
                        for r in range(dil):
                            vt, vsm = vcr[r].next()
                            rows = S["V"][tok_lo:tok_hi, row0:row0 + 256].rearrange("(i r) (h d) -> r i h d", r=dil, d=64)[r]
                            k.dma("sp", vsm, vt[lo:hi, :, 0:64], rows, [S["V_b"]], [vt])
                            vs.append(vt)
                        vch[j] = vs

                    load_chunk(0)
                    for w in range(nqb):
                        load_chunk(w + 1)
                        qs2 = qwr2[qwr.i]
                        qt, qs = qwr.next()
                        qsrc = S["QT"][row0:row0 + 256, w * W_:(w + 1) * W_].rearrange("(hp p) t -> p hp t", p=128)
                        k.dma("sp", qs, qt[0:64, :, 0, :], qsrc[0:64], [S["QT_b"]], [qt])
                        k.dma("sp", qs2, qt[64:128, :, 1, :], qsrc[64:128], [S["QT_b"]], [qt])
                        for r in range(dil):
                            po = pso[cnt["o"] % 2]
                            cnt["o"] += 1
                            for c in (0, 1):
                                if ATT_STAGE < 2:
                                    break
                                j = w + c
                                lo = 64 if j == 0 else 0
                                hi = 64 if j == nqb else 128
                                kt = kch[j]
                                vt = vch[j][r]
                                ps_ = pst_[cnt["st"] % 3]
                                cnt["st"] += 1
                                for hp in range(2):
                                    kv = kt[:, hp, :].rearrange("p (i r) -> p i r", r=dil)[:, :, r]
                                    qv = qt[:, hp, :, :].rearrange("p a (i r) -> p a i r", r=dil)[:, :, :, r]
                                    k.op("pe", [kt, qt], [ps_], lambda e: e.matmul(
                                        ps_[:, hp * 256:(hp + 1) * 256].rearrange("p (a i) -> p a i", a=2), lhsT=kv, rhs=qv,
                                        start=(hp == 0), stop=False))
                                if c == 0:
                                    mk = C["maskA0"] if j == 0 else C["maskA"]
                                else:
                                    mk = C["maskB1"] if j == nqb else C["maskB"]
                                k.op("pe", [identb, mk], [ps_], lambda e: e.matmul(
                                    ps_[:, :], lhsT=identb[:], rhs=mk[:], start=False, stop=True))
                                if ATT_STAGE < 3:
                                    continue
                                pt, _ = ptr_.next()
                                k.op("act", [ps_], [pt], lambda e: e.activation(out=pt[:, :], in_=ps_[:, :],
                                                                               func=AF.Exp, scale=0.125))
                                if ATT_STAGE < 4:
                                    continue
                                for hh in range(4):
                                    k.op("pe", [pt, vt], [po], lambda e: e.matmul(
                                        po[:, hh * 65:(hh + 1) * 65], lhsT=pt[:, hh * 128:(hh + 1) * 128],
                                        rhs=vt[:, hh, :], start=(c == 0 and hh == 0), stop=(c == 1 and hh == 3)))
                            if ATT_STAGE < 5:
                                continue
                            ot, osm = otr.next()
                            k.op("act", [po], [ot], lambda e: e.copy(out=ot[:], in_=po[:, 0:260].rearrange("p (h d) -> p h d", h=4)))
                            dst = S["O%d" % g][w * W_:(w + 1) * W_, hg * 260:(hg + 1) * 260].rearrange(
                                "(i r) (h d) -> r i h d", r=dil, d=65)[r]
                            k.dma("sp", osm, dst, ot[:], [ot], [S["O%d_b" % g]])
                k.barrier()
            k.pop()
        if ATT_STAGE < 6:
            return
        k.push()
        with contextlib.ExitStack() as es2:
            identf = C["identf"]
            sem = k.newsem()
            nw = k.sb(es2, [128, D_ATT], F32, "attn")
            k.dma("sp", sem, nw[:], W["att_norm"][lyr:lyr + 1, :].broadcast_to([128, D_ATT]), [], [nw], sync=True)
            o0 = Ring(k, es2, 2, [128, 12, 65], F32, "o0")
            o1 = Ring(k, es2, 2, [128, 12, 65], F32, "o1")
            o2 = Ring(k, es2, 2, [128, 12, 65], F32, "o2")
            rl = k.sb(es2, [128, 12], F32, "rl")
            ov = k.sb(es2, [128, 12, 64], F32, "ov")
            junk = k.sb(es2, [128, D_ATT], F32, "junk")
            ssq = k.sb(es2, [128, 1], F32, "ssq")
            rstd = k.sb(es2, [128, 1], F32, "rstd")
            ytr = Ring(k, es2, 2, [128, 6, 128], BF16, "aytr")
            ptr = [k.ps(es2, [128, 512], F32, "ptr") for _ in range(2)]
            for tt in range(L // 128):
                r0 = tt * 128
                a0, s0 = o0.next()
                a1, s1 = o1.next()
                a2, s2 = o2.next()
                k.dma("sp", s0, a0[:], S["O0"][r0:r0 + 128, :].rearrange("p (h d) -> p h d", h=12), [S["O0_b"]], [a0])
                k.dma("sp", s1, a1[:], S["O1"][r0:r0 + 128, :].rearrange("p (h d) -> p h d", h=12), [S["O1_b"]], [a1])
                k.dma("sp", s2, a2[:], S["O2"][r0:r0 + 128, :].rearrange("p (h d) -> p h d", h=12), [S["O2_b"]], [a2])
                k.op("pool", [a0, a1], [a0], lambda e: e.tensor_tensor(out=a0[:], in0=a0[:], in1=a1[:], op=ALU.add))
                k.op("dve", [a0, a2], [a0], lambda e: e.tensor_tensor(out=a0[:], in0=a0[:], in1=a2[:], op=ALU.add))
                k.op("dve", [a0], [rl], lambda e: e.reciprocal(out=rl[:], in_=a0[:, :, 64]))
                k.op("dve", [a0, rl], [ov], lambda e: e.tensor_tensor(
                    out=ov[:], in0=a0[:, :, 0:64], in1=rl[:, :].unsqueeze(2).to_broadcast([128, 12, 64]), op=ALU.mult))
                ovf = ov[:, :, :].rearrange("p h d -> p (h d)")
                k.op("act", [ov], [junk, ssq], lambda e: e.activation(out=junk[:], in_=ovf, func=AF.Square,
                                                                      accum_out=ssq[:, 0:1]))
                rmsnorm_rstd(k, ssq, rstd, D_ATT)
                k.op("dve", [ov, rstd, nw], [junk], lambda e: e.scalar_tensor_tensor(
                    out=junk[:], in0=ovf, scalar=rstd[:, 0:1], in1=nw[:], op0=ALU.mult, op1=ALU.mult))
                yt, ysm = ytr.next()
                for q in range(2):
                    pt = ptr[q]
                    n = 4 if q == 0 else 2
                    for i in range(n):
                        cc = q * 4 + i
                        k.op("pe", [junk, identf], [pt], lambda e: e.transpose(
                            out=pt[:, i * 128:(i + 1) * 128], in_=junk[:, cc * 128:(cc + 1) * 128], identity=identf[:]))
                    k.op("act", [pt], [yt], lambda e: e.copy(
                        out=yt[:, q * 4:q * 4 + n, :], in_=pt[:, 0:n * 128].rearrange("p (a t) -> p a t", a=n)))
                k.dma("sp", ysm, S["YT"][2304:3072, r0:r0 + 128].rearrange("(c p) t -> p c t", p=128), yt[:],
                      [yt], [S["YT_b"]])
            k.barrier()
        k.pop()


def phase_wout(k, L, lyr, xsrc, xsrc_b, W, S, C):
    k.push()
    with contextlib.ExitStack() as es:
        wo = k.sb(es, [128, 24, D_MODEL], BF16, "wo")
        stg = Ring(k, es, 2, [128, 2, D_MODEL], F32, "wstg")
        for c4 in range(12):
            st, ss = stg.next()
            k.dma("sp", ss, st[:], W["w_out"][lyr, c4 * 256:(c4 + 1) * 256, :].rearrange("(c p) n -> p c n", p=128), [], [st])
            eng = "pool" if c4 % 2 == 0 else "act"
            if eng == "pool":
                k.op("pool", [st], [wo], lambda e: e.tensor_copy(out=wo[:, c4 * 2:(c4 + 1) * 2, :], in_=st[:]))
            else:
                k.op("act", [st], [wo], lambda e: e.copy(out=wo[:, c4 * 2:(c4 + 1) * 2, :], in_=st[:]))
        ytr = Ring(k, es, 2, [128, 24, 128], BF16, "yT")
        xr = Ring(k, es, 2, [128, D_MODEL], F32, "xo")
        outr = Ring(k, es, 2, [128, D_MODEL], F32, "xn")
        psm = [k.ps(es, [128, 512], F32, "psm") for _ in range(4)]
        for tt in range(L // 128):
            r0 = tt * 128
            yt, ys = ytr.next()
            k.dma("sp", ys, yt[:], S["YT"][:, r0:r0 + 128].rearrange("(c p) t -> p c t", p=128), [S["YT_b"]], [yt])
            xt, xs = xr.next()
            k.dma("sp", xs, xt[:], xsrc[r0:r0 + 128, :], [xsrc_b], [xt])
            ot, osm = outr.next()
            for nb in range(4):
                pm = psm[nb]
                for c in range(24):
                    k.op("pe", [yt, wo], [pm], lambda e: e.matmul(pm[:, :], lhsT=yt[:, c, :], rhs=wo[:, c, nb * 512:(nb + 1) * 512],
                                                                   start=(c == 0), stop=(c == 23)))
                k.op("dve", [pm, xt], [ot], lambda e: e.tensor_tensor(out=ot[:, nb * 512:(nb + 1) * 512], in0=pm[:],
                                                                      in1=xt[:, nb * 512:(nb + 1) * 512], op=ALU.add))
            k.dma("sp", osm, S["XRES"][r0:r0 + 128, :], ot[:], [ot], [S["XRES_b"]])
        k.barrier()
    k.pop()


def peer_tables_bf16(k, lyr, W, S):
    k.push()
    with contextlib.ExitStack() as es:
        stg = Ring(k, es, 3, [128, 4096], F32, "tstg")
        outb = Ring(k, es, 3, [128, 4096], BF16, "tout")
        n = 0
        for name, col0 in (("peer_u", 0), ("peer_v", D_MODEL)):
            dst = "UV"
            src = W[name][lyr * N_EXP:(lyr + 1) * N_EXP, :].rearrange("(c p two) d -> c p (two d)", p=128, two=2)
            dv = S["UV"][:, col0:col0 + D_MODEL].rearrange("(c p two) d -> c p two d", p=128, two=2)
            for c in range(N_EXP // 256):
                st, ss = stg.next()
                ob, os_ = outb.next()
                k.dma("sp", ss, st[:], src[c], [], [st])
                e = ("act", "pool", "dve")[n % 3]
                n += 1
                if e == "act":
                    k.op("act", [st], [ob], lambda en: en.copy(out=ob[:], in_=st[:]))
                else:
                    k.op(e, [st], [ob], lambda en: en.tensor_copy(out=ob[:], in_=st[:]))
                k.dma("sp", os_, dv[c], ob[:, :].rearrange("p (two d) -> p two d", two=2), [ob], [S[dst + "_b"]])
        k.barrier()
    k.pop()


def phase_peer_a(k, L, lyr, W, S, C):
    identf, identb = C["identf"], C["identb"]
    NEG = -1.0e30
    k.push()
    with contextlib.ExitStack() as es:
        sem = k.newsem()
        gB = k.sb(es, [128, D_MODEL], F32, "gB")
        k.dma("sp", sem, gB[:], W["norm_ffn"][lyr:lyr + 1, :].broadcast_to([128, D_MODEL]), [], [gB], sync=True)
        wq = k.sb(es, [128, 16, D_MODEL], BF16, "wq")
        skT = k.sb(es, [128, 16, 128], BF16, "skT")
        pq = [k.ps(es, [128, 512], F32, "pq") for _ in range(4)]
        pqs = [k.ps(es, [128, 512], F32, "pqs") for _ in range(2)]
        pst = [k.ps(es, [128, 1024], BF16, "pst") for _ in range(2)]
        with contextlib.ExitStack() as es1:
            stg = Ring(k, es1, 2, [128, 2, D_MODEL], F32, "wstg")
            for c2 in range(8):
                st, ss = stg.next()
                k.dma("sp", ss, st[:], W["peer_wq"][lyr, c2 * 256:(c2 + 1) * 256, :].rearrange("(c p) n -> p c n", p=128), [], [st])
                if c2 % 2 == 0:
                    k.op("pool", [st], [wq], lambda e: e.tensor_copy(out=wq[:, c2 * 2:(c2 + 1) * 2, :], in_=st[:]))
                else:
                    k.op("act", [st], [wq], lambda e: e.copy(out=wq[:, c2 * 2:(c2 + 1) * 2, :], in_=st[:]))
            skf = k.sb(es1, [128, 16, 128], F32, "skf")
            k.dma("sp", sem, skf[:], W["peer_subkeys"][lyr].rearrange("m n d -> n m d"), [], [skf], sync=True)
            for q4 in range(4):
                pm = pq[q4]
                for i in range(4):
                    m = q4 * 4 + i
                    k.op("pe", [skf, identf], [pm], lambda e: e.transpose(out=pm[:, i * 128:(i + 1) * 128], in_=skf[:, m, :],
                                                                          identity=identf[:]))
                k.op("act", [pm], [skT], lambda e: e.copy(out=skT[:, q4 * 4:(q4 + 1) * 4, :],
                                                          in_=pm[:, :].rearrange("p (a n) -> p a n", a=4)))
            k.barrier()
        xr = Ring(k, es, 2, [128, D_MODEL], F32, "xm")
        ssq = k.sb(es, [128, 1], F32, "ssq")
        rstd = k.sb(es, [128, 1], F32, "rstd")
        hn = k.sb(es, [128, D_MODEL], F32, "hn")
        hnbr = Ring(k, es, 2, [128, D_MODEL], BF16, "hnb")
        eidr = Ring(k, es, 2, [128, 128], I32, "eido")
        gater = Ring(k, es, 2, [128, 8, 16], F32, "gateo")
        hnT = k.sb(es, [128, 16, 128], BF16, "hnT")
        qTb = k.sb(es, [128, 16, 128], BF16, "qTb")
        sc = k.sb(es, [128, 16, 128], F32, "sc")
        scw = k.sb(es, [128, 128], F32, "scw")
        sv = k.sb(es, [128, 16, 16], F32, "sv")
        si = k.sb(es, [128, 16, 16], U32, "si")
        sif = k.sb(es, [128, 16, 16], F32, "sif")
        cand = k.sb(es, [128, 8, 256], F32, "cand")
        cidx = k.sb(es, [128, 8, 256], F32, "cidx")
        cw_ = k.sb(es, [128, 256], F32, "cw_")
        top = k.sb(es, [128, 8, 16], F32, "top")
        zs = k.sb(es, [128, 8], F32, "zs")
        j256 = k.sb(es, [128, 256], F32, "j256")
        eidf = k.sb(es, [128, 128], F32, "eidf")
        eidf_w = Buf(multi=True)
        pre = k.sb(es, [128, 128], F32, "pre")
        pre_w = Buf(multi=True)
        actg = k.sb(es, [128, 128], F32, "actg")

        def stage_a(tt):
            r0 = tt * 128
            eid, eid_s = eidr.next()
            gate, gate_s = gater.next()
            hnb, hnb_s = hnbr.next()
            xm, xs = xr.next()
            k.dma("sp", xs, xm[:], S["XRES"][r0:r0 + 128, :], [S["XRES_b"]], [xm])
            k.op("act", [xm], [hnb, ssq], lambda e: e.activation(out=hnb[:], in_=xm[:], func=AF.Square,
                                                                   accum_out=ssq[:, 0:1]))
            rmsnorm_rstd(k, ssq, rstd, D_MODEL)
            k.op("dve", [xm, rstd, gB], [hn], lambda e: e.scalar_tensor_tensor(
                out=hn[:], in0=xm[:], scalar=rstd[:, 0:1], in1=gB[:], op0=ALU.mult, op1=ALU.mult))
            k.op("act", [hn], [hnb], lambda e: e.copy(out=hnb[:], in_=hn[:]))
            for half in range(2):
                pt = pst[half]
                for j in range(8):
                    kc = half * 8 + j
                    k.op("pe", [hnb, identb], [pt], lambda e: e.transpose(
                        out=pt[:, j * 128:(j + 1) * 128], in_=hnb[:, kc * 128:(kc + 1) * 128], identity=identb[:]))
                k.op("act", [pt], [hnT], lambda e: e.copy(out=hnT[:, half * 8:(half + 1) * 8, :],
                                                          in_=pt[:, :].rearrange("p (j t) -> p j t", j=8)))
            for q4 in range(4):
                pm = pqs[q4 % 2]
                for i in range(4):
                    m = q4 * 4 + i
                    for kc in range(16):
                        k.op("pe", [wq, hnT], [pm], lambda e: e.matmul(
                            pm[:, i * 128:(i + 1) * 128], lhsT=wq[:, kc, m * 128:(m + 1) * 128], rhs=hnT[:, kc, :],
                            start=(kc == 0), stop=(kc == 15)))
                k.op("act", [pm], [qTb], lambda e: e.copy(out=qTb[:, q4 * 4:(q4 + 1) * 4, :],
                                                          in_=pm[:, :].rearrange("p (a t) -> p a t", a=4)))
            for q4 in range(4):
                pm = pqs[q4 % 2]
                for i in range(4):
                    m = q4 * 4 + i
                    k.op("pe", [qTb, skT], [pm], lambda e: e.matmul(
                        pm[:, i * 128:(i + 1) * 128], lhsT=qTb[:, m, :], rhs=skT[:, m, :], start=True, stop=True))
                k.op("act", [pm], [sc], lambda e: e.copy(out=sc[:, q4 * 4:(q4 + 1) * 4, :],
                                                         in_=pm[:, :].rearrange("p (a n) -> p a n", a=4)))
            for m in range(16):
                k.op("dve", [sc], [sv], lambda e: e.max(out=sv[:, m, 0:8], in_=sc[:, m, :]))
                k.op("dve", [sc, sv], [scw], lambda e: e.match_replace(out=scw[:], in_to_replace=sv[:, m, 0:8],
                                                                       in_values=sc[:, m, :], imm_value=NEG))
                k.op("dve", [scw], [sv], lambda e: e.max(out=sv[:, m, 8:16], in_=scw[:]))
                k.op("dve", [sc, sv], [si], lambda e: e.max_index(out=si[:, m, 0:8], in_max=sv[:, m, 0:8], in_values=sc[:, m, :]))
                k.op("dve", [sc, sv], [si], lambda e: e.max_index(out=si[:, m, 8:16], in_max=sv[:, m, 8:16], in_values=sc[:, m, :]))
            k.op("dve", [si], [sif], lambda e: e.tensor_copy(out=sif[:], in_=si[:]))
            svv = sv[:, :, :].rearrange("p (h two) a -> p h two a", two=2)
            sfv = sif[:, :, :].rearrange("p (h two) a -> p h two a", two=2)
            c4 = cand[:, :, :].rearrange("p h (a b) -> p h a b", a=16)
            x4 = cidx[:, :, :].rearrange("p h (a b) -> p h a b", a=16)
            k.op("dve", [sv], [cand], lambda e: e.tensor_tensor(
                out=c4, in0=svv[:, :, 0, :].unsqueeze(3).to_broadcast([128, 8, 16, 16]),
                in1=svv[:, :, 1, :].unsqueeze(2).to_broadcast([128, 8, 16, 16]), op=ALU.add))
            k.op("dve", [sif], [sif], lambda e: e.tensor_scalar(out=sfv[:, :, 0, :], in0=sfv[:, :, 0, :], scalar1=128.0,
                                                                scalar2=None, op0=ALU.mult))
            k.op("dve", [sif], [cidx], lambda e: e.tensor_tensor(
                out=x4, in0=sfv[:, :, 0, :].unsqueeze(3).to_broadcast([128, 8, 16, 16]),
                in1=sfv[:, :, 1, :].unsqueeze(2).to_broadcast([128, 8, 16, 16]), op=ALU.add))
            for h in range(8):
                k.op("dve", [cand], [top], lambda e: e.max(out=top[:, h, 0:8], in_=cand[:, h, :]))
                k.op("dve", [cand, top], [cw_], lambda e: e.match_replace(out=cw_[:], in_to_replace=top[:, h, 0:8],
                                                                          in_values=cand[:, h, :], imm_value=NEG))
                k.op("dve", [cw_], [top], lambda e: e.max(out=top[:, h, 8:16], in_=cw_[:]))
            k.op("dve", [top], [gate], lambda e: e.tensor_tensor(
                out=gate[:], in0=top[:], in1=top[:, :, 0:1].to_broadcast([128, 8, 16]), op=ALU.subtract))
            k.op("act", [gate], [gate], lambda e: e.activation(out=gate[:], in_=gate[:], func=AF.Exp))
            k.op("dve", [gate], [zs], lambda e: e.tensor_reduce(out=zs[:], in_=gate[:], axis=AX.X, op=ALU.add))
            k.op("dve", [zs], [zs], lambda e: e.reciprocal(out=zs[:], in_=zs[:]))
            k.op("dve", [gate, zs], [gate], lambda e: e.tensor_tensor(
                out=gate[:], in0=gate[:], in1=zs[:, :].unsqueeze(2).to_broadcast([128, 8, 16]), op=ALU.mult))
            k.op("dve", [], [eidf], lambda e: e.memset(eidf[:], 0.0))
            for h in range(8):
                for kk in range(16):
                    k.op("dve", [cand, top, cidx, eidf], [j256, eidf_w], lambda e: e.scalar_tensor_tensor(
                        out=j256[:], in0=cand[:, h, :], scalar=top[:, h, kk:kk + 1], in1=cidx[:, h, :],
                        op0=ALU.is_equal, op1=ALU.mult, accum_out=eidf[:, h * 16 + kk:h * 16 + kk + 1]))
            k.op("dve", [eidf, eidf_w], [eidf], lambda e: e.tensor_scalar(out=eidf[:], in0=eidf[:], scalar1=0.0,
                                                                          scalar2=float(N_EXP - 1), op0=ALU.max, op1=ALU.min))
            k.op("dve", [eidf], [eid], lambda e: e.tensor_copy(out=eid[:], in_=eidf[:]))

            k.dma("sp", hnb_s, S["HNB"][r0:r0 + 128, :], hnb[:], [hnb], [S["HNB_b"]])
            k.dma("sp", eid_s, S["EID"][r0:r0 + 128, :], eid[:], [eid], [S["EID_b"]])
            k.dma("sp", gate_s, S["GATE"][r0:r0 + 128, :], gate[:, :, :].rearrange("p h a -> p (h a)"), [gate], [S["GATE_b"]])

        for tt in range(L // 128):
            stage_a(tt)
        k.barrier()
    k.pop()


def phase_peer_b(k, L, lyr, W, S, C):
    identb = C["identb"]
    NT = L // 128
    k.push()
    with contextlib.ExitStack() as es:
        hnbr = Ring(k, es, 2, [128, D_MODEL], BF16, "hnbi")
        eidr = Ring(k, es, 2, [128, 128], I32, "eidi")
        gater = Ring(k, es, 2, [128, 128], F32, "gatei")
        xmr = Ring(k, es, 2, [128, D_MODEL], F32, "xmi")
        gr = Ring(k, es, 12, [128, 2 * D_MODEL], BF16, "uvg", sw=True)
        dgr = [k.sb(es, [128, 128], BF16, "dg") for _ in range(4)]
        pre = k.sb(es, [128, 128], F32, "pre")
        preb = [Buf() for _ in range(128)]
        gel = k.sb(es, [128, 128], F32, "gel")
        gelb = [Buf() for _ in range(128)]
        outr = Ring(k, es, 2, [128, D_MODEL], F32, "xn")
        pq = [k.ps(es, [128, 512], F32, "pq") for _ in range(4)]
        uv = S["UV"]
        st = {}

        def load(tt):
            r0 = tt * 128
            hnb, s1 = hnbr.next()
            eid, s2 = eidr.next()
            gate, s3 = gater.next()
            xm, s4 = xmr.next()
            k.dma("sp", s1, hnb[:], S["HNB"][r0:r0 + 128, :], [S["HNB_b"]], [hnb])
            k.dma("sp", s2, eid[:], S["EID"][r0:r0 + 128, :], [S["EID_b"]], [eid])
            k.dma("sp", s3, gate[:], S["GATE"][r0:r0 + 128, :], [S["GATE_b"]], [gate])
            k.dma("sp", s4, xm[:], S["XRES"][r0:r0 + 128, :], [S["XRES_b"]], [xm])
            st[tt] = (hnb, eid, gate, xm)

        load(0)
        for tt in range(NT):
            if tt + 1 < NT:
                load(tt + 1)
            hnb, eid, gate, xm = st.pop(tt)
            r0 = tt * 128
            k.op("dve", [], preb, lambda e: e.memset(pre[:], 0.0))
            gts = {}
            for s_ in range(129):
                if s_ < 128:
                    hk = s_
                    gt, gs = gr.next()
                    gts[hk] = gt
                    k.dma("pool", gs, gt[:], uv, [eid, S["UV_b"]], [gt],
                          indirect=bass.IndirectOffsetOnAxis(ap=eid[:, hk:hk + 1], axis=0))
                    k.op("dve", [gt, hnb], [gt, preb[hk]], lambda e: e.scalar_tensor_tensor(
                        out=gt[:, 0:D_MODEL], in0=gt[:, 0:D_MODEL], scalar=1.0, in1=hnb[:], op0=ALU.mult, op1=ALU.mult,
                        accum_out=pre[:, hk:hk + 1]))
                    k.op("act", [preb[hk]], [gelb[hk]], lambda e: e.activation(
                        out=gel[:, hk:hk + 1], in_=pre[:, hk:hk + 1], func=AF.Gelu))
                if s_ >= 1:
                    hk = s_ - 1
                    d_ = dgr[hk % 4]
                    gt = gts.pop(hk)
                    k.op("dve", [gelb[hk], gate, identb], [d_], lambda e: e.tensor_scalar(
                        out=d_[:], in0=identb[:], scalar1=gel[:, hk:hk + 1], scalar2=gate[:, hk:hk + 1],
                        op0=ALU.mult, op1=ALU.mult))
                    for nb in range(4):
                        k.op("pe", [d_, gt], [pq[nb]], lambda e: e.matmul(
                            pq[nb][:, :], lhsT=d_[:], rhs=gt[:, D_MODEL + nb * 512:D_MODEL + (nb + 1) * 512],
                            start=(hk == 0), stop=(hk == 127)))
            ot, osm = outr.next()
            for nb in range(4):
                k.op("dve", [pq[nb], xm], [ot], lambda e: e.tensor_tensor(
                    out=ot[:, nb * 512:(nb + 1) * 512], in0=pq[nb][:], in1=xm[:, nb * 512:(nb + 1) * 512], op=ALU.add))
            k.dma("sp", osm, S["XRES"][r0:r0 + 128, :], ot[:], [ot], [S["XRES_b"]])
        k.barrier()
    k.pop()


def phase_peer(k, L, lyr, W, S, C):
    peer_tables_bf16(k, lyr, W, S)
    phase_peer_a(k, L, lyr, W, S, C)
    phase_peer_b(k, L, lyr, W, S, C)


def rope_consts(L):
    inv = (500000.0 ** (-np.arange(0, 16, 2, dtype=np.float32) / 16)).astype(np.float32)
    ang = np.arange(L, dtype=np.float32)[:, None] * inv[None, :]
    cosT = np.ones((128, L), np.float32)
    sinT = np.zeros((128, L), np.float32)
    for p in range(128):
        d = p % 64
        if d < 16:
            cosT[p] = np.cos(ang[:, d % 8])
            sinT[p] = np.sin(ang[:, d % 8])
    rotT = np.zeros((128, 128), np.float32)
    for m in range(128):
        d = m % 64
        if d < 8:
            rotT[m + 8, m] = -1.0
        elif d < 16:
            rotT[m - 8, m] = 1.0
    return cosT, sinT, rotT


WEIGHT_SHAPES = {
    "norm_mix": (DEPTH, D_MODEL), "w_in": (DEPTH, D_MODEL, D_IN), "ssd_dt_bias": (DEPTH, 48), "ssd_a_log": (DEPTH, 48),
    "ssd_d": (DEPTH, SSD_HEADS), "ssd_norm": (DEPTH, D_SSD),
    "ssd_conv_wl": (DEPTH, 128, 100), "ssd_conv_bl": (DEPTH, 128, 20), "sc_conv_wl": (DEPTH, 128, 18),
    "sc_norm_l": (DEPTH, 128, 6), "att_norm": (DEPTH, D_ATT), "w_out": (DEPTH, D_MIX, D_MODEL),
    "norm_ffn": (DEPTH, D_MODEL), "peer_wq": (DEPTH, D_MODEL, D_MODEL), "peer_subkeys": (DEPTH, 16, 128, 128),
    "peer_u": (DEPTH * N_EXP, D_MODEL), "peer_v": (DEPTH * N_EXP, D_MODEL), "norm_final": (1, D_MODEL),
}


class LazyW(dict):
    def __init__(self, nc):
        super().__init__()
        self.nc = nc

    def __missing__(self, n):
        v = self.nc.dram_tensor(n, list(WEIGHT_SHAPES[n]), F32, kind="ExternalInput").ap()
        self[n] = v
        return v


def build(L, phases=ALL_PHASES, debug=(), nlayers=DEPTH, feed=()):
    nc = bass.Bass("TRN2", target_bir_lowering=False)
    x = nc.dram_tensor("x", [L, D_MODEL], F32, kind="ExternalInput").ap()
    y = nc.dram_tensor("y", [L, D_MODEL], F32, kind="ExternalOutput").ap()
    W = LazyW(nc)
    S = {}

    def scratch(name, shape, dt):
        kind = "ExternalOutput" if name in debug else ("ExternalInput" if name in feed else "Internal")
        S[name] = nc.dram_tensor("s_" + name, list(shape), dt, kind=kind).ap()
        S[name + "_b"] = Buf(multi=True)

    scratch("Z", [L, D_SSD], F32)
    scratch("XBCT", [XBC, L], F32)
    scratch("DT", [L, 48], F32)
    scratch("SCT", [3 * D_SC, L], F32)
    scratch("QT", [2304, L], BF16)
    scratch("KT", [2304, L], BF16)
    scratch("V", [L, 2304], BF16)
    scratch("XRES", [L, D_MODEL], F32)
    scratch("X", [L, D_SSD], F32)
    scratch("B", [L, 512], BF16)
    scratch("BT", [512, L], BF16)
    scratch("CT", [512, L], BF16)
    scratch("YB", [L, D_SSD], F32)
    scratch("YT", [D_MIX, L], BF16)
    scratch("O0", [L, 780], F32)
    scratch("O1", [L, 780], F32)
    scratch("O2", [L, 780], F32)
    scratch("UV", [N_EXP, 2 * D_MODEL], BF16)
    scratch("HNB", [L, D_MODEL], BF16)
    scratch("EID", [L, 128], I32)
    scratch("GATE", [L, 128], F32)
    x_b = Buf(multi=True)
    y_b = Buf(multi=True)
    with contextlib.ExitStack() as es:
        k = KB(nc, es)
        C = {}
        csem = k.newsem()

        def cload(name, shape=(128, 128), dt=F32):
            d = nc.dram_tensor(name, list(shape), F32, kind="ExternalInput").ap()
            if dt == BF16:
                tb = k.sb(es, list(shape), BF16, name + "b")
                t = k.sb(es_stage, list(shape), F32, name)
                k.dma("sp", csem, t[:], d[:, :], [], [t], sync=True)
                k.op("dve", [t], [tb], lambda e: e.tensor_copy(out=tb[:], in_=t[:]))
                return tb
            t = k.sb(es, list(shape), F32, name)
            k.dma("sp", csem, t[:], d[:, :], [], [t], sync=True)
            return t

        C["identf"] = cload("ident")
        C["rotT"] = cload("rotT")
        C["onesf"] = cload("onesf")
        for n in ("U", "Lo", "nU", "nLo", "nmf", "nmb"):
            C[n] = cload(n)
        onec = k.sb(es, [128, 1], F32, "onec")
        k.op("dve", [], [onec], lambda e: e.memset(onec[:], 1.0))
        C["onec"] = onec
        C["identb"] = k.sb(es, [128, 128], BF16, "identb")
        k.op("dve", [C["identf"]], [C["identb"]], lambda e: e.tensor_copy(out=C["identb"][:], in_=C["identf"][:]))
        bnames = ("maskA", "maskB", "maskA0", "maskB1")
        tbs = {n: k.sb(es, [128, 512], BF16, n + "b") for n in bnames}
        with contextlib.ExitStack() as es_stage:
            for n in bnames:
                d = nc.dram_tensor(n, [128, 512], F32, kind="ExternalInput").ap()
                t = k.sb(es_stage, [128, 512], F32, n)
                k.dma("sp", csem, t[:], d[:, :], [], [t], sync=True)
                k.op("dve", [t], [tbs[n]], lambda e: e.tensor_copy(out=tbs[n][:], in_=t[:]))
                C[n] = tbs[n]
            k.barrier()
        C["cosT"] = nc.dram_tensor("cosT", [128, L], F32, kind="ExternalInput").ap()
        C["sinT"] = nc.dram_tensor("sinT", [128, L], F32, kind="ExternalInput").ap()
        k.barrier()
        xsrc, xsrc_b = x, x_b
        for lyr in range(nlayers):
            if "a" in phases:
                phase_a(k, L, lyr, xsrc, xsrc_b, W, S, C)
            if "conv" in phases:
                phase_conv(k, L, lyr, W, S, C)
            if "ssd" in phases:
                phase_ssd(k, L, lyr, W, S, C)
            if "sc" in phases:
                phase_sc(k, L, lyr, W, S, C)
            if "att" in phases:
                phase_att(k, L, lyr, W, S, C)
            if "wout" in phases:
                phase_wout(k, L, lyr, xsrc, xsrc_b, W, S, C)
            if "peer" in phases:
                phase_peer(k, L, lyr, W, S, C)
            if "wout" in phases:
                xsrc, xsrc_b = S["XRES"], S["XRES_b"]
        if "final" in phases:
            phase_final(k, L, xsrc, xsrc_b, W, y, y_b)
        k.barrier()
    return nc


def host_consts(L):
    cosT, sinT, rotT = rope_consts(L)
    i = np.arange(128)
    U = (i[:, None] <= i[None, :]).astype(np.float32)
    Lo = (i[:, None] >= i[None, :]).astype(np.float32)
    NEGM = -30000.0
    nmf = np.where(i[None, :] >= i[:, None], 0.0, NEGM).astype(np.float32)
    nmb = np.where(i[None, :] <= i[:, None], 0.0, NEGM).astype(np.float32)
    mA = np.where(i[:, None] >= i[None, :], 0.0, NEGM).astype(np.float32)
    mB = np.where(i[:, None] <= i[None, :], 0.0, NEGM).astype(np.float32)
    eye = np.eye(128, dtype=np.float32)
    mA0 = mA.copy()
    mA0[0:64, :] = NEGM
    mB1 = mB.copy()
    mB1[64:128, :] = NEGM
    return {"maskA0": np.ascontiguousarray(np.tile(mA0, (1, 4))), "maskB1": np.ascontiguousarray(np.tile(mB1, (1, 4))),"cosT": cosT, "sinT": sinT, "rotT": rotT, "ident": eye,
            "onesf": np.ones((128, 128), np.float32), "U": U, "Lo": Lo, "nU": -U, "nLo": -Lo, "nmf": nmf, "nmb": nmb,
            "maskA": np.ascontiguousarray(np.tile(mA, (1, 4))), "maskB": np.ascontiguousarray(np.tile(mB, (1, 4)))}


def prep_weights(inp):
    w = {}
    f = lambda a: np.asarray(a, dtype=np.float32)
    for n, s in WEIGHT_SHAPES.items():
        if n in inp:
            w[n] = np.ascontiguousarray(f(inp[n]).reshape(s))
    if "ssd_conv_w" in inp:
        cw = f(inp["ssd_conv_w"]).reshape(DEPTH, 5, 20, 128)
        w["ssd_conv_wl"] = np.ascontiguousarray(cw.transpose(0, 3, 2, 1).reshape(DEPTH, 128, 100))
        cb = f(inp["ssd_conv_b"]).reshape(DEPTH, 20, 128)
        w["ssd_conv_bl"] = np.ascontiguousarray(cb.transpose(0, 2, 1))
    if "sc_conv_w" in inp:
        sw = f(inp["sc_conv_w"]).reshape(DEPTH, 3, 6, 128)
        w["sc_conv_wl"] = np.ascontiguousarray(sw.transpose(0, 3, 2, 1).reshape(DEPTH, 128, 18))
        sn = f(inp["sc_norm"]).reshape(DEPTH, 6, 128)
        w["sc_norm_l"] = np.ascontiguousarray(sn.transpose(0, 2, 1))
    return w


def kernel(**inputs):
    L = 8192
    xs = [np.asarray(inputs["x_prompt"][i]) for i in range(2)] + [np.asarray(inputs["x_sample"][i]) for i in range(4)]
    xs = xs + [xs[0], xs[1]]
    w = prep_weights(inputs)
    consts = host_consts(L)
    nc = build(L)
    in_maps = []
    for c in range(8):
        m = {"x": np.ascontiguousarray(xs[c], dtype=np.float32)}
        m.update(w)
        m.update(consts)
        in_maps.append(m)
    res = run_bass_kernel_spmd(nc, in_maps, core_ids=list(range(8)))
    ys = [res.results[c]["y"] for c in range(6)]
    return (np.stack(ys[0:2], axis=0).astype(np.float32), np.stack(ys[2:6], axis=0).astype(np.float32))
```

```python
import contextlib
import numpy as np
import ml_dtypes
import concourse.bass as bass
import concourse.mybir as mybir
from concourse.bass_utils import run_bass_kernel_spmd

F32 = mybir.dt.float32
BF16 = mybir.dt.bfloat16
I32 = mybir.dt.int32
U32 = mybir.dt.uint32
AF = mybir.ActivationFunctionType
ALU = mybir.AluOpType
AX = mybir.AxisListType

D_MODEL = 2048
DEPTH = 2
D_SSD = 1536
SSD_HEADS = 24
XBC = 2560
D_SC = 768
D_ATT = 768
D_MIX = 3072
D_IN = 13360
EPS = 1e-6
C_Z = 0
C_XBC = 1536
C_DT = 4096
C_SC = 4144
C_ATT = 6448
N_EXP = 16384
ATT_GROUPS = (0, 1, 2)
ATT_STAGE = 9
PEER_DBG = ""
STORES_ON_ACT = True
ALL_PHASES = ("a", "conv", "ssd", "sc", "att", "wout", "peer", "final")


class Buf:
    __slots__ = ("w", "r", "multi")

    def __init__(self, multi=False):
        self.w = {}
        self.r = {}
        self.multi = multi


class T:
    def __init__(self, t, b=None):
        self.t = t
        self.b = b if b is not None else Buf()

    def __getitem__(self, idx):
        return self.t[idx]


class KB:
    def __init__(self, nc, es):
        self.nc = nc
        self.es = es
        self.eng = {"pe": nc.tensor, "act": nc.scalar, "dve": nc.vector, "pool": nc.gpsimd, "sp": nc.sync}
        self.sems = {}
        self.cnt = {}
        for e in ("pe", "act", "dve", "pool"):
            self.sems[e] = es.enter_context(nc.semaphore("c_" + e))
            self.cnt[e] = 0
        self.waited = {e: {} for e in self.eng}
        self.nd = 0
        self.uid = 0
        self.free = []
        self.free_sw = []
        self.scopes = []

    def newsem(self, sw=False):
        fl = self.free_sw if sw else self.free
        if fl:
            key = fl.pop()
        else:
            key = ("dw%d" if sw else "d%d") % self.nd
            self.nd += 1
            self.sems[key] = self.es.enter_context(self.nc.semaphore(key))
            self.cnt[key] = 0
        for sc in self.scopes:
            sc.append(key)
        return key

    def push(self):
        self.scopes.append([])

    def pop(self):
        sc = self.scopes.pop()
        for key in sc:
            fl = self.free_sw if key.startswith("dw") else self.free
            if key not in fl:
                fl.append(key)

    def _deps(self, e, reads, writes):
        need = {}
        for b in reads:
            for s, v in b.w.items():
                if v > need.get(s, 0):
                    need[s] = v
        for b in writes:
            if not b.multi:
                for s, v in b.w.items():
                    if v > need.get(s, 0):
                        need[s] = v
            for s, v in b.r.items():
                if v > need.get(s, 0):
                    need[s] = v
        wd = self.waited[e]
        for s, v in need.items():
            if s == e and e == "pe":
                continue
            if wd.get(s, 0) >= v:
                continue
            if s[0] == "d":
                v = self.cnt[s]
            self.eng[e].wait_ge(self.sems[s], v)
            wd[s] = v

    def _rec(self, key, val, reads, writes):
        for b in reads:
            b.r[key] = val
        for b in writes:
            if b.multi:
                b.w[key] = val
            else:
                b.w = {key: val}
            b.r = {}

    def op(self, e, reads, writes, fn):
        reads = [x.b if isinstance(x, T) else x for x in reads]
        writes = [x.b if isinstance(x, T) else x for x in writes]
        self._deps(e, reads, writes)
        ins = fn(self.eng[e])
        self.cnt[e] += 1
        ins.then_inc(self.sems[e], 1)
        self._rec(e, self.cnt[e], reads, writes)
        return ins

    def dma(self, q, sem, out, in_, reads, writes, indirect=None, sync=False, **kw):
        reads = [x.b if isinstance(x, T) else x for x in reads]
        writes = [x.b if isinstance(x, T) else x for x in writes]
        if q == "sp" and STORES_ON_ACT and any(b.multi for b in writes):
            q = "act"
        self._deps(q, reads, writes)
        if indirect is not None:
            ins = self.eng[q].indirect_dma_start(out=out, out_offset=None, in_=in_, in_offset=indirect, **kw)
        else:
            ins = self.eng[q].dma_start(out=out, in_=in_, **kw)
        self.cnt[sem] += 16
        ins.then_inc(self.sems[sem], 16)
        self._rec(sem, self.cnt[sem], reads, writes)
        if sync:
            self.eng[q].wait_ge(self.sems[sem], self.cnt[sem])
            self.waited[q][sem] = self.cnt[sem]
        return ins

    def barrier(self):
        for e in self.eng:
            wd = self.waited[e]
            for s, v in self.cnt.items():
                if s == e or v == 0 or wd.get(s, 0) >= v:
                    continue
                self.eng[e].wait_ge(self.sems[s], v)
                wd[s] = v

    def sb(self, es, shape, dt, name=None):
        self.uid += 1
        t = es.enter_context(self.nc.sbuf_tensor("%s_%d" % (name or "t", self.uid), list(shape), dt))
        return T(t)

    def ps(self, es, shape, dt, name=None):
        self.uid += 1
        t = es.enter_context(self.nc.psum_tensor("%s_%d" % (name or "p", self.uid), list(shape), dt))
        return T(t)


class Ring:
    def __init__(self, k, es, n, shape, dt, name, dma=True, sw=False):
        self.tiles = [k.sb(es, shape, dt, name) for _ in range(n)]
        self.sems = [k.newsem(sw) for _ in range(n)] if dma else [None] * n
        self.i = 0

    def next(self):
        t, s = self.tiles[self.i], self.sems[self.i]
        self.i = (self.i + 1) % len(self.tiles)
        return t, s


def rmsnorm_rstd(k, ssq, rstd, n):
    k.op("dve", [ssq], [rstd], lambda e: e.tensor_scalar(out=rstd[:, 0:1], in0=ssq[:, 0:1], scalar1=1.0 / n,
                                                         scalar2=EPS, op0=ALU.mult, op1=ALU.add))
    k.op("act", [rstd], [rstd], lambda e: e.activation(out=rstd[:, 0:1], in_=rstd[:, 0:1], func=AF.Sqrt))
    k.op("dve", [rstd], [rstd], lambda e: e.reciprocal(out=rstd[:, 0:1], in_=rstd[:, 0:1]))


def phase_a(k, L, lyr, xsrc, xsrc_b, W, S, C):
    nc = k.nc
    TB = 1024 if L >= 1024 else L
    NT = TB // 128
    k.push()
    with contextlib.ExitStack() as es:
        gB = k.sb(es, [128, D_MODEL], F32, "gB")
        gsem = k.newsem()
        k.dma("sp", gsem, gB[:], W["norm_mix"][lyr:lyr + 1, :].broadcast_to([128, D_MODEL]), [], [gB], sync=True)
        xr = Ring(k, es, 2, [128, D_MODEL], F32, "xt")
        junk = k.sb(es, [128, D_MODEL], BF16, "junk")
        hb = [k.sb(es, [128, D_MODEL], BF16, "hb") for _ in range(2)]
        ssq = [k.sb(es, [128, 1], F32, "ssq") for _ in range(2)]
        rstd = [k.sb(es, [128, 1], F32, "rstd") for _ in range(2)]
        hT = k.sb(es, [128, 16, TB], BF16, "hT")
        hTb = [Buf() for _ in range(NT)]
        wf = Ring(k, es, 2, [128, 16, 512], F32, "wf")
        wb = [k.sb(es, [128, 16, 512], BF16, "wb") for _ in range(2)]
        ev = Ring(k, es, 3, [128, 512], F32, "ev")
        evb = Ring(k, es, 3, [128, 512], BF16, "evb")
        cs = Ring(k, es, 2, [128, 512], F32, "cos")
        sn = Ring(k, es, 2, [128, 512], F32, "sin")
        qsb = [k.sb(es, [128, 512], F32, "qsb") for _ in range(2)]
        t1 = [k.sb(es, [128, 512], F32, "t1") for _ in range(2)]
        t2 = [k.sb(es, [128, 512], F32, "t2") for _ in range(2)]
        pst = [k.ps(es, [128, 1024], BF16, "pst") for _ in range(2)]
        psm = [k.ps(es, [128, 512], F32, "psm") for _ in range(4)]
        psr = [k.ps(es, [128, 512], F32, "psr") for _ in range(2)]
        identb = C["identb"]
        rotT = C["rotT"]
        wi = 0
        mi = 0
        ri = 0
        segs = [("z", C_Z, 1536, 512), ("xbc", C_XBC, 2560, 512), ("dt", C_DT, 48, 48),
                ("sc", C_SC, 2304, 384), ("q", C_ATT, 2304, 384), ("k", C_ATT + 2304, 2304, 384),
                ("v", C_ATT + 4608, 2304, 384)]
        for sbk in range(L // TB):
            tok0 = sbk * TB
            for tt in range(NT):
                xt, xs = xr.next()
                k.dma("sp", xs, xt[:], xsrc[tok0 + tt * 128: tok0 + (tt + 1) * 128, :], [xsrc_b], [xt])
                p = tt % 2
                k.op("act", [xt], [junk, ssq[p]], lambda e: e.activation(out=junk[:], in_=xt[:], func=AF.Square,
                                                                          accum_out=ssq[p][:, 0:1]))
                rmsnorm_rstd(k, ssq[p], rstd[p], D_MODEL)
                k.op("dve", [xt, rstd[p], gB], [hb[p]], lambda e: e.scalar_tensor_tensor(
                    out=hb[p][:], in0=xt[:], scalar=rstd[p][:, 0:1], in1=gB[:], op0=ALU.mult, op1=ALU.mult))
                for half in range(2):
                    pt = pst[half]
                    for j in range(8):
                        kc = half * 8 + j
                        k.op("pe", [hb[p], identb], [pt], lambda e: e.transpose(
                            out=pt[:, j * 128:(j + 1) * 128], in_=hb[p][:, kc * 128:(kc + 1) * 128],
                            identity=identb[:]))
                    eng = "act" if half == 0 else "dve"
                    src = pt[:, :].rearrange("p (j t) -> p j t", j=8)
                    dst = hT[:, half * 8:(half + 1) * 8, tt * 128:(tt + 1) * 128]
                    if eng == "act":
                        k.op("act", [pt], [hTb[tt]], lambda e: e.copy(out=dst, in_=src))
                    else:
                        k.op("dve", [pt], [hTb[tt]], lambda e: e.tensor_copy(out=dst, in_=src))
            for kind, c0, ncs, bw in segs:
                for blk in range(ncs // bw):
                    cb = c0 + blk * bw
                    wft, wfs = wf.next()
                    wbt = wb[wi % 2]
                    wi += 1
                    k.dma("sp", wfs, wft[:, :, 0:bw],
                          W["w_in"][lyr, :, cb:cb + bw].rearrange("(kc p) c -> p kc c", p=128), [], [wft])
                    k.op("pool", [wft], [wbt], lambda e: e.tensor_copy(out=wbt[:, :, 0:bw], in_=wft[:, :, 0:bw]))
                    if kind in ("z", "dt", "v"):
                        for tt in range(NT):
                            pm = psm[mi % 4]
                            mi += 1
                            for kc in range(16):
                                k.op("pe", [hTb[tt], wbt], [pm], lambda e: e.matmul(
                                    pm[:, 0:bw], lhsT=hT[:, kc, tt * 128:(tt + 1) * 128], rhs=wbt[:, kc, 0:bw],
                                    start=(kc == 0), stop=(kc == 15)))
                            r0 = tok0 + tt * 128
                            if kind == "v":
                                et, esm = evb.next()
                                k.op("act", [pm], [et], lambda e: e.copy(out=et[:, 0:bw], in_=pm[:, 0:bw]))
                                k.dma("sp", esm, S["V"][r0:r0 + 128, blk * bw:(blk + 1) * bw], et[:, 0:bw],
                                      [et], [S["V_b"]])
                            else:
                                et, esm = ev.next()
                                k.op("act", [pm], [et], lambda e: e.copy(out=et[:, 0:bw], in_=pm[:, 0:bw]))
                                dst = S["Z"] if kind == "z" else S["DT"]
                                dstb = S["Z_b"] if kind == "z" else S["DT_b"]
                                k.dma("sp", esm, dst[r0:r0 + 128, blk * bw:(blk + 1) * bw], et[:, 0:bw],
                                      [et], [dstb])
                    else:
                        for tb in range(TB // 512):
                            t0 = tok0 + tb * 512
                            hdeps = [hTb[tb * 4 + i] for i in range(4)]
                            if kind in ("q", "k"):
                                ct, csm = cs.next()
                                st, ssm = sn.next()
                                k.dma("sp", csm, ct[:], C["cosT"][:, t0:t0 + 512], [], [ct])
                                k.dma("sp", ssm, st[:], C["sinT"][:, t0:t0 + 512], [], [st])
                            for ch in range(bw // 128):
                                pm = psm[mi % 4]
                                mi += 1
                                for kc in range(16):
                                    k.op("pe", hdeps + [wbt], [pm], lambda e: e.matmul(
                                        pm[:, :], lhsT=wbt[:, kc, ch * 128:(ch + 1) * 128],
                                        rhs=hT[:, kc, tb * 512:(tb + 1) * 512], start=(kc == 0), stop=(kc == 15)))
                                row = blk * bw + ch * 128
                                if kind in ("xbc", "sc"):
                                    et, esm = ev.next()
                                    k.op("act", [pm], [et], lambda e: e.copy(out=et[:], in_=pm[:]))
                                    dst = S["XBCT"] if kind == "xbc" else S["SCT"]
                                    dstb = S["XBCT_b"] if kind == "xbc" else S["SCT_b"]
                                    k.dma("sp", esm, dst[row:row + 128, t0:t0 + 512], et[:], [et], [dstb])
                                else:
                                    r = ri % 2
                                    ri += 1
                                    pr = psr[r]
                                    k.op("act", [pm], [qsb[r]], lambda e: e.copy(out=qsb[r][:], in_=pm[:]))
                                    k.op("pe", [qsb[r], rotT], [pr], lambda e: e.matmul(
                                        pr[:, :], lhsT=rotT[:], rhs=qsb[r][:], start=True, stop=True))
                                    k.op("pool", [qsb[r], ct], [t1[r]], lambda e: e.tensor_tensor(
                                        out=t1[r][:], in0=qsb[r][:], in1=ct[:], op=ALU.mult))
                                    k.op("dve", [pr, st], [t2[r]], lambda e: e.tensor_tensor(
                                        out=t2[r][:], in0=pr[:], in1=st[:], op=ALU.mult))
                                    et, esm = evb.next()
                                    k.op("dve", [t1[r], t2[r]], [et], lambda e: e.tensor_tensor(
                                        out=et[:], in0=t1[r][:], in1=t2[r][:], op=ALU.add))
                                    dst = S["QT"] if kind == "q" else S["KT"]
                                    dstb = S["QT_b"] if kind == "q" else S["KT_b"]
                                    k.dma("sp", esm, dst[row:row + 128, t0:t0 + 512], et[:], [et], [dstb])
        k.barrier()
    k.pop()


def phase_final(k, L, xsrc, xsrc_b, W, y, y_b):
    k.push()
    with contextlib.ExitStack() as es:
        gB = k.sb(es, [128, D_MODEL], F32, "gB")
        gsem = k.newsem()
        k.dma("sp", gsem, gB[:], W["norm_final"][0:1, :].broadcast_to([128, D_MODEL]), [], [gB], sync=True)
        xr = Ring(k, es, 2, [128, D_MODEL], F32, "xt")
        yr = Ring(k, es, 2, [128, D_MODEL], F32, "yt")
        junk = k.sb(es, [128, D_MODEL], BF16, "junk")
        ssq = [k.sb(es, [128, 1], F32, "ssq") for _ in range(2)]
        rstd = [k.sb(es, [128, 1], F32, "rstd") for _ in range(2)]
        for tt in range(L // 128):
            xt, xs = xr.next()
            yt, ys = yr.next()
            p = tt % 2
            k.dma("sp", xs, xt[:], xsrc[tt * 128:(tt + 1) * 128, :], [xsrc_b], [xt])
            k.op("act", [xt], [junk, ssq[p]], lambda e: e.activation(out=junk[:], in_=xt[:], func=AF.Square,
                                                                      accum_out=ssq[p][:, 0:1]))
            rmsnorm_rstd(k, ssq[p], rstd[p], D_MODEL)
            k.op("dve", [xt, rstd[p], gB], [yt], lambda e: e.scalar_tensor_tensor(
                out=yt[:], in0=xt[:], scalar=rstd[p][:, 0:1], in1=gB[:], op0=ALU.mult, op1=ALU.mult))
            k.dma("sp", ys, y[tt * 128:(tt + 1) * 128, :], yt[:], [yt], [y_b])
        k.barrier()
    k.pop()


def phase_conv(k, L, lyr, W, S, C):
    LB = min(L, 4096)
    identf, identb = C["identf"], C["identb"]
    k.push()
    with contextlib.ExitStack() as es:
        cw = k.sb(es, [128, 100], F32, "cw")
        cbias = k.sb(es, [128, 20], F32, "cbias")
        sem = k.newsem()
        k.dma("sp", sem, cw[:], W["ssd_conv_wl"][lyr], [], [cw], sync=True)
        k.dma("sp", sem, cbias[:], W["ssd_conv_bl"][lyr], [], [cbias], sync=True)
        xin = Ring(k, es, 2, [128, LB + 4], F32, "cin")
        acc = k.sb(es, [128, LB], F32, "cacc")
        so = Ring(k, es, 2, [128, LB], F32, "so")
        sob = Ring(k, es, 2, [128, LB], BF16, "sob")
        tr = Ring(k, es, 3, [128, 4, 128], F32, "tr")
        trb = Ring(k, es, 3, [128, 4, 128], BF16, "trb")
        ptr = [k.ps(es, [128, 512], F32, "ptr") for _ in range(2)]
        ptb = [k.ps(es, [128, 512], BF16, "ptb") for _ in range(2)]
        pi = 0
        for c in range(20):
            for lb in range(L // LB):
                t0 = lb * LB
                xt, xs = xin.next()
                lo = 2 if t0 == 0 else 0
                hi = 2 if t0 + LB == L else 0
                if lo:
                    k.op("pool", [], [xt], lambda e: e.memset(xt[:, 0:2], 0.0))
                if hi:
                    k.op("pool", [], [xt], lambda e: e.memset(xt[:, LB + 2:LB + 4], 0.0))
                k.dma("sp", xs, xt[:, lo:LB + 4 - hi], S["XBCT"][c * 128:(c + 1) * 128, t0 - 2 + lo:t0 + LB + 2 - hi],
                      [S["XBCT_b"]], [xt])
                k.op("dve", [xt, cw, cbias], [acc], lambda e: e.tensor_scalar(
                    out=acc[:], in0=xt[:, 0:LB], scalar1=cw[:, c * 5:c * 5 + 1], scalar2=cbias[:, c:c + 1],
                    op0=ALU.mult, op1=ALU.add))
                for t in range(1, 5):
                    k.op("dve", [xt, cw, acc], [acc], lambda e: e.scalar_tensor_tensor(
                        out=acc[:], in0=xt[:, t:t + LB], scalar=cw[:, c * 5 + t:c * 5 + t + 1], in1=acc[:],
                        op0=ALU.mult, op1=ALU.add))
                if c < 12:
                    st, ssm = so.next()
                    k.op("act", [acc], [st], lambda e: e.activation(out=st[:], in_=acc[:], func=AF.Silu))
                    for tg in range(LB // 512):
                        pt = ptr[pi % 2]
                        pi += 1
                        for i in range(4):
                            k.op("pe", [st, identf], [pt], lambda e: e.transpose(
                                out=pt[:, i * 128:(i + 1) * 128], in_=st[:, (tg * 4 + i) * 128:(tg * 4 + i + 1) * 128],
                                identity=identf[:]))
                        tt, ts = tr.next()
                        k.op("act", [pt], [tt], lambda e: e.copy(out=tt[:], in_=pt[:, :].rearrange("p (a c) -> p a c", a=4)))
                        r0 = t0 + tg * 512
                        k.dma("sp", ts, S["X"][r0:r0 + 512, c * 128:(c + 1) * 128].rearrange("(a p) c -> p a c", p=128),
                              tt[:], [tt], [S["X_b"]])
                else:
                    st, ssm = sob.next()
                    k.op("act", [acc], [st], lambda e: e.activation(out=st[:], in_=acc[:], func=AF.Silu))
                    name = "BT" if c < 16 else "CT"
                    rr = (c - 12) % 4
                    k.dma("sp", ssm, S[name][rr * 128:(rr + 1) * 128, t0:t0 + LB], st[:], [st], [S[name + "_b"]])
                    if c < 16:
                        for tg in range(LB // 512):
                            pt = ptb[pi % 2]
                            pi += 1
                            for i in range(4):
                                k.op("pe", [st, identb], [pt], lambda e: e.transpose(
                                    out=pt[:, i * 128:(i + 1) * 128],
                                    in_=st[:, (tg * 4 + i) * 128:(tg * 4 + i + 1) * 128], identity=identb[:]))
                            tt, ts = trb.next()
                            k.op("dve", [pt], [tt], lambda e: e.tensor_copy(
                                out=tt[:], in_=pt[:, :].rearrange("p (a c) -> p a c", a=4)))
                            r0 = t0 + tg * 512
                            k.dma("sp", ts,
                                  S["B"][r0:r0 + 512, rr * 128:(rr + 1) * 128].rearrange("(a p) c -> p a c", p=128),
                                  tt[:], [tt], [S["B_b"]])
        k.barrier()
    k.pop()


SSD_RUNS = {0: [(0, 0, 4)], 1: [(0, 4, 6), (1, 6, 8)], 2: [(1, 8, 12)], 3: [(2, 12, 16)],
            4: [(2, 16, 18), (3, 18, 20)], 5: [(3, 20, 24)]}


def phase_ssd(k, L, lyr, W, S, C):
    NCH = L // 128
    identf, onesf, onec = C["identf"], C["onesf"], C["onec"]
    k.push()
    with contextlib.ExitStack() as es:
        sem = k.newsem()
        Abc = k.sb(es, [128, 48], F32, "Abc")
        dtb = k.sb(es, [128, 48], F32, "dtb")
        Dsk = k.sb(es, [128, 24], F32, "Dsk")
        normw = k.sb(es, [128, D_SSD], F32, "normw")
        k.dma("sp", sem, Abc[:], W["ssd_a_log"][lyr:lyr + 1, :].broadcast_to([128, 48]), [], [Abc], sync=True)
        k.dma("sp", sem, dtb[:], W["ssd_dt_bias"][lyr:lyr + 1, :].broadcast_to([128, 48]), [], [dtb], sync=True)
        k.dma("sp", sem, Dsk[:], W["ssd_d"][lyr:lyr + 1, :].broadcast_to([128, 24]), [], [Dsk], sync=True)
        k.dma("sp", sem, normw[:], W["ssd_norm"][lyr:lyr + 1, :].broadcast_to([128, D_SSD]), [], [normw], sync=True)
        k.op("act", [Abc], [Abc], lambda e: e.activation(out=Abc[:], in_=Abc[:], func=AF.Exp))
        k.op("dve", [Abc], [Abc], lambda e: e.tensor_scalar(out=Abc[:], in0=Abc[:], scalar1=-1.0, scalar2=None,
                                                            op0=ALU.mult))
        xr = Ring(k, es, 2, [128, 24, 64], F32, "xk")
        zr = Ring(k, es, 2, [128, D_SSD], F32, "zk")
        ybr = Ring(k, es, 2, [128, D_SSD], F32, "ybk")
        dtr = Ring(k, es, 2, [128, 48], F32, "dtk")
        bkr = Ring(k, es, 2, [128, 512], BF16, "bk")
        btr = Ring(k, es, 2, [128, 4, 128], BF16, "btk")
        ctr = Ring(k, es, 2, [128, 4, 128], BF16, "ctk")
        dts = k.sb(es, [128, 48], F32, "dts")
        dt = k.sb(es, [128, 48], F32, "dt")
        a = k.sb(es, [128, 24], F32, "a")
        sm = k.sb(es, [128, 64], F32, "sm")
        eac = k.sb(es, [128, 24], F32, "eac")
        dend = k.sb(es, [128, 24], F32, "dend")
        cdec = k.sb(es, [128, 24], F32, "cdec")
        w2 = k.sb(es, [128, 24], F32, "w2")
        xdt = k.sb(es, [128, D_SSD], BF16, "xdt")
        xde = k.sb(es, [128, D_SSD], BF16, "xde")
        cbs = k.sb(es, [128, 4, 128], F32, "cbs")
        rhs1 = k.sb(es, [128, 24, 128], F32, "rhs1")
        dec = [k.sb(es, [128, 4, 128], F32, "dec") for _ in range(3)]
        MT = [k.sb(es, [128, 12, 128], BF16, "MT") for _ in range(2)]
        H = [k.sb(es, [128, 768], F32, "H") for _ in range(2)]
        Hb = [k.sb(es, [128, 768], BF16, "Hb") for _ in range(2)]
        yo = k.sb(es, [128, 24, 64], F32, "yo")
        yd = k.sb(es, [128, 24, 64], F32, "yd")
        ydb = [Buf(), Buf()]
        ystore = Ring(k, es, 2, [128, D_SSD], F32, "ystore")
        y1 = k.sb(es, [128, D_SSD], F32, "y1")
        sz = k.sb(es, [128, D_SSD], F32, "sz")
        junk = k.sb(es, [128, 384], F32, "junk")
        gss = k.sb(es, [128, 4], F32, "gss")
        ytr = Ring(k, es, 2, [128, 12, 128], BF16, "ytr")
        big = [k.ps(es, [128, 1024], F32, "big") for _ in range(2)]
        pseg = [k.ps(es, [128, 512], F32, "pseg") for _ in range(2)]
        pcb = k.ps(es, [128, 512], F32, "pcb")
        ptr = k.ps(es, [128, 512], F32, "ptr")
        st = {"big": 0, "seg": 0, "dec": 0}

        def nbig():
            st["big"] += 1
            return big[st["big"] % 2]

        for d in (1, 0):
            Td = C["U"] if d == 0 else C["Lo"]
            nTd = C["nU"] if d == 0 else C["nLo"]
            nm = C["nmf"] if d == 0 else C["nmb"]
            for hf in range(2):
                k.op("pool", [], [H[hf]], lambda e: e.memset(H[hf][:], 0.0))
                k.op("pool", [], [Hb[hf]], lambda e: e.memset(Hb[hf][:], 0.0))
            order = range(NCH) if d == 0 else range(NCH - 1, -1, -1)
            for c in order:
                r0 = c * 128
                xk, s1 = xr.next()
                k.dma("sp", s1, xk[:], S["X"][r0:r0 + 128, :].rearrange("p (h d) -> p h d", h=24), [S["X_b"]], [xk])
                dtk, s2 = dtr.next()
                k.dma("sp", s2, dtk[:], S["DT"][r0:r0 + 128, :], [S["DT_b"]], [dtk])
                bk, s3 = bkr.next()
                k.dma("sp", s3, bk[:], S["B"][r0:r0 + 128, :], [S["B_b"]], [bk])
                btk, s4 = btr.next()
                k.dma("sp", s4, btk[:], S["BT"][:, r0:r0 + 128].rearrange("(g n) t -> n g t", n=128), [S["BT_b"]], [btk])
                ctk, s5 = ctr.next()
                k.dma("sp", s5, ctk[:], S["CT"][:, r0:r0 + 128].rearrange("(g n) t -> n g t", n=128), [S["CT_b"]], [ctk])
                if d == 0:
                    zk, s6 = zr.next()
                    k.dma("sp", s6, zk[:], S["Z"][r0:r0 + 128, :], [S["Z_b"]], [zk])
                    ybk, s7 = ybr.next()
                    k.dma("sp", s7, ybk[:], S["YB"][r0:r0 + 128, :], [S["YB_b"]], [ybk])
                k.op("dve", [dtk, dtb], [dts], lambda e: e.tensor_tensor(out=dts[:], in0=dtk[:], in1=dtb[:], op=ALU.add))
                k.op("act", [dts], [dts], lambda e: e.activation(out=dts[:], in_=dts[:], func=AF.Exp))
                k.op("act", [dts, onec], [dt], lambda e: e.activation(out=dt[:], in_=dts[:], func=AF.Ln,
                                                                      bias=onec[:, 0:1], scale=1.0))
                dtd = dt[:, d * 24:(d + 1) * 24]
                k.op("dve", [dt, Abc], [a], lambda e: e.tensor_tensor(out=a[:], in0=dtd, in1=Abc[:, d * 24:(d + 1) * 24],
                                                                      op=ALU.mult))
                pb = nbig()
                k.op("pe", [Td, a], [pb], lambda e: e.matmul(pb[:, 0:24], lhsT=Td[:], rhs=a[:], start=True, stop=True))
                k.op("pe", [onesf, a], [pb], lambda e: e.matmul(pb[:, 24:48], lhsT=onesf[:], rhs=a[:], start=True, stop=True))
                k.op("act", [pb], [sm], lambda e: e.copy(out=sm[:, 0:48], in_=pb[:, 0:48]))
                k.op("act", [sm], [eac], lambda e: e.activation(out=eac[:], in_=sm[:, 0:24], func=AF.Exp))
                k.op("dve", [sm], [dend], lambda e: e.tensor_tensor(out=dend[:], in0=sm[:, 24:48], in1=sm[:, 0:24],
                                                                   op=ALU.subtract))
                k.op("act", [dend], [dend], lambda e: e.activation(out=dend[:], in_=dend[:], func=AF.Exp))
                k.op("act", [sm], [cdec], lambda e: e.activation(out=cdec[:], in_=sm[:, 24:48], func=AF.Exp))
                k.op("dve", [dt, dend], [w2], lambda e: e.tensor_tensor(out=w2[:], in0=dtd, in1=dend[:], op=ALU.mult))
                k.op("dve", [xk, dt], [xdt], lambda e: e.tensor_tensor(
                    out=xdt[:, :].rearrange("p (h d) -> p h d", d=64), in0=xk[:], in1=dtd.unsqueeze(2).to_broadcast([128, 24, 64]), op=ALU.mult))
                k.op("pool", [xk, w2], [xde], lambda e: e.tensor_tensor(
                    out=xde[:, :].rearrange("p (h d) -> p h d", d=64), in0=xk[:], in1=w2[:, :].unsqueeze(2).to_broadcast([128, 24, 64]), op=ALU.mult))
                for g in range(4):
                    k.op("pe", [btk, ctk], [pcb], lambda e: e.matmul(
                        pcb[:, g * 128:(g + 1) * 128], lhsT=btk[:, g, :], rhs=ctk[:, g, :], start=True, stop=True))
                k.op("act", [pcb], [cbs], lambda e: e.copy(out=cbs[:], in_=pcb[:, :].rearrange("p (g l) -> p g l", g=4)))
                k.op("pool", [a, Td], [rhs1], lambda e: e.tensor_tensor(
                    out=rhs1[:], in0=a[:, :].unsqueeze(2).to_broadcast([128, 24, 128]),
                    in1=Td[:, :].unsqueeze(1).to_broadcast([128, 24, 128]), op=ALU.mult))
                for hf in range(2):
                    h0 = hf * 12
                    pb = nbig()
                    for u in range(3 * hf, 3 * hf + 3):
                        for (g, ha, hb_) in SSD_RUNS[u]:
                            k.op("pe", [ctk, Hb[hf]], [pb], lambda e: e.matmul(
                                pb[:, (ha - h0) * 64:(hb_ - h0) * 64], lhsT=ctk[:, g, :],
                                rhs=Hb[hf][:, (ha - h0) * 64:(hb_ - h0) * 64], start=True, stop=True))
                    k.op("dve", [pb, eac], [yo], lambda e: e.tensor_tensor(
                        out=yo[:, h0:h0 + 12, :], in0=pb[:, 0:768].rearrange("p (h d) -> p h d", h=12),
                        in1=eac[:, h0:h0 + 12].unsqueeze(2).to_broadcast([128, 12, 64]), op=ALU.mult))
                    mt = MT[hf]
                    for u in range(3 * hf, 3 * hf + 3):
                        pg = pseg[st["seg"] % 2]
                        st["seg"] += 1
                        pg3 = pg[:, :].rearrange("p (h l) -> p h l", h=4)
                        k.op("pe", [onesf, rhs1], [pg], lambda e: e.matmul(
                            pg3, lhsT=onesf[:], rhs=rhs1[:, 4 * u:4 * u + 4, :], start=True, stop=False))
                        k.op("pe", [nTd, a], [pg], lambda e: e.matmul(
                            pg3, lhsT=nTd[:], rhs=a[:, 4 * u:4 * u + 4].unsqueeze(2).to_broadcast([128, 4, 128]),
                            start=False, stop=False))
                        k.op("pe", [identf, nm], [pg], lambda e: e.matmul(
                            pg3, lhsT=identf[:], rhs=nm[:, :].unsqueeze(1).to_broadcast([128, 4, 128]),
                            start=False, stop=True))
                        dc = dec[st["dec"] % 3]
                        st["dec"] += 1
                        k.op("act", [pg], [dc], lambda e: e.activation(out=dc[:], in_=pg3, func=AF.Exp))
                        for (g, ha, hb_) in SSD_RUNS[u]:
                            n = hb_ - ha
                            k.op("dve", [dc, cbs], [mt], lambda e: e.tensor_tensor(
                                out=mt[:, ha - h0:hb_ - h0, :], in0=dc[:, ha - 4 * u:hb_ - 4 * u, :],
                                in1=cbs[:, g:g + 1, :].to_broadcast([128, n, 128]), op=ALU.mult))
                    pb2 = nbig()
                    for h in range(h0, h0 + 12):
                        k.op("pe", [mt, xdt], [pb2], lambda e: e.matmul(
                            pb2[:, (h - h0) * 64:(h - h0 + 1) * 64], lhsT=mt[:, h - h0, :], rhs=xdt[:, h * 64:(h + 1) * 64],
                            start=True, stop=True))
                    k.op("dve", [pb2, yo], [ydb[hf]], lambda e: e.tensor_tensor(
                        out=yd[:, h0:h0 + 12, :], in0=pb2[:, 0:768].rearrange("p (h d) -> p h d", h=12),
                        in1=yo[:, h0:h0 + 12, :], op=ALU.add))
                for hf in range(2):
                    h0 = hf * 12
                    pb3 = nbig()
                    for u in range(3 * hf, 3 * hf + 3):
                        for (g, ha, hb_) in SSD_RUNS[u]:
                            k.op("pe", [bk, xde], [pb3], lambda e: e.matmul(
                                pb3[:, (ha - h0) * 64:(hb_ - h0) * 64], lhsT=bk[:, g * 128:(g + 1) * 128],
                                rhs=xde[:, ha * 64:hb_ * 64], start=True, stop=True))
                    k.op("pool", [H[hf], cdec], [H[hf]], lambda e: e.tensor_tensor(
                        out=H[hf][:, :].rearrange("p (h d) -> p h d", d=64), in0=H[hf][:, :].rearrange("p (h d) -> p h d", d=64),
                        in1=cdec[:, h0:h0 + 12].unsqueeze(2).to_broadcast([128, 12, 64]), op=ALU.mult))
                    k.op("dve", [H[hf], pb3], [H[hf]], lambda e: e.tensor_tensor(
                        out=H[hf][:], in0=H[hf][:], in1=pb3[:, 0:768], op=ALU.add))
                    k.op("act", [H[hf]], [Hb[hf]], lambda e: e.copy(out=Hb[hf][:], in_=H[hf][:]))
                ydf = yd[:, :, :].rearrange("p h d -> p (h d)")
                if d == 1:
                    yt, ysm = ystore.next()
                    k.op("pool", ydb, [yt], lambda e: e.tensor_copy(out=yt[:], in_=ydf))
                    k.dma("sp", ysm, S["YB"][r0:r0 + 128, :], yt[:], [yt], [S["YB_b"]])
                else:
                    k.op("dve", ydb + [ybk], [y1], lambda e: e.tensor_tensor(out=y1[:], in0=ydf, in1=ybk[:], op=ALU.add))
                    k.op("pool", [xk, Dsk], [sz], lambda e: e.tensor_tensor(
                        out=sz[:, :].rearrange("p (h d) -> p h d", h=24), in0=xk[:],
                        in1=Dsk[:, :].unsqueeze(2).to_broadcast([128, 24, 64]), op=ALU.mult))
                    k.op("dve", [y1, sz], [y1], lambda e: e.tensor_tensor(out=y1[:], in0=y1[:], in1=sz[:], op=ALU.add))
                    k.op("act", [zk], [sz], lambda e: e.activation(out=sz[:], in_=zk[:], func=AF.Silu))
                    k.op("dve", [y1, sz], [y1], lambda e: e.tensor_tensor(out=y1[:], in0=y1[:], in1=sz[:], op=ALU.mult))
                    for g in range(4):
                        k.op("act", [y1], [junk, gss], lambda e: e.activation(
                            out=junk[:], in_=y1[:, g * 384:(g + 1) * 384], func=AF.Square, accum_out=gss[:, g:g + 1]))
                    k.op("dve", [gss], [gss], lambda e: e.tensor_scalar(out=gss[:], in0=gss[:], scalar1=1.0 / 384,
                                                                        scalar2=EPS, op0=ALU.mult, op1=ALU.add))
                    k.op("act", [gss], [gss], lambda e: e.activation(out=gss[:], in_=gss[:], func=AF.Sqrt))
                    k.op("dve", [gss], [gss], lambda e: e.reciprocal(out=gss[:], in_=gss[:]))
                    k.op("dve", [y1, gss], [y1], lambda e: e.tensor_tensor(
                        out=y1[:, :].rearrange("p (g c) -> p g c", g=4), in0=y1[:, :].rearrange("p (g c) -> p g c", g=4),
                        in1=gss[:, :].unsqueeze(2).to_broadcast([128, 4, 384]), op=ALU.mult))
                    k.op("pool", [y1, normw], [y1], lambda e: e.tensor_tensor(out=y1[:], in0=y1[:], in1=normw[:], op=ALU.mult))
                    yt, ysm = ytr.next()
                    for q in range(3):
                        for i in range(4):
                            cc = q * 4 + i
                            k.op("pe", [y1, identf], [ptr], lambda e: e.transpose(
                                out=ptr[:, i * 128:(i + 1) * 128], in_=y1[:, cc * 128:(cc + 1) * 128], identity=identf[:]))
                        k.op("act", [ptr], [yt], lambda e: e.copy(
                            out=yt[:, q * 4:(q + 1) * 4, :], in_=ptr[:, :].rearrange("p (a t) -> p a t", a=4)))
                    k.dma("sp", ysm, S["YT"][0:D_SSD, r0:r0 + 128].rearrange("(c p) t -> p c t", p=128), yt[:],
                          [yt], [S["YT_b"]])
        k.barrier()
    k.pop()


def phase_sc(k, L, lyr, W, S, C):
    onesf = C["onesf"]
    TBK = 512
    k.push()
    with contextlib.ExitStack() as es:
        sem = k.newsem()
        cw = k.sb(es, [128, 18], F32, "scw")
        nw = k.sb(es, [128, 6], F32, "scn")
        k.dma("sp", sem, cw[:], W["sc_conv_wl"][lyr], [], [cw], sync=True)
        k.dma("sp", sem, nw[:], W["sc_norm_l"][lyr], [], [nw], sync=True)
        bgr = Ring(k, es, 2, [128, TBK], F32, "bg")
        cgr = Ring(k, es, 2, [128, TBK + 2], F32, "cg")
        hxr = Ring(k, es, 2, [128, TBK + 2], F32, "hx")
        prod = k.sb(es, [128, TBK + 2], F32, "prod")
        yc = [k.sb(es, [128, TBK], F32, "yc") for _ in range(6)]
        ysq = [k.sb(es, [128, TBK], F32, "ysq") for _ in range(2)]
        rst = k.sb(es, [128, TBK], F32, "rst")
        outr = Ring(k, es, 3, [128, TBK], BF16, "sco")
        pss = k.ps(es, [128, 512], F32, "pss")
        for tb in range(L // TBK):
            t0 = tb * TBK
            lo = 1 if t0 == 0 else 0
            hi = 1 if t0 + TBK == L else 0
            for c in range(6):
                bg, s0 = bgr.next()
                cg, s1 = cgr.next()
                hx, s2 = hxr.next()
                k.dma("sp", s0, bg[:], S["SCT"][c * 128:(c + 1) * 128, t0:t0 + TBK], [S["SCT_b"]], [bg])
                for (tl, sm_, base) in ((cg, s1, 768), (hx, s2, 1536)):
                    if lo:
                        k.op("pool", [], [tl], lambda e: e.memset(tl[:, 0:1], 0.0))
                    if hi:
                        k.op("pool", [], [tl], lambda e: e.memset(tl[:, TBK + 1:TBK + 2], 0.0))
                    k.dma("sp", sm_, tl[:, lo:TBK + 2 - hi],
                          S["SCT"][base + c * 128:base + (c + 1) * 128, t0 - 1 + lo:t0 + TBK + 1 - hi],
                          [S["SCT_b"]], [tl])
                k.op("pool", [cg, hx], [prod], lambda e: e.tensor_tensor(out=prod[:], in0=cg[:], in1=hx[:], op=ALU.mult))
                y = yc[c]
                k.op("dve", [prod, cw], [y], lambda e: e.tensor_scalar(
                    out=y[:], in0=prod[:, 0:TBK], scalar1=cw[:, c * 3:c * 3 + 1], scalar2=None, op0=ALU.mult))
                for t in (1, 2):
                    k.op("dve", [prod, cw, y], [y], lambda e: e.scalar_tensor_tensor(
                        out=y[:], in0=prod[:, t:t + TBK], scalar=cw[:, c * 3 + t:c * 3 + t + 1], in1=y[:],
                        op0=ALU.mult, op1=ALU.add))
                k.op("dve", [y, bg], [y], lambda e: e.tensor_tensor(out=y[:], in0=y[:], in1=bg[:], op=ALU.mult))
                q = ysq[c % 2]
                k.op("act", [y], [q], lambda e: e.activation(out=q[:], in_=y[:], func=AF.Square))
                k.op("pe", [onesf, q], [pss], lambda e: e.matmul(pss[:, :], lhsT=onesf[:], rhs=q[:],
                                                                 start=(c == 0), stop=(c == 5)))
            k.op("dve", [pss], [rst], lambda e: e.tensor_scalar(out=rst[:], in0=pss[:], scalar1=1.0 / D_SC, scalar2=EPS,
                                                                op0=ALU.mult, op1=ALU.add))
            k.op("act", [rst], [rst], lambda e: e.activation(out=rst[:], in_=rst[:], func=AF.Sqrt))
            k.op("dve", [rst], [rst], lambda e: e.reciprocal(out=rst[:], in_=rst[:]))
            for c in range(6):
                o, so_ = outr.next()
                k.op("dve", [yc[c], nw, rst], [o], lambda e: e.scalar_tensor_tensor(
                    out=o[:], in0=yc[c][:], scalar=nw[:, c:c + 1], in1=rst[:], op0=ALU.mult, op1=ALU.mult))
                k.dma("sp", so_, S["YT"][D_SSD + c * 128:D_SSD + (c + 1) * 128, t0:t0 + TBK], o[:], [o], [S["YT_b"]])
        k.barrier()
    k.pop()


def phase_att(k, L, lyr, W, S, C):
    identb = C["identb"]
    with contextlib.ExitStack() as es:
        for g, dil in enumerate((1, 4, 16)):
            if g not in ATT_GROUPS:
                continue
            Lsub = L // dil
            nqb = Lsub // 128
            W_ = 128 * dil
            k.push()
            with contextlib.ExitStack() as es2:
                kcr = Ring(k, es2, 3, [128, 2, W_], BF16, "kc")
                qwr = Ring(k, es2, 2, [128, 2, 2, W_], BF16, "qw")
                qwr2 = [k.newsem() for _ in range(2)]
                vcr = [Ring(k, es2, 3, [128, 4, 65], BF16, "vc") for _ in range(dil)]
                ptr_ = Ring(k, es2, 3, [128, 512], BF16, "pT", dma=False)
                otr = Ring(k, es2, 3, [128, 4, 65], F32, "ot")
                pst_ = [k.ps(es2, [128, 512], F32, "pst") for _ in range(3)]
                pso = [k.ps(es2, [128, 512], F32, "pso") for _ in range(2)]
                for r in range(dil):
                    for t, _s in zip(vcr[r].tiles, vcr[r].sems):
                        k.op("pool", [], [t], lambda e: e.memset(t[:], 1.0))
                for t in kcr.tiles + qwr.tiles:
                    k.op("pool", [], [t], lambda e: e.memset(t[:], 0.0))
                cnt = {"st": 0, "o": 0}
                for hg in range(3):
                    row0 = g * 768 + hg * 256
                    kch = {}
                    vch = {}

                    def load_chunk(j):
                        p0 = 128 * j - 64
                        lo = 64 if j == 0 else 0
                        hi = 64 if j == nqb else 128
                        kt, ks = kcr.next()
                        tok_lo = (p0 + lo) * dil
                        tok_hi = (p0 + hi) * dil
                        k.dma("sp", ks, kt[:, :, lo * dil:hi * dil],
                              S["KT"][row0:row0 + 256, tok_lo:tok_hi].rearrange("(hp p) t -> p hp t", p=128),
                              [S["KT_b"]], [kt])
                        kch[j] = kt
                        vs = []
                        for r in range(dil):
                            vt, vsm = vcr[r].next()
                            rows = S["V"][tok_lo:tok_hi, row0:row0 + 256].rearrange("(i r) (h d) -> r i h d", r=dil, d=64)[r]
                            k.dma("sp", vsm, vt[lo:hi, :, 0:64], rows, [S["V_b"]], [vt])
                            vs.append(vt)
                        vch[j] = vs

                    load_chunk(0)
                    for w in range(nqb):
                        load_chunk(w + 1)
                        qs2 = qwr2[qwr.i]
                        qt, qs = qwr.next()
                        qsrc = S["QT"][row0:row0 + 256, w * W_:(w + 1) * W_].rearrange("(hp p) t -> p hp t", p=128)
                        k.dma("sp", qs, qt[0:64, :, 0, :], qsrc[0:64], [S["QT_b"]], [qt])
                        k.dma("sp", qs2, qt[64:128, :, 1, :], qsrc[64:128], [S["QT_b"]], [qt])
                        for r in range(dil):
                            po = pso[cnt["o"] % 2]
                            cnt["o"] += 1
                            for c in (0, 1):
                                if ATT_STAGE < 2:
                                    break
                                j = w + c
                                lo = 64 if j == 0 else 0
                                hi = 64 if j == nqb else 128
                                kt = kch[j]
                                vt = vch[j][r]
                                ps_ = pst_[cnt["st"] % 3]
                                cnt["st"] += 1
                                for hp in range(2):
                                    kv = kt[:, hp, :].rearrange("p (i r) -> p i r", r=dil)[:, :, r]
                                    qv = qt[:, hp, :, :].rearrange("p a (i r) -> p a i r", r=dil)[:, :, :, r]
                                    k.op("pe", [kt, qt], [ps_], lambda e: e.matmul(
                                        ps_[:, hp * 256:(hp + 1) * 256].rearrange("p (a i) -> p a i", a=2), lhsT=kv, rhs=qv,
                                        start=(hp == 0), stop=False))
                                if c == 0:
                                    mk = C["maskA0"] if j == 0 else C["maskA"]
                                else:
                                    mk = C["maskB1"] if j == nqb else C["maskB"]
                                k.op("pe", [identb, mk], [ps_], lambda e: e.matmul(
                                    ps_[:, :], lhsT=identb[:], rhs=mk[:], start=False, stop=True))
                                if ATT_STAGE < 3:
                                    continue
                                pt, _ = ptr_.next()
                                k.op("act", [ps_], [pt], lambda e: e.activation(out=pt[:, :], in_=ps_[:, :],
                                                                               func=AF.Exp, scale=0.125))
                                if ATT_STAGE < 4:
                                    continue
                                for hh in range(4):
                                    k.op("pe", [pt, vt], [po], lambda e: e.matmul(
                                        po[:, hh * 65:(hh + 1) * 65], lhsT=pt[:, hh * 128:(hh + 1) * 128],
                                        rhs=vt[:, hh, :], start=(c == 0 and hh == 0), stop=(c == 1 and hh == 3)))
                            if ATT_STAGE < 5:
                                continue
                            ot, osm = otr.next()
                            k.op("act", [po], [ot], lambda e: e.copy(out=ot[:], in_=po[:, 0:260].rearrange("p (h d) -> p h d", h=4)))
                            dst = S["O%d" % g][w * W_:(w + 1) * W_, hg * 260:(hg + 1) * 260].rearrange(
                                "(i r) (h d) -> r i h d", r=dil, d=65)[r]
                            k.dma("sp", osm, dst, ot[:], [ot], [S["O%d_b" % g]])
                k.barrier()
            k.pop()
        if ATT_STAGE < 6:
            return
        k.push()
        with contextlib.ExitStack() as es2:
            identf = C["identf"]
            sem = k.newsem()
            nw = k.sb(es2, [128, D_ATT], F32, "attn")
            k.dma("sp", sem, nw[:], W["att_norm"][lyr:lyr + 1, :].broadcast_to([128, D_ATT]), [], [nw], sync=True)
            o0 = Ring(k, es2, 2, [128, 12, 65], F32, "o0")
            o1 = Ring(k, es2, 2, [128, 12, 65], F32, "o1")
            o2 = Ring(k, es2, 2, [128, 12, 65], F32, "o2")
            rl = k.sb(es2, [128, 12], F32, "rl")
            ov = k.sb(es2, [128, 12, 64], F32, "ov")
            junk = k.sb(es2, [128, D_ATT], F32, "junk")
            ssq = k.sb(es2, [128, 1], F32, "ssq")
            rstd = k.sb(es2, [128, 1], F32, "rstd")
            ytr = Ring(k, es2, 2, [128, 6, 128], BF16, "aytr")
            ptr = [k.ps(es2, [128, 512], F32, "ptr") for _ in range(2)]
            for tt in range(L // 128):
                r0 = tt * 128
                a0, s0 = o0.next()
                a1, s1 = o1.next()
                a2, s2 = o2.next()
                k.dma("sp", s0, a0[:], S["O0"][r0:r0 + 128, :].rearrange("p (h d) -> p h d", h=12), [S["O0_b"]], [a0])
                k.dma("sp", s1, a1[:], S["O1"][r0:r0 + 128, :].rearrange("p (h d) -> p h d", h=12), [S["O1_b"]], [a1])
                k.dma("sp", s2, a2[:], S["O2"][r0:r0 + 128, :].rearrange("p (h d) -> p h d", h=12), [S["O2_b"]], [a2])
                k.op("pool", [a0, a1], [a0], lambda e: e.tensor_tensor(out=a0[:], in0=a0[:], in1=a1[:], op=ALU.add))
                k.op("dve", [a0, a2], [a0], lambda e: e.tensor_tensor(out=a0[:], in0=a0[:], in1=a2[:], op=ALU.add))
                k.op("dve", [a0], [rl], lambda e: e.reciprocal(out=rl[:], in_=a0[:, :, 64]))
                k.op("dve", [a0, rl], [ov], lambda e: e.tensor_tensor(
                    out=ov[:], in0=a0[:, :, 0:64], in1=rl[:, :].unsqueeze(2).to_broadcast([128, 12, 64]), op=ALU.mult))
                ovf = ov[:, :, :].rearrange("p h d -> p (h d)")
                k.op("act", [ov], [junk, ssq], lambda e: e.activation(out=junk[:], in_=ovf, func=AF.Square,
                                                                      accum_out=ssq[:, 0:1]))
                rmsnorm_rstd(k, ssq, rstd, D_ATT)
                k.op("dve", [ov, rstd, nw], [junk], lambda e: e.scalar_tensor_tensor(
                    out=junk[:], in0=ovf, scalar=rstd[:, 0:1], in1=nw[:], op0=ALU.mult, op1=ALU.mult))
                yt, ysm = ytr.next()
                for q in range(2):
                    pt = ptr[q]
                    n = 4 if q == 0 else 2
                    for i in range(n):
                        cc = q * 4 + i
                        k.op("pe", [junk, identf], [pt], lambda e: e.transpose(
                            out=pt[:, i * 128:(i + 1) * 128], in_=junk[:, cc * 128:(cc + 1) * 128], identity=identf[:]))
                    k.op("act", [pt], [yt], lambda e: e.copy(
                        out=yt[:, q * 4:q * 4 + n, :], in_=pt[:, 0:n * 128].rearrange("p (a t) -> p a t", a=n)))
                k.dma("sp", ysm, S["YT"][2304:3072, r0:r0 + 128].rearrange("(c p) t -> p c t", p=128), yt[:],
                      [yt], [S["YT_b"]])
            k.barrier()
        k.pop()


def phase_wout(k, L, lyr, xsrc, xsrc_b, W, S, C):
    k.push()
    with contextlib.ExitStack() as es:
        wo = k.sb(es, [128, 24, D_MODEL], BF16, "wo")
        stg = Ring(k, es, 2, [128, 2, D_MODEL], F32, "wstg")
        for c4 in range(12):
            st, ss = stg.next()
            k.dma("sp", ss, st[:], W["w_out"][lyr, c4 * 256:(c4 + 1) * 256, :].rearrange("(c p) n -> p c n", p=128), [], [st])
            eng = "pool" if c4 % 2 == 0 else "act"
            if eng == "pool":
                k.op("pool", [st], [wo], lambda e: e.tensor_copy(out=wo[:, c4 * 2:(c4 + 1) * 2, :], in_=st[:]))
            else:
                k.op("act", [st], [wo], lambda e: e.copy(out=wo[:, c4 * 2:(c4 + 1) * 2, :], in_=st[:]))
        ytr = Ring(k, es, 2, [128, 24, 128], BF16, "yT")
        xr = Ring(k, es, 2, [128, D_MODEL], F32, "xo")
        outr = Ring(k, es, 2, [128, D_MODEL], F32, "xn")
        psm = [k.ps(es, [128, 512], F32, "psm") for _ in range(4)]
        for tt in range(L // 128):
            r0 = tt * 128
            yt, ys = ytr.next()
            k.dma("sp", ys, yt[:], S["YT"][:, r0:r0 + 128].rearrange("(c p) t -> p c t", p=128), [S["YT_b"]], [yt])
            xt, xs = xr.next()
            k.dma("sp", xs, xt[:], xsrc[r0:r0 + 128, :], [xsrc_b], [xt])
            ot, osm = outr.next()
            for nb in range(4):
                pm = psm[nb]
                for c in range(24):
                    k.op("pe", [yt, wo], [pm], lambda e: e.matmul(pm[:, :], lhsT=yt[:, c, :], rhs=wo[:, c, nb * 512:(nb + 1) * 512],
                                                                   start=(c == 0), stop=(c == 23)))
                k.op("dve", [pm, xt], [ot], lambda e: e.tensor_tensor(out=ot[:, nb * 512:(nb + 1) * 512], in0=pm[:],
                                                                      in1=xt[:, nb * 512:(nb + 1) * 512], op=ALU.add))
            k.dma("sp", osm, S["XRES"][r0:r0 + 128, :], ot[:], [ot], [S["XRES_b"]])
        k.barrier()
    k.pop()


def peer_tables_bf16(k, lyr, W, S):
    k.push()
    with contextlib.ExitStack() as es:
        stg = Ring(k, es, 3, [128, 4096], F32, "tstg")
        outb = Ring(k, es, 3, [128, 4096], BF16, "tout")
        n = 0
        for name, col0 in (("peer_u", 0), ("peer_v", D_MODEL)):
            dst = "UV"
            src = W[name][lyr * N_EXP:(lyr + 1) * N_EXP, :].rearrange("(c p two) d -> c p (two d)", p=128, two=2)
            dv = S["UV"][:, col0:col0 + D_MODEL].rearrange("(c p two) d -> c p two d", p=128, two=2)
            for c in range(N_EXP // 256):
                st, ss = stg.next()
                ob, os_ = outb.next()
                k.dma("sp", ss, st[:], src[c], [], [st])
                e = ("act", "pool", "dve")[n % 3]
                n += 1
                if e == "act":
                    k.op("act", [st], [ob], lambda en: en.copy(out=ob[:], in_=st[:]))
                else:
                    k.op(e, [st], [ob], lambda en: en.tensor_copy(out=ob[:], in_=st[:]))
                k.dma("sp", os_, dv[c], ob[:, :].rearrange("p (two d) -> p two d", two=2), [ob], [S[dst + "_b"]])
        k.barrier()
    k.pop()


def phase_peer_a(k, L, lyr, W, S, C):
    identf, identb = C["identf"], C["identb"]
    NEG = -1.0e30
    k.push()
    with contextlib.ExitStack() as es:
        sem = k.newsem()
        gB = k.sb(es, [128, D_MODEL], F32, "gB")
        k.dma("sp", sem, gB[:], W["norm_ffn"][lyr:lyr + 1, :].broadcast_to([128, D_MODEL]), [], [gB], sync=True)
        wq = k.sb(es, [128, 16, D_MODEL], BF16, "wq")
        skT = k.sb(es, [128, 16, 128], BF16, "skT")
        pq = [k.ps(es, [128, 512], F32, "pq") for _ in range(4)]
        pqs = [k.ps(es, [128, 512], F32, "pqs") for _ in range(2)]
        pst = [k.ps(es, [128, 1024], BF16, "pst") for _ in range(2)]
        with contextlib.ExitStack() as es1:
            stg = Ring(k, es1, 2, [128, 2, D_MODEL], F32, "wstg")
            for c2 in range(8):
                st, ss = stg.next()
                k.dma("sp", ss, st[:], W["peer_wq"][lyr, c2 * 256:(c2 + 1) * 256, :].rearrange("(c p) n -> p c n", p=128), [], [st])
                if c2 % 2 == 0:
                    k.op("pool", [st], [wq], lambda e: e.tensor_copy(out=wq[:, c2 * 2:(c2 + 1) * 2, :], in_=st[:]))
                else:
                    k.op("act", [st], [wq], lambda e: e.copy(out=wq[:, c2 * 2:(c2 + 1) * 2, :], in_=st[:]))
            skf = k.sb(es1, [128, 16, 128], F32, "skf")
            k.dma("sp", sem, skf[:], W["peer_subkeys"][lyr].rearrange("m n d -> n m d"), [], [skf], sync=True)
            for q4 in range(4):
                pm = pq[q4]
                for i in range(4):
                    m = q4 * 4 + i
                    k.op("pe", [skf, identf], [pm], lambda e: e.transpose(out=pm[:, i * 128:(i + 1) * 128], in_=skf[:, m, :],
                                                                          identity=identf[:]))
                k.op("act", [pm], [skT], lambda e: e.copy(out=skT[:, q4 * 4:(q4 + 1) * 4, :],
                                                          in_=pm[:, :].rearrange("p (a n) -> p a n", a=4)))
            k.barrier()
        xr = Ring(k, es, 2, [128, D_MODEL], F32, "xm")
        ssq = k.sb(es, [128, 1], F32, "ssq")
        rstd = k.sb(es, [128, 1], F32, "rstd")
        hn = k.sb(es, [128, D_MODEL], F32, "hn")
        hnbr = Ring(k, es, 2, [128, D_MODEL], BF16, "hnb")
        eidr = Ring(k, es, 2, [128, 128], I32, "eido")
        gater = Ring(k, es, 2, [128, 8, 16], F32, "gateo")
        hnT = k.sb(es, [128, 16, 128], BF16, "hnT")
        qTb = k.sb(es, [128, 16, 128], BF16, "qTb")
        sc = k.sb(es, [128, 16, 128], F32, "sc")
        scw = k.sb(es, [128, 128], F32, "scw")
        sv = k.sb(es, [128, 16, 16], F32, "sv")
        si = k.sb(es, [128, 16, 16], U32, "si")
        sif = k.sb(es, [128, 16, 16], F32, "sif")
        cand = k.sb(es, [128, 8, 256], F32, "cand")
        cidx = k.sb(es, [128, 8, 256], F32, "cidx")
        cw_ = k.sb(es, [128, 256], F32, "cw_")
        top = k.sb(es, [128, 8, 16], F32, "top")
        zs = k.sb(es, [128, 8], F32, "zs")
        j256 = k.sb(es, [128, 256], F32, "j256")
        eidf = k.sb(es, [128, 128], F32, "eidf")
        eidf_w = Buf(multi=True)
        pre = k.sb(es, [128, 128], F32, "pre")
        pre_w = Buf(multi=True)
        actg = k.sb(es, [128, 128], F32, "actg")

        def stage_a(tt):
            r0 = tt * 128
            eid, eid_s = eidr.next()
            gate, gate_s = gater.next()
            hnb, hnb_s = hnbr.next()
            xm, xs = xr.next()
            k.dma("sp", xs, xm[:], S["XRES"][r0:r0 + 128, :], [S["XRES_b"]], [xm])
            k.op("act", [xm], [hnb, ssq], lambda e: e.activation(out=hnb[:], in_=xm[:], func=AF.Square,
                                                                   accum_out=ssq[:, 0:1]))
            rmsnorm_rstd(k, ssq, rstd, D_MODEL)
            k.op("dve", [xm, rstd, gB], [hn], lambda e: e.scalar_tensor_tensor(
                out=hn[:], in0=xm[:], scalar=rstd[:, 0:1], in1=gB[:], op0=ALU.mult, op1=ALU.mult))
            k.op("act", [hn], [hnb], lambda e: e.copy(out=hnb[:], in_=hn[:]))
            for half in range(2):
                pt = pst[half]
                for j in range(8):
                    kc = half * 8 + j
                    k.op("pe", [hnb, identb], [pt], lambda e: e.transpose(
                        out=pt[:, j * 128:(j + 1) * 128], in_=hnb[:, kc * 128:(kc + 1) * 128], identity=identb[:]))
                k.op("act", [pt], [hnT], lambda e: e.copy(out=hnT[:, half * 8:(half + 1) * 8, :],
                                                          in_=pt[:, :].rearrange("p (j t) -> p j t", j=8)))
            for q4 in range(4):
                pm = pqs[q4 % 2]
                for i in range(4):
                    m = q4 * 4 + i
                    for kc in range(16):
                        k.op("pe", [wq, hnT], [pm], lambda e: e.matmul(
                            pm[:, i * 128:(i + 1) * 128], lhsT=wq[:, kc, m * 128:(m + 1) * 128], rhs=hnT[:, kc, :],
                            start=(kc == 0), stop=(kc == 15)))
                k.op("act", [pm], [qTb], lambda e: e.copy(out=qTb[:, q4 * 4:(q4 + 1) * 4, :],
                                                          in_=pm[:, :].rearrange("p (a t) -> p a t", a=4)))
            for q4 in range(4):
                pm = pqs[q4 % 2]
                for i in range(4):
                    m = q4 * 4 + i
                    k.op("pe", [qTb, skT], [pm], lambda e: e.matmul(
                        pm[:, i * 128:(i + 1) * 128], lhsT=qTb[:, m, :], rhs=skT[:, m, :], start=True, stop=True))
                k.op("act", [pm], [sc], lambda e: e.copy(out=sc[:, q4 * 4:(q4 + 1) * 4, :],
                                                         in_=pm[:, :].rearrange("p (a n) -> p a n", a=4)))
            for m in range(16):
                k.op("dve", [sc], [sv], lambda e: e.max(out=sv[:, m, 0:8], in_=sc[:, m, :]))
                k.op("dve", [sc, sv], [scw], lambda e: e.match_replace(out=scw[:], in_to_replace=sv[:, m, 0:8],
                                                                       in_values=sc[:, m, :], imm_value=NEG))
                k.op("dve", [scw], [sv], lambda e: e.max(out=sv[:, m, 8:16], in_=scw[:]))
                k.op("dve", [sc, sv], [si], lambda e: e.max_index(out=si[:, m, 0:8], in_max=sv[:, m, 0:8], in_values=sc[:, m, :]))
                k.op("dve", [sc, sv], [si], lambda e: e.max_index(out=si[:, m, 8:16], in_max=sv[:, m, 8:16], in_values=sc[:, m, :]))
            k.op("dve", [si], [sif], lambda e: e.tensor_copy(out=sif[:], in_=si[:]))
            svv = sv[:, :, :].rearrange("p (h two) a -> p h two a", two=2)
            sfv = sif[:, :, :].rearrange("p (h two) a -> p h two a", two=2)
            c4 = cand[:, :, :].rearrange("p h (a b) -> p h a b", a=16)
            x4 = cidx[:, :, :].rearrange("p h (a b) -> p h a b", a=16)
            k.op("dve", [sv], [cand], lambda e: e.tensor_tensor(
                out=c4, in0=svv[:, :, 0, :].unsqueeze(3).to_broadcast([128, 8, 16, 16]),
                in1=svv[:, :, 1, :].unsqueeze(2).to_broadcast([128, 8, 16, 16]), op=ALU.add))
            k.op("dve", [sif], [sif], lambda e: e.tensor_scalar(out=sfv[:, :, 0, :], in0=sfv[:, :, 0, :], scalar1=128.0,
                                                                scalar2=None, op0=ALU.mult))
            k.op("dve", [sif], [cidx], lambda e: e.tensor_tensor(
                out=x4, in0=sfv[:, :, 0, :].unsqueeze(3).to_broadcast([128, 8, 16, 16]),
                in1=sfv[:, :, 1, :].unsqueeze(2).to_broadcast([128, 8, 16, 16]), op=ALU.add))
            for h in range(8):
                k.op("dve", [cand], [top], lambda e: e.max(out=top[:, h, 0:8], in_=cand[:, h, :]))
                k.op("dve", [cand, top], [cw_], lambda e: e.match_replace(out=cw_[:], in_to_replace=top[:, h, 0:8],
                                                                          in_values=cand[:, h, :], imm_value=NEG))
                k.op("dve", [cw_], [top], lambda e: e.max(out=top[:, h, 8:16], in_=cw_[:]))
            k.op("dve", [top], [gate], lambda e: e.tensor_tensor(
                out=gate[:], in0=top[:], in1=top[:, :, 0:1].to_broadcast([128, 8, 16]), op=ALU.subtract))
            k.op("act", [gate], [gate], lambda e: e.activation(out=gate[:], in_=gate[:], func=AF.Exp))
            k.op("dve", [gate], [zs], lambda e: e.tensor_reduce(out=zs[:], in_=gate[:], axis=AX.X, op=ALU.add))
            k.op("dve", [zs], [zs], lambda e: e.reciprocal(out=zs[:], in_=zs[:]))
            k.op("dve", [gate, zs], [gate], lambda e: e.tensor_tensor(
                out=gate[:], in0=gate[:], in1=zs[:, :].unsqueeze(2).to_broadcast([128, 8, 16]), op=ALU.mult))
            k.op("dve", [], [eidf], lambda e: e.memset(eidf[:], 0.0))
            for h in range(8):
                for kk in range(16):
                    k.op("dve", [cand, top, cidx, eidf], [j256, eidf_w], lambda e: e.scalar_tensor_tensor(
                        out=j256[:], in0=cand[:, h, :], scalar=top[:, h, kk:kk + 1], in1=cidx[:, h, :],
                        op0=ALU.is_equal, op1=ALU.mult, accum_out=eidf[:, h * 16 + kk:h * 16 + kk + 1]))
            k.op("dve", [eidf, eidf_w], [eidf], lambda e: e.tensor_scalar(out=eidf[:], in0=eidf[:], scalar1=0.0,
                                                                          scalar2=float(N_EXP - 1), op0=ALU.max, op1=ALU.min))
            k.op("dve", [eidf], [eid], lambda e: e.tensor_copy(out=eid[:], in_=eidf[:]))

            k.dma("sp", hnb_s, S["HNB"][r0:r0 + 128, :], hnb[:], [hnb], [S["HNB_b"]])
            k.dma("sp", eid_s, S["EID"][r0:r0 + 128, :], eid[:], [eid], [S["EID_b"]])
            k.dma("sp", gate_s, S["GATE"][r0:r0 + 128, :], gate[:, :, :].rearrange("p h a -> p (h a)"), [gate], [S["GATE_b"]])

        for tt in range(L // 128):
            stage_a(tt)
        k.barrier()
    k.pop()


def phase_peer_b(k, L, lyr, W, S, C):
    identb = C["identb"]
    NT = L // 128
    k.push()
    with contextlib.ExitStack() as es:
        hnbr = Ring(k, es, 2, [128, D_MODEL], BF16, "hnbi")
        eidr = Ring(k, es, 2, [128, 128], I32, "eidi")
        gater = Ring(k, es, 2, [128, 128], F32, "gatei")
        xmr = Ring(k, es, 2, [128, D_MODEL], F32, "xmi")
        gr = Ring(k, es, 18, [128, 2 * D_MODEL], BF16, "uvg", sw=True)
        dgr = [k.sb(es, [128, 128], BF16, "dg") for _ in range(4)]
        pre = k.sb(es, [128, 128], F32, "pre")
        preb = [Buf() for _ in range(128)]
        gel = k.sb(es, [128, 128], F32, "gel")
        gelb = [Buf() for _ in range(128)]
        outr = Ring(k, es, 2, [128, D_MODEL], F32, "xn")
        pq = [k.ps(es, [128, 512], F32, "pq") for _ in range(4)]
        uv = S["UV"]
        st = {}

        def load(tt):
            r0 = tt * 128
            hnb, s1 = hnbr.next()
            eid, s2 = eidr.next()
            gate, s3 = gater.next()
            xm, s4 = xmr.next()
            k.dma("sp", s1, hnb[:], S["HNB"][r0:r0 + 128, :], [S["HNB_b"]], [hnb])
            k.dma("sp", s2, eid[:], S["EID"][r0:r0 + 128, :], [S["EID_b"]], [eid])
            k.dma("sp", s3, gate[:], S["GATE"][r0:r0 + 128, :], [S["GATE_b"]], [gate])
            k.dma("sp", s4, xm[:], S["XRES"][r0:r0 + 128, :], [S["XRES_b"]], [xm])
            st[tt] = (hnb, eid, gate, xm)

        load(0)
        for tt in range(NT):
            if tt + 1 < NT:
                load(tt + 1)
            hnb, eid, gate, xm = st.pop(tt)
            r0 = tt * 128
            k.op("dve", [], preb, lambda e: e.memset(pre[:], 0.0))
            gts = {}
            for s_ in range(129):
                if s_ < 128:
                    hk = s_
                    gt, gs = gr.next()
                    gts[hk] = gt
                    k.dma("pool", gs, gt[:], uv, [eid, S["UV_b"]], [gt],
                          indirect=bass.IndirectOffsetOnAxis(ap=eid[:, hk:hk + 1], axis=0))
                    k.op("dve", [gt, hnb], [gt, preb[hk]], lambda e: e.scalar_tensor_tensor(
                        out=gt[:, 0:D_MODEL], in0=gt[:, 0:D_MODEL], scalar=1.0, in1=hnb[:], op0=ALU.mult, op1=ALU.mult,
                        accum_out=pre[:, hk:hk + 1]))
                    k.op("act", [preb[hk]], [gelb[hk]], lambda e: e.activation(
                        out=gel[:, hk:hk + 1], in_=pre[:, hk:hk + 1], func=AF.Gelu))
                    k.op("act", [gelb[hk], gate], [gelb[hk]], lambda e: e.activation(
                        out=gel[:, hk:hk + 1], in_=gel[:, hk:hk + 1], func=AF.Copy, scale=gate[:, hk:hk + 1]))
                if s_ >= 1:
                    hk = s_ - 1
                    d_ = dgr[hk % 4]
                    gt = gts.pop(hk)
                    k.op("act", [gelb[hk], identb], [d_], lambda e: e.activation(
                        out=d_[:], in_=identb[:], func=AF.Copy, scale=gel[:, hk:hk + 1]))
                    for nb in range(4):
                        k.op("pe", [d_, gt], [pq[nb]], lambda e: e.matmul(
                            pq[nb][:, :], lhsT=d_[:], rhs=gt[:, D_MODEL + nb * 512:D_MODEL + (nb + 1) * 512],
                            start=(hk == 0), stop=(hk == 127)))
            ot, osm = outr.next()
            for nb in range(4):
                k.op("dve", [pq[nb], xm], [ot], lambda e: e.tensor_tensor(
                    out=ot[:, nb * 512:(nb + 1) * 512], in0=pq[nb][:], in1=xm[:, nb * 512:(nb + 1) * 512], op=ALU.add))
            k.dma("sp", osm, S["XRES"][r0:r0 + 128, :], ot[:], [ot], [S["XRES_b"]])
        k.barrier()
    k.pop()


def phase_peer(k, L, lyr, W, S, C):
    peer_tables_bf16(k, lyr, W, S)
    phase_peer_a(k, L, lyr, W, S, C)
    phase_peer_b(k, L, lyr, W, S, C)


def rope_consts(L):
    inv = (500000.0 ** (-np.arange(0, 16, 2, dtype=np.float32) / 16)).astype(np.float32)
    ang = np.arange(L, dtype=np.float32)[:, None] * inv[None, :]
    cosT = np.ones((128, L), np.float32)
    sinT = np.zeros((128, L), np.float32)
    for p in range(128):
        d = p % 64
        if d < 16:
            cosT[p] = np.cos(ang[:, d % 8])
            sinT[p] = np.sin(ang[:, d % 8])
    rotT = np.zeros((128, 128), np.float32)
    for m in range(128):
        d = m % 64
        if d < 8:
            rotT[m + 8, m] = -1.0
        elif d < 16:
            rotT[m - 8, m] = 1.0
    return cosT, sinT, rotT


WEIGHT_SHAPES = {
    "norm_mix": (DEPTH, D_MODEL), "w_in": (DEPTH, D_MODEL, D_IN), "ssd_dt_bias": (DEPTH, 48), "ssd_a_log": (DEPTH, 48),
    "ssd_d": (DEPTH, SSD_HEADS), "ssd_norm": (DEPTH, D_SSD),
    "ssd_conv_wl": (DEPTH, 128, 100), "ssd_conv_bl": (DEPTH, 128, 20), "sc_conv_wl": (DEPTH, 128, 18),
    "sc_norm_l": (DEPTH, 128, 6), "att_norm": (DEPTH, D_ATT), "w_out": (DEPTH, D_MIX, D_MODEL),
    "norm_ffn": (DEPTH, D_MODEL), "peer_wq": (DEPTH, D_MODEL, D_MODEL), "peer_subkeys": (DEPTH, 16, 128, 128),
    "peer_u": (DEPTH * N_EXP, D_MODEL), "peer_v": (DEPTH * N_EXP, D_MODEL), "norm_final": (1, D_MODEL),
}


class LazyW(dict):
    def __init__(self, nc):
        super().__init__()
        self.nc = nc

    def __missing__(self, n):
        v = self.nc.dram_tensor(n, list(WEIGHT_SHAPES[n]), F32, kind="ExternalInput").ap()
        self[n] = v
        return v


def build(L, phases=ALL_PHASES, debug=(), nlayers=DEPTH, feed=()):
    nc = bass.Bass("TRN2", target_bir_lowering=False)
    x = nc.dram_tensor("x", [L, D_MODEL], F32, kind="ExternalInput").ap()
    y = nc.dram_tensor("y", [L, D_MODEL], F32, kind="ExternalOutput").ap()
    W = LazyW(nc)
    S = {}

    def scratch(name, shape, dt):
        kind = "ExternalOutput" if name in debug else ("ExternalInput" if name in feed else "Internal")
        S[name] = nc.dram_tensor("s_" + name, list(shape), dt, kind=kind).ap()
        S[name + "_b"] = Buf(multi=True)

    scratch("Z", [L, D_SSD], F32)
    scratch("XBCT", [XBC, L], F32)
    scratch("DT", [L, 48], F32)
    scratch("SCT", [3 * D_SC, L], F32)
    scratch("QT", [2304, L], BF16)
    scratch("KT", [2304, L], BF16)
    scratch("V", [L, 2304], BF16)
    scratch("XRES", [L, D_MODEL], F32)
    scratch("X", [L, D_SSD], F32)
    scratch("B", [L, 512], BF16)
    scratch("BT", [512, L], BF16)
    scratch("CT", [512, L], BF16)
    scratch("YB", [L, D_SSD], F32)
    scratch("YT", [D_MIX, L], BF16)
    scratch("O0", [L, 780], F32)
    scratch("O1", [L, 780], F32)
    scratch("O2", [L, 780], F32)
    scratch("UV", [N_EXP, 2 * D_MODEL], BF16)
    scratch("HNB", [L, D_MODEL], BF16)
    scratch("EID", [L, 128], I32)
    scratch("GATE", [L, 128], F32)
    x_b = Buf(multi=True)
    y_b = Buf(multi=True)
    with contextlib.ExitStack() as es:
        k = KB(nc, es)
        C = {}
        csem = k.newsem()

        def cload(name, shape=(128, 128), dt=F32):
            d = nc.dram_tensor(name, list(shape), F32, kind="ExternalInput").ap()
            if dt == BF16:
                tb = k.sb(es, list(shape), BF16, name + "b")
                t = k.sb(es_stage, list(shape), F32, name)
                k.dma("sp", csem, t[:], d[:, :], [], [t], sync=True)
                k.op("dve", [t], [tb], lambda e: e.tensor_copy(out=tb[:], in_=t[:]))
                return tb
            t = k.sb(es, list(shape), F32, name)
            k.dma("sp", csem, t[:], d[:, :], [], [t], sync=True)
            return t

        C["identf"] = cload("ident")
        C["rotT"] = cload("rotT")
        C["onesf"] = cload("onesf")
        for n in ("U", "Lo", "nU", "nLo", "nmf", "nmb"):
            C[n] = cload(n)
        onec = k.sb(es, [128, 1], F32, "onec")
        k.op("dve", [], [onec], lambda e: e.memset(onec[:], 1.0))
        C["onec"] = onec
        C["identb"] = k.sb(es, [128, 128], BF16, "identb")
        k.op("dve", [C["identf"]], [C["identb"]], lambda e: e.tensor_copy(out=C["identb"][:], in_=C["identf"][:]))
        bnames = ("maskA", "maskB", "maskA0", "maskB1")
        tbs = {n: k.sb(es, [128, 512], BF16, n + "b") for n in bnames}
        with contextlib.ExitStack() as es_stage:
            for n in bnames:
                d = nc.dram_tensor(n, [128, 512], F32, kind="ExternalInput").ap()
                t = k.sb(es_stage, [128, 512], F32, n)
                k.dma("sp", csem, t[:], d[:, :], [], [t], sync=True)
                k.op("dve", [t], [tbs[n]], lambda e: e.tensor_copy(out=tbs[n][:], in_=t[:]))
                C[n] = tbs[n]
            k.barrier()
        C["cosT"] = nc.dram_tensor("cosT", [128, L], F32, kind="ExternalInput").ap()
        C["sinT"] = nc.dram_tensor("sinT", [128, L], F32, kind="ExternalInput").ap()
        k.barrier()
        xsrc, xsrc_b = x, x_b
        for lyr in range(nlayers):
            if "a" in phases:
                phase_a(k, L, lyr, xsrc, xsrc_b, W, S, C)
            if "conv" in phases:
                phase_conv(k, L, lyr, W, S, C)
            if "ssd" in phases:
                phase_ssd(k, L, lyr, W, S, C)
            if "sc" in phases:
                phase_sc(k, L, lyr, W, S, C)
            if "att" in phases:
                phase_att(k, L, lyr, W, S, C)
            if "wout" in phases:
                phase_wout(k, L, lyr, xsrc, xsrc_b, W, S, C)
            if "peer" in phases:
                phase_peer(k, L, lyr, W, S, C)
            if "wout" in phases:
                xsrc, xsrc_b = S["XRES"], S["XRES_b"]
        if "final" in phases:
            phase_final(k, L, xsrc, xsrc_b, W, y, y_b)
        k.barrier()
    return nc


def host_consts(L):
    cosT, sinT, rotT = rope_consts(L)
    i = np.arange(128)
    U = (i[:, None] <= i[None, :]).astype(np.float32)
    Lo = (i[:, None] >= i[None, :]).astype(np.float32)
    NEGM = -30000.0
    nmf = np.where(i[None, :] >= i[:, None], 0.0, NEGM).astype(np.float32)
    nmb = np.where(i[None, :] <= i[:, None], 0.0, NEGM).astype(np.float32)
    mA = np.where(i[:, None] >= i[None, :], 0.0, NEGM).astype(np.float32)
    mB = np.where(i[:, None] <= i[None, :], 0.0, NEGM).astype(np.float32)
    eye = np.eye(128, dtype=np.float32)
    mA0 = mA.copy()
    mA0[0:64, :] = NEGM
    mB1 = mB.copy()
    mB1[64:128, :] = NEGM
    return {"maskA0": np.ascontiguousarray(np.tile(mA0, (1, 4))), "maskB1": np.ascontiguousarray(np.tile(mB1, (1, 4))),"cosT": cosT, "sinT": sinT, "rotT": rotT, "ident": eye,
            "onesf": np.ones((128, 128), np.float32), "U": U, "Lo": Lo, "nU": -U, "nLo": -Lo, "nmf": nmf, "nmb": nmb,
            "maskA": np.ascontiguousarray(np.tile(mA, (1, 4))), "maskB": np.ascontiguousarray(np.tile(mB, (1, 4)))}


def prep_weights(inp):
    w = {}
    f = lambda a: np.asarray(a, dtype=np.float32)
    for n, s in WEIGHT_SHAPES.items():
        if n in inp:
            w[n] = np.ascontiguousarray(f(inp[n]).reshape(s))
    if "ssd_conv_w" in inp:
        cw = f(inp["ssd_conv_w"]).reshape(DEPTH, 5, 20, 128)
        w["ssd_conv_wl"] = np.ascontiguousarray(cw.transpose(0, 3, 2, 1).reshape(DEPTH, 128, 100))
        cb = f(inp["ssd_conv_b"]).reshape(DEPTH, 20, 128)
        w["ssd_conv_bl"] = np.ascontiguousarray(cb.transpose(0, 2, 1))
    if "sc_conv_w" in inp:
        sw = f(inp["sc_conv_w"]).reshape(DEPTH, 3, 6, 128)
        w["sc_conv_wl"] = np.ascontiguousarray(sw.transpose(0, 3, 2, 1).reshape(DEPTH, 128, 18))
        sn = f(inp["sc_norm"]).reshape(DEPTH, 6, 128)
        w["sc_norm_l"] = np.ascontiguousarray(sn.transpose(0, 2, 1))
    return w


def kernel(**inputs):
    L = 8192
    xs = [np.asarray(inputs["x_prompt"][i]) for i in range(2)] + [np.asarray(inputs["x_sample"][i]) for i in range(4)]
    xs = xs + [xs[0], xs[1]]
    w = prep_weights(inputs)
    consts = host_consts(L)
    nc = build(L)
    in_maps = []
    for c in range(8):
        m = {"x": np.ascontiguousarray(xs[c], dtype=np.float32)}
        m.update(w)
        m.update(consts)
        in_maps.append(m)
    res = run_bass_kernel_spmd(nc, in_maps, core_ids=list(range(8)))
    ys = [res.results[c]["y"] for c in range(6)]
    return (np.stack(ys[0:2], axis=0).astype(np.float32), np.stack(ys[2:6], axis=0).astype(np.float32))
```

```python
import contextlib
import numpy as np
import ml_dtypes
import concourse.bass as bass
import concourse.mybir as mybir
from concourse.bass_utils import run_bass_kernel_spmd

F32 = mybir.dt.float32
BF16 = mybir.dt.bfloat16
I32 = mybir.dt.int32
U32 = mybir.dt.uint32
AF = mybir.ActivationFunctionType
ALU = mybir.AluOpType
AX = mybir.AxisListType

D_MODEL = 2048
DEPTH = 2
D_SSD = 1536
SSD_HEADS = 24
XBC = 2560
D_SC = 768
D_ATT = 768
D_MIX = 3072
D_IN = 13360
EPS = 1e-6
C_Z = 0
C_XBC = 1536
C_DT = 4096
C_SC = 4144
C_ATT = 6448
N_EXP = 16384
ATT_GROUPS = (0, 1, 2)
ATT_STAGE = 9
PEER_DBG = ""
STORES_ON_ACT = True
ALL_PHASES = ("a", "conv", "ssd", "sc", "att", "wout", "peer", "final")


class Buf:
    __slots__ = ("w", "r", "multi")

    def __init__(self, multi=False):
        self.w = {}
        self.r = {}
        self.multi = multi


class T:
    def __init__(self, t, b=None):
        self.t = t
        self.b = b if b is not None else Buf()

    def __getitem__(self, idx):
        return self.t[idx]


class KB:
    def __init__(self, nc, es):
        self.nc = nc
        self.es = es
        self.eng = {"pe": nc.tensor, "act": nc.scalar, "dve": nc.vector, "pool": nc.gpsimd, "sp": nc.sync}
        self.sems = {}
        self.cnt = {}
        for e in ("pe", "act", "dve", "pool"):
            self.sems[e] = es.enter_context(nc.semaphore("c_" + e))
            self.cnt[e] = 0
        self.waited = {e: {} for e in self.eng}
        self.nd = 0
        self.uid = 0
        self.free = []
        self.free_sw = []
        self.scopes = []

    def newsem(self, sw=False):
        fl = self.free_sw if sw else self.free
        if fl:
            key = fl.pop()
        else:
            key = ("dw%d" if sw else "d%d") % self.nd
            self.nd += 1
            self.sems[key] = self.es.enter_context(self.nc.semaphore(key))
            self.cnt[key] = 0
        for sc in self.scopes:
            sc.append(key)
        return key

    def push(self):
        self.scopes.append([])

    def pop(self):
        sc = self.scopes.pop()
        for key in sc:
            fl = self.free_sw if key.startswith("dw") else self.free
            if key not in fl:
                fl.append(key)

    def _deps(self, e, reads, writes):
        need = {}
        for b in reads:
            for s, v in b.w.items():
                if v > need.get(s, 0):
                    need[s] = v
        for b in writes:
            if not b.multi:
                for s, v in b.w.items():
                    if v > need.get(s, 0):
                        need[s] = v
            for s, v in b.r.items():
                if v > need.get(s, 0):
                    need[s] = v
        wd = self.waited[e]
        for s, v in need.items():
            if s == e and e == "pe":
                continue
            if wd.get(s, 0) >= v:
                continue
            if s[0] == "d":
                v = self.cnt[s]
            self.eng[e].wait_ge(self.sems[s], v)
            wd[s] = v

    def _rec(self, key, val, reads, writes):
        for b in reads:
            b.r[key] = val
        for b in writes:
            if b.multi:
                b.w[key] = val
            else:
                b.w = {key: val}
            b.r = {}

    def op(self, e, reads, writes, fn):
        reads = [x.b if isinstance(x, T) else x for x in reads]
        writes = [x.b if isinstance(x, T) else x for x in writes]
        self._deps(e, reads, writes)
        ins = fn(self.eng[e])
        self.cnt[e] += 1
        ins.then_inc(self.sems[e], 1)
        self._rec(e, self.cnt[e], reads, writes)
        return ins

    def dma(self, q, sem, out, in_, reads, writes, indirect=None, sync=False, **kw):
        reads = [x.b if isinstance(x, T) else x for x in reads]
        writes = [x.b if isinstance(x, T) else x for x in writes]
        if q == "sp" and STORES_ON_ACT and any(b.multi for b in writes):
            q = "act"
        self._deps(q, reads, writes)
        if indirect is not None:
            ins = self.eng[q].indirect_dma_start(out=out, out_offset=None, in_=in_, in_offset=indirect, **kw)
        else:
            ins = self.eng[q].dma_start(out=out, in_=in_, **kw)
        self.cnt[sem] += 16
        ins.then_inc(self.sems[sem], 16)
        self._rec(sem, self.cnt[sem], reads, writes)
        if sync:
            self.eng[q].wait_ge(self.sems[sem], self.cnt[sem])
            self.waited[q][sem] = self.cnt[sem]
        return ins

    def barrier(self):
        for e in self.eng:
            wd = self.waited[e]
            for s, v in self.cnt.items():
                if s == e or v == 0 or wd.get(s, 0) >= v:
                    continue
                self.eng[e].wait_ge(self.sems[s], v)
                wd[s] = v

    def sb(self, es, shape, dt, name=None):
        self.uid += 1
        t = es.enter_context(self.nc.sbuf_tensor("%s_%d" % (name or "t", self.uid), list(shape), dt))
        return T(t)

    def ps(self, es, shape, dt, name=None):
        self.uid += 1
        t = es.enter_context(self.nc.psum_tensor("%s_%d" % (name or "p", self.uid), list(shape), dt))
        return T(t)


class Ring:
    def __init__(self, k, es, n, shape, dt, name, dma=True, sw=False):
        self.tiles = [k.sb(es, shape, dt, name) for _ in range(n)]
        self.sems = [k.newsem(sw) for _ in range(n)] if dma else [None] * n
        self.i = 0

    def next(self):
        t, s = self.tiles[self.i], self.sems[self.i]
        self.i = (self.i + 1) % len(self.tiles)
        return t, s


def rmsnorm_rstd(k, ssq, rstd, n):
    k.op("dve", [ssq], [rstd], lambda e: e.tensor_scalar(out=rstd[:, 0:1], in0=ssq[:, 0:1], scalar1=1.0 / n,
                                                         scalar2=EPS, op0=ALU.mult, op1=ALU.add))
    k.op("act", [rstd], [rstd], lambda e: e.activation(out=rstd[:, 0:1], in_=rstd[:, 0:1], func=AF.Sqrt))
    k.op("dve", [rstd], [rstd], lambda e: e.reciprocal(out=rstd[:, 0:1], in_=rstd[:, 0:1]))


def phase_a(k, L, lyr, xsrc, xsrc_b, W, S, C):
    nc = k.nc
    TB = 1024 if L >= 1024 else L
    NT = TB // 128
    k.push()
    with contextlib.ExitStack() as es:
        gB = k.sb(es, [128, D_MODEL], F32, "gB")
        gsem = k.newsem()
        k.dma("sp", gsem, gB[:], W["norm_mix"][lyr:lyr + 1, :].broadcast_to([128, D_MODEL]), [], [gB], sync=True)
        xr = Ring(k, es, 2, [128, D_MODEL], F32, "xt")
        junk = k.sb(es, [128, D_MODEL], BF16, "junk")
        hb = [k.sb(es, [128, D_MODEL], BF16, "hb") for _ in range(2)]
        ssq = [k.sb(es, [128, 1], F32, "ssq") for _ in range(2)]
        rstd = [k.sb(es, [128, 1], F32, "rstd") for _ in range(2)]
        hT = k.sb(es, [128, 16, TB], BF16, "hT")
        hTb = [Buf() for _ in range(NT)]
        wf = Ring(k, es, 2, [128, 16, 512], F32, "wf")
        wb = [k.sb(es, [128, 16, 512], BF16, "wb") for _ in range(2)]
        ev = Ring(k, es, 3, [128, 512], F32, "ev")
        evb = Ring(k, es, 3, [128, 512], BF16, "evb")
        cs = Ring(k, es, 2, [128, 512], F32, "cos")
        sn = Ring(k, es, 2, [128, 512], F32, "sin")
        qsb = [k.sb(es, [128, 512], F32, "qsb") for _ in range(2)]
        t1 = [k.sb(es, [128, 512], F32, "t1") for _ in range(2)]
        t2 = [k.sb(es, [128, 512], F32, "t2") for _ in range(2)]
        pst = [k.ps(es, [128, 1024], BF16, "pst") for _ in range(2)]
        psm = [k.ps(es, [128, 512], F32, "psm") for _ in range(4)]
        psr = [k.ps(es, [128, 512], F32, "psr") for _ in range(2)]
        identb = C["identb"]
        rotT = C["rotT"]
        wi = 0
        mi = 0
        ri = 0
        segs = [("z", C_Z, 1536, 512), ("xbc", C_XBC, 2560, 512), ("dt", C_DT, 48, 48),
                ("sc", C_SC, 2304, 384), ("q", C_ATT, 2304, 384), ("k", C_ATT + 2304, 2304, 384),
                ("v", C_ATT + 4608, 2304, 384)]
        for sbk in range(L // TB):
            tok0 = sbk * TB
            for tt in range(NT):
                xt, xs = xr.next()
                k.dma("sp", xs, xt[:], xsrc[tok0 + tt * 128: tok0 + (tt + 1) * 128, :], [xsrc_b], [xt])
                p = tt % 2
                k.op("act", [xt], [junk, ssq[p]], lambda e: e.activation(out=junk[:], in_=xt[:], func=AF.Square,
                                                                          accum_out=ssq[p][:, 0:1]))
                rmsnorm_rstd(k, ssq[p], rstd[p], D_MODEL)
                k.op("dve", [xt, rstd[p], gB], [hb[p]], lambda e: e.scalar_tensor_tensor(
                    out=hb[p][:], in0=xt[:], scalar=rstd[p][:, 0:1], in1=gB[:], op0=ALU.mult, op1=ALU.mult))
                for half in range(2):
                    pt = pst[half]
                    for j in range(8):
                        kc = half * 8 + j
                        k.op("pe", [hb[p], identb], [pt], lambda e: e.transpose(
                            out=pt[:, j * 128:(j + 1) * 128], in_=hb[p][:, kc * 128:(kc + 1) * 128],
                            identity=identb[:]))
                    eng = "act" if half == 0 else "dve"
                    src = pt[:, :].rearrange("p (j t) -> p j t", j=8)
                    dst = hT[:, half * 8:(half + 1) * 8, tt * 128:(tt + 1) * 128]
                    if eng == "act":
                        k.op("act", [pt], [hTb[tt]], lambda e: e.copy(out=dst, in_=src))
                    else:
                        k.op("dve", [pt], [hTb[tt]], lambda e: e.tensor_copy(out=dst, in_=src))
            for kind, c0, ncs, bw in segs:
                for blk in range(ncs // bw):
                    cb = c0 + blk * bw
                    wft, wfs = wf.next()
                    wbt = wb[wi % 2]
                    wi += 1
                    k.dma("sp", wfs, wft[:, :, 0:bw],
                          W["w_in"][lyr, :, cb:cb + bw].rearrange("(kc p) c -> p kc c", p=128), [], [wft])
                    k.op("pool", [wft], [wbt], lambda e: e.tensor_copy(out=wbt[:, :, 0:bw], in_=wft[:, :, 0:bw]))
                    if kind in ("z", "dt", "v"):
                        for tt in range(NT):
                            pm = psm[mi % 4]
                            mi += 1
                            for kc in range(16):
                                k.op("pe", [hTb[tt], wbt], [pm], lambda e: e.matmul(
                                    pm[:, 0:bw], lhsT=hT[:, kc, tt * 128:(tt + 1) * 128], rhs=wbt[:, kc, 0:bw],
                                    start=(kc == 0), stop=(kc == 15)))
                            r0 = tok0 + tt * 128
                            if kind == "v":
                                et, esm = evb.next()
                                k.op("act", [pm], [et], lambda e: e.copy(out=et[:, 0:bw], in_=pm[:, 0:bw]))
                                k.dma("sp", esm, S["V"][r0:r0 + 128, blk * bw:(blk + 1) * bw], et[:, 0:bw],
                                      [et], [S["V_b"]])
                            else:
                                et, esm = ev.next()
                                k.op("act", [pm], [et], lambda e: e.copy(out=et[:, 0:bw], in_=pm[:, 0:bw]))
                                dst = S["Z"] if kind == "z" else S["DT"]
                                dstb = S["Z_b"] if kind == "z" else S["DT_b"]
                                k.dma("sp", esm, dst[r0:r0 + 128, blk * bw:(blk + 1) * bw], et[:, 0:bw],
                                      [et], [dstb])
                    else:
                        for tb in range(TB // 512):
                            t0 = tok0 + tb * 512
                            hdeps = [hTb[tb * 4 + i] for i in range(4)]
                            if kind in ("q", "k"):
                                ct, csm = cs.next()
                                st, ssm = sn.next()
                                k.dma("sp", csm, ct[:], C["cosT"][:, t0:t0 + 512], [], [ct])
                                k.dma("sp", ssm, st[:], C["sinT"][:, t0:t0 + 512], [], [st])
                            for ch in range(bw // 128):
                                pm = psm[mi % 4]
                                mi += 1
                                for kc in range(16):
                                    k.op("pe", hdeps + [wbt], [pm], lambda e: e.matmul(
                                        pm[:, :], lhsT=wbt[:, kc, ch * 128:(ch + 1) * 128],
                                        rhs=hT[:, kc, tb * 512:(tb + 1) * 512], start=(kc == 0), stop=(kc == 15)))
                                row = blk * bw + ch * 128
                                if kind in ("xbc", "sc"):
                                    et, esm = ev.next()
                                    k.op("act", [pm], [et], lambda e: e.copy(out=et[:], in_=pm[:]))
                                    dst = S["XBCT"] if kind == "xbc" else S["SCT"]
                                    dstb = S["XBCT_b"] if kind == "xbc" else S["SCT_b"]
                                    k.dma("sp", esm, dst[row:row + 128, t0:t0 + 512], et[:], [et], [dstb])
                                else:
                                    r = ri % 2
                                    ri += 1
                                    pr = psr[r]
                                    k.op("act", [pm], [qsb[r]], lambda e: e.copy(out=qsb[r][:], in_=pm[:]))
                                    k.op("pe", [qsb[r], rotT], [pr], lambda e: e.matmul(
                                        pr[:, :], lhsT=rotT[:], rhs=qsb[r][:], start=True, stop=True))
                                    k.op("pool", [qsb[r], ct], [t1[r]], lambda e: e.tensor_tensor(
                                        out=t1[r][:], in0=qsb[r][:], in1=ct[:], op=ALU.mult))
                                    k.op("dve", [pr, st], [t2[r]], lambda e: e.tensor_tensor(
                                        out=t2[r][:], in0=pr[:], in1=st[:], op=ALU.mult))
                                    et, esm = evb.next()
                                    k.op("dve", [t1[r], t2[r]], [et], lambda e: e.tensor_tensor(
                                        out=et[:], in0=t1[r][:], in1=t2[r][:], op=ALU.add))
                                    dst = S["QT"] if kind == "q" else S["KT"]
                                    dstb = S["QT_b"] if kind == "q" else S["KT_b"]
                                    k.dma("sp", esm, dst[row:row + 128, t0:t0 + 512], et[:], [et], [dstb])
        k.barrier()
    k.pop()


def phase_final(k, L, xsrc, xsrc_b, W, y, y_b):
    k.push()
    with contextlib.ExitStack() as es:
        gB = k.sb(es, [128, D_MODEL], F32, "gB")
        gsem = k.newsem()
        k.dma("sp", gsem, gB[:], W["norm_final"][0:1, :].broadcast_to([128, D_MODEL]), [], [gB], sync=True)
        xr = Ring(k, es, 2, [128, D_MODEL], F32, "xt")
        yr = Ring(k, es, 2, [128, D_MODEL], F32, "yt")
        junk = k.sb(es, [128, D_MODEL], BF16, "junk")
        ssq = [k.sb(es, [128, 1], F32, "ssq") for _ in range(2)]
        rstd = [k.sb(es, [128, 1], F32, "rstd") for _ in range(2)]
        for tt in range(L // 128):
            xt, xs = xr.next()
            yt, ys = yr.next()
            p = tt % 2
            k.dma("sp", xs, xt[:], xsrc[tt * 128:(tt + 1) * 128, :], [xsrc_b], [xt])
            k.op("act", [xt], [junk, ssq[p]], lambda e: e.activation(out=junk[:], in_=xt[:], func=AF.Square,
                                                                      accum_out=ssq[p][:, 0:1]))
            rmsnorm_rstd(k, ssq[p], rstd[p], D_MODEL)
            k.op("dve", [xt, rstd[p], gB], [yt], lambda e: e.scalar_tensor_tensor(
                out=yt[:], in0=xt[:], scalar=rstd[p][:, 0:1], in1=gB[:], op0=ALU.mult, op1=ALU.mult))
            k.dma("sp", ys, y[tt * 128:(tt + 1) * 128, :], yt[:], [yt], [y_b])
        k.barrier()
    k.pop()


def phase_conv(k, L, lyr, W, S, C):
    LB = min(L, 4096)
    identf, identb = C["identf"], C["identb"]
    k.push()
    with contextlib.ExitStack() as es:
        cw = k.sb(es, [128, 100], F32, "cw")
        cbias = k.sb(es, [128, 20], F32, "cbias")
        sem = k.newsem()
        k.dma("sp", sem, cw[:], W["ssd_conv_wl"][lyr], [], [cw], sync=True)
        k.dma("sp", sem, cbias[:], W["ssd_conv_bl"][lyr], [], [cbias], sync=True)
        xin = Ring(k, es, 2, [128, LB + 4], F32, "cin")
        acc = k.sb(es, [128, LB], F32, "cacc")
        so = Ring(k, es, 2, [128, LB], F32, "so")
        sob = Ring(k, es, 2, [128, LB], BF16, "sob")
        tr = Ring(k, es, 3, [128, 4, 128], F32, "tr")
        trb = Ring(k, es, 3, [128, 4, 128], BF16, "trb")
        ptr = [k.ps(es, [128, 512], F32, "ptr") for _ in range(2)]
        ptb = [k.ps(es, [128, 512], BF16, "ptb") for _ in range(2)]
        pi = 0
        for c in range(20):
            for lb in range(L // LB):
                t0 = lb * LB
                xt, xs = xin.next()
                lo = 2 if t0 == 0 else 0
                hi = 2 if t0 + LB == L else 0
                if lo:
                    k.op("pool", [], [xt], lambda e: e.memset(xt[:, 0:2], 0.0))
                if hi:
                    k.op("pool", [], [xt], lambda e: e.memset(xt[:, LB + 2:LB + 4], 0.0))
                k.dma("sp", xs, xt[:, lo:LB + 4 - hi], S["XBCT"][c * 128:(c + 1) * 128, t0 - 2 + lo:t0 + LB + 2 - hi],
                      [S["XBCT_b"]], [xt])
                k.op("dve", [xt, cw, cbias], [acc], lambda e: e.tensor_scalar(
                    out=acc[:], in0=xt[:, 0:LB], scalar1=cw[:, c * 5:c * 5 + 1], scalar2=cbias[:, c:c + 1],
                    op0=ALU.mult, op1=ALU.add))
                for t in range(1, 5):
                    k.op("dve", [xt, cw, acc], [acc], lambda e: e.scalar_tensor_tensor(
                        out=acc[:], in0=xt[:, t:t + LB], scalar=cw[:, c * 5 + t:c * 5 + t + 1], in1=acc[:],
                        op0=ALU.mult, op1=ALU.add))
                if c < 12:
                    st, ssm = so.next()
                    k.op("act", [acc], [st], lambda e: e.activation(out=st[:], in_=acc[:], func=AF.Silu))
                    for tg in range(LB // 512):
                        pt = ptr[pi % 2]
                        pi += 1
                        for i in range(4):
                            k.op("pe", [st, identf], [pt], lambda e: e.transpose(
                                out=pt[:, i * 128:(i + 1) * 128], in_=st[:, (tg * 4 + i) * 128:(tg * 4 + i + 1) * 128],
                                identity=identf[:]))
                        tt, ts = tr.next()
                        k.op("act", [pt], [tt], lambda e: e.copy(out=tt[:], in_=pt[:, :].rearrange("p (a c) -> p a c", a=4)))
                        r0 = t0 + tg * 512
                        k.dma("sp", ts, S["X"][r0:r0 + 512, c * 128:(c + 1) * 128].rearrange("(a p) c -> p a c", p=128),
                              tt[:], [tt], [S["X_b"]])
                else:
                    st, ssm = sob.next()
                    k.op("act", [acc], [st], lambda e: e.activation(out=st[:], in_=acc[:], func=AF.Silu))
                    name = "BT" if c < 16 else "CT"
                    rr = (c - 12) % 4
                    k.dma("sp", ssm, S[name][rr * 128:(rr + 1) * 128, t0:t0 + LB], st[:], [st], [S[name + "_b"]])
                    if c < 16:
                        for tg in range(LB // 512):
                            pt = ptb[pi % 2]
                            pi += 1
                            for i in range(4):
                                k.op("pe", [st, identb], [pt], lambda e: e.transpose(
                                    out=pt[:, i * 128:(i + 1) * 128],
                                    in_=st[:, (tg * 4 + i) * 128:(tg * 4 + i + 1) * 128], identity=identb[:]))
                            tt, ts = trb.next()
                            k.op("dve", [pt], [tt], lambda e: e.tensor_copy(
                                out=tt[:], in_=pt[:, :].rearrange("p (a c) -> p a c", a=4)))
                            r0 = t0 + tg * 512
                            k.dma("sp", ts,
                                  S["B"][r0:r0 + 512, rr * 128:(rr + 1) * 128].rearrange("(a p) c -> p a c", p=128),
                                  tt[:], [tt], [S["B_b"]])
        k.barrier()
    k.pop()


SSD_RUNS = {0: [(0, 0, 4)], 1: [(0, 4, 6), (1, 6, 8)], 2: [(1, 8, 12)], 3: [(2, 12, 16)],
            4: [(2, 16, 18), (3, 18, 20)], 5: [(3, 20, 24)]}


def phase_ssd(k, L, lyr, W, S, C):
    NCH = L // 128
    identf, onesf, onec = C["identf"], C["onesf"], C["onec"]
    k.push()
    with contextlib.ExitStack() as es:
        sem = k.newsem()
        Abc = k.sb(es, [128, 48], F32, "Abc")
        dtb = k.sb(es, [128, 48], F32, "dtb")
        Dsk = k.sb(es, [128, 24], F32, "Dsk")
        normw = k.sb(es, [128, D_SSD], F32, "normw")
        k.dma("sp", sem, Abc[:], W["ssd_a_log"][lyr:lyr + 1, :].broadcast_to([128, 48]), [], [Abc], sync=True)
        k.dma("sp", sem, dtb[:], W["ssd_dt_bias"][lyr:lyr + 1, :].broadcast_to([128, 48]), [], [dtb], sync=True)
        k.dma("sp", sem, Dsk[:], W["ssd_d"][lyr:lyr + 1, :].broadcast_to([128, 24]), [], [Dsk], sync=True)
        k.dma("sp", sem, normw[:], W["ssd_norm"][lyr:lyr + 1, :].broadcast_to([128, D_SSD]), [], [normw], sync=True)
        k.op("act", [Abc], [Abc], lambda e: e.activation(out=Abc[:], in_=Abc[:], func=AF.Exp))
        k.op("dve", [Abc], [Abc], lambda e: e.tensor_scalar(out=Abc[:], in0=Abc[:], scalar1=-1.0, scalar2=None,
                                                            op0=ALU.mult))
        xr = Ring(k, es, 2, [128, 24, 64], F32, "xk")
        zr = Ring(k, es, 2, [128, D_SSD], F32, "zk")
        ybr = Ring(k, es, 2, [128, D_SSD], F32, "ybk")
        dtr = Ring(k, es, 2, [128, 48], F32, "dtk")
        bkr = Ring(k, es, 2, [128, 512], BF16, "bk")
        btr = Ring(k, es, 2, [128, 4, 128], BF16, "btk")
        ctr = Ring(k, es, 2, [128, 4, 128], BF16, "ctk")
        dts = k.sb(es, [128, 48], F32, "dts")
        dt = k.sb(es, [128, 48], F32, "dt")
        a = k.sb(es, [128, 24], F32, "a")
        sm = k.sb(es, [128, 64], F32, "sm")
        eac = k.sb(es, [128, 24], F32, "eac")
        dend = k.sb(es, [128, 24], F32, "dend")
        cdec = k.sb(es, [128, 24], F32, "cdec")
        w2 = k.sb(es, [128, 24], F32, "w2")
        xdt = k.sb(es, [128, D_SSD], BF16, "xdt")
        xde = k.sb(es, [128, D_SSD], BF16, "xde")
        cbs = k.sb(es, [128, 4, 128], F32, "cbs")
        rhs1 = k.sb(es, [128, 24, 128], F32, "rhs1")
        dec = [k.sb(es, [128, 4, 128], F32, "dec") for _ in range(3)]
        MT = [k.sb(es, [128, 12, 128], BF16, "MT") for _ in range(2)]
        H = [k.sb(es, [128, 768], F32, "H") for _ in range(2)]
        Hb = [k.sb(es, [128, 768], BF16, "Hb") for _ in range(2)]
        yo = k.sb(es, [128, 24, 64], F32, "yo")
        yd = k.sb(es, [128, 24, 64], F32, "yd")
        ydb = [Buf(), Buf()]
        ystore = Ring(k, es, 2, [128, D_SSD], F32, "ystore")
        y1 = k.sb(es, [128, D_SSD], F32, "y1")
        sz = k.sb(es, [128, D_SSD], F32, "sz")
        junk = k.sb(es, [128, 384], F32, "junk")
        gss = k.sb(es, [128, 4], F32, "gss")
        ytr = Ring(k, es, 2, [128, 12, 128], BF16, "ytr")
        big = [k.ps(es, [128, 1024], F32, "big") for _ in range(2)]
        pseg = [k.ps(es, [128, 512], F32, "pseg") for _ in range(2)]
        pcb = k.ps(es, [128, 512], F32, "pcb")
        ptr = k.ps(es, [128, 512], F32, "ptr")
        st = {"big": 0, "seg": 0, "dec": 0}

        def nbig():
            st["big"] += 1
            return big[st["big"] % 2]

        for d in (1, 0):
            Td = C["U"] if d == 0 else C["Lo"]
            nTd = C["nU"] if d == 0 else C["nLo"]
            nm = C["nmf"] if d == 0 else C["nmb"]
            for hf in range(2):
                k.op("pool", [], [H[hf]], lambda e: e.memset(H[hf][:], 0.0))
                k.op("pool", [], [Hb[hf]], lambda e: e.memset(Hb[hf][:], 0.0))
            order = range(NCH) if d == 0 else range(NCH - 1, -1, -1)
            for c in order:
                r0 = c * 128
                xk, s1 = xr.next()
                k.dma("sp", s1, xk[:], S["X"][r0:r0 + 128, :].rearrange("p (h d) -> p h d", h=24), [S["X_b"]], [xk])
                dtk, s2 = dtr.next()
                k.dma("sp", s2, dtk[:], S["DT"][r0:r0 + 128, :], [S["DT_b"]], [dtk])
                bk, s3 = bkr.next()
                k.dma("sp", s3, bk[:], S["B"][r0:r0 + 128, :], [S["B_b"]], [bk])
                btk, s4 = btr.next()
                k.dma("sp", s4, btk[:], S["BT"][:, r0:r0 + 128].rearrange("(g n) t -> n g t", n=128), [S["BT_b"]], [btk])
                ctk, s5 = ctr.next()
                k.dma("sp", s5, ctk[:], S["CT"][:, r0:r0 + 128].rearrange("(g n) t -> n g t", n=128), [S["CT_b"]], [ctk])
                if d == 0:
                    zk, s6 = zr.next()
                    k.dma("sp", s6, zk[:], S["Z"][r0:r0 + 128, :], [S["Z_b"]], [zk])
                    ybk, s7 = ybr.next()
                    k.dma("sp", s7, ybk[:], S["YB"][r0:r0 + 128, :], [S["YB_b"]], [ybk])
                k.op("dve", [dtk, dtb], [dts], lambda e: e.tensor_tensor(out=dts[:], in0=dtk[:], in1=dtb[:], op=ALU.add))
                k.op("act", [dts], [dts], lambda e: e.activation(out=dts[:], in_=dts[:], func=AF.Exp))
                k.op("act", [dts, onec], [dt], lambda e: e.activation(out=dt[:], in_=dts[:], func=AF.Ln,
                                                                      bias=onec[:, 0:1], scale=1.0))
                dtd = dt[:, d * 24:(d + 1) * 24]
                k.op("dve", [dt, Abc], [a], lambda e: e.tensor_tensor(out=a[:], in0=dtd, in1=Abc[:, d * 24:(d + 1) * 24],
                                                                      op=ALU.mult))
                pb = nbig()
                k.op("pe", [Td, a], [pb], lambda e: e.matmul(pb[:, 0:24], lhsT=Td[:], rhs=a[:], start=True, stop=True))
                k.op("pe", [onesf, a], [pb], lambda e: e.matmul(pb[:, 24:48], lhsT=onesf[:], rhs=a[:], start=True, stop=True))
                k.op("act", [pb], [sm], lambda e: e.copy(out=sm[:, 0:48], in_=pb[:, 0:48]))
                k.op("act", [sm], [eac], lambda e: e.activation(out=eac[:], in_=sm[:, 0:24], func=AF.Exp))
                k.op("dve", [sm], [dend], lambda e: e.tensor_tensor(out=dend[:], in0=sm[:, 24:48], in1=sm[:, 0:24],
                                                                   op=ALU.subtract))
                k.op("act", [dend], [dend], lambda e: e.activation(out=dend[:], in_=dend[:], func=AF.Exp))
                k.op("act", [sm], [cdec], lambda e: e.activation(out=cdec[:], in_=sm[:, 24:48], func=AF.Exp))
                k.op("dve", [dt, dend], [w2], lambda e: e.tensor_tensor(out=w2[:], in0=dtd, in1=dend[:], op=ALU.mult))
                k.op("dve", [xk, dt], [xdt], lambda e: e.tensor_tensor(
                    out=xdt[:, :].rearrange("p (h d) -> p h d", d=64), in0=xk[:], in1=dtd.unsqueeze(2).to_broadcast([128, 24, 64]), op=ALU.mult))
                k.op("pool", [xk, w2], [xde], lambda e: e.tensor_tensor(
                    out=xde[:, :].rearrange("p (h d) -> p h d", d=64), in0=xk[:], in1=w2[:, :].unsqueeze(2).to_broadcast([128, 24, 64]), op=ALU.mult))
                for g in range(4):
                    k.op("pe", [btk, ctk], [pcb], lambda e: e.matmul(
                        pcb[:, g * 128:(g + 1) * 128], lhsT=btk[:, g, :], rhs=ctk[:, g, :], start=True, stop=True))
                k.op("act", [pcb], [cbs], lambda e: e.copy(out=cbs[:], in_=pcb[:, :].rearrange("p (g l) -> p g l", g=4)))
                k.op("pool", [a, Td], [rhs1], lambda e: e.tensor_tensor(
                    out=rhs1[:], in0=a[:, :].unsqueeze(2).to_broadcast([128, 24, 128]),
                    in1=Td[:, :].unsqueeze(1).to_broadcast([128, 24, 128]), op=ALU.mult))
                for hf in range(2):
                    h0 = hf * 12
                    pb = nbig()
                    for u in range(3 * hf, 3 * hf + 3):
                        for (g, ha, hb_) in SSD_RUNS[u]:
                            k.op("pe", [ctk, Hb[hf]], [pb], lambda e: e.matmul(
                                pb[:, (ha - h0) * 64:(hb_ - h0) * 64], lhsT=ctk[:, g, :],
                                rhs=Hb[hf][:, (ha - h0) * 64:(hb_ - h0) * 64], start=True, stop=True))
                    k.op("dve", [pb, eac], [yo], lambda e: e.tensor_tensor(
                        out=yo[:, h0:h0 + 12, :], in0=pb[:, 0:768].rearrange("p (h d) -> p h d", h=12),
                        in1=eac[:, h0:h0 + 12].unsqueeze(2).to_broadcast([128, 12, 64]), op=ALU.mult))
                    mt = MT[hf]
                    for u in range(3 * hf, 3 * hf + 3):
                        pg = pseg[st["seg"] % 2]
                        st["seg"] += 1
                        pg3 = pg[:, :].rearrange("p (h l) -> p h l", h=4)
                        k.op("pe", [onesf, rhs1], [pg], lambda e: e.matmul(
                            pg3, lhsT=onesf[:], rhs=rhs1[:, 4 * u:4 * u + 4, :], start=True, stop=False))
                        k.op("pe", [nTd, a], [pg], lambda e: e.matmul(
                            pg3, lhsT=nTd[:], rhs=a[:, 4 * u:4 * u + 4].unsqueeze(2).to_broadcast([128, 4, 128]),
                            start=False, stop=False))
                        k.op("pe", [identf, nm], [pg], lambda e: e.matmul(
                            pg3, lhsT=identf[:], rhs=nm[:, :].unsqueeze(1).to_broadcast([128, 4, 128]),
                            start=False, stop=True))
                        dc = dec[st["dec"] % 3]
                        st["dec"] += 1
                        k.op("act", [pg], [dc], lambda e: e.activation(out=dc[:], in_=pg3, func=AF.Exp))
                        for (g, ha, hb_) in SSD_RUNS[u]:
                            n = hb_ - ha
                            k.op("dve", [dc, cbs], [mt], lambda e: e.tensor_tensor(
                                out=mt[:, ha - h0:hb_ - h0, :], in0=dc[:, ha - 4 * u:hb_ - 4 * u, :],
                                in1=cbs[:, g:g + 1, :].to_broadcast([128, n, 128]), op=ALU.mult))
                    pb2 = nbig()
                    for h in range(h0, h0 + 12):
                        k.op("pe", [mt, xdt], [pb2], lambda e: e.matmul(
                            pb2[:, (h - h0) * 64:(h - h0 + 1) * 64], lhsT=mt[:, h - h0, :], rhs=xdt[:, h * 64:(h + 1) * 64],
                            start=True, stop=True))
                    k.op("dve", [pb2, yo], [ydb[hf]], lambda e: e.tensor_tensor(
                        out=yd[:, h0:h0 + 12, :], in0=pb2[:, 0:768].rearrange("p (h d) -> p h d", h=12),
                        in1=yo[:, h0:h0 + 12, :], op=ALU.add))
                for hf in range(2):
                    h0 = hf * 12
                    pb3 = nbig()
                    for u in range(3 * hf, 3 * hf + 3):
                        for (g, ha, hb_) in SSD_RUNS[u]:
                            k.op("pe", [bk, xde], [pb3], lambda e: e.matmul(
                                pb3[:, (ha - h0) * 64:(hb_ - h0) * 64], lhsT=bk[:, g * 128:(g + 1) * 128],
                                rhs=xde[:, ha * 64:hb_ * 64], start=True, stop=True))
                    k.op("pool", [H[hf], cdec], [H[hf]], lambda e: e.tensor_tensor(
                        out=H[hf][:, :].rearrange("p (h d) -> p h d", d=64), in0=H[hf][:, :].rearrange("p (h d) -> p h d", d=64),
                        in1=cdec[:, h0:h0 + 12].unsqueeze(2).to_broadcast([128, 12, 64]), op=ALU.mult))
                    k.op("dve", [H[hf], pb3], [H[hf]], lambda e: e.tensor_tensor(
                        out=H[hf][:], in0=H[hf][:], in1=pb3[:, 0:768], op=ALU.add))
                    k.op("act", [H[hf]], [Hb[hf]], lambda e: e.copy(out=Hb[hf][:], in_=H[hf][:]))
                ydf = yd[:, :, :].rearrange("p h d -> p (h d)")
                if d == 1:
                    yt, ysm = ystore.next()
                    k.op("pool", ydb, [yt], lambda e: e.tensor_copy(out=yt[:], in_=ydf))
                    k.dma("sp", ysm, S["YB"][r0:r0 + 128, :], yt[:], [yt], [S["YB_b"]])
                else:
                    k.op("dve", ydb + [ybk], [y1], lambda e: e.tensor_tensor(out=y1[:], in0=ydf, in1=ybk[:], op=ALU.add))
                    k.op("pool", [xk, Dsk], [sz], lambda e: e.tensor_tensor(
                        out=sz[:, :].rearrange("p (h d) -> p h d", h=24), in0=xk[:],
                        in1=Dsk[:, :].unsqueeze(2).to_broadcast([128, 24, 64]), op=ALU.mult))
                    k.op("dve", [y1, sz], [y1], lambda e: e.tensor_tensor(out=y1[:], in0=y1[:], in1=sz[:], op=ALU.add))
                    k.op("act", [zk], [sz], lambda e: e.activation(out=sz[:], in_=zk[:], func=AF.Silu))
                    k.op("dve", [y1, sz], [y1], lambda e: e.tensor_tensor(out=y1[:], in0=y1[:], in1=sz[:], op=ALU.mult))
                    for g in range(4):
                        k.op("act", [y1], [junk, gss], lambda e: e.activation(
                            out=junk[:], in_=y1[:, g * 384:(g + 1) * 384], func=AF.Square, accum_out=gss[:, g:g + 1]))
                    k.op("dve", [gss], [gss], lambda e: e.tensor_scalar(out=gss[:], in0=gss[:], scalar1=1.0 / 384,
                                                                        scalar2=EPS, op0=ALU.mult, op1=ALU.add))
                    k.op("act", [gss], [gss], lambda e: e.activation(out=gss[:], in_=gss[:], func=AF.Sqrt))
                    k.op("dve", [gss], [gss], lambda e: e.reciprocal(out=gss[:], in_=gss[:]))
                    k.op("dve", [y1, gss], [y1], lambda e: e.tensor_tensor(
                        out=y1[:, :].rearrange("p (g c) -> p g c", g=4), in0=y1[:, :].rearrange("p (g c) -> p g c", g=4),
                        in1=gss[:, :].unsqueeze(2).to_broadcast([128, 4, 384]), op=ALU.mult))
                    k.op("pool", [y1, normw], [y1], lambda e: e.tensor_tensor(out=y1[:], in0=y1[:], in1=normw[:], op=ALU.mult))
                    yt, ysm = ytr.next()
                    for q in range(3):
                        for i in range(4):
                            cc = q * 4 + i
                            k.op("pe", [y1, identf], [ptr], lambda e: e.transpose(
                                out=ptr[:, i * 128:(i + 1) * 128], in_=y1[:, cc * 128:(cc + 1) * 128], identity=identf[:]))
                        k.op("act", [ptr], [yt], lambda e: e.copy(
                            out=yt[:, q * 4:(q + 1) * 4, :], in_=ptr[:, :].rearrange("p (a t) -> p a t", a=4)))
                    k.dma("sp", ysm, S["YT"][0:D_SSD, r0:r0 + 128].rearrange("(c p) t -> p c t", p=128), yt[:],
                          [yt], [S["YT_b"]])
        k.barrier()
    k.pop()


def phase_sc(k, L, lyr, W, S, C):
    onesf = C["onesf"]
    TBK = 512
    k.push()
    with contextlib.ExitStack() as es:
        sem = k.newsem()
        cw = k.sb(es, [128, 18], F32, "scw")
        nw = k.sb(es, [128, 6], F32, "scn")
        k.dma("sp", sem, cw[:], W["sc_conv_wl"][lyr], [], [cw], sync=True)
        k.dma("sp", sem, nw[:], W["sc_norm_l"][lyr], [], [nw], sync=True)
        bgr = Ring(k, es, 2, [128, TBK], F32, "bg")
        cgr = Ring(k, es, 2, [128, TBK + 2], F32, "cg")
        hxr = Ring(k, es, 2, [128, TBK + 2], F32, "hx")
        prod = k.sb(es, [128, TBK + 2], F32, "prod")
        yc = [k.sb(es, [128, TBK], F32, "yc") for _ in range(6)]
        ysq = [k.sb(es, [128, TBK], F32, "ysq") for _ in range(2)]
        rst = k.sb(es, [128, TBK], F32, "rst")
        outr = Ring(k, es, 3, [128, TBK], BF16, "sco")
        pss = k.ps(es, [128, 512], F32, "pss")
        for tb in range(L // TBK):
            t0 = tb * TBK
            lo = 1 if t0 == 0 else 0
            hi = 1 if t0 + TBK == L else 0
            for c in range(6):
                bg, s0 = bgr.next()
                cg, s1 = cgr.next()
                hx, s2 = hxr.next()
                k.dma("sp", s0, bg[:], S["SCT"][c * 128:(c + 1) * 128, t0:t0 + TBK], [S["SCT_b"]], [bg])
                for (tl, sm_, base) in ((cg, s1, 768), (hx, s2, 1536)):
                    if lo:
                        k.op("pool", [], [tl], lambda e: e.memset(tl[:, 0:1], 0.0))
                    if hi:
                        k.op("pool", [], [tl], lambda e: e.memset(tl[:, TBK + 1:TBK + 2], 0.0))
                    k.dma("sp", sm_, tl[:, lo:TBK + 2 - hi],
                          S["SCT"][base + c * 128:base + (c + 1) * 128, t0 - 1 + lo:t0 + TBK + 1 - hi],
                          [S["SCT_b"]], [tl])
                k.op("pool", [cg, hx], [prod], lambda e: e.tensor_tensor(out=prod[:], in0=cg[:], in1=hx[:], op=ALU.mult))
                y = yc[c]
                k.op("dve", [prod, cw], [y], lambda e: e.tensor_scalar(
                    out=y[:], in0=prod[:, 0:TBK], scalar1=cw[:, c * 3:c * 3 + 1], scalar2=None, op0=ALU.mult))
                for t in (1, 2):
                    k.op("dve", [prod, cw, y], [y], lambda e: e.scalar_tensor_tensor(
                        out=y[:], in0=prod[:, t:t + TBK], scalar=cw[:, c * 3 + t:c * 3 + t + 1], in1=y[:],
                        op0=ALU.mult, op1=ALU.add))
                k.op("dve", [y, bg], [y], lambda e: e.tensor_tensor(out=y[:], in0=y[:], in1=bg[:], op=ALU.mult))
                q = ysq[c % 2]
                k.op("act", [y], [q], lambda e: e.activation(out=q[:], in_=y[:], func=AF.Square))
                k.op("pe", [onesf, q], [pss], lambda e: e.matmul(pss[:, :], lhsT=onesf[:], rhs=q[:],
                                                                 start=(c == 0), stop=(c == 5)))
            k.op("dve", [pss], [rst], lambda e: e.tensor_scalar(out=rst[:], in0=pss[:], scalar1=1.0 / D_SC, scalar2=EPS,
                                                                op0=ALU.mult, op1=ALU.add))
            k.op("act", [rst], [rst], lambda e: e.activation(out=rst[:], in_=rst[:], func=AF.Sqrt))
            k.op("dve", [rst], [rst], lambda e: e.reciprocal(out=rst[:], in_=rst[:]))
            for c in range(6):
                o, so_ = outr.next()
                k.op("dve", [yc[c], nw, rst], [o], lambda e: e.scalar_tensor_tensor(
                    out=o[:], in0=yc[c][:], scalar=nw[:, c:c + 1], in1=rst[:], op0=ALU.mult, op1=ALU.mult))
                k.dma("sp", so_, S["YT"][D_SSD + c * 128:D_SSD + (c + 1) * 128, t0:t0 + TBK], o[:], [o], [S["YT_b"]])
        k.barrier()
    k.pop()


def phase_att(k, L, lyr, W, S, C):
    identb = C["identb"]
    with contextlib.ExitStack() as es:
        for g, dil in enumerate((1, 4, 16)):
            if g not in ATT_GROUPS:
                continue
            Lsub = L // dil
            nqb = Lsub // 128
            W_ = 128 * dil
            k.push()
            with contextlib.ExitStack() as es2:
                kcr = Ring(k, es2, 3, [128, 2, W_], BF16, "kc")
                qwr = Ring(k, es2, 2, [128, 2, 2, W_], BF16, "qw")
                qwr2 = [k.newsem() for _ in range(2)]
                vcr = [Ring(k, es2, 3, [128, 4, 65], BF16, "vc") for _ in range(dil)]
                ptr_ = Ring(k, es2, 3, [128, 512], BF16, "pT", dma=False)
                otr = Ring(k, es2, 3, [128, 4, 65], F32, "ot")
                pst_ = [k.ps(es2, [128, 512], F32, "pst") for _ in range(3)]
                pso = [k.ps(es2, [128, 512], F32, "pso") for _ in range(2)]
                for r in range(dil):
                    for t, _s in zip(vcr[r].tiles, vcr[r].sems):
                        k.op("pool", [], [t], lambda e: e.memset(t[:], 1.0))
                for t in kcr.tiles + qwr.tiles:
                    k.op("pool", [], [t], lambda e: e.memset(t[:], 0.0))
                cnt = {"st": 0, "o": 0}
                for hg in range(3):
                    row0 = g * 768 + hg * 256
                    kch = {}
                    vch = {}

                    def load_chunk(j):
                        p0 = 128 * j - 64
                        lo = 64 if j == 0 else 0
                        hi = 64 if j == nqb else 128
                        kt, ks = kcr.next()
                        tok_lo = (p0 + lo) * dil
                        tok_hi = (p0 + hi) * dil
                        k.dma("sp", ks, kt[:, :, lo * dil:hi * dil],
                              S["KT"][row0:row0 + 256, tok_lo:tok_hi].rearrange("(hp p) t -> p hp t", p=128),
                              [S["KT_b"]], [kt])
                        kch[j] = kt
                        vs = []
                        for r in range(dil):
                            vt, vsm = vcr[r].next()
                            rows = S["V"][tok_lo:tok_hi, row0:row0 + 256].rearrange("(i r) (h d) -> r i h d", r=dil, d=64)[r]
                            k.dma("sp", vsm, vt[lo:hi, :, 0:64], rows, [S["V_b"]], [vt])
                            vs.append(vt)
                        vch[j] = vs

                    load_chunk(0)
                    for w in range(nqb):
                        load_chunk(w + 1)
                        qs2 = qwr2[qwr.i]
                        qt, qs = qwr.next()
                        qsrc = S["QT"][row0:row0 + 256, w * W_:(w + 1) * W_].rearrange("(hp p) t -> p hp t", p=128)
                        k.dma("sp", qs, qt[0:64, :, 0, :], qsrc[0:64], [S["QT_b"]], [qt])
                        k.dma("sp", qs2, qt[64:128, :, 1, :], qsrc[64:128], [S["QT_b"]], [qt])
                        for r in range(dil):
                            po = pso[cnt["o"] % 2]
                            cnt["o"] += 1
                            for c in (0, 1):
                                if ATT_STAGE < 2:
                                    break
                                j = w + c
                                lo = 64 if j == 0 else 0
                                hi = 64 if j == nqb else 128
                                kt = kch[j]
                                vt = vch[j][r]
                                ps_ = pst_[cnt["st"] % 3]
                                cnt["st"] += 1
                                for hp in range(2):
                                    kv = kt[:, hp, :].rearrange("p (i r) -> p i r", r=dil)[:, :, r]
                                    qv = qt[:, hp, :, :].rearrange("p a (i r) -> p a i r", r=dil)[:, :, :, r]
                                    k.op("pe", [kt, qt], [ps_], lambda e: e.matmul(
                                        ps_[:, hp * 256:(hp + 1) * 256].rearrange("p (a i) -> p a i", a=2), lhsT=kv, rhs=qv,
                                        start=(hp == 0), stop=False))
                                if c == 0:
                                    mk = C["maskA0"] if j == 0 else C["maskA"]
                                else:
                                    mk = C["maskB1"] if j == nqb else C["maskB"]
                                k.op("pe", [identb, mk], [ps_], lambda e: e.matmul(
                                    ps_[:, :], lhsT=identb[:], rhs=mk[:], start=False, stop=True))
                                if ATT_STAGE < 3:
                                    continue
                                pt, _ = ptr_.next()
                                k.op("act", [ps_], [pt], lambda e: e.activation(out=pt[:, :], in_=ps_[:, :],
                                                                               func=AF.Exp, scale=0.125))
                                if ATT_STAGE < 4:
                                    continue
                                for hh in range(4):
                                    k.op("pe", [pt, vt], [po], lambda e: e.matmul(
                                        po[:, hh * 65:(hh + 1) * 65], lhsT=pt[:, hh * 128:(hh + 1) * 128],
                                        rhs=vt[:, hh, :], start=(c == 0 and hh == 0), stop=(c == 1 and hh == 3)))
                            if ATT_STAGE < 5:
                                continue
                            ot, osm = otr.next()
                            k.op("act", [po], [ot], lambda e: e.copy(out=ot[:], in_=po[:, 0:260].rearrange("p (h d) -> p h d", h=4)))
                            dst = S["O%d" % g][w * W_:(w + 1) * W_, hg * 260:(hg + 1) * 260].rearrange(
                                "(i r) (h d) -> r i h d", r=dil, d=65)[r]
                            k.dma("sp", osm, dst, ot[:], [ot], [S["O%d_b" % g]])
                k.barrier()
            k.pop()
        if ATT_STAGE < 6:
            return
        k.push()
        with contextlib.ExitStack() as es2:
            identf = C["identf"]
            sem = k.newsem()
            nw = k.sb(es2, [128, D_ATT], F32, "attn")
            k.dma("sp", sem, nw[:], W["att_norm"][lyr:lyr + 1, :].broadcast_to([128, D_ATT]), [], [nw], sync=True)
            o0 = Ring(k, es2, 2, [128, 12, 65], F32, "o0")
            o1 = Ring(k, es2, 2, [128, 12, 65], F32, "o1")
            o2 = Ring(k, es2, 2, [128, 12, 65], F32, "o2")
            rl = k.sb(es2, [128, 12], F32, "rl")
            ov = k.sb(es2, [128, 12, 64], F32, "ov")
            junk = k.sb(es2, [128, D_ATT], F32, "junk")
            ssq = k.sb(es2, [128, 1], F32, "ssq")
            rstd = k.sb(es2, [128, 1], F32, "rstd")
            ytr = Ring(k, es2, 2, [128, 6, 128], BF16, "aytr")
            ptr = [k.ps(es2, [128, 512], F32, "ptr") for _ in range(2)]
            for tt in range(L // 128):
                r0 = tt * 128
                a0, s0 = o0.next()
                a1, s1 = o1.next()
                a2, s2 = o2.next()
                k.dma("sp", s0, a0[:], S["O0"][r0:r0 + 128, :].rearrange("p (h d) -> p h d", h=12), [S["O0_b"]], [a0])
                k.dma("sp", s1, a1[:], S["O1"][r0:r0 + 128, :].rearrange("p (h d) -> p h d", h=12), [S["O1_b"]], [a1])
                k.dma("sp", s2, a2[:], S["O2"][r0:r0 + 128, :].rearrange("p (h d) -> p h d", h=12), [S["O2_b"]], [a2])
                k.op("pool", [a0, a1], [a0], lambda e: e.tensor_tensor(out=a0[:], in0=a0[:], in1=a1[:], op=ALU.add))
                k.op("dve", [a0, a2], [a0], lambda e: e.tensor_tensor(out=a0[:], in0=a0[:], in1=a2[:], op=ALU.add))
                k.op("dve", [a0], [rl], lambda e: e.reciprocal(out=rl[:], in_=a0[:, :, 64]))
                k.op("dve", [a0, rl], [ov], lambda e: e.tensor_tensor(
                    out=ov[:], in0=a0[:, :, 0:64], in1=rl[:, :].unsqueeze(2).to_broadcast([128, 12, 64]), op=ALU.mult))
                ovf = ov[:, :, :].rearrange("p h d -> p (h d)")
                k.op("act", [ov], [junk, ssq], lambda e: e.activation(out=junk[:], in_=ovf, func=AF.Square,
                                                                      accum_out=ssq[:, 0:1]))
                rmsnorm_rstd(k, ssq, rstd, D_ATT)
                k.op("dve", [ov, rstd, nw], [junk], lambda e: e.scalar_tensor_tensor(
                    out=junk[:], in0=ovf, scalar=rstd[:, 0:1], in1=nw[:], op0=ALU.mult, op1=ALU.mult))
                yt, ysm = ytr.next()
                for q in range(2):
                    pt = ptr[q]
                    n = 4 if q == 0 else 2
                    for i in range(n):
                        cc = q * 4 + i
                        k.op("pe", [junk, identf], [pt], lambda e: e.transpose(
                            out=pt[:, i * 128:(i + 1) * 128], in_=junk[:, cc * 128:(cc + 1) * 128], identity=identf[:]))
                    k.op("act", [pt], [yt], lambda e: e.copy(
                        out=yt[:, q * 4:q * 4 + n, :], in_=pt[:, 0:n * 128].rearrange("p (a t) -> p a t", a=n)))
                k.dma("sp", ysm, S["YT"][2304:3072, r0:r0 + 128].rearrange("(c p) t -> p c t", p=128), yt[:],
                      [yt], [S["YT_b"]])
            k.barrier()
        k.pop()


def phase_wout(k, L, lyr, xsrc, xsrc_b, W, S, C):
    k.push()
    with contextlib.ExitStack() as es:
        wo = k.sb(es, [128, 24, D_MODEL], BF16, "wo")
        stg = Ring(k, es, 2, [128, 2, D_MODEL], F32, "wstg")
        for c4 in range(12):
            st, ss = stg.next()
            k.dma("sp", ss, st[:], W["w_out"][lyr, c4 * 256:(c4 + 1) * 256, :].rearrange("(c p) n -> p c n", p=128), [], [st])
            eng = "pool" if c4 % 2 == 0 else "act"
            if eng == "pool":
                k.op("pool", [st], [wo], lambda e: e.tensor_copy(out=wo[:, c4 * 2:(c4 + 1) * 2, :], in_=st[:]))
            else:
                k.op("act", [st], [wo], lambda e: e.copy(out=wo[:, c4 * 2:(c4 + 1) * 2, :], in_=st[:]))
        ytr = Ring(k, es, 2, [128, 24, 128], BF16, "yT")
        xr = Ring(k, es, 2, [128, D_MODEL], F32, "xo")
        outr = Ring(k, es, 2, [128, D_MODEL], F32, "xn")
        psm = [k.ps(es, [128, 512], F32, "psm") for _ in range(4)]
        for tt in range(L // 128):
            r0 = tt * 128
            yt, ys = ytr.next()
            k.dma("sp", ys, yt[:], S["YT"][:, r0:r0 + 128].rearrange("(c p) t -> p c t", p=128), [S["YT_b"]], [yt])
            xt, xs = xr.next()
            k.dma("sp", xs, xt[:], xsrc[r0:r0 + 128, :], [xsrc_b], [xt])
            ot, osm = outr.next()
            for nb in range(4):
                pm = psm[nb]
                for c in range(24):
                    k.op("pe", [yt, wo], [pm], lambda e: e.matmul(pm[:, :], lhsT=yt[:, c, :], rhs=wo[:, c, nb * 512:(nb + 1) * 512],
                                                                   start=(c == 0), stop=(c == 23)))
                k.op("dve", [pm, xt], [ot], lambda e: e.tensor_tensor(out=ot[:, nb * 512:(nb + 1) * 512], in0=pm[:],
                                                                      in1=xt[:, nb * 512:(nb + 1) * 512], op=ALU.add))
            k.dma("sp", osm, S["XRES"][r0:r0 + 128, :], ot[:], [ot], [S["XRES_b"]])
        k.barrier()
    k.pop()


def peer_tables_bf16(k, lyr, W, S):
    k.push()
    with contextlib.ExitStack() as es:
        stg = Ring(k, es, 3, [128, 4096], F32, "tstg")
        outb = Ring(k, es, 3, [128, 4096], BF16, "tout")
        n = 0
        for name, col0 in (("peer_u", 0), ("peer_v", D_MODEL)):
            dst = "UV"
            src = W[name][lyr * N_EXP:(lyr + 1) * N_EXP, :].rearrange("(c p two) d -> c p (two d)", p=128, two=2)
            dv = S["UV"][:, col0:col0 + D_MODEL].rearrange("(c p two) d -> c p two d", p=128, two=2)
            for c in range(N_EXP // 256):
                st, ss = stg.next()
                ob, os_ = outb.next()
                k.dma("sp", ss, st[:], src[c], [], [st])
                e = ("act", "pool", "dve")[n % 3]
                n += 1
                if e == "act":
                    k.op("act", [st], [ob], lambda en: en.copy(out=ob[:], in_=st[:]))
                else:
                    k.op(e, [st], [ob], lambda en: en.tensor_copy(out=ob[:], in_=st[:]))
                k.dma("sp", os_, dv[c], ob[:, :].rearrange("p (two d) -> p two d", two=2), [ob], [S[dst + "_b"]])
        k.barrier()
    k.pop()


def phase_peer_a(k, L, lyr, W, S, C):
    identf, identb = C["identf"], C["identb"]
    NEG = -1.0e30
    k.push()
    with contextlib.ExitStack() as es:
        sem = k.newsem()
        gB = k.sb(es, [128, D_MODEL], F32, "gB")
        k.dma("sp", sem, gB[:], W["norm_ffn"][lyr:lyr + 1, :].broadcast_to([128, D_MODEL]), [], [gB], sync=True)
        wq = k.sb(es, [128, 16, D_MODEL], BF16, "wq")
        skT = k.sb(es, [128, 16, 128], BF16, "skT")
        pq = [k.ps(es, [128, 512], F32, "pq") for _ in range(4)]
        pqs = [k.ps(es, [128, 512], F32, "pqs") for _ in range(2)]
        pst = [k.ps(es, [128, 1024], BF16, "pst") for _ in range(2)]
        with contextlib.ExitStack() as es1:
            stg = Ring(k, es1, 2, [128, 2, D_MODEL], F32, "wstg")
            for c2 in range(8):
                st, ss = stg.next()
                k.dma("sp", ss, st[:], W["peer_wq"][lyr, c2 * 256:(c2 + 1) * 256, :].rearrange("(c p) n -> p c n", p=128), [], [st])
                if c2 % 2 == 0:
                    k.op("pool", [st], [wq], lambda e: e.tensor_copy(out=wq[:, c2 * 2:(c2 + 1) * 2, :], in_=st[:]))
                else:
                    k.op("act", [st], [wq], lambda e: e.copy(out=wq[:, c2 * 2:(c2 + 1) * 2, :], in_=st[:]))
            skf = k.sb(es1, [128, 16, 128], F32, "skf")
            k.dma("sp", sem, skf[:], W["peer_subkeys"][lyr].rearrange("m n d -> n m d"), [], [skf], sync=True)
            for q4 in range(4):
                pm = pq[q4]
                for i in range(4):
                    m = q4 * 4 + i
                    k.op("pe", [skf, identf], [pm], lambda e: e.transpose(out=pm[:, i * 128:(i + 1) * 128], in_=skf[:, m, :],
                                                                          identity=identf[:]))
                k.op("act", [pm], [skT], lambda e: e.copy(out=skT[:, q4 * 4:(q4 + 1) * 4, :],
                                                          in_=pm[:, :].rearrange("p (a n) -> p a n", a=4)))
            k.barrier()
        xr = Ring(k, es, 2, [128, D_MODEL], F32, "xm")
        ssq = k.sb(es, [128, 1], F32, "ssq")
        rstd = k.sb(es, [128, 1], F32, "rstd")
        hn = k.sb(es, [128, D_MODEL], F32, "hn")
        hnbr = Ring(k, es, 2, [128, D_MODEL], BF16, "hnb")
        eidr = Ring(k, es, 2, [128, 128], I32, "eido")
        gater = Ring(k, es, 2, [128, 8, 16], F32, "gateo")
        hnT = k.sb(es, [128, 16, 128], BF16, "hnT")
        qTb = k.sb(es, [128, 16, 128], BF16, "qTb")
        scs = [k.sb(es, [128, 16, 128], F32, "sc") for _ in range(2)]
        carry = {}
        scw = k.sb(es, [128, 128], F32, "scw")
        sv = k.sb(es, [128, 16, 16], F32, "sv")
        si = k.sb(es, [128, 16, 16], U32, "si")
        sif = k.sb(es, [128, 16, 16], F32, "sif")
        cand = k.sb(es, [128, 8, 256], F32, "cand")
        cidx = k.sb(es, [128, 8, 256], F32, "cidx")
        cw_ = k.sb(es, [128, 256], F32, "cw_")
        top = k.sb(es, [128, 8, 16], F32, "top")
        zs = k.sb(es, [128, 8], F32, "zs")
        j256 = k.sb(es, [128, 256], F32, "j256")
        eidf = k.sb(es, [128, 128], F32, "eidf")
        eidf_w = Buf(multi=True)
        pre = k.sb(es, [128, 128], F32, "pre")
        pre_w = Buf(multi=True)
        actg = k.sb(es, [128, 128], F32, "actg")

        def stage_front(tt):
            r0 = tt * 128
            eid, eid_s = eidr.next()
            gate, gate_s = gater.next()
            hnb, hnb_s = hnbr.next()
            sc = scs[tt % 2]
            xm, xs = xr.next()
            k.dma("sp", xs, xm[:], S["XRES"][r0:r0 + 128, :], [S["XRES_b"]], [xm])
            k.op("act", [xm], [hnb, ssq], lambda e: e.activation(out=hnb[:], in_=xm[:], func=AF.Square,
                                                                   accum_out=ssq[:, 0:1]))
            rmsnorm_rstd(k, ssq, rstd, D_MODEL)
            k.op("dve", [xm, rstd, gB], [hn], lambda e: e.scalar_tensor_tensor(
                out=hn[:], in0=xm[:], scalar=rstd[:, 0:1], in1=gB[:], op0=ALU.mult, op1=ALU.mult))
            k.op("act", [hn], [hnb], lambda e: e.copy(out=hnb[:], in_=hn[:]))
            for half in range(2):
                pt = pst[half]
                for j in range(8):
                    kc = half * 8 + j
                    k.op("pe", [hnb, identb], [pt], lambda e: e.transpose(
                        out=pt[:, j * 128:(j + 1) * 128], in_=hnb[:, kc * 128:(kc + 1) * 128], identity=identb[:]))
                k.op("act", [pt], [hnT], lambda e: e.copy(out=hnT[:, half * 8:(half + 1) * 8, :],
                                                          in_=pt[:, :].rearrange("p (j t) -> p j t", j=8)))
            for q4 in range(4):
                pm = pqs[q4 % 2]
                for i in range(4):
                    m = q4 * 4 + i
                    for kc in range(16):
                        k.op("pe", [wq, hnT], [pm], lambda e: e.matmul(
                            pm[:, i * 128:(i + 1) * 128], lhsT=wq[:, kc, m * 128:(m + 1) * 128], rhs=hnT[:, kc, :],
                            start=(kc == 0), stop=(kc == 15)))
                k.op("act", [pm], [qTb], lambda e: e.copy(out=qTb[:, q4 * 4:(q4 + 1) * 4, :],
                                                          in_=pm[:, :].rearrange("p (a t) -> p a t", a=4)))
            for q4 in range(4):
                pm = pqs[q4 % 2]
                for i in range(4):
                    m = q4 * 4 + i
                    k.op("pe", [qTb, skT], [pm], lambda e: e.matmul(
                        pm[:, i * 128:(i + 1) * 128], lhsT=qTb[:, m, :], rhs=skT[:, m, :], start=True, stop=True))
                k.op("act", [pm], [sc], lambda e: e.copy(out=sc[:, q4 * 4:(q4 + 1) * 4, :],
                                                         in_=pm[:, :].rearrange("p (a n) -> p a n", a=4)))
            carry[tt] = (eid, eid_s, gate, gate_s, hnb, hnb_s)

        def stage_back(tt):
            r0 = tt * 128
            sc = scs[tt % 2]
            eid, eid_s, gate, gate_s, hnb, hnb_s = carry.pop(tt)
            for m in range(16):
                k.op("dve", [sc], [sv], lambda e: e.max(out=sv[:, m, 0:8], in_=sc[:, m, :]))
                k.op("dve", [sc, sv], [scw], lambda e: e.match_replace(out=scw[:], in_to_replace=sv[:, m, 0:8],
                                                                       in_values=sc[:, m, :], imm_value=NEG))
                k.op("dve", [scw], [sv], lambda e: e.max(out=sv[:, m, 8:16], in_=scw[:]))
                k.op("dve", [sc, sv], [si], lambda e: e.max_index(out=si[:, m, 0:8], in_max=sv[:, m, 0:8], in_values=sc[:, m, :]))
                k.op("dve", [sc, sv], [si], lambda e: e.max_index(out=si[:, m, 8:16], in_max=sv[:, m, 8:16], in_values=sc[:, m, :]))
            k.op("dve", [si], [sif], lambda e: e.tensor_copy(out=sif[:], in_=si[:]))
            svv = sv[:, :, :].rearrange("p (h two) a -> p h two a", two=2)
            sfv = sif[:, :, :].rearrange("p (h two) a -> p h two a", two=2)
            c4 = cand[:, :, :].rearrange("p h (a b) -> p h a b", a=16)
            x4 = cidx[:, :, :].rearrange("p h (a b) -> p h a b", a=16)
            k.op("dve", [sv], [cand], lambda e: e.tensor_tensor(
                out=c4, in0=svv[:, :, 0, :].unsqueeze(3).to_broadcast([128, 8, 16, 16]),
                in1=svv[:, :, 1, :].unsqueeze(2).to_broadcast([128, 8, 16, 16]), op=ALU.add))
            k.op("dve", [sif], [sif], lambda e: e.tensor_scalar(out=sfv[:, :, 0, :], in0=sfv[:, :, 0, :], scalar1=128.0,
                                                                scalar2=None, op0=ALU.mult))
            k.op("dve", [sif], [cidx], lambda e: e.tensor_tensor(
                out=x4, in0=sfv[:, :, 0, :].unsqueeze(3).to_broadcast([128, 8, 16, 16]),
                in1=sfv[:, :, 1, :].unsqueeze(2).to_broadcast([128, 8, 16, 16]), op=ALU.add))
            for h in range(8):
                k.op("dve", [cand], [top], lambda e: e.max(out=top[:, h, 0:8], in_=cand[:, h, :]))
                k.op("dve", [cand, top], [cw_], lambda e: e.match_replace(out=cw_[:], in_to_replace=top[:, h, 0:8],
                                                                          in_values=cand[:, h, :], imm_value=NEG))
                k.op("dve", [cw_], [top], lambda e: e.max(out=top[:, h, 8:16], in_=cw_[:]))
            k.op("dve", [top], [gate], lambda e: e.tensor_tensor(
                out=gate[:], in0=top[:], in1=top[:, :, 0:1].to_broadcast([128, 8, 16]), op=ALU.subtract))
            k.op("act", [gate], [gate], lambda e: e.activation(out=gate[:], in_=gate[:], func=AF.Exp))
            k.op("dve", [gate], [zs], lambda e: e.tensor_reduce(out=zs[:], in_=gate[:], axis=AX.X, op=ALU.add))
            k.op("dve", [zs], [zs], lambda e: e.reciprocal(out=zs[:], in_=zs[:]))
            k.op("dve", [gate, zs], [gate], lambda e: e.tensor_tensor(
                out=gate[:], in0=gate[:], in1=zs[:, :].unsqueeze(2).to_broadcast([128, 8, 16]), op=ALU.mult))
            k.op("dve", [], [eidf], lambda e: e.memset(eidf[:], 0.0))
            for h in range(8):
                for kk in range(16):
                    k.op("dve", [cand, top, cidx, eidf], [j256, eidf_w], lambda e: e.scalar_tensor_tensor(
                        out=j256[:], in0=cand[:, h, :], scalar=top[:, h, kk:kk + 1], in1=cidx[:, h, :],
                        op0=ALU.is_equal, op1=ALU.mult, accum_out=eidf[:, h * 16 + kk:h * 16 + kk + 1]))
            k.op("dve", [eidf, eidf_w], [eidf], lambda e: e.tensor_scalar(out=eidf[:], in0=eidf[:], scalar1=0.0,
                                                                          scalar2=float(N_EXP - 1), op0=ALU.max, op1=ALU.min))
            k.op("dve", [eidf], [eid], lambda e: e.tensor_copy(out=eid[:], in_=eidf[:]))

            k.dma("sp", hnb_s, S["HNB"][r0:r0 + 128, :], hnb[:], [hnb], [S["HNB_b"]])
            k.dma("sp", eid_s, S["EID"][r0:r0 + 128, :], eid[:], [eid], [S["EID_b"]])
            k.dma("sp", gate_s, S["GATE"][r0:r0 + 128, :], gate[:, :, :].rearrange("p h a -> p (h a)"), [gate], [S["GATE_b"]])

        NT_ = L // 128
        stage_front(0)
        for tt in range(NT_):
            if tt + 1 < NT_:
                stage_front(tt + 1)
            stage_back(tt)
        k.barrier()
    k.pop()


def phase_peer_b(k, L, lyr, W, S, C):
    identb = C["identb"]
    NT = L // 128
    k.push()
    with contextlib.ExitStack() as es:
        hnbr = Ring(k, es, 2, [128, D_MODEL], BF16, "hnbi")
        eidr = Ring(k, es, 2, [128, 128], I32, "eidi")
        gater = Ring(k, es, 2, [128, 128], F32, "gatei")
        xmr = Ring(k, es, 2, [128, D_MODEL], F32, "xmi")
        gr = Ring(k, es, 18, [128, 2 * D_MODEL], BF16, "uvg", sw=True)
        dgr = [k.sb(es, [128, 128], BF16, "dg") for _ in range(4)]
        pre = k.sb(es, [128, 128], F32, "pre")
        preb = [Buf() for _ in range(128)]
        gel = k.sb(es, [128, 128], F32, "gel")
        gelb = [Buf() for _ in range(128)]
        outr = Ring(k, es, 2, [128, D_MODEL], F32, "xn")
        pq = [k.ps(es, [128, 512], F32, "pq") for _ in range(4)]
        uv = S["UV"]
        st = {}

        def load(tt):
            r0 = tt * 128
            hnb, s1 = hnbr.next()
            eid, s2 = eidr.next()
            gate, s3 = gater.next()
            xm, s4 = xmr.next()
            k.dma("sp", s1, hnb[:], S["HNB"][r0:r0 + 128, :], [S["HNB_b"]], [hnb])
            k.dma("sp", s2, eid[:], S["EID"][r0:r0 + 128, :], [S["EID_b"]], [eid])
            k.dma("sp", s3, gate[:], S["GATE"][r0:r0 + 128, :], [S["GATE_b"]], [gate])
            k.dma("sp", s4, xm[:], S["XRES"][r0:r0 + 128, :], [S["XRES_b"]], [xm])
            st[tt] = (hnb, eid, gate, xm)

        load(0)
        for tt in range(NT):
            if tt + 1 < NT:
                load(tt + 1)
            hnb, eid, gate, xm = st.pop(tt)
            r0 = tt * 128
            k.op("dve", [], preb, lambda e: e.memset(pre[:], 0.0))
            gts = {}
            for s_ in range(129):
                if s_ < 128:
                    hk = s_
                    gt, gs = gr.next()
                    gts[hk] = gt
                    k.dma("pool", gs, gt[:], uv, [eid, S["UV_b"]], [gt],
                          indirect=bass.IndirectOffsetOnAxis(ap=eid[:, hk:hk + 1], axis=0))
                    k.op("dve", [gt, hnb], [gt, preb[hk]], lambda e: e.scalar_tensor_tensor(
                        out=gt[:, 0:D_MODEL], in0=gt[:, 0:D_MODEL], scalar=1.0, in1=hnb[:], op0=ALU.mult, op1=ALU.mult,
                        accum_out=pre[:, hk:hk + 1]))
                    k.op("act", [preb[hk]], [gelb[hk]], lambda e: e.activation(
                        out=gel[:, hk:hk + 1], in_=pre[:, hk:hk + 1], func=AF.Gelu))
                    k.op("act", [gelb[hk], gate], [gelb[hk]], lambda e: e.activation(
                        out=gel[:, hk:hk + 1], in_=gel[:, hk:hk + 1], func=AF.Copy, scale=gate[:, hk:hk + 1]))
                if s_ >= 1:
                    hk = s_ - 1
                    d_ = dgr[hk % 4]
                    gt = gts.pop(hk)
                    k.op("act", [gelb[hk], identb], [d_], lambda e: e.activation(
                        out=d_[:], in_=identb[:], func=AF.Copy, scale=gel[:, hk:hk + 1]))
                    for nb in range(4):
                        k.op("pe", [d_, gt], [pq[nb]], lambda e: e.matmul(
                            pq[nb][:, :], lhsT=d_[:], rhs=gt[:, D_MODEL + nb * 512:D_MODEL + (nb + 1) * 512],
                            start=(hk == 0), stop=(hk == 127)))
            ot, osm = outr.next()
            for nb in range(4):
                k.op("dve", [pq[nb], xm], [ot], lambda e: e.tensor_tensor(
                    out=ot[:, nb * 512:(nb + 1) * 512], in0=pq[nb][:], in1=xm[:, nb * 512:(nb + 1) * 512], op=ALU.add))
            k.dma("sp", osm, S["XRES"][r0:r0 + 128, :], ot[:], [ot], [S["XRES_b"]])
        k.barrier()
    k.pop()


def phase_peer(k, L, lyr, W, S, C):
    peer_tables_bf16(k, lyr, W, S)
    phase_peer_a(k, L, lyr, W, S, C)
    phase_peer_b(k, L, lyr, W, S, C)


def rope_consts(L):
    inv = (500000.0 ** (-np.arange(0, 16, 2, dtype=np.float32) / 16)).astype(np.float32)
    ang = np.arange(L, dtype=np.float32)[:, None] * inv[None, :]
    cosT = np.ones((128, L), np.float32)
    sinT = np.zeros((128, L), np.float32)
    for p in range(128):
        d = p % 64
        if d < 16:
            cosT[p] = np.cos(ang[:, d % 8])
            sinT[p] = np.sin(ang[:, d % 8])
    rotT = np.zeros((128, 128), np.float32)
    for m in range(128):
        d = m % 64
        if d < 8:
            rotT[m + 8, m] = -1.0
        elif d < 16:
            rotT[m - 8, m] = 1.0
    return cosT, sinT, rotT


WEIGHT_SHAPES = {
    "norm_mix": (DEPTH, D_MODEL), "w_in": (DEPTH, D_MODEL, D_IN), "ssd_dt_bias": (DEPTH, 48), "ssd_a_log": (DEPTH, 48),
    "ssd_d": (DEPTH, SSD_HEADS), "ssd_norm": (DEPTH, D_SSD),
    "ssd_conv_wl": (DEPTH, 128, 100), "ssd_conv_bl": (DEPTH, 128, 20), "sc_conv_wl": (DEPTH, 128, 18),
    "sc_norm_l": (DEPTH, 128, 6), "att_norm": (DEPTH, D_ATT), "w_out": (DEPTH, D_MIX, D_MODEL),
    "norm_ffn": (DEPTH, D_MODEL), "peer_wq": (DEPTH, D_MODEL, D_MODEL), "peer_subkeys": (DEPTH, 16, 128, 128),
    "peer_u": (DEPTH * N_EXP, D_MODEL), "peer_v": (DEPTH * N_EXP, D_MODEL), "norm_final": (1, D_MODEL),
}


class LazyW(dict):
    def __init__(self, nc):
        super().__init__()
        self.nc = nc

    def __missing__(self, n):
        v = self.nc.dram_tensor(n, list(WEIGHT_SHAPES[n]), F32, kind="ExternalInput").ap()
        self[n] = v
        return v


def build(L, phases=ALL_PHASES, debug=(), nlayers=DEPTH, feed=()):
    nc = bass.Bass("TRN2", target_bir_lowering=False)
    x = nc.dram_tensor("x", [L, D_MODEL], F32, kind="ExternalInput").ap()
    y = nc.dram_tensor("y", [L, D_MODEL], F32, kind="ExternalOutput").ap()
    W = LazyW(nc)
    S = {}

    def scratch(name, shape, dt):
        kind = "ExternalOutput" if name in debug else ("ExternalInput" if name in feed else "Internal")
        S[name] = nc.dram_tensor("s_" + name, list(shape), dt, kind=kind).ap()
        S[name + "_b"] = Buf(multi=True)

    scratch("Z", [L, D_SSD], F32)
    scratch("XBCT", [XBC, L], F32)
    scratch("DT", [L, 48], F32)
    scratch("SCT", [3 * D_SC, L], F32)
    scratch("QT", [2304, L], BF16)
    scratch("KT", [2304, L], BF16)
    scratch("V", [L, 2304], BF16)
    scratch("XRES", [L, D_MODEL], F32)
    scratch("X", [L, D_SSD], F32)
    scratch("B", [L, 512], BF16)
    scratch("BT", [512, L], BF16)
    scratch("CT", [512, L], BF16)
    scratch("YB", [L, D_SSD], F32)
    scratch("YT", [D_MIX, L], BF16)
    scratch("O0", [L, 780], F32)
    scratch("O1", [L, 780], F32)
    scratch("O2", [L, 780], F32)
    scratch("UV", [N_EXP, 2 * D_MODEL], BF16)
    scratch("HNB", [L, D_MODEL], BF16)
    scratch("EID", [L, 128], I32)
    scratch("GATE", [L, 128], F32)
    x_b = Buf(multi=True)
    y_b = Buf(multi=True)
    with contextlib.ExitStack() as es:
        k = KB(nc, es)
        C = {}
        csem = k.newsem()

        def cload(name, shape=(128, 128), dt=F32):
            d = nc.dram_tensor(name, list(shape), F32, kind="ExternalInput").ap()
            if dt == BF16:
                tb = k.sb(es, list(shape), BF16, name + "b")
                t = k.sb(es_stage, list(shape), F32, name)
                k.dma("sp", csem, t[:], d[:, :], [], [t], sync=True)
                k.op("dve", [t], [tb], lambda e: e.tensor_copy(out=tb[:], in_=t[:]))
                return tb
            t = k.sb(es, list(shape), F32, name)
            k.dma("sp", csem, t[:], d[:, :], [], [t], sync=True)
            return t

        C["identf"] = cload("ident")
        C["rotT"] = cload("rotT")
        C["onesf"] = cload("onesf")
        for n in ("U", "Lo", "nU", "nLo", "nmf", "nmb"):
            C[n] = cload(n)
        onec = k.sb(es, [128, 1], F32, "onec")
        k.op("dve", [], [onec], lambda e: e.memset(onec[:], 1.0))
        C["onec"] = onec
        C["identb"] = k.sb(es, [128, 128], BF16, "identb")
        k.op("dve", [C["identf"]], [C["identb"]], lambda e: e.tensor_copy(out=C["identb"][:], in_=C["identf"][:]))
        bnames = ("maskA", "maskB", "maskA0", "maskB1")
        tbs = {n: k.sb(es, [128, 512], BF16, n + "b") for n in bnames}
        with contextlib.ExitStack() as es_stage:
            for n in bnames:
                d = nc.dram_tensor(n, [128, 512], F32, kind="ExternalInput").ap()
                t = k.sb(es_stage, [128, 512], F32, n)
                k.dma("sp", csem, t[:], d[:, :], [], [t], sync=True)
                k.op("dve", [t], [tbs[n]], lambda e: e.tensor_copy(out=tbs[n][:], in_=t[:]))
                C[n] = tbs[n]
            k.barrier()
        C["cosT"] = nc.dram_tensor("cosT", [128, L], F32, kind="ExternalInput").ap()
        C["sinT"] = nc.dram_tensor("sinT", [128, L], F32, kind="ExternalInput").ap()
        k.barrier()
        xsrc, xsrc_b = x, x_b
        for lyr in range(nlayers):
            if "a" in phases:
                phase_a(k, L, lyr, xsrc, xsrc_b, W, S, C)
            if "conv" in phases:
                phase_conv(k, L, lyr, W, S, C)
            if "ssd" in phases:
                phase_ssd(k, L, lyr, W, S, C)
            if "sc" in phases:
                phase_sc(k, L, lyr, W, S, C)
            if "att" in phases:
                phase_att(k, L, lyr, W, S, C)
            if "wout" in phases:
                phase_wout(k, L, lyr, xsrc, xsrc_b, W, S, C)
            if "peer" in phases:
                phase_peer(k, L, lyr, W, S, C)
            if "wout" in phases:
                xsrc, xsrc_b = S["XRES"], S["XRES_b"]
        if "final" in phases:
            phase_final(k, L, xsrc, xsrc_b, W, y, y_b)
        k.barrier()
    return nc


def host_consts(L):
    cosT, sinT, rotT = rope_consts(L)
    i = np.arange(128)
    U = (i[:, None] <= i[None, :]).astype(np.float32)
    Lo = (i[:, None] >= i[None, :]).astype(np.float32)
    NEGM = -30000.0
    nmf = np.where(i[None, :] >= i[:, None], 0.0, NEGM).astype(np.float32)
    nmb = np.where(i[None, :] <= i[:, None], 0.0, NEGM).astype(np.float32)
    mA = np.where(i[:, None] >= i[None, :], 0.0, NEGM).astype(np.float32)
    mB = np.where(i[:, None] <= i[None, :], 0.0, NEGM).astype(np.float32)
    eye = np.eye(128, dtype=np.float32)
    mA0 = mA.copy()
    mA0[0:64, :] = NEGM
    mB1 = mB.copy()
    mB1[64:128, :] = NEGM
    return {"maskA0": np.ascontiguousarray(np.tile(mA0, (1, 4))), "maskB1": np.ascontiguousarray(np.tile(mB1, (1, 4))),"cosT": cosT, "sinT": sinT, "rotT": rotT, "ident": eye,
            "onesf": np.ones((128, 128), np.float32), "U": U, "Lo": Lo, "nU": -U, "nLo": -Lo, "nmf": nmf, "nmb": nmb,
            "maskA": np.ascontiguousarray(np.tile(mA, (1, 4))), "maskB": np.ascontiguousarray(np.tile(mB, (1, 4)))}


def prep_weights(inp):
    w = {}
    f = lambda a: np.asarray(a, dtype=np.float32)
    for n, s in WEIGHT_SHAPES.items():
        if n in inp:
            w[n] = np.ascontiguousarray(f(inp[n]).reshape(s))
    if "ssd_conv_w" in inp:
        cw = f(inp["ssd_conv_w"]).reshape(DEPTH, 5, 20, 128)
        w["ssd_conv_wl"] = np.ascontiguousarray(cw.transpose(0, 3, 2, 1).reshape(DEPTH, 128, 100))
        cb = f(inp["ssd_conv_b"]).reshape(DEPTH, 20, 128)
        w["ssd_conv_bl"] = np.ascontiguousarray(cb.transpose(0, 2, 1))
    if "sc_conv_w" in inp:
        sw = f(inp["sc_conv_w"]).reshape(DEPTH, 3, 6, 128)
        w["sc_conv_wl"] = np.ascontiguousarray(sw.transpose(0, 3, 2, 1).reshape(DEPTH, 128, 18))
        sn = f(inp["sc_norm"]).reshape(DEPTH, 6, 128)
        w["sc_norm_l"] = np.ascontiguousarray(sn.transpose(0, 2, 1))
    return w


def kernel(**inputs):
    L = 8192
    xs = [np.asarray(inputs["x_prompt"][i]) for i in range(2)] + [np.asarray(inputs["x_sample"][i]) for i in range(4)]
    xs = xs + [xs[0], xs[1]]
    w = prep_weights(inputs)
    consts = host_consts(L)
    nc = build(L)
    in_maps = []
    for c in range(8):
        m = {"x": np.ascontiguousarray(xs[c], dtype=np.float32)}
        m.update(w)
        m.update(consts)
        in_maps.append(m)
    res = run_bass_kernel_spmd(nc, in_maps, core_ids=list(range(8)))
    ys = [res.results[c]["y"] for c in range(6)]
    return (np.stack(ys[0:2], axis=0).astype(np.float32), np.stack(ys[2:6], axis=0).astype(np.float32))
```
